# Optimizing a Trainium2 kernel written in Bass

```python
import math
import jax, jax.numpy as jnp
from jax import lax
import numpy as np

D_MODEL = 1024
BATCH = 4
SEQ = 8192
DEPTH = 2

PLE_DIM = 256
N_EVEN = (DEPTH + 1) // 2
N_ODD = DEPTH // 2

DA_HEADS = 4
DA_HEAD_DIM = 64
DA_V_DIM = 2 * DA_HEAD_DIM
DA_QK_WIDTH = DA_HEADS * 2 * DA_HEAD_DIM
DA_WIDTH = DA_HEADS * DA_V_DIM

HG_HEADS = 4
HG_KDIM = 128
HG_VDIM = 128
HG_KWIDTH = HG_HEADS * HG_KDIM
HG_WIDTH = HG_HEADS * HG_VDIM
HG_CHUNK = 64

DL_HEADS = 16
DL_HEAD_DIM = D_MODEL // DL_HEADS
DL_PAIRS = ((128, 1), (512, 4), (2048, 16))

FFN_DIM = 4 * D_MODEL
ROPE_THETA = 500000.0
ROT_DIM = 64 // 4
ALPHA = (2 * DEPTH) ** 0.25
BETA = (8 * DEPTH) ** -0.25
LN_EPS = 1e-5
Q_BLOCK = 128

kernel_name = "hybrid_diffattn_hgrn2_dilated_deepnorm"


def layer_norm(x, g, b):
    xf = x.astype(jnp.float32)
    mu = jnp.mean(xf, -1, keepdims=True)
    var = jnp.mean(jnp.square(xf - mu), -1, keepdims=True)
    return ((xf - mu) * lax.rsqrt(var + LN_EPS) * g + b).astype(x.dtype)


def rms_norm(x, g):
    xf = x.astype(jnp.float32)
    return (xf * lax.rsqrt(jnp.mean(xf * xf, -1, keepdims=True) + LN_EPS) * g).astype(x.dtype)


def rope_tables(seq):
    inv = ROPE_THETA ** (-jnp.arange(0, ROT_DIM, 2, dtype=jnp.float32) / ROT_DIM)
    ang = jnp.arange(seq, dtype=jnp.float32)[:, None] * inv[None, :]
    return jnp.cos(ang), jnp.sin(ang)


def partial_rope(x, cos, sin):
    half = ROT_DIM // 2
    x1, x2, xp = x[..., :half], x[..., half:ROT_DIM], x[..., ROT_DIM:]
    c, s = cos.astype(x.dtype), sin.astype(x.dtype)
    return jnp.concatenate([x1 * c - x2 * s, x1 * s + x2 * c, xp], axis=-1)


def diff_attention(qa, ka, va, lam_params, sub_g, lam_init, cos, sin):
    B, S = qa.shape[:2]
    q = partial_rope(qa.transpose(0, 2, 3, 1, 4), cos, sin) * (DA_HEAD_DIM ** -0.5)
    k = partial_rope(ka.transpose(0, 2, 3, 1, 4), cos, sin)
    v = va.transpose(0, 2, 1, 3)
    lp = lam_params.astype(jnp.float32)
    lam = jnp.exp(jnp.sum(lp[0] * lp[1])) - jnp.exp(jnp.sum(lp[2] * lp[3])) + lam_init
    nb = S // Q_BLOCK
    qb = q.reshape(B, DA_HEADS, 2, nb, Q_BLOCK, DA_HEAD_DIM).transpose(3, 0, 1, 2, 4, 5)
    kpos = jnp.arange(S)

    def block(args):
        qi, i = args
        s = jnp.einsum('bhcqd,bhckd->bhcqk', qi, k).astype(jnp.float32)
        qpos = i * Q_BLOCK + jnp.arange(Q_BLOCK)
        s = jnp.where(kpos[None, :] <= qpos[:, None], s, -jnp.inf)
        pr = jax.nn.softmax(s, axis=-1)
        a = pr[:, :, 0] - lam * pr[:, :, 1]
        return jnp.einsum('bhqk,bhkd->bhqd', a.astype(v.dtype), v)

    o = lax.map(block, (qb, jnp.arange(nb)))
    o = o.transpose(1, 2, 0, 3, 4).reshape(B, DA_HEADS, S, DA_V_DIM)
    o = rms_norm(o, sub_g) * (1.0 - lam_init)
    return o.transpose(0, 2, 1, 3).reshape(B, S, DA_WIDTH)


def hgrn2(hq, hf, hi, hg, lb, out_g):
    B, S, H, dk = hq.shape
    dv = hi.shape[-1]
    f32 = jnp.float32
    q = jax.nn.silu(hq.astype(f32))
    f = lb + (1.0 - lb) * jax.nn.sigmoid(hf.astype(f32))
    kk = 1.0 - f
    logf = jnp.log(f)
    C = HG_CHUNK
    n = S // C

    def chunks(t):
        return t.reshape(B, n, C, H, t.shape[-1]).transpose(1, 0, 3, 2, 4)

    qc, kc, lc, vc = chunks(q), chunks(kk), chunks(logf), chunks(hi.astype(f32))
    b = jnp.cumsum(lc, axis=-2)
    b_last = b[..., C - 1:C, :]
    b_mid = b[..., C // 2 - 1:C // 2, :]
    a = jnp.einsum('nbhtk,nbhsk->nbhts', qc * jnp.exp(b - b_mid), kc * jnp.exp(b_mid - b))
    a = jnp.where(jnp.tril(jnp.ones((C, C), bool)), a, 0.0)
    o_intra = jnp.einsum('nbhts,nbhsv->nbhtv', a, vc)
    q_out = qc * jnp.exp(b)
    k_st = kc * jnp.exp(b_last - b)
    decay = jnp.exp(b_last[..., 0, :])

    def step(state, xs):
        qo, ks, vv, dc = xs
        o_inter = jnp.einsum('bhtk,bhkv->bhtv', qo, state)
        state = dc[..., None] * state + jnp.einsum('bhsk,bhsv->bhkv', ks, vv)
        return state, o_inter

    s0 = jnp.zeros((B, H, dk, dv), f32)
    _, o_inter = lax.scan(step, s0, (q_out, k_st, vc, decay))
    o = (o_intra + o_inter).transpose(1, 0, 3, 2, 4).reshape(B, S, H, dv)
    o = rms_norm(o, out_g) * jax.nn.silu(hg.astype(f32))
    return o.reshape(B, S, H * dv).astype(hi.dtype)


def even_mixer(x, w_in, w_out, lam_params, sub_g, lb, hg_g, lam_init, cos, sin):
    B, S, _ = x.shape
    h = x @ w_in
    sizes = (DA_QK_WIDTH, DA_QK_WIDTH, DA_WIDTH, HG_KWIDTH, HG_KWIDTH, HG_WIDTH, HG_WIDTH)
    cuts = [sum(sizes[:i + 1]) for i in range(len(sizes) - 1)]
    qa, ka, va, hq, hf, hi, hg = jnp.split(h, cuts, axis=-1)
    o_a = diff_attention(qa.reshape(B, S, DA_HEADS, 2, DA_HEAD_DIM),
                         ka.reshape(B, S, DA_HEADS, 2, DA_HEAD_DIM),
                         va.reshape(B, S, DA_HEADS, DA_V_DIM),
                         lam_params, sub_g, lam_init, cos, sin)
    o_b = hgrn2(hq.reshape(B, S, HG_HEADS, HG_KDIM), hf.reshape(B, S, HG_HEADS, HG_KDIM),
                hi.reshape(B, S, HG_HEADS, HG_VDIM), hg.reshape(B, S, HG_HEADS, HG_VDIM),
                lb.reshape(HG_HEADS, HG_KDIM), hg_g)
    return jnp.concatenate([o_a.astype(x.dtype), o_b.astype(x.dtype)], axis=-1) @ w_out


def dilated_branch(q, k, v, dil, span):
    B, H, S, hd = q.shape
    L = S // dil

    def to_res(t):
        return t.reshape(B, H, L, dil, hd).transpose(0, 1, 3, 2, 4)

    blk = span
    nb = -(-L // blk)
    pad = nb * blk - L

    def padl(t, front):
        return jnp.pad(t, ((0, 0), (0, 0), (0, 0), (front, pad), (0, 0)))

    qp = padl(to_res(q), 0).reshape(B, H, dil, nb, blk, hd)
    kp = padl(to_res(k), blk).reshape(B, H, dil, nb + 1, blk, hd)
    vp = padl(to_res(v), blk).reshape(B, H, dil, nb + 1, blk, hd)
    kw = jnp.concatenate([kp[:, :, :, :-1], kp[:, :, :, 1:]], axis=-2)
    vw = jnp.concatenate([vp[:, :, :, :-1], vp[:, :, :, 1:]], axis=-2)
    s = jnp.einsum('bhrnqd,bhrnkd->bhrnqk', qp, kw).astype(jnp.float32)
    a_idx = jnp.arange(blk)[:, None]
    c_idx = jnp.arange(2 * blk)[None, :]
    dist = blk + a_idx - c_idx
    key_idx = (jnp.arange(nb)[:, None, None] - 1) * blk + c_idx[None]
    valid = (dist >= 0)[None] & (dist <= span)[None] & (key_idx >= 0)
    s = jnp.where(valid, s, -jnp.inf)
    m = jnp.max(s, axis=-1, keepdims=True)
    e = jnp.exp(s - m)
    den = jnp.sum(e, axis=-1)
    o = jnp.einsum('bhrnqk,bhrnkd->bhrnqd', e.astype(v.dtype), vw) / den[..., None].astype(v.dtype)
    lse = m[..., 0] + jnp.log(den)
    o = o.reshape(B, H, dil, nb * blk, hd)[:, :, :, :L].transpose(0, 1, 3, 2, 4).reshape(B, H, S, hd)
    lse = lse.reshape(B, H, dil, nb * blk)[..., :L].transpose(0, 1, 3, 2).reshape(B, H, S)
    return o, lse


def odd_mixer(x, w_in, w_out, cos, sin):
    B, S, D = x.shape
    q, k, v = jnp.split(x @ w_in, 3, axis=-1)

    def heads(t):
        return t.reshape(B, S, DL_HEADS, DL_HEAD_DIM).transpose(0, 2, 1, 3)

    q = partial_rope(heads(q), cos, sin) * (DL_HEAD_DIM ** -0.5)
    k = partial_rope(heads(k), cos, sin)
    v = heads(v)
    outs, lses = [], []
    for (w, d) in DL_PAIRS:
        o, lse = dilated_branch(q, k, v, d, w // d)
        outs.append(o)
        lses.append(lse)
    wts = jax.nn.softmax(jnp.stack(lses, axis=0), axis=0)
    o = jnp.einsum('gbhs,gbhsd->bhsd', wts.astype(v.dtype), jnp.stack(outs, axis=0))
    return o.transpose(0, 2, 1, 3).reshape(B, S, D) @ w_out


def setup_inputs(seed: int = 0) -> dict:
    key = jax.random.key(seed)
    ks = jax.random.split(key, 20)
    f32 = jnp.float32

    def nrm(k, shape, s):
        return jax.random.normal(k, shape, f32) * s

    def gain(k, shape):
        return 1.0 + 0.02 * jax.random.normal(k, shape, f32)

    even_in = 2 * DA_QK_WIDTH + DA_WIDTH + 2 * HG_KWIDTH + 2 * HG_WIDTH
    return {
        "x": nrm(ks[0], (BATCH, SEQ, D_MODEL), 1.0),
        "p": nrm(ks[1], (DEPTH, BATCH, SEQ, PLE_DIM), 1.0),
        "ev_w_in": nrm(ks[2], (N_EVEN, D_MODEL, even_in), D_MODEL ** -0.5),
        "ev_w_out": nrm(ks[3], (N_EVEN, DA_WIDTH + HG_WIDTH, D_MODEL), BETA * (DA_WIDTH + HG_WIDTH) ** -0.5),
        "da_lambda": nrm(ks[4], (N_EVEN, 4, DA_HEAD_DIM), 0.1),
        "da_subln_g": gain(ks[5], (N_EVEN, DA_V_DIM)),
        "hg_lb_logits": nrm(ks[6], (N_EVEN + 1, HG_KWIDTH), 0.1),
        "hg_norm_g": gain(ks[7], (N_EVEN, HG_VDIM)),
        "od_w_in": nrm(ks[8], (N_ODD, D_MODEL, 3 * D_MODEL), D_MODEL ** -0.5),
        "od_w_out": nrm(ks[9], (N_ODD, D_MODEL, D_MODEL), BETA * D_MODEL ** -0.5),
        "ln1_g": gain(ks[10], (DEPTH, D_MODEL)),
        "ln1_b": nrm(ks[11], (DEPTH, D_MODEL), 0.02),
        "ffn_w1": nrm(ks[12], (DEPTH, D_MODEL, FFN_DIM), D_MODEL ** -0.5),
        "ffn_w2": nrm(ks[13], (DEPTH, FFN_DIM, D_MODEL), BETA * FFN_DIM ** -0.5),
        "ln2_g": gain(ks[14], (DEPTH, D_MODEL)),
        "ln2_b": nrm(ks[15], (DEPTH, D_MODEL), 0.02),
        "ple_w_proj": nrm(ks[16], (DEPTH, PLE_DIM, D_MODEL), PLE_DIM ** -0.5),
        "ple_w_gate": nrm(ks[17], (DEPTH, D_MODEL, D_MODEL), D_MODEL ** -0.5),
        "ple_norm_g": gain(ks[18], (DEPTH, D_MODEL)),
    }


def reference(x, p, ev_w_in, ev_w_out, da_lambda, da_subln_g, hg_lb_logits, hg_norm_g,
              od_w_in, od_w_out, ln1_g, ln1_b, ffn_w1, ffn_w2, ln2_g, ln2_b,
              ple_w_proj, ple_w_gate, ple_norm_g):
    S = x.shape[1]
    cos, sin = rope_tables(S)
    lb_all = jnp.cumsum(jax.nn.softmax(hg_lb_logits.astype(jnp.float32), axis=0), axis=0)
    for l in range(DEPTH):
        j = l // 2
        if l % 2 == 0:
            lam_init = 0.8 - 0.6 * math.exp(-0.3 * l)
            mix = even_mixer(x, ev_w_in[j], ev_w_out[j], da_lambda[j], da_subln_g[j],
                             lb_all[j], hg_norm_g[j], lam_init, cos, sin)
        else:
            mix = odd_mixer(x, od_w_in[j], od_w_out[j], cos, sin)
        x = layer_norm(ALPHA * x + mix, ln1_g[l], ln1_b[l])
        hdn = jnp.square(jax.nn.relu(x @ ffn_w1[l]))
        x = layer_norm(ALPHA * x + hdn @ ffn_w2[l], ln2_g[l], ln2_b[l])
        e = rms_norm(p[l] @ ple_w_proj[l], ple_norm_g[l])
        x = x + jax.nn.sigmoid(x @ ple_w_gate[l]) * e
    return x
```

```python
import contextlib
import math
import numpy as np
import ml_dtypes
import concourse.bass as bass
import concourse.mybir as mybir
from concourse.bass_utils import run_bass_kernel_spmd

F32 = mybir.dt.float32
BF16 = mybir.dt.bfloat16
AF = mybir.ActivationFunctionType
ALU = mybir.AluOpType
AX = mybir.AxisListType
NPBF = ml_dtypes.bfloat16

D = 1024
FFN = 4096
PLE = 256
ALPHA = 4.0 ** 0.25
LN_EPS = 1e-5
LAM_INIT0 = 0.8 - 0.6 * math.exp(-0.3 * 0)
NEG = -30000.0
A2_STOP = 99
ROPE_THETA = 500000.0


class Buf:
    __slots__ = ("t", "w", "r", "dsem", "psum")

    def __init__(self, t):
        self.t = t
        self.w = None
        self.r = []
        self.dsem = None
        self.psum = False

    def __getitem__(self, idx):
        return self.t[idx]


class Ring:
    def __init__(self, bufs):
        self.bufs = bufs
        self.i = 0

    def next(self):
        b = self.bufs[self.i % len(self.bufs)]
        self.i += 1
        return b


class Eng:
    def __init__(self, name, eng, semidx):
        self.name = name
        self.eng = eng
        self.semidx = semidx
        self.waited = {}


class KB:
    def __init__(self, nc):
        self.nc = nc
        self.es = contextlib.ExitStack()
        self.sems = []
        self.semcnt = []
        self.semdma = []
        self.free_dsems = []
        self.E = {}
        for name, eng in (("pe", nc.tensor), ("act", nc.scalar), ("dve", nc.vector),
                          ("pool", nc.gpsimd), ("sp", nc.sync)):
            si = self._newsem("e_" + name, False)
            self.E[name] = Eng(name, eng, si)
        self.n_inst = 0
        self.phase_bufs = []

    def _newsem(self, name, isdma):
        s = self.es.enter_context(self.nc.semaphore(name))
        self.sems.append(s)
        self.semcnt.append(0)
        self.semdma.append(isdma)
        return len(self.sems) - 1

    def get_dsem(self):
        if self.free_dsems:
            return self.free_dsems.pop()
        return self._newsem("d%d" % len(self.sems), True)

    def sb(self, ph, name, shape, dtype):
        self.uid = getattr(self, "uid", 0) + 1
        name = "%s_u%d" % (name, self.uid)
        b = Buf(ph.enter_context(self.nc.sbuf_tensor(name, list(shape), dtype)))
        self.phase_bufs.append(b)
        return b

    def uname(self, name):
        self.uid = getattr(self, "uid", 0) + 1
        return "%s_u%d" % (name, self.uid)

    def view(self, ap):
        b = Buf(ap)
        self.phase_bufs.append(b)
        return b

    def _wait(self, E, toks):
        for (si, val) in toks:
            if self.semdma[si]:
                val = self.semcnt[si]
            if si == E.semidx and E.name == "pe":
                continue
            if E.waited.get(si, 0) >= val:
                continue
            E.eng.wait_ge(self.sems[si], val)
            E.waited[si] = val
            self.n_inst += 1

    def _deps(self, reads, writes):
        need = []
        for b in reads:
            if b.w is not None:
                need.append(b.w)
            if b.psum:
                need.extend(b.r)
        for b in writes:
            if b.w is not None:
                need.append(b.w)
            need.extend(b.r)
        return need

    def _commit(self, tok, reads, writes):
        for b in reads:
            if b.psum:
                b.w = tok
                b.r = []
            else:
                b.r.append(tok)
        for b in writes:
            b.w = tok
            b.r = []

    def op(self, en, fn, reads=(), writes=()):
        E = self.E[en]
        self._wait(E, self._deps(reads, writes))
        inst = fn(E.eng)
        si = E.semidx
        self.semcnt[si] += 1
        inst.then_inc(self.sems[si], 1)
        tok = (si, self.semcnt[si])
        self._commit(tok, reads, writes)
        self.n_inst += 1
        return tok

    def dma(self, pairs, reads=(), writes=(), sembuf=None, q="sp"):
        E = self.E[q]
        self._wait(E, self._deps(reads, writes))
        if sembuf.dsem is None:
            sembuf.dsem = self.get_dsem()
        si = sembuf.dsem
        for (o, i) in pairs:
            E.eng.dma_start(out=o, in_=i).then_inc(self.sems[si], 16)
            self.semcnt[si] += 16
            self.n_inst += 1
        tok = (si, self.semcnt[si])
        self._commit(tok, reads, writes)
        return tok

    def barrier(self):
        allt = [(si, self.semcnt[si]) for si in range(len(self.sems)) if self.semcnt[si] > 0]
        for E in self.E.values():
            self._wait(E, allt)

    def end_phase(self):
        self.barrier()
        for b in self.phase_bufs:
            if b.dsem is not None:
                self.free_dsems.append(b.dsem)
                b.dsem = None
        self.phase_bufs = []


def TT(kb, en, out, a, b, op, reads, writes):
    return kb.op(en, lambda e: e.tensor_tensor(out=out, in0=a, in1=b, op=op), reads, writes)


def TS(kb, en, out, a, s1, s2, op0, op1, reads, writes):
    if op1 is None:
        return kb.op(en, lambda e: e.tensor_scalar(out=out, in0=a, scalar1=s1, scalar2=None, op0=op0), reads, writes)
    return kb.op(en, lambda e: e.tensor_scalar(out=out, in0=a, scalar1=s1, scalar2=s2, op0=op0, op1=op1), reads, writes)


def STT(kb, out, a, s, b, op0, op1, reads, writes):
    return kb.op("dve", lambda e: e.scalar_tensor_tensor(out=out, in0=a, scalar=s, in1=b, op0=op0, op1=op1), reads, writes)


def ACT(kb, out, in_, func, reads, writes, scale=1.0, bias=0.0, accum=None):
    if accum is not None:
        return kb.op("act", lambda e: e.activation(out=out, in_=in_, func=func, scale=scale, bias=bias, accum_out=accum), reads, writes)
    return kb.op("act", lambda e: e.activation(out=out, in_=in_, func=func, scale=scale, bias=bias), reads, writes)


def CP(kb, en, out, in_, reads, writes):
    if en == "act":
        return kb.op("act", lambda e: e.copy(out=out, in_=in_), reads, writes)
    return kb.op(en, lambda e: e.tensor_copy(out=out, in_=in_), reads, writes)


def rstd_act(kb, out, in_, tmp, scale, reads_buf, tmp_buf, out_buf):
    ACT(kb, tmp, in_, AF.Ln, [reads_buf], [tmp_buf], scale=scale, bias=LN_EPS)
    ACT(kb, out, tmp, AF.Exp, [tmp_buf], [out_buf], scale=-0.5)


def sigmoid_exp(kb, out, in_, in_buf, t1, t2, out_buf, engs=None):
    ACT(kb, t1[:], in_, AF.Exp, [in_buf], [t1], scale=-1.0)
    ACT(kb, t2[:], t1[:], AF.Ln, [t1], [t2], bias=1.0)
    ACT(kb, out, t2[:], AF.Exp, [t2], [out_buf], scale=-1.0)


def run_pairs(n, body, load, width=2, prefetch=2):
    state = {"n": 0}

    def ensure(upto):
        while state["n"] <= min(upto, n - 1):
            load(state["n"])
            state["n"] += 1
    j = 0
    while j < n:
        grp = list(range(j, min(n, j + width)))
        ensure(grp[-1] + prefetch)
        alive = [body(b) for b in grp]
        while alive:
            for g in list(alive):
                try:
                    next(g)
                except StopIteration:
                    alive.remove(g)
        j += width


class PSUM:
    def __init__(self, kb, ph):
        kb.uid = getattr(kb, "uid", 0) + 1
        self.t = ph.enter_context(kb.nc.psum_tensor("PSALL_u%d" % kb.uid, [128, 4096], F32))
        self.kb = kb

    def bank(self, k, dtype=F32):
        ap = self.t[:, k * 512:(k + 1) * 512]
        if dtype == BF16:
            ap = ap.bitcast(BF16)
        b = self.kb.view(ap)
        b.psum = True
        return b


def load_weight(kb, W, src, col_slices, krows):
    ncols = sum(hi - lo for lo, hi in col_slices)
    with contextlib.ExitStack() as st:
        PW = 2048
        stage = Ring([kb.sb(st, "wst%d" % i, [128, PW], F32) for i in range(3)])
        engs = ["dve", "dve", "act"]
        n = 0
        for kc in range(krows):
            dst0 = 0
            for (lo, hi) in col_slices:
                c = lo
                while c < hi:
                    w = min(PW, hi - c)
                    sbuf = stage.next()
                    kb.dma([(sbuf[:, 0:w], src[kc * 128:(kc + 1) * 128, c:c + w])], writes=[sbuf], sembuf=sbuf)
                    CP(kb, engs[n % 2], W[kc][:, dst0:dst0 + w], sbuf[:, 0:w], [sbuf], [W[kc]])
                    n += 1
                    dst0 += w
                    c += w
        kb.barrier()


def transposes(kb, psT, src, ident, nblk, reads):
    def f(e):
        for c in range(nblk):
            i = e.transpose(psT[:, c * 128:(c + 1) * 128], src[:, c * 128:(c + 1) * 128], ident[:])
        return i
    return kb.op("pe", f, reads + [ident], [psT])


def rope(kb, src, dst, rp, ng, tmps):
    s3 = src[:].rearrange("p (g d) -> p g d", g=ng)
    d3 = dst[:].rearrange("p (g d) -> p g d", g=ng)
    cos = rp[:, 0:ng * 8].rearrange("p (g d) -> p g d", g=ng)
    sin = rp[:, ng * 8:2 * ng * 8].rearrange("p (g d) -> p g d", g=ng)
    t1, t2, t3, t4 = tmps
    v = lambda t: t[:, 0:ng * 8].rearrange("p (g d) -> p g d", g=ng)
    TT(kb, "dve", v(t1), s3[:, :, 0:8], cos, ALU.mult, [src, rp], [t1])
    TT(kb, "dve", v(t2), s3[:, :, 8:16], sin, ALU.mult, [src, rp], [t2])
    TT(kb, "dve", v(t3), s3[:, :, 0:8], sin, ALU.mult, [src, rp], [t3])
    TT(kb, "dve", v(t4), s3[:, :, 8:16], cos, ALU.mult, [src, rp], [t4])
    TT(kb, "dve", d3[:, :, 0:8], v(t1), v(t2), ALU.subtract, [t1, t2], [dst])
    TT(kb, "dve", d3[:, :, 8:16], v(t3), v(t4), ALU.add, [t3, t4], [dst])
    CP(kb, "dve", d3[:, :, 16:64], s3[:, :, 16:64], [src], [dst])


def layer_norm(kb, y, g, b, st, mv, sd, rs, tmp):
    ACT(kb, tmp[:], y[:], AF.Copy, [y], [tmp, st], accum=st[:, 0:1])
    ACT(kb, tmp[:], y[:], AF.Square, [y], [tmp, st], accum=st[:, 1:2])
    TS(kb, "dve", mv[:, 0:1], st[:, 0:1], 1.0 / D, None, ALU.mult, None, [st], [mv])
    TT(kb, "dve", mv[:, 1:2], mv[:, 0:1], mv[:, 0:1], ALU.mult, [mv], [mv])
    STT(kb, mv[:, 2:3], st[:, 1:2], 1.0 / D, mv[:, 1:2], ALU.mult, ALU.subtract, [st, mv], [mv])
    rstd_act(kb, rs[:, 0:1], mv[:, 2:3], sd[:, 0:1], 1.0, mv, sd, rs)
    STT(kb, rs[:, 1:2], mv[:, 0:1], -1.0, rs[:, 0:1], ALU.mult, ALU.mult, [mv, rs], [rs])
    kb.op("act", lambda e: e.activation(out=tmp[:], in_=y[:], func=AF.Identity, scale=rs[:, 0:1], bias=rs[:, 1:2]),
          [y, rs], [tmp])
    TT(kb, "dve", y[:], tmp[:], g[:], ALU.mult, [tmp, g], [y])
    TT(kb, "dve", tmp[:], y[:], b[:], ALU.add, [y, b], [tmp])
    return tmp


def phase_C(kb, nbo, mix_src, xres, w_out, ln_g, ln_b, xa_out, consts, mix_loader=None):
    with contextlib.ExitStack() as ph:
        Wt = ph.enter_context(kb.nc.sbuf_tensor(kb.uname("C_W"), [128, 8, 1024], BF16))
        W = [kb.view(Wt[:, k, :]) for k in range(8)]
        load_weight(kb, W, w_out, [(0, 1024)], 8)
        ident = kb.sb(ph, "C_id", [128, 128], BF16)
        kb.dma([(ident[:], consts["ident"][:, :])], writes=[ident], sembuf=ident)
        gt = kb.sb(ph, "C_g", [128, 1024], F32)
        bt = kb.sb(ph, "C_b", [128, 1024], F32)
        kb.dma([(gt[:], ln_g.partition_broadcast(128))], writes=[gt], sembuf=gt)
        kb.dma([(bt[:], ln_b.partition_broadcast(128))], writes=[bt], sembuf=bt)
        ps = PSUM(kb, ph)
        pT = Ring([ps.bank(0, BF16), ps.bank(1, BF16)])
        po = Ring([[ps.bank(2), ps.bank(3)], [ps.bank(4), ps.bank(5)]])
        mixr = Ring([kb.sb(ph, "C_mix%d" % i, [128, 1024], BF16) for i in range(6)])
        xr = Ring([kb.sb(ph, "C_x%d" % i, [128, 1024], F32) for i in range(6)])
        mT = Ring([kb.sb(ph, "C_mT%d" % i, [128, 1024], BF16) for i in range(4)])
        yr = Ring([kb.sb(ph, "C_y%d" % i, [128, 1024], F32) for i in range(4)])
        tmpr = Ring([kb.sb(ph, "C_t%d" % i, [128, 1024], F32) for i in range(4)])
        sm = Ring([[kb.sb(ph, "C_s%d_%d" % (i, k), [128, 12], F32) for k in range(4)] for i in range(4)])
        extra = mix_loader.alloc(kb, ph) if mix_loader is not None else None
        loaded = {}

        def load(j):
            x = xr.next()
            kb.dma([(x[:], xres[j * 128:(j + 1) * 128, :])], writes=[x], sembuf=x)
            m = mixr.next()
            if mix_loader is None:
                kb.dma([(m[:], mix_src[j * 128:(j + 1) * 128, :])], writes=[m], sembuf=m)
            else:
                mix_loader.load(kb, extra, j, m)
            loaded[j] = (x, m)

        def body(j):
            x, m = loaded.pop(j)
            p = pT.next()
            transposes(kb, p, m, ident, 8, [m])
            t = mT.next()
            CP(kb, "act", t[:], p[:], [p], [t])
            yield
            pa, pb = po.next()
            for half, pp in ((0, pa), (1, pb)):
                def f(e, half=half, pp=pp):
                    for k in range(8):
                        i = e.matmul(pp[:], lhsT=t[:, k * 128:(k + 1) * 128], rhs=W[k][:, half * 512:(half + 1) * 512],
                                     start=(k == 0), stop=(k == 7))
                    return i
                kb.op("pe", f, [t] + W, [pp])
            y = yr.next()
            STT(kb, y[:, 0:512], x[:, 0:512], ALPHA, pa[:], ALU.mult, ALU.add, [x, pa], [y])
            STT(kb, y[:, 512:1024], x[:, 512:1024], ALPHA, pb[:], ALU.mult, ALU.add, [x, pb], [y])
            yield
            s = sm.next()
            o = layer_norm(kb, y, gt, bt, s[0], s[1], s[2], s[3], tmpr.next())
            kb.dma([(xa_out[j * 128:(j + 1) * 128, :], o[:])], reads=[o], sembuf=o)

        run_pairs(nbo, body, load, width=4)
        kb.end_phase()


def phase_D1(kb, nbo, xa, w1, w2, ln_g, ln_b, xb_out, consts):
    assert nbo % 4 == 0
    with contextlib.ExitStack() as ph:
        W1t = ph.enter_context(kb.nc.sbuf_tensor(kb.uname("D_W1"), [128, 8, FFN], BF16))
        W2t = ph.enter_context(kb.nc.sbuf_tensor(kb.uname("D_W2"), [128, 32, D], BF16))
        W1 = [kb.view(W1t[:, k, :]) for k in range(8)]
        W2 = [kb.view(W2t[:, k, :]) for k in range(32)]
        load_weight(kb, W1, w1, [(0, FFN)], 8)
        load_weight(kb, W2, w2, [(0, D)], 32)
        ident = kb.sb(ph, "D_id", [128, 128], BF16)
        kb.dma([(ident[:], consts["ident"][:, :])], writes=[ident], sembuf=ident)
        gt = kb.sb(ph, "D_g", [128, 1024], F32)
        bt = kb.sb(ph, "D_b", [128, 1024], F32)
        kb.dma([(gt[:], ln_g.partition_broadcast(128))], writes=[gt], sembuf=gt)
        kb.dma([(bt[:], ln_b.partition_broadcast(128))], writes=[bt], sembuf=bt)
        ps = PSUM(kb, ph)
        pT = Ring([ps.bank(0, BF16)])
        ph_ = Ring([ps.bank(1), ps.bank(2), ps.bank(3)])
        po = Ring([[ps.bank(4), ps.bank(5)], [ps.bank(6), ps.bank(7)]])
        xr = Ring([kb.sb(ph, "D_x%d" % i, [128, 1024], F32) for i in range(2)])
        xbr = Ring([kb.sb(ph, "D_xb%d" % i, [128, 1024], BF16) for i in range(2)])
        xT = kb.sb(ph, "D_xT", [128, 8, 512], BF16)
        hT = [kb.sb(ph, "D_hT%d" % c, [128, 512], BF16) for c in range(32)]
        rl = Ring([kb.sb(ph, "D_rl%d" % i, [128, 512], F32) for i in range(2)])
        yr = Ring([kb.sb(ph, "D_y%d" % i, [128, 1024], F32) for i in range(1)])
        tmpr = Ring([kb.sb(ph, "D_t%d" % i, [128, 1024], F32) for i in range(2)])
        sm = Ring([[kb.sb(ph, "D_s%d_%d" % (i, k), [128, 12], F32) for k in range(4)] for i in range(2)])
        ng = nbo // 4
        for g in range(ng):
            for tb in range(4):
                x = xr.next()
                j = g * 4 + tb
                kb.dma([(x[:], xa[j * 128:(j + 1) * 128, :])], writes=[x], sembuf=x)
                xb = xbr.next()
                CP(kb, "dve" if tb % 2 == 0 else "dve", xb[:], x[:], [x], [xb])
                p = pT.next()
                transposes(kb, p, xb, ident, 8, [xb])
                CP(kb, "act", xT[:, :, tb * 128:(tb + 1) * 128], p[:].rearrange("p (k t) -> p k t", k=8), [p], [xT])
            for c in range(32):
                pp = ph_.next()

                def f(e, c=c, pp=pp):
                    for k in range(8):
                        i = e.matmul(pp[:], lhsT=W1[k][:, c * 128:(c + 1) * 128], rhs=xT[:, k, :], start=(k == 0), stop=(k == 7))
                    return i
                kb.op("pe", f, [xT] + W1, [pp])
                r = rl.next()
                ACT(kb, r[:], pp[:], AF.Relu, [pp], [r])
                TT(kb, "dve" if c % 2 == 0 else "dve", hT[c][:], r[:], r[:], ALU.mult, [r], [hT[c]])
            for tb in range(4):
                pa, pb = po.next()
                for half, pp in ((0, pa), (1, pb)):
                    def f(e, half=half, pp=pp, tb=tb):
                        for c in range(32):
                            i = e.matmul(pp[:], lhsT=hT[c][:, tb * 128:(tb + 1) * 128], rhs=W2[c][:, half * 512:(half + 1) * 512],
                                         start=(c == 0), stop=(c == 31))
                        return i
                    kb.op("pe", f, hT + W2, [pp])
                x = xr.next()
                j = g * 4 + tb
                kb.dma([(x[:], xa[j * 128:(j + 1) * 128, :])], writes=[x], sembuf=x)
                y = yr.next()
                STT(kb, y[:, 0:512], x[:, 0:512], ALPHA, pa[:], ALU.mult, ALU.add, [x, pa], [y])
                STT(kb, y[:, 512:1024], x[:, 512:1024], ALPHA, pb[:], ALU.mult, ALU.add, [x, pb], [y])
                s = sm.next()
                o = layer_norm(kb, y, gt, bt, s[0], s[1], s[2], s[3], tmpr.next())
                kb.dma([(xb_out[j * 128:(j + 1) * 128, :], o[:])], reads=[o], sembuf=o)
        kb.end_phase()


def phase_D2(kb, nbo, xb_in, p_in, w_gate, w_proj, norm_g, out, consts):
    with contextlib.ExitStack() as ph:
        Wgt = ph.enter_context(kb.nc.sbuf_tensor(kb.uname("E_Wg"), [128, 8, D], BF16))
        Wpt = ph.enter_context(kb.nc.sbuf_tensor(kb.uname("E_Wp"), [128, 2, D], BF16))
        Wg = [kb.view(Wgt[:, k, :]) for k in range(8)]
        Wp = [kb.view(Wpt[:, k, :]) for k in range(2)]
        load_weight(kb, Wg, w_gate, [(0, D)], 8)
        load_weight(kb, Wp, w_proj, [(0, D)], 2)
        ident = kb.sb(ph, "E_id", [128, 128], BF16)
        kb.dma([(ident[:], consts["ident"][:, :])], writes=[ident], sembuf=ident)
        gt = kb.sb(ph, "E_g", [128, 1024], F32)
        kb.dma([(gt[:], norm_g.partition_broadcast(128))], writes=[gt], sembuf=gt)
        ps = PSUM(kb, ph)
        pT = Ring([ps.bank(0, BF16), ps.bank(1, BF16)])
        pg = Ring([[ps.bank(2), ps.bank(3)]])
        pe_ = Ring([[ps.bank(4), ps.bank(5)], [ps.bank(6), ps.bank(7)]])
        er = Ring([kb.sb(ph, "E_er%d" % i, [128, 1024], F32) for i in range(4)])
        xr = Ring([kb.sb(ph, "E_x%d" % i, [128, 1024], F32) for i in range(6)])
        pr = Ring([kb.sb(ph, "E_p%d" % i, [128, 256], F32) for i in range(6)])
        xbr = Ring([kb.sb(ph, "E_xb%d" % i, [128, 1280], BF16) for i in range(2)])
        xT = Ring([kb.sb(ph, "E_xT%d" % i, [128, 1280], BF16) for i in range(4)])
        gsb = Ring([kb.sb(ph, "E_gs%d" % i, [128, 1024], F32) for i in range(4)])
        gt1 = Ring([kb.sb(ph, "E_g1%d" % i, [128, 1024], F32) for i in range(4)])
        gt2 = Ring([kb.sb(ph, "E_g2%d" % i, [128, 1024], F32) for i in range(4)])
        esb = Ring([kb.sb(ph, "E_es%d" % i, [128, 1024], F32) for i in range(2)])
        e2 = Ring([kb.sb(ph, "E_e2%d" % i, [128, 1024], F32) for i in range(2)])
        junk = Ring([kb.sb(ph, "E_jk%d" % i, [128, 512], F32) for i in range(2)])
        outr = Ring([kb.sb(ph, "E_o%d" % i, [128, 1024], F32) for i in range(2)])
        sm = Ring([[kb.sb(ph, "E_s%d_%d" % (i, k), [128, 4], F32) for k in range(4)] for i in range(4)])
        loaded = {}

        def load(j):
            x = xr.next()
            kb.dma([(x[:], xb_in[j * 128:(j + 1) * 128, :])], writes=[x], sembuf=x)
            p = pr.next()
            kb.dma([(p[:], p_in[j * 128:(j + 1) * 128, :])], writes=[p], sembuf=p)
            loaded[j] = (x, p)

        def body(j):
            x, p = loaded.pop(j)
            xb = xbr.next()
            CP(kb, "dve", xb[:, 0:1024], x[:], [x], [xb])
            CP(kb, "dve", xb[:, 1024:1280], p[:], [p], [xb])
            pt1 = pT.next()
            transposes(kb, pt1, xb, ident, 8, [xb])
            t = xT.next()
            CP(kb, "act", t[:, 0:1024], pt1[:], [pt1], [t])
            pt2 = pT.next()

            def f2(e, xb=xb, pt2=pt2):
                for c in range(2):
                    i = e.transpose(pt2[:, c * 128:(c + 1) * 128], xb[:, 1024 + c * 128:1024 + (c + 1) * 128], ident[:])
                return i
            kb.op("pe", f2, [xb, ident], [pt2])
            CP(kb, "dve", t[:, 1024:1280], pt2[:, 0:256], [pt2], [t])
            yield
            ga, gb = pg.next()
            ea, eb = pe_.next()
            for half, pp in ((0, ga), (1, gb)):
                def f(e, half=half, pp=pp, t=t):
                    for k in range(8):
                        i = e.matmul(pp[:], lhsT=t[:, k * 128:(k + 1) * 128], rhs=Wg[k][:, half * 512:(half + 1) * 512],
                                     start=(k == 0), stop=(k == 7))
                    return i
                kb.op("pe", f, [t] + Wg, [pp])
            for half, pp in ((0, ea), (1, eb)):
                def f(e, half=half, pp=pp, t=t):
                    for k in range(2):
                        i = e.matmul(pp[:], lhsT=t[:, 1024 + k * 128:1024 + (k + 1) * 128], rhs=Wp[k][:, half * 512:(half + 1) * 512],
                                     start=(k == 0), stop=(k == 1))
                    return i
                kb.op("pe", f, [t] + Wp, [pp])
            s = sm.next()
            gs = gsb.next()
            g1_, g2_ = gt1.next(), gt2.next()
            ACT(kb, g1_[:, 0:512], ga[:], AF.Exp, [ga], [g1_], scale=-1.0)
            ACT(kb, g1_[:, 512:1024], gb[:], AF.Exp, [gb], [g1_], scale=-1.0)
            jk = junk.next()
            ACT(kb, jk[:], ea[:], AF.Square, [ea], [jk, s[0]], accum=s[0][:, 0:1])
            jk = junk.next()
            ACT(kb, jk[:], eb[:], AF.Square, [eb], [jk, s[0]], accum=s[0][:, 1:2])
            e_r = er.next()
            CP(kb, "dve", e_r[:, 0:512], ea[:], [ea], [e_r])
            CP(kb, "dve", e_r[:, 512:1024], eb[:], [eb], [e_r])
            yield
            ACT(kb, g2_[:], g1_[:], AF.Ln, [g1_], [g2_], bias=1.0)
            ACT(kb, gs[:], g2_[:], AF.Exp, [g2_], [gs], scale=-1.0)
            TT(kb, "dve", s[1][:, 0:1], s[0][:, 0:1], s[0][:, 1:2], ALU.add, [s[0]], [s[1]])
            rstd_act(kb, s[3][:, 0:1], s[1][:, 0:1], s[2][:, 0:1], 1.0 / D, s[1], s[2], s[3])
            es = esb.next()
            kb.op("act", lambda e, es=es, e_r=e_r, s=s: e.activation(out=es[:], in_=e_r[:], func=AF.Copy, scale=s[3][:, 0:1]), [e_r, s[3]], [es])
            ee = e2.next()
            TT(kb, "dve", ee[:], es[:], gt[:], ALU.mult, [es, gt], [ee])
            TT(kb, "dve", es[:], ee[:], gs[:], ALU.mult, [ee, gs], [es])
            o = outr.next()
            TT(kb, "dve", o[:], es[:], x[:], ALU.add, [es, x], [o])
            kb.dma([(out[j * 128:(j + 1) * 128, :], o[:])], reads=[o], sembuf=o)

        run_pairs(nbo, body, load, width=4)
        kb.end_phase()


def hgrn_gate_tables(kb, ph, lb_logits):
    lt = kb.sb(ph, "lbl", [128, 2, 512], F32)
    kb.dma([(lt[:], lb_logits.partition_broadcast(128))], writes=[lt], sembuf=lt)
    dl = kb.sb(ph, "lbd", [128, 512], F32)
    lb = kb.sb(ph, "lb", [128, 512], F32)
    oml = kb.sb(ph, "oml", [128, 512], F32)
    TT(kb, "dve", dl[:], lt[:, 0, :], lt[:, 1, :], ALU.subtract, [lt], [dl])
    lt1 = kb.sb(ph, "lbt1", [128, 512], F32)
    lt2 = kb.sb(ph, "lbt2", [128, 512], F32)
    sigmoid_exp(kb, lb[:], dl[:], dl, lt1, lt2, lb)
    TS(kb, "dve", oml[:], lb[:], -1.0, 1.0, ALU.mult, ALU.add, [lb], [oml])
    return lb, oml


def phase_A1(kb, S, x_all, rope_all, w_in, lb_logits, consts, KT, V, SS, blended=True):
    nb = S // 128
    with contextlib.ExitStack() as ph:
        Wt = ph.enter_context(kb.nc.sbuf_tensor(kb.uname("A1_W"), [128, 8, 2048], BF16))
        W = [kb.view(Wt[:, k, :]) for k in range(8)]
        load_weight(kb, W, w_in, [(512, 1024), (1024, 1536), (2048, 2560), (2560, 3072)], 8)
        ident = kb.sb(ph, "A1_id", [128, 128], BF16)
        kb.dma([(ident[:], consts["ident"][:, :])], writes=[ident], sembuf=ident)
        cU = kb.sb(ph, "A1_cU", [128, 4, 128], F32)
        kb.dma([(cU[:], consts["cU"][:, :, :])], writes=[cU], sembuf=cU)
        sel = kb.sb(ph, "A1_sel", [128, 2], F32)
        kb.dma([(sel[:], consts["sel"][:, :])], writes=[sel], sembuf=sel)
        lb, oml = hgrn_gate_tables(kb, ph, lb_logits)
        ps = PSUM(kb, ph)
        pT = Ring([ps.bank(0, BF16)])
        pp = Ring([ps.bank(1), ps.bank(2), ps.bank(3)])
        pkT = Ring([ps.bank(4, BF16)])
        pbd = Ring([ps.bank(5)])
        pdec = Ring([ps.bank(6)])
        pS = Ring([ps.bank(7)])
        xf = Ring([kb.sb(ph, "A1_xf%d" % i, [128, 1024], F32) for i in range(5)])
        rp = Ring([kb.sb(ph, "A1_rp%d" % i, [128, 128], F32) for i in range(5)])
        xb = Ring([kb.sb(ph, "A1_xb%d" % i, [128, 1024], BF16) for i in range(2)])
        xT = Ring([kb.sb(ph, "A1_xT%d" % i, [128, 1024], BF16) for i in range(3)])
        kf = Ring([kb.sb(ph, "A1_kf%d" % i, [128, 512], F32) for i in range(3)])
        kbf = Ring([kb.sb(ph, "A1_kb%d" % i, [128, 512], BF16) for i in range(2)])
        rt = Ring([[kb.sb(ph, "A1_rt%d_%d" % (i, k), [128, 64], F32) for k in range(4)] for i in range(2)])
        kTs = Ring([kb.sb(ph, "A1_kT%d" % i, [128, 512], BF16) for i in range(3)])
        vb = Ring([kb.sb(ph, "A1_vb%d" % i, [128, 516], BF16) for i in range(3)])
        sg = Ring([kb.sb(ph, "A1_sg%d" % i, [128, 512], F32) for i in range(3)])
        sgt1 = Ring([kb.sb(ph, "A1_sgt1%d" % i, [128, 512], F32) for i in range(2)])
        sgt2 = Ring([kb.sb(ph, "A1_sgt2%d" % i, [128, 512], F32) for i in range(2)])
        hib = Ring([kb.sb(ph, "A1_hi%d" % i, [128, 512], BF16) for i in range(3)])
        f1 = Ring([kb.sb(ph, "A1_f1%d" % i, [128, 512], F32) for i in range(2)])
        ff = Ring([kb.sb(ph, "A1_ff%d" % i, [128, 512], F32) for i in range(2)])
        lf = Ring([kb.sb(ph, "A1_lf%d" % i, [128, 512], F32) for i in range(2)])
        kk = Ring([kb.sb(ph, "A1_kk%d" % i, [128, 512], F32) for i in range(3)])
        ed = Ring([kb.sb(ph, "A1_ed%d" % i, [128, 512], F32) for i in range(3)])
        kst = Ring([kb.sb(ph, "A1_ks%d" % i, [128, 512], BF16) for i in range(2)])
        dec = Ring([kb.sb(ph, "A1_dc%d" % i, [128, 8], F32) for i in range(3)])
        Sr = Ring([kb.sb(ph, "A1_S%d" % i, [128, 512], F32) for i in range(2)])
        Sa = [kb.sb(ph, "A1_Sa%d" % c, [128, 512], F32) for c in range(2)]
        Sb = Ring([kb.sb(ph, "A1_Sb%d" % i, [128, 512], BF16) for i in range(4)])
        for v in vb.bufs:
            kb.op("pool", lambda e, v=v: e.memset(v[:], 1.0), [], [v])
        S = Sr.next()
        kb.op("pool", lambda e: e.memset(S[:], 0.0), [], [S])
        loaded = {}

        def load(tb):
            x = xf.next()
            kb.dma([(x[:], x_all[tb * 128:(tb + 1) * 128, :])], writes=[x], sembuf=x)
            r = rp.next()
            kb.dma([(r[:], rope_all[tb * 128:(tb + 1) * 128, :])], writes=[r], sembuf=r)
            loaded[tb] = (x, r)

        def mm_group(pbuf, t, col0):
            def f(e):
                for k in range(8):
                    i = e.matmul(pbuf[:], lhsT=t[:, k * 128:(k + 1) * 128], rhs=W[k][:, col0:col0 + 512], start=(k == 0), stop=(k == 7))
                return i
            kb.op("pe", f, [t] + W, [pbuf])

        stS = {'S': S}

        def body(tb):
            x, r = loaded.pop(tb)
            b = xb.next()
            CP(kb, "dve", b[:], x[:], [x], [b])
            p = pT.next()
            transposes(kb, p, b, ident, 8, [b])
            t = xT.next()
            CP(kb, "act", t[:], p[:], [p], [t])
            yield
            pk = pp.next()
            mm_group(pk, t, 0)
            k_f = kf.next()
            CP(kb, "act", k_f[:], pk[:], [pk], [k_f])
            pv = pp.next()
            mm_group(pv, t, 512)
            v = vb.next()
            CP(kb, "act", v[:].rearrange("p (h d) -> p h d", h=4)[:, :, 0:128], pv[:].rearrange("p (h d) -> p h d", h=4), [pv], [v])
            kb.dma([(V[tb, :, :], v[:])], reads=[v], sembuf=v)
            phf = pp.next()
            mm_group(phf, t, 1024)
            s_g = sg.next()
            sigmoid_exp(kb, s_g[:], phf[:], phf, sgt1.next(), sgt2.next(), s_g)
            phi = pp.next()
            mm_group(phi, t, 1536)
            h_i = hib.next()
            CP(kb, "dve", h_i[:], phi[:], [phi], [h_i])
            yield
            k_b = kbf.next()
            rope(kb, k_f, k_b, r, 8, rt.next())
            pkt = pkT.next()
            transposes(kb, pkt, k_b, ident, 4, [k_b])
            kt = kTs.next()
            CP(kb, "act", kt[:], pkt[:, 0:512], [pkt], [kt])
            kb.dma([(KT[tb, :, :], kt[:])], reads=[kt], sembuf=kt)
            f_1 = f1.next()
            TT(kb, "dve", f_1[:], s_g[:], oml[:], ALU.mult, [s_g, oml], [f_1])
            f_ = ff.next()
            TT(kb, "dve", f_[:], f_1[:], lb[:], ALU.add, [f_1, lb], [f_])
            l_f = lf.next()
            ACT(kb, l_f[:], f_[:], AF.Ln, [f_], [l_f])
            k_k = kk.next()
            ACT(kb, k_k[:], f_[:], AF.Copy, [f_], [k_k], scale=-1.0, bias=1.0)
            pb = pbd.next()
            kb.op("pe", lambda e: e.matmul(pb[:], lhsT=cU[:, 2, :], rhs=l_f[:], start=True, stop=True), [cU, l_f], [pb])
            pd = pdec.next()

            def fdec(e, pd=pd, l_f=l_f):
                for h in range(4):
                    i = e.matmul(pd[:, h * 2:(h + 1) * 2], lhsT=l_f[:, h * 128:(h + 1) * 128], rhs=cU[:, 3, 0:2], start=True, stop=True)
                return i
            kb.op("pe", fdec, [cU, l_f], [pd])
            e_d = ed.next()
            ACT(kb, e_d[:], pb[:], AF.Exp, [pb], [e_d])
            d_c = dec.next()
            ACT(kb, d_c[:], pd[:, 0:8], AF.Exp, [pd], [d_c])
            yield
            k_s = kst.next()
            TT(kb, "dve", k_s[:], k_k[:], e_d[:], ALU.mult, [k_k, e_d], [k_s])
            j = tb // 2
            for c in range(2):
                if not blended:
                    sb_ = Sb.next()
                    CP(kb, "act", sb_[:], stS['S'][:], [stS['S']], [sb_])
                    kb.dma([(SS[tb, c, :, :], sb_[:])], reads=[sb_], sembuf=sb_)
                elif tb % 2 == 0:
                    TS(kb, "dve", Sa[c][:], stS['S'][:], sel[:, 0:1], None, ALU.mult, None, [stS['S'], sel], [Sa[c]])
                else:
                    sb_ = Sb.next()
                    STT(kb, sb_[:], stS['S'][:], sel[:, 1:2], Sa[c][:], ALU.mult, ALU.add, [stS['S'], sel, Sa[c]], [sb_])
                    kb.dma([(SS[j, c, :, :], sb_[:])], reads=[sb_], sembuf=sb_)
                psb = pS.next()

                def fS(e, psb=psb, k_s=k_s, h_i=h_i, c=c):
                    for h in range(4):
                        i = e.matmul(psb[:, h * 128:(h + 1) * 128], lhsT=k_s[c * 64:(c + 1) * 64, h * 128:(h + 1) * 128],
                                     rhs=h_i[c * 64:(c + 1) * 64, h * 128:(h + 1) * 128], start=True, stop=True)
                    return i
                kb.op("pe", fS, [k_s, h_i], [psb])
                Sn = Sr.next()
                for h in range(4):
                    STT(kb, Sn[:, h * 128:(h + 1) * 128], stS['S'][:, h * 128:(h + 1) * 128], d_c[:, h * 2 + c:h * 2 + c + 1],
                        psb[:, h * 128:(h + 1) * 128], ALU.mult, ALU.add, [stS['S'], d_c, psb], [Sn])
                stS['S'] = Sn

        run_pairs(nb, body, load, width=3)
        kb.end_phase()


def phase_A2(kb, nbo, x_own, rope_own, w_in, lb_logits, hg_norm_g, consts, SS, QT, MIX, ss_pick=None):
    with contextlib.ExitStack() as ph:
        Wt = ph.enter_context(kb.nc.sbuf_tensor(kb.uname("A2_W"), [128, 8, 2560], BF16))
        W = [kb.view(Wt[:, k, :]) for k in range(8)]
        load_weight(kb, W, w_in, [(0, 512), (1536, 3584)], 8)
        ident = kb.sb(ph, "A2_id", [128, 128], BF16)
        kb.dma([(ident[:], consts["ident"][:, :])], writes=[ident], sembuf=ident)
        cU = kb.sb(ph, "A2_cU", [128, 4, 128], F32)
        kb.dma([(cU[:], consts["cU"][:, :, :])], writes=[cU], sembuf=cU)
        mhg = kb.sb(ph, "A2_mhg", [128, 128], F32)
        kb.dma([(mhg[:], consts["cU"][:, 0, :])], writes=[mhg], sembuf=mhg)
        gtab = kb.sb(ph, "A2_g", [128, 128], F32)
        kb.dma([(gtab[:], hg_norm_g.partition_broadcast(128))], writes=[gtab], sembuf=gtab)
        lb, oml = hgrn_gate_tables(kb, ph, lb_logits)
        ps = PSUM(kb, ph)
        pT = Ring([ps.bank(0, BF16)])
        pp = Ring([ps.bank(1), ps.bank(2), ps.bank(3)])
        pcs = Ring([[ps.bank(4), ps.bank(5)]])
        pt3 = Ring([ps.bank(6, BF16)])
        pA = Ring([ps.bank(7)])
        xf = Ring([kb.sb(ph, "A2_xf%d" % i, [128, 1024], F32) for i in range(5)])
        rp = Ring([kb.sb(ph, "A2_rp%d" % i, [128, 128], F32) for i in range(5)])
        s0r = Ring([kb.sb(ph, "A2_s0%d" % i, [128, 512], BF16) for i in range(5)])
        s1r = Ring([kb.sb(ph, "A2_s1%d" % i, [128, 512], BF16) for i in range(5)])
        xb = Ring([kb.sb(ph, "A2_xb%d" % i, [128, 1024], BF16) for i in range(2)])
        xT = Ring([kb.sb(ph, "A2_xT%d" % i, [128, 1024], BF16) for i in range(3)])
        qf = Ring([kb.sb(ph, "A2_qf%d" % i, [128, 512], F32) for i in range(3)])
        qbf = Ring([kb.sb(ph, "A2_qb%d" % i, [128, 512], BF16) for i in range(2)])
        rt = Ring([[kb.sb(ph, "A2_rt%d_%d" % (i, k), [128, 64], F32) for k in range(4)] for i in range(2)])
        qTs = Ring([kb.sb(ph, "A2_qT%d" % i, [128, 512], BF16) for i in range(3)])
        qs = Ring([kb.sb(ph, "A2_qs%d" % i, [128, 512], F32) for i in range(3)])
        sg = Ring([kb.sb(ph, "A2_sg%d" % i, [128, 512], F32) for i in range(3)])
        sgt1 = Ring([kb.sb(ph, "A2_sgt1%d" % i, [128, 512], F32) for i in range(1)])
        sgt2 = Ring([kb.sb(ph, "A2_sgt2%d" % i, [128, 512], F32) for i in range(1)])
        sgt3 = Ring([kb.sb(ph, "A2_sgt3%d" % i, [128, 512], F32) for i in range(1)])
        hib = Ring([kb.sb(ph, "A2_hi%d" % i, [128, 512], BF16) for i in range(3)])
        gs = Ring([kb.sb(ph, "A2_gs%d" % i, [128, 512], F32) for i in range(3)])
        f1 = Ring([kb.sb(ph, "A2_f1%d" % i, [128, 512], F32) for i in range(1)])
        ff = Ring([kb.sb(ph, "A2_ff%d" % i, [128, 512], F32) for i in range(1)])
        lf = Ring([kb.sb(ph, "A2_lf%d" % i, [128, 512], F32) for i in range(1)])
        kk = Ring([kb.sb(ph, "A2_kk%d" % i, [128, 512], F32) for i in range(3)])
        bsb = Ring([kb.sb(ph, "A2_bs%d" % i, [128, 512], F32) for i in range(3)])
        d1 = Ring([kb.sb(ph, "A2_d1%d" % i, [128, 512], F32) for i in range(3)])
        e1 = Ring([kb.sb(ph, "A2_e1%d" % i, [128, 512], F32) for i in range(1)])
        e2 = Ring([kb.sb(ph, "A2_e2%d" % i, [128, 512], F32) for i in range(1)])
        e3 = Ring([kb.sb(ph, "A2_e3%d" % i, [128, 512], F32) for i in range(1)])
        Qh = Ring([kb.sb(ph, "A2_Qh%d" % i, [128, 512], BF16) for i in range(2)])
        Kh = Ring([kb.sb(ph, "A2_Kh%d" % i, [128, 512], BF16) for i in range(2)])
        qo = Ring([kb.sb(ph, "A2_qo%d" % i, [128, 512], BF16) for i in range(2)])
        QhT = Ring([kb.sb(ph, "A2_QhT%d" % i, [128, 512], BF16) for i in range(3)])
        KhT = Ring([kb.sb(ph, "A2_KhT%d" % i, [128, 512], BF16) for i in range(3)])
        qz0 = Ring([kb.sb(ph, "A2_qz0%d" % i, [128, 4, 128], BF16) for i in range(3)])
        qz1 = Ring([kb.sb(ph, "A2_qz1%d" % i, [128, 4, 128], BF16) for i in range(3)])
        Am = Ring([kb.sb(ph, "A2_Am%d" % i, [128, 512], BF16) for i in range(3)])
        jk = Ring([kb.sb(ph, "A2_jk%d" % i, [128, 128], F32) for i in range(2)])
        sm = Ring([[kb.sb(ph, "A2_sm%d_%d" % (i, k), [128, 4], F32) for k in range(4)] for i in range(2)])
        on = Ring([kb.sb(ph, "A2_on%d" % i, [128, 512], F32) for i in range(1)])
        on2 = Ring([kb.sb(ph, "A2_o2%d" % i, [128, 512], F32) for i in range(1)])
        ob = Ring([kb.sb(ph, "A2_ob%d" % i, [128, 512], BF16) for i in range(3)])
        for z in qz0.bufs + qz1.bufs:
            kb.op("pool", lambda e, z=z: e.memset(z[:], 0.0), [], [z])
        loaded = {}
        if ss_pick is not None:
            cand = Ring([kb.sb(ph, "A2_cd%d" % i, [128, 512], BF16) for i in range(4)])
            candf = Ring([kb.sb(ph, "A2_cf%d" % i, [128, 512], F32) for i in range(2)])
            sel = kb.sb(ph, "A2_sel", [128, 2], F32)
            kb.dma([(sel[:], consts["sel"][:, :])], writes=[sel], sembuf=sel)

        def load(j):
            x = xf.next()
            kb.dma([(x[:], x_own[j * 128:(j + 1) * 128, :])], writes=[x], sembuf=x)
            r = rp.next()
            kb.dma([(r[:], rope_own[j * 128:(j + 1) * 128, :])], writes=[r], sembuf=r)
            a0 = s0r.next()
            a1 = s1r.next()
            if ss_pick is None:
                kb.dma([(a0[:], SS[j, 0, :, :])], writes=[a0], sembuf=a0)
                kb.dma([(a1[:], SS[j, 1, :, :])], writes=[a1], sembuf=a1)
            else:
                g0, g1 = ss_pick(j)
                for c, dst in ((0, a0), (1, a1)):
                    c0 = cand.next()
                    kb.dma([(c0[:], SS[g0, c, :, :])], writes=[c0], sembuf=c0)
                    c1 = cand.next()
                    kb.dma([(c1[:], SS[g1, c, :, :])], writes=[c1], sembuf=c1)
                    tmpb = candf.next()
                    kb.op("act", lambda e, tmpb=tmpb, c0=c0: e.activation(out=tmpb[:], in_=c0[:], func=AF.Copy, scale=sel[:, 0:1]), [c0, sel], [tmpb])
                    STT(kb, dst[:], c1[:], sel[:, 1:2], tmpb[:], ALU.mult, ALU.add, [c1, sel, tmpb], [dst])
            loaded[j] = (x, r, a0, a1)

        def mm_group(pbuf, t, col0):
            def f(e):
                for k in range(8):
                    i = e.matmul(pbuf[:], lhsT=t[:, k * 128:(k + 1) * 128], rhs=W[k][:, col0:col0 + 512], start=(k == 0), stop=(k == 7))
                return i
            kb.op("pe", f, [t] + W, [pbuf])

        def body(j):
            x, r, S0, S1 = loaded.pop(j)
            b = xb.next()
            CP(kb, "dve", b[:], x[:], [x], [b])
            p = pT.next()
            transposes(kb, p, b, ident, 8, [b])
            t = xT.next()
            CP(kb, "act", t[:], p[:], [p], [t])
            yield
            pq = pp.next()
            mm_group(pq, t, 0)
            q_f = qf.next()
            CP(kb, "act", q_f[:], pq[:], [pq], [q_f])
            phq = pp.next()
            mm_group(phq, t, 512)
            q_s = qs.next()
            sq_ = sgt3.next()
            sigmoid_exp(kb, sq_[:], phq[:], phq, sgt1.next(), sgt2.next(), sq_)
            TT(kb, "dve", q_s[:], sq_[:], phq[:], ALU.mult, [sq_, phq], [q_s])
            phf = pp.next()
            mm_group(phf, t, 1024)
            s_g = sg.next()
            sigmoid_exp(kb, s_g[:], phf[:], phf, sgt1.next(), sgt2.next(), s_g)
            phi = pp.next()
            mm_group(phi, t, 1536)
            h_i = hib.next()
            CP(kb, "dve", h_i[:], phi[:], [phi], [h_i])
            phg = pp.next()
            mm_group(phg, t, 2048)
            g_s = gs.next()
            sq_ = sgt3.next()
            sigmoid_exp(kb, sq_[:], phg[:], phg, sgt1.next(), sgt2.next(), sq_)
            TT(kb, "dve", g_s[:], sq_[:], phg[:], ALU.mult, [sq_, phg], [g_s])
            yield
            q_b = qbf.next()
            rope(kb, q_f, q_b, r, 8, rt.next())
            p3 = pt3.next()
            transposes(kb, kb.view(p3[:, 0:512]) if False else p3, q_b, ident, 4, [q_b])
            qt = qTs.next()
            CP(kb, "act", qt[:], p3[:, 0:512], [p3], [qt])
            kb.dma([(QT[j, :, :], qt[:])], reads=[qt], sembuf=qt)
            f_1 = f1.next()
            TT(kb, "dve", f_1[:], s_g[:], oml[:], ALU.mult, [s_g, oml], [f_1])
            f_ = ff.next()
            TT(kb, "dve", f_[:], f_1[:], lb[:], ALU.add, [f_1, lb], [f_])
            l_f = lf.next()
            ACT(kb, l_f[:], f_[:], AF.Ln, [f_], [l_f])
            k_k = kk.next()
            ACT(kb, k_k[:], f_[:], AF.Copy, [f_], [k_k], scale=-1.0, bias=1.0)
            pb, pbm = pcs.next()
            kb.op("pe", lambda e: e.matmul(pb[:], lhsT=cU[:, 0, :], rhs=l_f[:], start=True, stop=True), [cU, l_f], [pb])
            kb.op("pe", lambda e: e.matmul(pbm[:], lhsT=cU[:, 1, :], rhs=l_f[:], start=True, stop=True), [cU, l_f], [pbm])
            b_s = bsb.next()
            CP(kb, "act", b_s[:], pb[:], [pb], [b_s])
            d_1 = d1.next()
            TT(kb, "dve", d_1[:], b_s[:], pbm[:], ALU.subtract, [b_s, pbm], [d_1])
            yield
            e_1 = e1.next()
            ACT(kb, e_1[:], d_1[:], AF.Exp, [d_1], [e_1])
            e_2 = e2.next()
            ACT(kb, e_2[:], d_1[:], AF.Exp, [d_1], [e_2], scale=-1.0)
            e_3 = e3.next()
            ACT(kb, e_3[:], b_s[:], AF.Exp, [b_s], [e_3])
            Q_h = Qh.next()
            TT(kb, "dve", Q_h[:], q_s[:], e_1[:], ALU.mult, [q_s, e_1], [Q_h])
            K_h = Kh.next()
            TT(kb, "dve", K_h[:], k_k[:], e_2[:], ALU.mult, [k_k, e_2], [K_h])
            q_o = qo.next()
            TT(kb, "dve", q_o[:], q_s[:], e_3[:], ALU.mult, [q_s, e_3], [q_o])
            p3 = pt3.next()
            transposes(kb, p3, Q_h, ident, 4, [Q_h])
            Q_T = QhT.next()
            CP(kb, "act", Q_T[:], p3[:, 0:512], [p3], [Q_T])
            p3 = pt3.next()
            transposes(kb, p3, K_h, ident, 4, [K_h])
            K_T = KhT.next()
            CP(kb, "dve", K_T[:], p3[:, 0:512], [p3], [K_T])
            p3 = pt3.next()
            transposes(kb, p3, q_o, ident, 4, [q_o])
            z0 = qz0.next()
            z1 = qz1.next()
            p3v = p3[:, 0:512].rearrange("p (h t) -> p h t", h=4)
            CP(kb, "act", z0[:, :, 0:64], p3v[:, :, 0:64], [p3], [z0])
            CP(kb, "dve", z1[:, :, 64:128], p3v[:, :, 64:128], [p3], [z1])
            yield
            pa = pA.next()

            def fA(e, pa=pa, K_T=K_T, Q_T=Q_T):
                for h in range(4):
                    i = e.matmul(pa[:, h * 128:(h + 1) * 128], lhsT=K_T[:, h * 128:(h + 1) * 128], rhs=Q_T[:, h * 128:(h + 1) * 128],
                                 start=True, stop=True)
                return i
            kb.op("pe", fA, [K_T, Q_T], [pa])
            A_m = Am.next()
            TT(kb, "dve", A_m[:].rearrange("p (h t) -> p h t", h=4), pa[:].rearrange("p (h t) -> p h t", h=4),
               mhg[:].unsqueeze(1).broadcast_to([128, 4, 128]), ALU.mult, [pa, mhg], [A_m])
            yield
            po = pp.next()

            def fo(e, po=po, A_m=A_m, h_i=h_i, z0=z0, z1=z1, S0=S0, S1=S1):
                for h in range(4):
                    hs = slice(h * 128, (h + 1) * 128)
                    e.matmul(po[:, hs], lhsT=A_m[:, hs], rhs=h_i[:, hs], start=True, stop=False)
                    e.matmul(po[:, hs], lhsT=z0[:, h, :], rhs=S0[:, hs], start=False, stop=False)
                    i = e.matmul(po[:, hs], lhsT=z1[:, h, :], rhs=S1[:, hs], start=False, stop=True)
                return i
            kb.op("pe", fo, [A_m, h_i, z0, z1, S0, S1], [po])
            s = sm.next()
            for h in range(4):
                j_ = jk.next()
                ACT(kb, j_[:], po[:, h * 128:(h + 1) * 128], AF.Square, [po], [j_, s[0]], accum=s[0][:, h:h + 1])
            rstd_act(kb, s[2][:, 0:4], s[0][:, 0:4], s[1][:, 0:4], 1.0 / 128.0, s[0], s[1], s[2])
            o_n = on.next()
            TT(kb, "dve", o_n[:].rearrange("p (h t) -> p h t", h=4), po[:].rearrange("p (h t) -> p h t", h=4),
               s[2][:, 0:4].unsqueeze(2).broadcast_to([128, 4, 128]), ALU.mult, [po, s[2]], [o_n])
            o_2 = on2.next()
            TT(kb, "dve", o_2[:].rearrange("p (h t) -> p h t", h=4), o_n[:].rearrange("p (h t) -> p h t", h=4),
               gtab[:].unsqueeze(1).broadcast_to([128, 4, 128]), ALU.mult, [o_n, gtab], [o_2])
            o_b = ob.next()
            TT(kb, "dve", o_b[:], o_2[:], g_s[:], ALU.mult, [o_2, g_s], [o_b])
            kb.dma([(MIX[j * 128:(j + 1) * 128, 512:1024], o_b[:])], reads=[o_b], sembuf=o_b)

        run_pairs(nbo, body, load, width=3)
        kb.end_phase()


def keyspec_parity(j):
    return [(k, (k - 2 * j) if k >= 2 * j else None, False) for k in range(2 * j + 2)]


def make_keyspec_ctx(nbh):
    def spec(i):
        g0, g1 = i - NCTX, i + nbh - NCTX
        out = []
        for k in range(g1 + 1):
            if k < g0:
                out.append((k, None, False))
            elif k == g0:
                out.append((k, 0, False))
            elif k < g1:
                out.append((k, None, True))
            else:
                out.append((k, 1, True))
        return out
    return spec


def phase_B2(kb, S, nbo, nbh, KT, V, QT, da_lambda, sub_g, consts, MIX):
    nb = S // 128
    with contextlib.ExitStack() as ph:
        KTt = ph.enter_context(kb.nc.sbuf_tensor(kb.uname("B_KT2"), [128, nb, 512], BF16))
        Vt = ph.enter_context(kb.nc.sbuf_tensor(kb.uname("B_V2"), [128, nb, 516], BF16))
        CH = 8
        nch = (nb + CH - 1) // CH
        KTc = [kb.view(KTt[:, c * CH:min(nb, (c + 1) * CH), :]) for c in range(nch)]
        Vc = [kb.view(Vt[:, c * CH:min(nb, (c + 1) * CH), :]) for c in range(nch)]
        ident = kb.sb(ph, "B_id2", [128, 128], BF16)
        kb.dma([(ident[:], consts["ident"][:, :])], writes=[ident], sembuf=ident)
        msk = kb.sb(ph, "B_msk2", [128, 2, 128], BF16)
        kb.dma([(msk[:], consts["mAB"][:, :, :])], writes=[msk], sembuf=msk)
        gtab = kb.sb(ph, "B_g2", [128, 128], F32)
        kb.dma([(gtab[:], sub_g.partition_broadcast(128))], writes=[gtab], sembuf=gtab)
        lt = kb.sb(ph, "B_lt2", [128, 4, 64], F32)
        kb.dma([(lt[:], da_lambda.partition_broadcast(128))], writes=[lt], sembuf=lt)
        l1 = kb.sb(ph, "B_l12", [128, 2, 64], F32)
        l2 = kb.sb(ph, "B_l22", [128, 2], F32)
        l3 = kb.sb(ph, "B_l32", [128, 2], F32)
        nlam = kb.sb(ph, "B_nlam2", [128, 1], F32)
        TT(kb, "dve", l1[:, 0, :], lt[:, 0, :], lt[:, 1, :], ALU.mult, [lt], [l1])
        TT(kb, "dve", l1[:, 1, :], lt[:, 2, :], lt[:, 3, :], ALU.mult, [lt], [l1])
        kb.op("dve", lambda e: e.reduce_sum(out=l2[:, 0:2], in_=l1[:], axis=AX.X), [l1], [l2])
        ACT(kb, l3[:], l2[:], AF.Exp, [l2], [l3])
        l4 = kb.sb(ph, "B_l42", [128, 1], F32)
        TT(kb, "dve", l4[:], l3[:, 1:2], l3[:, 0:1], ALU.subtract, [l3], [l4])
        TS(kb, "dve", nlam[:], l4[:], -LAM_INIT0, None, ALU.add, None, [l4], [nlam])
        for c in range(nch):
            lo, hi = c * CH, min(nb, (c + 1) * CH)
            kb.dma([(KTc[c][:], KT[lo:hi, :, :].rearrange("n p f -> p n f"))], writes=[KTc[c]], sembuf=KTc[c])
            kb.dma([(Vc[c][:], V[lo:hi, :, :].rearrange("n p f -> p n f"))], writes=[Vc[c]], sembuf=Vc[c])
        ps = PSUM(kb, ph)
        pst = Ring([ps.bank(0), ps.bank(1), ps.bank(2)])
        pO = [ps.bank(3), ps.bank(4), ps.bank(5), ps.bank(6)]
        hb = kb.sb(ph, "B2_hb", [128, 1], F32)
        kb.dma([(hb[:], consts["hb"][:, :])], writes=[hb], sembuf=hb)
        q4r = Ring([kb.sb(ph, "B2_q%d" % i, [128, 4, 512], BF16) for i in range(3)])
        ptr = Ring([kb.sb(ph, "B2_pt%d" % i, [128, 512], BF16) for i in range(4)])
        ocp = Ring([kb.sb(ph, "B2_oc%d" % i, [128, 258], F32) for i in range(8)])
        rd = Ring([kb.sb(ph, "B2_rd%d" % i, [128, 2], F32) for i in range(2)])
        dn = Ring([kb.sb(ph, "B2_dn%d" % i, [128, 2], F32) for i in range(2)])
        cf = Ring([kb.sb(ph, "B2_cf%d" % i, [128, 1], F32) for i in range(2)])
        t0r = Ring([kb.sb(ph, "B2_t0%d" % i, [128, 128], F32) for i in range(2)])
        orr = Ring([kb.sb(ph, "B2_o%d" % i, [128, 128], F32) for i in range(2)])
        jk = Ring([kb.sb(ph, "B2_jk%d" % i, [128, 128], F32) for i in range(2)])
        sm = Ring([[kb.sb(ph, "B2_sm%d_%d" % (i, k), [128, 1], F32) for k in range(3)] for i in range(2)])
        on = Ring([kb.sb(ph, "B2_on%d" % i, [128, 128], F32) for i in range(2)])
        mix4r = Ring([[kb.sb(ph, "B2_mx%d_%d" % (i, t), [128, 512], BF16) for t in range(4)] for i in range(2)])
        assert nbo % 4 == 0
        nJ = nbo // 4
        loaded = {}

        def load(J):
            q = q4r.next()
            kb.dma([(q[:, :, t * 128:(t + 1) * 128], QT[4 * J + t, :, :].rearrange("p (h q) -> p h q", h=4)) for t in range(4)],
                   writes=[q], sembuf=q)
            loaded[J] = q

        def kbuf(kbk):
            return KTc[kbk // CH], Vc[kbk // CH]

        units = []
        for J in range(nJ):
            i0 = 4 * J
            gmax = i0 + 3 + nbh - NCTX
            for h in range(4):
                for c in range(2):
                    for kbk in range(gmax + 1):
                        tL = max(0, kbk + NCTX - i0)
                        tH = max(0, kbk - (nbh - NCTX) - i0)
                        units.append((J, h, c, kbk, tL, tH))
        state = {"mix": None}

        def emit_ST(u):
            J, h, c, kbk, tL, tH = u
            i0 = 4 * J
            q = loaded[J]
            st = pst.next()
            ktb, _ = kbuf(kbk)
            lhs = KTt[c * 64:(c + 1) * 64, kbk, h * 128:(h + 1) * 128]
            ldiag = kbk + NCTX - i0
            hdiag = kbk - (nbh - NCTX) - i0
            hi_end = min(tL, 4)

            def f(e):
                e_ = None
                t0 = tL
                if 0 <= ldiag < 4:
                    e.matmul(st[:, ldiag * 128:(ldiag + 1) * 128], lhsT=lhs, rhs=q[c * 64:(c + 1) * 64, h, ldiag * 128:(ldiag + 1) * 128],
                             start=True, stop=False)
                    e_ = e.matmul(st[:, ldiag * 128:(ldiag + 1) * 128], lhsT=ident[:], rhs=msk[:, 0, :], start=False, stop=True)
                    t0 = ldiag + 1
                if t0 < 4:
                    e_ = e.matmul(st[:, t0 * 128:512], lhsT=lhs, rhs=q[c * 64:(c + 1) * 64, h, t0 * 128:512], start=True, stop=True)
                t1 = tH
                if 0 <= hdiag < hi_end:
                    e.matmul(st[:, hdiag * 128:(hdiag + 1) * 128], lhsT=lhs, rhs=q[c * 64:(c + 1) * 64, h, hdiag * 128:(hdiag + 1) * 128],
                             start=True, stop=False)
                    e_ = e.matmul(st[:, hdiag * 128:(hdiag + 1) * 128], lhsT=ident[:], rhs=msk[:, 1, :], start=False, stop=True)
                    t1 = hdiag + 1
                if t1 < hi_end:
                    e_ = e.matmul(st[:, t1 * 128:hi_end * 128], lhsT=lhs, rhs=q[c * 64:(c + 1) * 64, h, t1 * 128:hi_end * 128],
                                  start=True, stop=True)
                return e_
            kb.op("pe", f, [ktb, q, ident, msk], [st])
            pt = ptr.next()
            if tL < 4:
                ACT(kb, pt[:, tL * 128:512], st[:, tL * 128:512], AF.Exp, [st], [pt], scale=0.125)
            if tH < hi_end:
                kb.op("act", lambda e: e.activation(out=pt[:, tH * 128:hi_end * 128], in_=st[:, tH * 128:hi_end * 128], func=AF.Exp,
                                                    scale=0.125, bias=hb[:, 0:1]), [st, hb], [pt])
            return pt

        def emit_PV(u, pt):
            J, h, c, kbk, tL, tH = u
            i0 = 4 * J
            _, vb_ = kbuf(kbk)
            ts_ = list(range(min(tL, tH), 4))
            gmax = i0 + 3 + nbh - NCTX

            def f(e):
                for t in ts_:
                    g1 = i0 + t + nbh - NCTX
                    e_ = e.matmul(pO[t][:, c * 129:(c + 1) * 129], lhsT=pt[:, t * 128:(t + 1) * 128], rhs=Vt[:, kbk, h * 129:(h + 1) * 129],
                                  start=(kbk == 0), stop=(kbk == g1))
                return e_
            kb.op("pe", f, [vb_, pt], [pO[t] for t in ts_])
            if c == 1 and kbk == gmax:
                if h == 0:
                    state["mix"] = mix4r.next()
                for t in range(4):
                    oc = ocp.next()
                    CP(kb, "act" if t % 2 == 0 else "dve", oc[:], pO[t][:, 0:258], [pO[t]], [oc])
                    finalize(i0 + t, h, oc, state["mix"][t])

        def finalize(j, h, O, mx):
            r_d = rd.next()
            d_n = dn.next()
            TS(kb, "dve", d_n[:, 0:2], O[:, 0:258].rearrange("p (c d) -> p c d", c=2)[:, :, 128], 1e-30, None, ALU.max, None, [O], [d_n])
            kb.op("dve", lambda e: e.reciprocal(out=r_d[:, 0:2], in_=d_n[:, 0:2]), [d_n], [r_d])
            c_f = cf.next()
            TT(kb, "dve", c_f[:], r_d[:, 1:2], nlam[:], ALU.mult, [r_d, nlam], [c_f])
            t_0 = t0r.next()
            TS(kb, "dve", t_0[:], O[:, 0:128], r_d[:, 0:1], None, ALU.mult, None, [O, r_d], [t_0])
            o_ = orr.next()
            STT(kb, o_[:], O[:, 129:257], c_f[:, 0:1], t_0[:], ALU.mult, ALU.add, [O, c_f, t_0], [o_])
            s_ = sm.next()
            j_ = jk.next()
            ACT(kb, j_[:], o_[:], AF.Square, [o_], [j_, s_[0]], accum=s_[0][:, 0:1])
            rstd_act(kb, s_[2][:], s_[0][:], s_[1][:], 1.0 / 128.0, s_[0], s_[1], s_[2])
            o_n = on.next()
            TS(kb, "dve", o_n[:], o_[:], s_[2][:, 0:1], 1.0 - LAM_INIT0, ALU.mult, ALU.mult, [o_, s_[2]], [o_n])
            TT(kb, "dve", mx[:, h * 128:(h + 1) * 128], o_n[:], gtab[:], ALU.mult, [o_n, gtab], [mx])
            if h == 3:
                kb.dma([(MIX[j * 128:(j + 1) * 128, 0:512], mx[:])], reads=[mx], sembuf=mx)

        load(0)
        if nJ > 1:
            load(1)
        nxt = min(2, nJ)
        prev = None
        for ui, u in enumerate(units):
            if ui > 0 and u[0] != units[ui - 1][0] and nxt < nJ:
                load(nxt)
                nxt += 1
            pt = emit_ST(u)
            if prev is not None:
                emit_PV(*prev)
            prev = (u, pt)
        emit_PV(*prev)
        kb.end_phase()


def phase_B(kb, S, nbo, KT, V, QT, da_lambda, sub_g, consts, MIX, keyspec=keyspec_parity):
    nb = S // 128
    with contextlib.ExitStack() as ph:
        KTt = ph.enter_context(kb.nc.sbuf_tensor(kb.uname("B_KT"), [128, nb, 512], BF16))
        Vt = ph.enter_context(kb.nc.sbuf_tensor(kb.uname("B_V"), [128, nb, 516], BF16))
        CH = 8
        nch = (nb + CH - 1) // CH
        KTc = [kb.view(KTt[:, c * CH:min(nb, (c + 1) * CH), :]) for c in range(nch)]
        Vc = [kb.view(Vt[:, c * CH:min(nb, (c + 1) * CH), :]) for c in range(nch)]
        ident = kb.sb(ph, "B_id", [128, 128], BF16)
        kb.dma([(ident[:], consts["ident"][:, :])], writes=[ident], sembuf=ident)
        msk = kb.sb(ph, "B_msk", [128, 2, 128], BF16)
        kb.dma([(msk[:], consts["mAB"][:, :, :])], writes=[msk], sembuf=msk)
        gtab = kb.sb(ph, "B_g", [128, 128], F32)
        kb.dma([(gtab[:], sub_g.partition_broadcast(128))], writes=[gtab], sembuf=gtab)
        lt = kb.sb(ph, "B_lt", [128, 4, 64], F32)
        kb.dma([(lt[:], da_lambda.partition_broadcast(128))], writes=[lt], sembuf=lt)
        l1 = kb.sb(ph, "B_l1", [128, 2, 64], F32)
        l2 = kb.sb(ph, "B_l2", [128, 2], F32)
        l3 = kb.sb(ph, "B_l3", [128, 2], F32)
        nlam = kb.sb(ph, "B_nlam", [128, 1], F32)
        TT(kb, "dve", l1[:, 0, :], lt[:, 0, :], lt[:, 1, :], ALU.mult, [lt], [l1])
        TT(kb, "dve", l1[:, 1, :], lt[:, 2, :], lt[:, 3, :], ALU.mult, [lt], [l1])
        kb.op("dve", lambda e: e.reduce_sum(out=l2[:, 0:2], in_=l1[:], axis=AX.X), [l1], [l2])
        ACT(kb, l3[:], l2[:], AF.Exp, [l2], [l3])
        l4 = kb.sb(ph, "B_l4", [128, 1], F32)
        TT(kb, "dve", l4[:], l3[:, 1:2], l3[:, 0:1], ALU.subtract, [l3], [l4])
        TS(kb, "dve", nlam[:], l4[:], -LAM_INIT0, None, ALU.add, None, [l4], [nlam])
        for c in range(nch):
            lo, hi = c * CH, min(nb, (c + 1) * CH)
            kb.dma([(KTc[c][:], KT[lo:hi, :, :].rearrange("n p f -> p n f"))], writes=[KTc[c]], sembuf=KTc[c])
            kb.dma([(Vc[c][:], V[lo:hi, :, :].rearrange("n p f -> p n f"))], writes=[Vc[c]], sembuf=Vc[c])
        ps = PSUM(kb, ph)
        pst = Ring([ps.bank(0), ps.bank(1), ps.bank(2)])
        pO = Ring([ps.bank(3), ps.bank(4)])
        qr = Ring([kb.sb(ph, "B_q%d" % i, [128, 512], BF16) for i in range(3)])
        ptr = Ring([kb.sb(ph, "B_pt%d" % i, [128, 512], BF16) for i in range(4)])
        rd = Ring([kb.sb(ph, "B_rd%d" % i, [128, 2], F32) for i in range(2)])
        dn = Ring([kb.sb(ph, "B_dn%d" % i, [128, 2], F32) for i in range(2)])
        cf = Ring([kb.sb(ph, "B_cf%d" % i, [128, 1], F32) for i in range(2)])
        t0r = Ring([kb.sb(ph, "B_t0%d" % i, [128, 128], F32) for i in range(2)])
        orr = Ring([kb.sb(ph, "B_o%d" % i, [128, 128], F32) for i in range(2)])
        jk = Ring([kb.sb(ph, "B_jk%d" % i, [128, 128], F32) for i in range(2)])
        sm = Ring([[kb.sb(ph, "B_sm%d_%d" % (i, k), [128, 1], F32) for k in range(3)] for i in range(2)])
        on = Ring([kb.sb(ph, "B_on%d" % i, [128, 128], F32) for i in range(2)])
        mixr = Ring([kb.sb(ph, "B_mx%d" % i, [128, 512], BF16) for i in range(2)])
        loaded = {}

        def load(j):
            q = qr.next()
            kb.dma([(q[:], QT[j, :, :])], writes=[q], sembuf=q)
            loaded[j] = q

        def chunk_bufs(kbs):
            cs = sorted(set(k // CH for k in kbs))
            return [KTc[c] for c in cs], [Vc[c] for c in cs]

        hb = kb.sb(ph, "B_hb", [128, 1], F32)
        kb.dma([(hb[:], consts["hb"][:, :])], writes=[hb], sembuf=hb)
        units = []
        for j in range(nbo):
            ents = keyspec(j)
            groups = []
            for ent in ents:
                if groups and len(groups[-1]) < 4 and groups[-1][-1][2] == ent[2]:
                    groups[-1].append(ent)
                else:
                    groups.append([ent])
            for h in range(4):
                for c in range(2):
                    for gi_, grp in enumerate(groups):
                        units.append((j, h, c, grp, gi_ == 0, gi_ == len(groups) - 1))
        state = {"O": None, "mix": None}

        def emit_ST(u):
            j, h, c, grp, first, last = u
            q = loaded[j]
            st = pst.next()
            kbs = [g_[0] for g_ in grp]
            kts, _ = chunk_bufs(kbs)
            high = grp[0][2]

            def f(e):
                for i, (kbk, mi, _) in enumerate(grp):
                    e_ = e.matmul(st[:, i * 128:(i + 1) * 128], lhsT=KTt[c * 64:(c + 1) * 64, kbk, h * 128:(h + 1) * 128],
                                  rhs=q[c * 64:(c + 1) * 64, h * 128:(h + 1) * 128], start=True, stop=(mi is None))
                    if mi is not None:
                        e_ = e.matmul(st[:, i * 128:(i + 1) * 128], lhsT=ident[:], rhs=msk[:, mi, :], start=False, stop=True)
                return e_
            kb.op("pe", f, kts + [q, ident, msk], [st])
            pt = ptr.next()
            n = len(kbs) * 128
            if high:
                kb.op("act", lambda e: e.activation(out=pt[:, 0:n], in_=st[:, 0:n], func=AF.Exp, scale=0.125, bias=hb[:, 0:1]), [st, hb], [pt])
            else:
                ACT(kb, pt[:, 0:n], st[:, 0:n], AF.Exp, [st], [pt], scale=0.125)
            return pt

        def emit_PV(u, pt):
            j, h, c, grp, first, last = u
            if c == 0 and first:
                state["O"] = pO.next()
            O = state["O"]
            kbs = [g_[0] for g_ in grp]
            _, vs = chunk_bufs(kbs)

            def f(e):
                for i, kbk in enumerate(kbs):
                    e_ = e.matmul(O[:, c * 129:(c + 1) * 129], lhsT=pt[:, i * 128:(i + 1) * 128], rhs=Vt[:, kbk, h * 129:(h + 1) * 129],
                                  start=(first and i == 0), stop=(last and i == len(kbs) - 1))
                return e_
            kb.op("pe", f, vs + [pt], [O])
            if c == 1 and last:
                finalize(j, h, O)

        def finalize(j, h, O):
            if h == 0:
                state["mix"] = mixr.next()
            mx = state["mix"]
            r_d = rd.next()
            d_n = dn.next()
            TS(kb, "dve", d_n[:, 0:2], O[:, 0:258].rearrange("p (c d) -> p c d", c=2)[:, :, 128], 1e-30, None, ALU.max, None, [O], [d_n])
            kb.op("dve", lambda e: e.reciprocal(out=r_d[:, 0:2], in_=d_n[:, 0:2]), [d_n], [r_d])
            c_f = cf.next()
            TT(kb, "dve", c_f[:], r_d[:, 1:2], nlam[:], ALU.mult, [r_d, nlam], [c_f])
            t_0 = t0r.next()
            TS(kb, "dve", t_0[:], O[:, 0:128], r_d[:, 0:1], None, ALU.mult, None, [O, r_d], [t_0])
            o_ = orr.next()
            STT(kb, o_[:], O[:, 129:257], c_f[:, 0:1], t_0[:], ALU.mult, ALU.add, [O, c_f, t_0], [o_])
            s = sm.next()
            j_ = jk.next()
            ACT(kb, j_[:], o_[:], AF.Square, [o_], [j_, s[0]], accum=s[0][:, 0:1])
            rstd_act(kb, s[2][:], s[0][:], s[1][:], 1.0 / 128.0, s[0], s[1], s[2])
            o_n = on.next()
            TS(kb, "dve", o_n[:], o_[:], s[2][:, 0:1], 1.0 - LAM_INIT0, ALU.mult, ALU.mult, [o_, s[2]], [o_n])
            TT(kb, "dve", mx[:, h * 128:(h + 1) * 128], o_n[:], gtab[:], ALU.mult, [o_n, gtab], [mx])
            if h == 3:
                kb.dma([(MIX[j * 128:(j + 1) * 128, 0:512], mx[:])], reads=[mx], sembuf=mx)

        for j in range(min(2, nbo)):
            load(j)
        nxt = min(2, nbo)
        prev = None
        for ui, u in enumerate(units):
            if ui > 0 and u[0] != units[ui - 1][0] and nxt < nbo:
                load(nxt)
                nxt += 1
            pt = emit_ST(u)
            if prev is not None:
                emit_PV(*prev)
            prev = (u, pt)
        emit_PV(*prev)
        kb.end_phase()


def rope_table(pos, ng):
    inv = ROPE_THETA ** (-np.arange(0, 16, 2, dtype=np.float32) / np.float32(16))
    ang = pos.astype(np.float32)[:, None] * inv[None, :].astype(np.float32)
    cos = np.cos(ang).astype(np.float32)
    sin = np.sin(ang).astype(np.float32)
    return np.concatenate([np.tile(cos, (1, ng)), np.tile(sin, (1, ng))], axis=1).astype(np.float32)


def const_cU():
    s = np.arange(128)[:, None]
    t = np.arange(128)[None, :]
    same = (s // 64) == (t // 64)
    cU = np.zeros((128, 4, 128), np.float32)
    cU[:, 0, :] = same & (s <= t)
    cU[:, 1, :] = same & ((s % 64) <= 31)
    cU[:, 2, :] = same & (s > t)
    cU[:, 3, 0] = (np.arange(128) // 64) == 0
    cU[:, 3, 1] = (np.arange(128) // 64) == 1
    return cU


def const_l0(h):
    k = np.arange(128)[:, None]
    q = np.arange(128)[None, :]
    tri = np.where(k <= q, 0.0, NEG).astype(np.float32)
    zero = np.zeros((128, 128), np.float32)
    full = np.full((128, 128), NEG, np.float32)
    mAB = np.stack([tri if h == 0 else zero, full if h == 0 else tri], axis=1).astype(NPBF)
    sel = np.zeros((128, 2), np.float32)
    sel[:, h] = 1.0
    return {"ident": np.eye(128).astype(NPBF), "cU": const_cU(), "sel": sel, "mAB": mAB, "hb": np.zeros((128, 1), np.float32)}


def dense_tail(kb, nbo, l, mix, xres, io, XA, XB, out, consts, w_out, mix_loader=None, p_in=None):
    phase_C(kb, nbo, mix, xres, w_out, io["ln1_g"][l:l + 1, :], io["ln1_b"][l:l + 1, :], XA, consts, mix_loader=mix_loader)
    phase_D1(kb, nbo, XA, io["ffn_w1"][l], io["ffn_w2"][l], io["ln2_g"][l:l + 1, :], io["ln2_b"][l:l + 1, :], XB, consts)
    phase_D2(kb, nbo, XB, io["p_own"] if p_in is None else p_in, io["ple_w_gate"][l], io["ple_w_proj"][l], io["ple_norm_g"][l:l + 1, :], out, consts)


WEIGHT_SHAPES = {
    "ev_w_in": [1, 1024, 3584], "ev_w_out": [1, 1024, 1024], "da_lambda": [1, 4, 64], "da_subln_g": [1, 128],
    "hg_lb_logits": [2, 512], "hg_norm_g": [1, 128], "od_w_in": [1, 1024, 3072], "od_w_out": [1, 1024, 1024],
    "ln1_g": [2, 1024], "ln1_b": [2, 1024], "ffn_w1": [2, 1024, 4096], "ffn_w2": [2, 4096, 1024],
    "ln2_g": [2, 1024], "ln2_b": [2, 1024], "ple_w_proj": [2, 256, 1024], "ple_w_gate": [2, 1024, 1024],
    "ple_norm_g": [2, 1024],
}
L0_W = ["ev_w_in", "ev_w_out", "da_lambda", "da_subln_g", "hg_lb_logits", "hg_norm_g", "ln1_g", "ln1_b", "ffn_w1",
        "ffn_w2", "ln2_g", "ln2_b", "ple_w_proj", "ple_w_gate", "ple_norm_g"]
L1_W = ["od_w_in", "od_w_out", "ln1_g", "ln1_b", "ffn_w1", "ffn_w2", "ln2_g", "ln2_b", "ple_w_proj", "ple_w_gate",
        "ple_norm_g"]


def build_l0(S, debug=False, phases="12BT"):
    nc = bass.Bass("TRN2", target_bir_lowering=False)
    nb, nbo = S // 128, S // 256
    So = S // 2

    def din(name, shape, dt=F32):
        return nc.dram_tensor(name, list(shape), dt, kind="ExternalInput").ap()

    def scr(name, shape, dt):
        return nc.dram_tensor(name, list(shape), dt, kind="ExternalOutput" if debug else "Internal").ap()

    io = {n: din(n, WEIGHT_SHAPES[n]) for n in L0_W}
    io["x_all"] = din("x_all", [S, D])
    io["x_own"] = din("x_own", [So, D])
    io["p_own"] = din("p_own", [So, PLE])
    io["rope_all"] = din("rope_all", [S, 128])
    io["rope_own"] = din("rope_own", [So, 128])
    consts = {"ident": din("ident", [128, 128], BF16), "cU": din("cU", [128, 4, 128]),
              "sel": din("sel", [128, 2]), "mAB": din("mAB", [128, 2, 128], BF16), "hb": din("hb", [128, 1])}
    out = nc.dram_tensor("x1_own", [So, D], F32, kind="ExternalOutput").ap()
    KT = scr("KT", [nb, 128, 512], BF16)
    V = scr("V", [nb, 128, 516], BF16)
    SS = scr("SS", [nbo, 2, 128, 512], BF16)
    QT = scr("QT", [nbo, 128, 512], BF16)
    MIX = scr("MIX", [So, D], BF16)
    XA = scr("XA", [So, D], F32)
    XB = scr("XB", [So, D], F32)
    kb = KB(nc)
    if "1" in phases:
        phase_A1(kb, S, io["x_all"], io["rope_all"], io["ev_w_in"][0], io["hg_lb_logits"], consts, KT, V, SS)
    if "2" in phases:
        phase_A2(kb, nbo, io["x_own"], io["rope_own"], io["ev_w_in"][0], io["hg_lb_logits"], io["hg_norm_g"], consts, SS, QT, MIX)
    if "B" in phases:
        phase_B(kb, S, nbo, KT, V, QT, io["da_lambda"][0], io["da_subln_g"], consts, MIX)
    if "T" in phases:
        dense_tail(kb, nbo, 0, MIX, io["x_own"], io, XA, XB, out, consts, io["ev_w_out"][0])
    return nc, kb


def l0_inputs(x_b, p0_b, weights, h):
    S = x_b.shape[0]
    nb = S // 128
    own = np.arange(nb).reshape(nb // 2, 2)[:, h]
    pos_own = (own[:, None] * 128 + np.arange(128)[None, :]).reshape(-1)
    m = {n: weights[n] for n in L0_W}
    m["x_all"] = x_b
    m["x_own"] = np.ascontiguousarray(x_b[pos_own])
    m["p_own"] = np.ascontiguousarray(p0_b[pos_own])
    m["rope_all"] = rope_table(np.arange(S), 8)
    m["rope_own"] = rope_table(pos_own, 8)
    m.update(const_l0(h))
    return m, pos_own


NCTX = 16


def phase_L1A(kb, nbo, x_co, rope_co, w_in, consts, KT1, QT1, V1):
    nblk = NCTX + nbo
    with contextlib.ExitStack() as ph:
        Wt = ph.enter_context(kb.nc.sbuf_tensor(kb.uname("F_W"), [128, 8, 3072], BF16))
        W = [kb.view(Wt[:, k, :]) for k in range(8)]
        load_weight(kb, W, w_in, [(0, 3072)], 8)
        ident = kb.sb(ph, "F_id", [128, 128], BF16)
        kb.dma([(ident[:], consts["ident"][:, :])], writes=[ident], sembuf=ident)
        ps = PSUM(kb, ph)
        pT = Ring([ps.bank(0, BF16)])
        pp = Ring([ps.bank(1), ps.bank(2), ps.bank(3), ps.bank(4)])
        pkT = Ring([ps.bank(5, BF16), ps.bank(6, BF16)])
        xf = Ring([kb.sb(ph, "F_xf%d" % i, [128, 1024], F32) for i in range(4)])
        rp = Ring([kb.sb(ph, "F_rp%d" % i, [128, 128], F32) for i in range(4)])
        xb = Ring([kb.sb(ph, "F_xb%d" % i, [128, 1024], BF16) for i in range(2)])
        xT = Ring([kb.sb(ph, "F_xT%d" % i, [128, 1024], BF16) for i in range(2)])
        kf = Ring([kb.sb(ph, "F_kf%d" % i, [128, 512], F32) for i in range(3)])
        kbf = Ring([kb.sb(ph, "F_kb%d" % i, [128, 1024], BF16) for i in range(4)])
        rt = Ring([[kb.sb(ph, "F_rt%d_%d" % (i, k), [128, 64], F32) for k in range(4)] for i in range(3)])
        kst = Ring([kb.sb(ph, "F_ks%d" % i, [128, 8, 512], BF16) for i in range(2)])
        qst = Ring([kb.sb(ph, "F_qs%d" % i, [128, 8, 512], BF16) for i in range(2)])
        vb = Ring([kb.sb(ph, "F_vb%d" % i, [128, 16, 65], BF16) for i in range(3)])
        for v in vb.bufs:
            kb.op("pool", lambda e, v=v: e.memset(v[:], 1.0), [], [v])
        loaded = {}

        def load(i):
            x = xf.next()
            kb.dma([(x[:], x_co[i * 128:(i + 1) * 128, :])], writes=[x], sembuf=x)
            r = rp.next()
            kb.dma([(r[:], rope_co[i * 128:(i + 1) * 128, :])], writes=[r], sembuf=r)
            loaded[i] = (x, r)

        def mm_group(pbuf, t, col0):
            def f(e):
                for k in range(8):
                    i_ = e.matmul(pbuf[:], lhsT=t[:, k * 128:(k + 1) * 128], rhs=W[k][:, col0:col0 + 512], start=(k == 0), stop=(k == 7))
                return i_
            kb.op("pe", f, [t] + W, [pbuf])

        def roped(t, r, col0):
            k_b = kbf.next()
            for half in range(2):
                pk = pp.next()
                mm_group(pk, t, col0 + half * 512)
                k_f = kf.next()
                CP(kb, "act", k_f[:], pk[:], [pk], [k_f])
                rope(kb, k_f, _Sub(k_b, half), r, 8, rt.next())
            return k_b

        def to_stage(k_b, stage, slot):
            pkt = pkT.next()
            transposes(kb, pkt, k_b, ident, 8, [k_b])
            CP(kb, "act", stage[:, :, slot * 128:(slot + 1) * 128], pkt[:].rearrange("p (h t) -> p h t", h=8), [pkt], [stage])

        stg = {}

        def body(i):
            x, r = loaded.pop(i)
            b = xb.next()
            CP(kb, "dve", b[:], x[:], [x], [b])
            p = pT.next()
            transposes(kb, p, b, ident, 8, [b])
            t = xT.next()
            CP(kb, "act", t[:], p[:], [p], [t])
            yield
            if i % 4 == 0:
                stg["k"] = kst.next()
            k_stage = stg["k"]
            k_b = roped(t, r, 1024)
            v = vb.next()
            for half in range(2):
                pv = pp.next()
                mm_group(pv, t, 2048 + half * 512)
                CP(kb, "dve" if half == 0 else "act", v[:, half * 8:(half + 1) * 8, 0:64], pv[:].rearrange("p (h d) -> p h d", h=8), [pv], [v])
            kb.dma([(V1[i * 128:(i + 1) * 128, :], v[:].rearrange("p h d -> p (h d)"))], reads=[v], sembuf=v)
            q_b = None
            if i >= NCTX:
                io_ = i - NCTX
                if io_ % 4 == 0:
                    stg["q"] = qst.next()
                q_stage = stg["q"]
                q_b = roped(t, r, 0)
            yield
            to_stage(k_b, k_stage, i % 4)
            if i % 4 == 3:
                g0 = (i // 4) * 512
                kb.dma([(KT1[:, :, g0:g0 + 512].rearrange("h p t -> p h t"), k_stage[:])], reads=[k_stage], sembuf=k_stage)
            if q_b is not None:
                to_stage(q_b, q_stage, io_ % 4)
                if io_ % 4 == 3:
                    g0 = (io_ // 4) * 512
                    kb.dma([(QT1[:, :, g0:g0 + 512].rearrange("h p t -> p h t"), q_stage[:])], reads=[q_stage], sembuf=q_stage)

        run_pairs(nblk, body, load)
        kb.end_phase()


class _Sub:
    def __init__(self, parent, half):
        self.parent = parent
        self.half = half

    def __getitem__(self, idx):
        return self.parent.t[:, self.half * 512:(self.half + 1) * 512][idx]

    def __getattr__(self, name):
        return getattr(self.parent, name)

    def __setattr__(self, name, value):
        if name in ("parent", "half"):
            object.__setattr__(self, name, value)
        else:
            setattr(self.parent, name, value)


def phase_L1B(kb, nreg, KT1, QT1, V1, consts, ON):
    DILS = (1, 4, 16)
    with contextlib.ExitStack() as ph:
        KTsb = kb.sb(ph, "G_KT", [128, 8, 4096], BF16)
        QTsb = kb.sb(ph, "G_QT", [128, 8, 2048], BF16)
        ident = kb.sb(ph, "G_id", [128, 128], BF16)
        kb.dma([(ident[:], consts["ident"][:, :])], writes=[ident], sembuf=ident)
        msk = kb.sb(ph, "G_msk", [128, 3, 128], BF16)
        kb.dma([(msk[:], consts["m1"][:, :, :])], writes=[msk], sembuf=msk)
        ps = PSUM(kb, ph)
        pst = Ring([ps.bank(0), ps.bank(1)])
        pO = Ring([[ps.bank(2), ps.bank(3), ps.bank(4)], [ps.bank(5), ps.bank(6), ps.bank(7)]])
        vr = Ring([kb.sb(ph, "G_v%d" % i, [128, 1040], BF16) for i in range(8)])
        ptr = Ring([kb.sb(ph, "G_pt%d" % i, [128, 512], BF16) for i in range(4)])
        osb = Ring([kb.sb(ph, "G_o%d" % i, [128, 1040], F32) for i in range(3)])
        for R in range(nreg):
            kb.dma([(KTsb[:, h, :], KT1[h, :, 2048 * R:2048 * R + 4096]) for h in range(8)], writes=[KTsb], sembuf=KTsb)
            kb.dma([(QTsb[:, h, :], QT1[h, :, 2048 * R:2048 * (R + 1)]) for h in range(8)], writes=[QTsb], sembuf=QTsb)
            qblocks = [(gi, d, r, n) for gi, d in enumerate(DILS) for n in range(16 // d) for r in range(d)]
            vt = {}

            def loadv(qi):
                gi, d, r, n = qblocks[qi]
                tiles = []
                for which in (0, 1):
                    U0 = 2048 * R + 2048 + 128 * d * (n - 1 + which)
                    v = vr.next()
                    src = V1.rearrange("(a s) f -> a s f", s=d)[U0 // d:U0 // d + 128, r, :]
                    kb.dma([(v[:], src)], writes=[v], sembuf=v)
                    tiles.append(v)
                vt[qi] = tiles

            units = [(qi, hp) for qi in range(len(qblocks)) for hp in range(8)]
            state = {}

            def emit_ST(u):
                qi, hp = u
                gi, d, r, n = qblocks[qi]
                st = pst.next()
                q0 = 128 * d * n + r
                k0 = 2048 + 128 * d * (n - 1) + r
                mprev = 2 if (R == 0 and n == 0) else 0

                def f(e):
                    for c in range(2):
                        qap = QTsb[c * 64:(c + 1) * 64, hp, q0:q0 + 127 * d + 1:d]
                        for which in range(2):
                            kk0 = k0 + which * 128 * d
                            col = (c * 2 + which) * 128
                            e.matmul(st[:, col:col + 128], lhsT=KTsb[c * 64:(c + 1) * 64, hp, kk0:kk0 + 127 * d + 1:d], rhs=qap,
                                     start=True, stop=False)
                            i_ = e.matmul(st[:, col:col + 128], lhsT=ident[:], rhs=msk[:, (mprev if which == 0 else 1), :],
                                          start=False, stop=True)
                    return i_
                kb.op("pe", f, [KTsb, QTsb, ident, msk], [st])
                pt = ptr.next()
                ACT(kb, pt[:], st[:], AF.Exp, [st], [pt], scale=0.125)
                return pt

            def emit_PV(u, pt):
                qi, hp = u
                gi, d, r, n = qblocks[qi]
                if hp == 0:
                    state["O"] = pO.next()
                O = state["O"]
                vp, vc = vt[qi]
                banks = set()

                def f(e):
                    for c in range(2):
                        h = hp * 2 + c
                        bk, off = h // 7, (h % 7) * 65
                        banks.add(bk)
                        e.matmul(O[bk][:, off:off + 65], lhsT=pt[:, (c * 2) * 128:(c * 2 + 1) * 128], rhs=vp[:, h * 65:(h + 1) * 65],
                                 start=True, stop=False)
                        i_ = e.matmul(O[bk][:, off:off + 65], lhsT=pt[:, (c * 2 + 1) * 128:(c * 2 + 2) * 128], rhs=vc[:, h * 65:(h + 1) * 65],
                                      start=False, stop=True)
                    return i_
                hs = (hp * 2, hp * 2 + 1)
                wb = [O[bk] for bk in sorted(set(h // 7 for h in hs))]
                kb.op("pe", f, [pt, vp, vc], wb)
                if hp == 7:
                    o = osb.next()
                    CP(kb, "dve", o[:, 0:455], O[0][:, 0:455], [O[0]], [o])
                    CP(kb, "act", o[:, 455:910], O[1][:, 0:455], [O[1]], [o])
                    CP(kb, "dve", o[:, 910:1040], O[2][:, 0:130], [O[2]], [o])
                    A0 = (2048 * R + 128 * d * n) // d
                    dst = ON[gi].rearrange("(a s) f -> a s f", s=d)[A0:A0 + 128, r, :]
                    kb.dma([(dst, o[:])], reads=[o], sembuf=o)
                    del vt[qi]

            loadv(0)
            loadv(1)
            prev = None
            for ui, u in enumerate(units):
                if u[1] == 0 and u[0] + 2 < len(qblocks):
                    loadv(u[0] + 2)
                pt = emit_ST(u)
                if prev is not None:
                    emit_PV(*prev)
                prev = (u, pt)
            emit_PV(*prev)
        kb.end_phase()


class MergeLoader:
    def __init__(self, ON):
        self.ON = ON

    def alloc(self, kb, ph):
        return {
            "a": Ring([[kb.sb(ph, "M_a%d_%d" % (i, g), [128, 1040], F32) for g in range(3)] for i in range(6)]),
            "s1": Ring([kb.sb(ph, "M_s1%d" % i, [128, 1040], F32) for i in range(2)]),
            "s2": Ring([kb.sb(ph, "M_s2%d" % i, [128, 1040], F32) for i in range(2)]),
            "rd": Ring([kb.sb(ph, "M_rd%d" % i, [128, 16], F32) for i in range(2)]),
        }

    def load(self, kb, ex, j, m):
        a = ex["a"].next()
        for g in range(3):
            kb.dma([(a[g][:], self.ON[g][j * 128:(j + 1) * 128, :])], writes=[a[g]], sembuf=a[g])
        s1 = ex["s1"].next()
        TT(kb, "dve", s1[:], a[0][:], a[1][:], ALU.add, [a[0], a[1]], [s1])
        s2 = ex["s2"].next()
        TT(kb, "dve", s2[:], s1[:], a[2][:], ALU.add, [s1, a[2]], [s2])
        rd = ex["rd"].next()
        s3 = s2[:].rearrange("p (h d) -> p h d", h=16)
        kb.op("dve", lambda e: e.reciprocal(out=rd[:], in_=s3[:, :, 64]), [s2], [rd])
        TT(kb, "dve", m[:].rearrange("p (h d) -> p h d", h=16), s3[:, :, 0:64], rd[:].unsqueeze(2).broadcast_to([128, 16, 64]),
           ALU.mult, [s2, rd], [m])


def const_l1(h):
    c = np.arange(128)[:, None]
    a = np.arange(128)[None, :]
    mprev = np.where(c >= a, 0.0, NEG).astype(np.float32)
    mcur = np.where(c <= a, 0.0, NEG).astype(np.float32)
    full = np.full((128, 128), NEG, np.float32)
    m1 = np.stack([mprev, mcur, full if h == 0 else mprev], axis=1).astype(NPBF)
    return {"ident": np.eye(128).astype(NPBF), "m1": m1}


def build_l1(S, debug=False, phases="ABT"):
    nc = bass.Bass("TRN2", target_bir_lowering=False)
    So = S // 2
    nbo = So // 128
    nreg = So // 2048
    T = 2048 + So

    def din(name, shape, dt=F32):
        return nc.dram_tensor(name, list(shape), dt, kind="ExternalInput").ap()

    def scr(name, shape, dt):
        return nc.dram_tensor(name, list(shape), dt, kind="ExternalOutput" if debug else "Internal").ap()

    io = {n: din(n, WEIGHT_SHAPES[n]) for n in L1_W}
    io["x_co"] = din("x_co", [T, D])
    io["p_own"] = din("p_own", [So, PLE])
    io["rope_co"] = din("rope_co", [T, 128])
    consts = {"ident": din("ident", [128, 128], BF16), "m1": din("m1", [128, 3, 128], BF16)}
    out = nc.dram_tensor("out_own", [So, D], F32, kind="ExternalOutput").ap()
    KT1 = scr("KT1", [8, 128, T], BF16)
    QT1 = scr("QT1", [8, 128, So], BF16)
    V1 = scr("V1", [T, 1040], BF16)
    ON = [scr("ON%d" % g, [So, 1040], F32) for g in range(3)]
    XA = scr("XA1", [So, D], F32)
    XB = scr("XB1", [So, D], F32)
    kb = KB(nc)
    if "A" in phases:
        phase_L1A(kb, nbo, io["x_co"], io["rope_co"], io["od_w_in"][0], consts, KT1, QT1, V1)
    if "B" in phases:
        phase_L1B(kb, nreg, KT1, QT1, V1, consts, ON)
    if "T" in phases:
        dense_tail(kb, nbo, 1, None, io["x_co"][2048:T, :], io, XA, XB, out, consts, io["od_w_out"][0], mix_loader=MergeLoader(ON))
    return nc, kb


def l1_inputs(x1_b, p1_b, weights, h):
    S = x1_b.shape[0]
    So = S // 2
    lo = So * h
    m = {n: weights[n] for n in L1_W}
    ctx = x1_b[lo - 2048:lo] if h == 1 else np.zeros((2048, D), np.float32)
    m["x_co"] = np.ascontiguousarray(np.concatenate([ctx, x1_b[lo:lo + So]], axis=0))
    m["p_own"] = np.ascontiguousarray(p1_b[lo:lo + So])
    m["rope_co"] = rope_table(np.arange(lo - 2048, lo + So), 8)
    m.update(const_l1(h))
    return m


ALL_W = list(WEIGHT_SHAPES.keys())


def const_fused(h):
    k = np.arange(128)[:, None]
    q = np.arange(128)[None, :]
    tri = np.where(k <= q, 0.0, NEG).astype(np.float32)
    zero = np.zeros((128, 128), np.float32)
    mAB = np.stack([tri if h == 0 else zero, zero if h == 0 else tri], axis=1).astype(NPBF)
    sel = np.zeros((128, 2), np.float32)
    sel[:, h] = 1.0
    c = {"ident": np.eye(128).astype(NPBF), "cU": const_cU(), "sel": sel, "mAB": mAB,
         "hb": np.full((128, 1), NEG if h == 0 else 0.0, np.float32)}
    c["m1"] = const_l1(h)["m1"]
    return c


def build_fused(S, debug=False):
    nc = bass.Bass("TRN2", target_bir_lowering=False)
    nb = S // 128
    So = S // 2
    nbh = So // 128
    nA = NCTX + nbh
    TA = nA * 128

    def din(name, shape, dt=F32):
        return nc.dram_tensor(name, list(shape), dt, kind="ExternalInput").ap()

    def scr(name, shape, dt):
        return nc.dram_tensor(name, list(shape), dt, kind="ExternalOutput" if debug else "Internal").ap()

    io = {n: din(n, WEIGHT_SHAPES[n]) for n in ALL_W}
    io["x_all"] = din("x_all", [S, D])
    io["x_A"] = din("x_A", [TA, D])
    io["p0_A"] = din("p0_A", [TA, PLE])
    io["p1_own"] = din("p1_own", [So, PLE])
    io["rope_all"] = din("rope_all", [S, 128])
    io["rope_A"] = din("rope_A", [TA, 128])
    consts = {"ident": din("ident", [128, 128], BF16), "cU": din("cU", [128, 4, 128]), "sel": din("sel", [128, 2]),
              "mAB": din("mAB", [128, 2, 128], BF16), "hb": din("hb", [128, 1]), "m1": din("m1", [128, 3, 128], BF16)}
    out = nc.dram_tensor("out_own", [So, D], F32, kind="ExternalOutput").ap()
    KT = scr("KT", [nb, 128, 512], BF16)
    V = scr("V", [nb, 128, 516], BF16)
    SS = scr("SS", [nb, 2, 128, 512], BF16)
    QT = scr("QT", [nA, 128, 512], BF16)
    MIX = scr("MIX", [TA, D], BF16)
    XA = scr("XA", [TA, D], F32)
    XB = scr("XB", [TA, D], F32)
    X1 = scr("X1", [TA, D], F32)
    KT1 = scr("KT1", [8, 128, TA], BF16)
    QT1 = scr("QT1", [8, 128, So], BF16)
    V1 = scr("V1", [TA, 1040], BF16)
    ON = [scr("ON%d" % g, [So, 1040], F32) for g in range(3)]
    XA1 = scr("XA1", [So, D], F32)
    XB1 = scr("XB1", [So, D], F32)
    kb = KB(nc)
    phase_A1(kb, S, io["x_all"], io["rope_all"], io["ev_w_in"][0], io["hg_lb_logits"], consts, KT, V, SS, blended=False)
    phase_A2(kb, nA, io["x_A"], io["rope_A"], io["ev_w_in"][0], io["hg_lb_logits"], io["hg_norm_g"], consts, SS, QT, MIX,
             ss_pick=lambda i: (max(i - NCTX, 0), i + nbh - NCTX))
    phase_B2(kb, S, nA, nbh, KT, V, QT, io["da_lambda"][0], io["da_subln_g"], consts, MIX)
    dense_tail(kb, nA, 0, MIX, io["x_A"], io, XA, XB, X1, consts, io["ev_w_out"][0], p_in=io["p0_A"])
    phase_L1A(kb, nbh, X1, io["rope_A"], io["od_w_in"][0], consts, KT1, QT1, V1)
    phase_L1B(kb, So // 2048, KT1, QT1, V1, consts, ON)
    dense_tail(kb, nbh, 1, None, X1[2048:TA, :], io, XA1, XB1, out, consts, io["od_w_out"][0], mix_loader=MergeLoader(ON),
               p_in=io["p1_own"])
    return nc, kb


def fused_inputs(x_b, p_b, weights, h):
    S = x_b.shape[0]
    So = S // 2
    lo = So * h
    m = {n: weights[n] for n in ALL_W}
    m["x_all"] = x_b
    if h == 0:
        m["x_A"] = np.ascontiguousarray(np.concatenate([np.zeros((2048, D), np.float32), x_b[0:So]], axis=0))
        m["p0_A"] = np.ascontiguousarray(np.concatenate([np.zeros((2048, PLE), np.float32), p_b[0, 0:So]], axis=0))
    else:
        m["x_A"] = np.ascontiguousarray(x_b[lo - 2048:lo + So])
        m["p0_A"] = np.ascontiguousarray(p_b[0, lo - 2048:lo + So])
    m["p1_own"] = np.ascontiguousarray(p_b[1, lo:lo + So])
    m["rope_all"] = rope_table(np.arange(S), 8)
    m["rope_A"] = rope_table(np.arange(lo - 2048, lo + So), 8)
    m.update(const_fused(h))
    return m


_PROGS = {}


def _prog(kind, S):
    key = (kind, S)
    if key not in _PROGS:
        _PROGS[key] = (build_l0(S) if kind == 0 else build_l1(S) if kind == 1 else build_fused(S))[0]
    return _PROGS[key]


def kernel(**inputs):
    inputs = {k: np.ascontiguousarray(np.asarray(v, dtype=np.float32)) for k, v in inputs.items()}
    x, p = inputs["x"], inputs["p"]
    B, S, _ = x.shape
    So = S // 2
    n = 2 * B
    weights = {k: v for k, v in inputs.items() if k not in ("x", "p")}
    maps = [fused_inputs(x[c // 2], p[:, c // 2], weights, c % 2) for c in range(n)]
    res = run_bass_kernel_spmd(_prog(2, S), maps, core_ids=list(range(n)))
    out = np.empty((B, S, D), np.float32)
    for c in range(n):
        b, h = c // 2, c % 2
        out[b, h * So:(h + 1) * So] = res.results[c]["out_own"]
    return out
```

```python
import contextlib
import math
import numpy as np
import ml_dtypes
import concourse.bass as bass
import concourse.mybir as mybir
from concourse.bass_utils import run_bass_kernel_spmd

F32 = mybir.dt.float32
BF16 = mybir.dt.bfloat16
AF = mybir.ActivationFunctionType
ALU = mybir.AluOpType
AX = mybir.AxisListType
NPBF = ml_dtypes.bfloat16

D = 1024
FFN = 4096
PLE = 256
ALPHA = 4.0 ** 0.25
LN_EPS = 1e-5
LAM_INIT0 = 0.8 - 0.6 * math.exp(-0.3 * 0)
NEG = -30000.0
A2_STOP = 99
ROPE_THETA = 500000.0


class Buf:
    __slots__ = ("t", "w", "r", "dsem", "psum")

    def __init__(self, t):
        self.t = t
        self.w = None
        self.r = []
        self.dsem = None
        self.psum = False

    def __getitem__(self, idx):
        return self.t[idx]


class Ring:
    def __init__(self, bufs):
        self.bufs = bufs
        self.i = 0

    def next(self):
        b = self.bufs[self.i % len(self.bufs)]
        self.i += 1
        return b


class Eng:
    def __init__(self, name, eng, semidx):
        self.name = name
        self.eng = eng
        self.semidx = semidx
        self.waited = {}


class KB:
    def __init__(self, nc):
        self.nc = nc
        self.es = contextlib.ExitStack()
        self.sems = []
        self.semcnt = []
        self.semdma = []
        self.free_dsems = []
        self.E = {}
        for name, eng in (("pe", nc.tensor), ("act", nc.scalar), ("dve", nc.vector),
                          ("pool", nc.gpsimd), ("sp", nc.sync)):
            si = self._newsem("e_" + name, False)
            self.E[name] = Eng(name, eng, si)
        self.n_inst = 0
        self.phase_bufs = []

    def _newsem(self, name, isdma):
        s = self.es.enter_context(self.nc.semaphore(name))
        self.sems.append(s)
        self.semcnt.append(0)
        self.semdma.append(isdma)
        return len(self.sems) - 1

    def get_dsem(self):
        if self.free_dsems:
            return self.free_dsems.pop()
        return self._newsem("d%d" % len(self.sems), True)

    def sb(self, ph, name, shape, dtype):
        self.uid = getattr(self, "uid", 0) + 1
        name = "%s_u%d" % (name, self.uid)
        b = Buf(ph.enter_context(self.nc.sbuf_tensor(name, list(shape), dtype)))
        self.phase_bufs.append(b)
        return b

    def uname(self, name):
        self.uid = getattr(self, "uid", 0) + 1
        return "%s_u%d" % (name, self.uid)

    def view(self, ap):
        b = Buf(ap)
        self.phase_bufs.append(b)
        return b

    def _wait(self, E, toks):
        for (si, val) in toks:
            if self.semdma[si]:
                val = self.semcnt[si]
            if si == E.semidx and E.name == "pe":
                continue
            if E.waited.get(si, 0) >= val:
                continue
            E.eng.wait_ge(self.sems[si], val)
            E.waited[si] = val
            self.n_inst += 1

    def _deps(self, reads, writes):
        need = []
        for b in reads:
            if b.w is not None:
                need.append(b.w)
            if b.psum:
                need.extend(b.r)
        for b in writes:
            if b.w is not None:
                need.append(b.w)
            need.extend(b.r)
        return need

    def _commit(self, tok, reads, writes):
        for b in reads:
            if b.psum:
                b.w = tok
                b.r = []
            else:
                b.r.append(tok)
        for b in writes:
            b.w = tok
            b.r = []

    def op(self, en, fn, reads=(), writes=()):
        E = self.E[en]
        self._wait(E, self._deps(reads, writes))
        inst = fn(E.eng)
        si = E.semidx
        self.semcnt[si] += 1
        inst.then_inc(self.sems[si], 1)
        tok = (si, self.semcnt[si])
        self._commit(tok, reads, writes)
        self.n_inst += 1
        return tok

    def dma(self, pairs, reads=(), writes=(), sembuf=None, q="sp"):
        E = self.E[q]
        self._wait(E, self._deps(reads, writes))
        if sembuf.dsem is None:
            sembuf.dsem = self.get_dsem()
        si = sembuf.dsem
        for (o, i) in pairs:
            E.eng.dma_start(out=o, in_=i).then_inc(self.sems[si], 16)
            self.semcnt[si] += 16
            self.n_inst += 1
        tok = (si, self.semcnt[si])
        self._commit(tok, reads, writes)
        return tok

    def barrier(self):
        allt = [(si, self.semcnt[si]) for si in range(len(self.sems)) if self.semcnt[si] > 0]
        for E in self.E.values():
            self._wait(E, allt)

    def end_phase(self):
        self.barrier()
        for b in self.phase_bufs:
            if b.dsem is not None:
                self.free_dsems.append(b.dsem)
                b.dsem = None
        self.phase_bufs = []


def TT(kb, en, out, a, b, op, reads, writes):
    return kb.op(en, lambda e: e.tensor_tensor(out=out, in0=a, in1=b, op=op), reads, writes)


def TS(kb, en, out, a, s1, s2, op0, op1, reads, writes):
    if op1 is None:
        return kb.op(en, lambda e: e.tensor_scalar(out=out, in0=a, scalar1=s1, scalar2=None, op0=op0), reads, writes)
    return kb.op(en, lambda e: e.tensor_scalar(out=out, in0=a, scalar1=s1, scalar2=s2, op0=op0, op1=op1), reads, writes)


def STT(kb, out, a, s, b, op0, op1, reads, writes):
    return kb.op("dve", lambda e: e.scalar_tensor_tensor(out=out, in0=a, scalar=s, in1=b, op0=op0, op1=op1), reads, writes)


def ACT(kb, out, in_, func, reads, writes, scale=1.0, bias=0.0, accum=None):
    if accum is not None:
        return kb.op("act", lambda e: e.activation(out=out, in_=in_, func=func, scale=scale, bias=bias, accum_out=accum), reads, writes)
    return kb.op("act", lambda e: e.activation(out=out, in_=in_, func=func, scale=scale, bias=bias), reads, writes)


def CP(kb, en, out, in_, reads, writes):
    if en == "act":
        return kb.op("act", lambda e: e.copy(out=out, in_=in_), reads, writes)
    return kb.op(en, lambda e: e.tensor_copy(out=out, in_=in_), reads, writes)


def rstd_act(kb, out, in_, tmp, scale, reads_buf, tmp_buf, out_buf):
    ACT(kb, tmp, in_, AF.Ln, [reads_buf], [tmp_buf], scale=scale, bias=LN_EPS)
    ACT(kb, out, tmp, AF.Exp, [tmp_buf], [out_buf], scale=-0.5)


def sigmoid_exp(kb, out, in_, in_buf, t1, t2, out_buf, engs=None):
    ACT(kb, t1[:], in_, AF.Exp, [in_buf], [t1], scale=-1.0)
    ACT(kb, t2[:], t1[:], AF.Ln, [t1], [t2], bias=1.0)
    ACT(kb, out, t2[:], AF.Exp, [t2], [out_buf], scale=-1.0)


def run_pairs(n, body, load, width=2, prefetch=2):
    state = {"n": 0}

    def ensure(upto):
        while state["n"] <= min(upto, n - 1):
            load(state["n"])
            state["n"] += 1
    j = 0
    while j < n:
        grp = list(range(j, min(n, j + width)))
        ensure(grp[-1] + prefetch)
        alive = [body(b) for b in grp]
        while alive:
            for g in list(alive):
                try:
                    next(g)
                except StopIteration:
                    alive.remove(g)
        j += width


class PSUM:
    def __init__(self, kb, ph):
        kb.uid = getattr(kb, "uid", 0) + 1
        self.t = ph.enter_context(kb.nc.psum_tensor("PSALL_u%d" % kb.uid, [128, 4096], F32))
        self.kb = kb

    def bank(self, k, dtype=F32):
        ap = self.t[:, k * 512:(k + 1) * 512]
        if dtype == BF16:
            ap = ap.bitcast(BF16)
        b = self.kb.view(ap)
        b.psum = True
        return b


def load_weight(kb, W, src, col_slices, krows):
    ncols = sum(hi - lo for lo, hi in col_slices)
    with contextlib.ExitStack() as st:
        PW = 2048
        stage = Ring([kb.sb(st, "wst%d" % i, [128, PW], F32) for i in range(3)])
        engs = ["dve", "dve", "act"]
        n = 0
        for kc in range(krows):
            dst0 = 0
            for (lo, hi) in col_slices:
                c = lo
                while c < hi:
                    w = min(PW, hi - c)
                    sbuf = stage.next()
                    kb.dma([(sbuf[:, 0:w], src[kc * 128:(kc + 1) * 128, c:c + w])], writes=[sbuf], sembuf=sbuf)
                    CP(kb, engs[n % 2], W[kc][:, dst0:dst0 + w], sbuf[:, 0:w], [sbuf], [W[kc]])
                    n += 1
                    dst0 += w
                    c += w
        kb.barrier()


def transposes(kb, psT, src, ident, nblk, reads):
    def f(e):
        for c in range(nblk):
            i = e.transpose(psT[:, c * 128:(c + 1) * 128], src[:, c * 128:(c + 1) * 128], ident[:])
        return i
    return kb.op("pe", f, reads + [ident], [psT])


def rope(kb, src, dst, rp, ng, tmps):
    s3 = src[:].rearrange("p (g d) -> p g d", g=ng)
    d3 = dst[:].rearrange("p (g d) -> p g d", g=ng)
    cos = rp[:, 0:ng * 8].rearrange("p (g d) -> p g d", g=ng)
    sin = rp[:, ng * 8:2 * ng * 8].rearrange("p (g d) -> p g d", g=ng)
    t1, t2, t3, t4 = tmps
    v = lambda t: t[:, 0:ng * 8].rearrange("p (g d) -> p g d", g=ng)
    TT(kb, "dve", v(t1), s3[:, :, 0:8], cos, ALU.mult, [src, rp], [t1])
    TT(kb, "dve", v(t2), s3[:, :, 8:16], sin, ALU.mult, [src, rp], [t2])
    TT(kb, "dve", v(t3), s3[:, :, 0:8], sin, ALU.mult, [src, rp], [t3])
    TT(kb, "dve", v(t4), s3[:, :, 8:16], cos, ALU.mult, [src, rp], [t4])
    TT(kb, "dve", d3[:, :, 0:8], v(t1), v(t2), ALU.subtract, [t1, t2], [dst])
    TT(kb, "dve", d3[:, :, 8:16], v(t3), v(t4), ALU.add, [t3, t4], [dst])
    CP(kb, "dve", d3[:, :, 16:64], s3[:, :, 16:64], [src], [dst])


def layer_norm(kb, y, g, b, st, mv, sd, rs, tmp):
    ACT(kb, tmp[:], y[:], AF.Copy, [y], [tmp, st], accum=st[:, 0:1])
    ACT(kb, tmp[:], y[:], AF.Square, [y], [tmp, st], accum=st[:, 1:2])
    TS(kb, "dve", mv[:, 0:1], st[:, 0:1], 1.0 / D, None, ALU.mult, None, [st], [mv])
    TT(kb, "dve", mv[:, 1:2], mv[:, 0:1], mv[:, 0:1], ALU.mult, [mv], [mv])
    STT(kb, mv[:, 2:3], st[:, 1:2], 1.0 / D, mv[:, 1:2], ALU.mult, ALU.subtract, [st, mv], [mv])
    rstd_act(kb, rs[:, 0:1], mv[:, 2:3], sd[:, 0:1], 1.0, mv, sd, rs)
    STT(kb, rs[:, 1:2], mv[:, 0:1], -1.0, rs[:, 0:1], ALU.mult, ALU.mult, [mv, rs], [rs])
    kb.op("act", lambda e: e.activation(out=tmp[:], in_=y[:], func=AF.Identity, scale=rs[:, 0:1], bias=rs[:, 1:2]),
          [y, rs], [tmp])
    TT(kb, "dve", y[:], tmp[:], g[:], ALU.mult, [tmp, g], [y])
    TT(kb, "dve", tmp[:], y[:], b[:], ALU.add, [y, b], [tmp])
    return tmp


def phase_C(kb, nbo, mix_src, xres, w_out, ln_g, ln_b, xa_out, consts, mix_loader=None):
    with contextlib.ExitStack() as ph:
        Wt = ph.enter_context(kb.nc.sbuf_tensor(kb.uname("C_W"), [128, 8, 1024], BF16))
        W = [kb.view(Wt[:, k, :]) for k in range(8)]
        load_weight(kb, W, w_out, [(0, 1024)], 8)
        ident = kb.sb(ph, "C_id", [128, 128], BF16)
        kb.dma([(ident[:], consts["ident"][:, :])], writes=[ident], sembuf=ident)
        gt = kb.sb(ph, "C_g", [128, 1024], F32)
        bt = kb.sb(ph, "C_b", [128, 1024], F32)
        kb.dma([(gt[:], ln_g.partition_broadcast(128))], writes=[gt], sembuf=gt)
        kb.dma([(bt[:], ln_b.partition_broadcast(128))], writes=[bt], sembuf=bt)
        ps = PSUM(kb, ph)
        pT = Ring([ps.bank(0, BF16), ps.bank(1, BF16)])
        po = Ring([[ps.bank(2), ps.bank(3)], [ps.bank(4), ps.bank(5)]])
        mixr = Ring([kb.sb(ph, "C_mix%d" % i, [128, 1024], BF16) for i in range(6)])
        xr = Ring([kb.sb(ph, "C_x%d" % i, [128, 1024], F32) for i in range(6)])
        mT = Ring([kb.sb(ph, "C_mT%d" % i, [128, 1024], BF16) for i in range(4)])
        yr = Ring([kb.sb(ph, "C_y%d" % i, [128, 1024], F32) for i in range(4)])
        tmpr = Ring([kb.sb(ph, "C_t%d" % i, [128, 1024], F32) for i in range(4)])
        sm = Ring([[kb.sb(ph, "C_s%d_%d" % (i, k), [128, 12], F32) for k in range(4)] for i in range(4)])
        extra = mix_loader.alloc(kb, ph) if mix_loader is not None else None
        loaded = {}

        def load(j):
            x = xr.next()
            kb.dma([(x[:], xres[j * 128:(j + 1) * 128, :])], writes=[x], sembuf=x)
            m = mixr.next()
            if mix_loader is None:
                kb.dma([(m[:], mix_src[j * 128:(j + 1) * 128, :])], writes=[m], sembuf=m)
            else:
                mix_loader.load(kb, extra, j, m)
            loaded[j] = (x, m)

        def body(j):
            x, m = loaded.pop(j)
            p = pT.next()
            transposes(kb, p, m, ident, 8, [m])
            t = mT.next()
            CP(kb, "act", t[:], p[:], [p], [t])
            yield
            pa, pb = po.next()
            for half, pp in ((0, pa), (1, pb)):
                def f(e, half=half, pp=pp):
                    for k in range(8):
                        i = e.matmul(pp[:], lhsT=t[:, k * 128:(k + 1) * 128], rhs=W[k][:, half * 512:(half + 1) * 512],
                                     start=(k == 0), stop=(k == 7))
                    return i
                kb.op("pe", f, [t] + W, [pp])
            y = yr.next()
            STT(kb, y[:, 0:512], x[:, 0:512], ALPHA, pa[:], ALU.mult, ALU.add, [x, pa], [y])
            STT(kb, y[:, 512:1024], x[:, 512:1024], ALPHA, pb[:], ALU.mult, ALU.add, [x, pb], [y])
            yield
            s = sm.next()
            o = layer_norm(kb, y, gt, bt, s[0], s[1], s[2], s[3], tmpr.next())
            kb.dma([(xa_out[j * 128:(j + 1) * 128, :], o[:])], reads=[o], sembuf=o)

        run_pairs(nbo, body, load, width=4)
        kb.end_phase()


def phase_D1(kb, nbo, xa, w1, w2, ln_g, ln_b, xb_out, consts):
    assert nbo % 4 == 0
    with contextlib.ExitStack() as ph:
        W1t = ph.enter_context(kb.nc.sbuf_tensor(kb.uname("D_W1"), [128, 8, FFN], BF16))
        W2t = ph.enter_context(kb.nc.sbuf_tensor(kb.uname("D_W2"), [128, 32, D], BF16))
        W1 = [kb.view(W1t[:, k, :]) for k in range(8)]
        W2 = [kb.view(W2t[:, k, :]) for k in range(32)]
        load_weight(kb, W1, w1, [(0, FFN)], 8)
        load_weight(kb, W2, w2, [(0, D)], 32)
        ident = kb.sb(ph, "D_id", [128, 128], BF16)
        kb.dma([(ident[:], consts["ident"][:, :])], writes=[ident], sembuf=ident)
        gt = kb.sb(ph, "D_g", [128, 1024], F32)
        bt = kb.sb(ph, "D_b", [128, 1024], F32)
        kb.dma([(gt[:], ln_g.partition_broadcast(128))], writes=[gt], sembuf=gt)
        kb.dma([(bt[:], ln_b.partition_broadcast(128))], writes=[bt], sembuf=bt)
        ps = PSUM(kb, ph)
        pT = Ring([ps.bank(0, BF16)])
        ph_ = Ring([ps.bank(1), ps.bank(2), ps.bank(3)])
        po = Ring([[ps.bank(4), ps.bank(5)], [ps.bank(6), ps.bank(7)]])
        xr = Ring([kb.sb(ph, "D_x%d" % i, [128, 1024], F32) for i in range(2)])
        xbr = Ring([kb.sb(ph, "D_xb%d" % i, [128, 1024], BF16) for i in range(2)])
        xT = kb.sb(ph, "D_xT", [128, 8, 512], BF16)
        hT = [kb.sb(ph, "D_hT%d" % c, [128, 512], BF16) for c in range(32)]
        rl = Ring([kb.sb(ph, "D_rl%d" % i, [128, 512], F32) for i in range(2)])
        yr = Ring([kb.sb(ph, "D_y%d" % i, [128, 1024], F32) for i in range(1)])
        tmpr = Ring([kb.sb(ph, "D_t%d" % i, [128, 1024], F32) for i in range(2)])
        sm = Ring([[kb.sb(ph, "D_s%d_%d" % (i, k), [128, 12], F32) for k in range(4)] for i in range(2)])
        ng = nbo // 4
        for g in range(ng):
            for tb in range(4):
                x = xr.next()
                j = g * 4 + tb
                kb.dma([(x[:], xa[j * 128:(j + 1) * 128, :])], writes=[x], sembuf=x)
                xb = xbr.next()
                CP(kb, "dve" if tb % 2 == 0 else "dve", xb[:], x[:], [x], [xb])
                p = pT.next()
                transposes(kb, p, xb, ident, 8, [xb])
                CP(kb, "act", xT[:, :, tb * 128:(tb + 1) * 128], p[:].rearrange("p (k t) -> p k t", k=8), [p], [xT])
            for c in range(32):
                pp = ph_.next()

                def f(e, c=c, pp=pp):
                    for k in range(8):
                        i = e.matmul(pp[:], lhsT=W1[k][:, c * 128:(c + 1) * 128], rhs=xT[:, k, :], start=(k == 0), stop=(k == 7))
                    return i
                kb.op("pe", f, [xT] + W1, [pp])
                r = rl.next()
                ACT(kb, r[:], pp[:], AF.Relu, [pp], [r])
                TT(kb, "dve" if c % 2 == 0 else "dve", hT[c][:], r[:], r[:], ALU.mult, [r], [hT[c]])
            for tb in range(4):
                pa, pb = po.next()
                for half, pp in ((0, pa), (1, pb)):
                    def f(e, half=half, pp=pp, tb=tb):
                        for c in range(32):
                            i = e.matmul(pp[:], lhsT=hT[c][:, tb * 128:(tb + 1) * 128], rhs=W2[c][:, half * 512:(half + 1) * 512],
                                         start=(c == 0), stop=(c == 31))
                        return i
                    kb.op("pe", f, hT + W2, [pp])
                x = xr.next()
                j = g * 4 + tb
                kb.dma([(x[:], xa[j * 128:(j + 1) * 128, :])], writes=[x], sembuf=x)
                y = yr.next()
                STT(kb, y[:, 0:512], x[:, 0:512], ALPHA, pa[:], ALU.mult, ALU.add, [x, pa], [y])
                STT(kb, y[:, 512:1024], x[:, 512:1024], ALPHA, pb[:], ALU.mult, ALU.add, [x, pb], [y])
                s = sm.next()
                o = layer_norm(kb, y, gt, bt, s[0], s[1], s[2], s[3], tmpr.next())
                kb.dma([(xb_out[j * 128:(j + 1) * 128, :], o[:])], reads=[o], sembuf=o)
        kb.end_phase()


def phase_D2(kb, nbo, xb_in, p_in, w_gate, w_proj, norm_g, out, consts):
    with contextlib.ExitStack() as ph:
        Wgt = ph.enter_context(kb.nc.sbuf_tensor(kb.uname("E_Wg"), [128, 8, D], BF16))
        Wpt = ph.enter_context(kb.nc.sbuf_tensor(kb.uname("E_Wp"), [128, 2, D], BF16))
        Wg = [kb.view(Wgt[:, k, :]) for k in range(8)]
        Wp = [kb.view(Wpt[:, k, :]) for k in range(2)]
        load_weight(kb, Wg, w_gate, [(0, D)], 8)
        load_weight(kb, Wp, w_proj, [(0, D)], 2)
        ident = kb.sb(ph, "E_id", [128, 128], BF16)
        kb.dma([(ident[:], consts["ident"][:, :])], writes=[ident], sembuf=ident)
        gt = kb.sb(ph, "E_g", [128, 1024], F32)
        kb.dma([(gt[:], norm_g.partition_broadcast(128))], writes=[gt], sembuf=gt)
        ps = PSUM(kb, ph)
        pT = Ring([ps.bank(0, BF16), ps.bank(1, BF16)])
        pg = Ring([[ps.bank(2), ps.bank(3)]])
        pe_ = Ring([[ps.bank(4), ps.bank(5)], [ps.bank(6), ps.bank(7)]])
        er = Ring([kb.sb(ph, "E_er%d" % i, [128, 1024], F32) for i in range(4)])
        xr = Ring([kb.sb(ph, "E_x%d" % i, [128, 1024], F32) for i in range(6)])
        pr = Ring([kb.sb(ph, "E_p%d" % i, [128, 256], F32) for i in range(6)])
        xbr = Ring([kb.sb(ph, "E_xb%d" % i, [128, 1280], BF16) for i in range(2)])
        xT = Ring([kb.sb(ph, "E_xT%d" % i, [128, 1280], BF16) for i in range(4)])
        gsb = Ring([kb.sb(ph, "E_gs%d" % i, [128, 1024], F32) for i in range(4)])
        gt1 = Ring([kb.sb(ph, "E_g1%d" % i, [128, 1024], F32) for i in range(4)])
        gt2 = Ring([kb.sb(ph, "E_g2%d" % i, [128, 1024], F32) for i in range(4)])
        esb = Ring([kb.sb(ph, "E_es%d" % i, [128, 1024], F32) for i in range(2)])
        e2 = Ring([kb.sb(ph, "E_e2%d" % i, [128, 1024], F32) for i in range(2)])
        junk = Ring([kb.sb(ph, "E_jk%d" % i, [128, 512], F32) for i in range(2)])
        outr = Ring([kb.sb(ph, "E_o%d" % i, [128, 1024], F32) for i in range(2)])
        sm = Ring([[kb.sb(ph, "E_s%d_%d" % (i, k), [128, 4], F32) for k in range(4)] for i in range(4)])
        loaded = {}

        def load(j):
            x = xr.next()
            kb.dma([(x[:], xb_in[j * 128:(j + 1) * 128, :])], writes=[x], sembuf=x)
            p = pr.next()
            kb.dma([(p[:], p_in[j * 128:(j + 1) * 128, :])], writes=[p], sembuf=p)
            loaded[j] = (x, p)

        def body(j):
            x, p = loaded.pop(j)
            xb = xbr.next()
            CP(kb, "dve", xb[:, 0:1024], x[:], [x], [xb])
            CP(kb, "dve", xb[:, 1024:1280], p[:], [p], [xb])
            pt1 = pT.next()
            transposes(kb, pt1, xb, ident, 8, [xb])
            t = xT.next()
            CP(kb, "act", t[:, 0:1024], pt1[:], [pt1], [t])
            pt2 = pT.next()

            def f2(e, xb=xb, pt2=pt2):
                for c in range(2):
                    i = e.transpose(pt2[:, c * 128:(c + 1) * 128], xb[:, 1024 + c * 128:1024 + (c + 1) * 128], ident[:])
                return i
            kb.op("pe", f2, [xb, ident], [pt2])
            CP(kb, "dve", t[:, 1024:1280], pt2[:, 0:256], [pt2], [t])
            yield
            ga, gb = pg.next()
            ea, eb = pe_.next()
            for half, pp in ((0, ga), (1, gb)):
                def f(e, half=half, pp=pp, t=t):
                    for k in range(8):
                        i = e.matmul(pp[:], lhsT=t[:, k * 128:(k + 1) * 128], rhs=Wg[k][:, half * 512:(half + 1) * 512],
                                     start=(k == 0), stop=(k == 7))
                    return i
                kb.op("pe", f, [t] + Wg, [pp])
            for half, pp in ((0, ea), (1, eb)):
                def f(e, half=half, pp=pp, t=t):
                    for k in range(2):
                        i = e.matmul(pp[:], lhsT=t[:, 1024 + k * 128:1024 + (k + 1) * 128], rhs=Wp[k][:, half * 512:(half + 1) * 512],
                                     start=(k == 0), stop=(k == 1))
                    return i
                kb.op("pe", f, [t] + Wp, [pp])
            s = sm.next()
            gs = gsb.next()
            g1_, g2_ = gt1.next(), gt2.next()
            ACT(kb, g1_[:, 0:512], ga[:], AF.Exp, [ga], [g1_], scale=-1.0)
            ACT(kb, g1_[:, 512:1024], gb[:], AF.Exp, [gb], [g1_], scale=-1.0)
            jk = junk.next()
            ACT(kb, jk[:], ea[:], AF.Square, [ea], [jk, s[0]], accum=s[0][:, 0:1])
            jk = junk.next()
            ACT(kb, jk[:], eb[:], AF.Square, [eb], [jk, s[0]], accum=s[0][:, 1:2])
            e_r = er.next()
            CP(kb, "dve", e_r[:, 0:512], ea[:], [ea], [e_r])
            CP(kb, "dve", e_r[:, 512:1024], eb[:], [eb], [e_r])
            yield
            ACT(kb, g2_[:], g1_[:], AF.Ln, [g1_], [g2_], bias=1.0)
            ACT(kb, gs[:], g2_[:], AF.Exp, [g2_], [gs], scale=-1.0)
            TT(kb, "dve", s[1][:, 0:1], s[0][:, 0:1], s[0][:, 1:2], ALU.add, [s[0]], [s[1]])
            rstd_act(kb, s[3][:, 0:1], s[1][:, 0:1], s[2][:, 0:1], 1.0 / D, s[1], s[2], s[3])
            es = esb.next()
            kb.op("act", lambda e, es=es, e_r=e_r, s=s: e.activation(out=es[:], in_=e_r[:], func=AF.Copy, scale=s[3][:, 0:1]), [e_r, s[3]], [es])
            ee = e2.next()
            TT(kb, "dve", ee[:], es[:], gt[:], ALU.mult, [es, gt], [ee])
            TT(kb, "dve", es[:], ee[:], gs[:], ALU.mult, [ee, gs], [es])
            o = outr.next()
            TT(kb, "dve", o[:], es[:], x[:], ALU.add, [es, x], [o])
            kb.dma([(out[j * 128:(j + 1) * 128, :], o[:])], reads=[o], sembuf=o)

        run_pairs(nbo, body, load, width=4)
        kb.end_phase()


def hgrn_gate_tables(kb, ph, lb_logits):
    lt = kb.sb(ph, "lbl", [128, 2, 512], F32)
    kb.dma([(lt[:], lb_logits.partition_broadcast(128))], writes=[lt], sembuf=lt)
    dl = kb.sb(ph, "lbd", [128, 512], F32)
    lb = kb.sb(ph, "lb", [128, 512], F32)
    oml = kb.sb(ph, "oml", [128, 512], F32)
    TT(kb, "dve", dl[:], lt[:, 0, :], lt[:, 1, :], ALU.subtract, [lt], [dl])
    lt1 = kb.sb(ph, "lbt1", [128, 512], F32)
    lt2 = kb.sb(ph, "lbt2", [128, 512], F32)
    sigmoid_exp(kb, lb[:], dl[:], dl, lt1, lt2, lb)
    TS(kb, "dve", oml[:], lb[:], -1.0, 1.0, ALU.mult, ALU.add, [lb], [oml])
    return lb, oml


def phase_A1(kb, S, x_all, rope_all, w_in, lb_logits, consts, KT, V, SS, blended=True):
    nb = S // 128
    with contextlib.ExitStack() as ph:
        Wt = ph.enter_context(kb.nc.sbuf_tensor(kb.uname("A1_W"), [128, 8, 2048], BF16))
        W = [kb.view(Wt[:, k, :]) for k in range(8)]
        load_weight(kb, W, w_in, [(512, 1024), (1024, 1536), (2048, 2560), (2560, 3072)], 8)
        ident = kb.sb(ph, "A1_id", [128, 128], BF16)
        kb.dma([(ident[:], consts["ident"][:, :])], writes=[ident], sembuf=ident)
        cU = kb.sb(ph, "A1_cU", [128, 4, 128], F32)
        kb.dma([(cU[:], consts["cU"][:, :, :])], writes=[cU], sembuf=cU)
        sel = kb.sb(ph, "A1_sel", [128, 2], F32)
        kb.dma([(sel[:], consts["sel"][:, :])], writes=[sel], sembuf=sel)
        lb, oml = hgrn_gate_tables(kb, ph, lb_logits)
        ps = PSUM(kb, ph)
        pT = Ring([ps.bank(0, BF16)])
        pp = Ring([ps.bank(1), ps.bank(2), ps.bank(3)])
        pkT = Ring([ps.bank(4, BF16)])
        pbd = Ring([ps.bank(5)])
        pdec = Ring([ps.bank(6)])
        pS = Ring([ps.bank(7)])
        xf = Ring([kb.sb(ph, "A1_xf%d" % i, [128, 1024], F32) for i in range(5)])
        rp = Ring([kb.sb(ph, "A1_rp%d" % i, [128, 128], F32) for i in range(5)])
        xb = Ring([kb.sb(ph, "A1_xb%d" % i, [128, 1024], BF16) for i in range(2)])
        xT = Ring([kb.sb(ph, "A1_xT%d" % i, [128, 1024], BF16) for i in range(3)])
        kf = Ring([kb.sb(ph, "A1_kf%d" % i, [128, 512], F32) for i in range(3)])
        kbf = Ring([kb.sb(ph, "A1_kb%d" % i, [128, 512], BF16) for i in range(2)])
        rt = Ring([[kb.sb(ph, "A1_rt%d_%d" % (i, k), [128, 64], F32) for k in range(4)] for i in range(2)])
        kTs = Ring([kb.sb(ph, "A1_kT%d" % i, [128, 512], BF16) for i in range(3)])
        vb = Ring([kb.sb(ph, "A1_vb%d" % i, [128, 516], BF16) for i in range(3)])
        sg = Ring([kb.sb(ph, "A1_sg%d" % i, [128, 512], F32) for i in range(3)])
        sgt1 = Ring([kb.sb(ph, "A1_sgt1%d" % i, [128, 512], F32) for i in range(2)])
        sgt2 = Ring([kb.sb(ph, "A1_sgt2%d" % i, [128, 512], F32) for i in range(2)])
        hib = Ring([kb.sb(ph, "A1_hi%d" % i, [128, 512], BF16) for i in range(3)])
        f1 = Ring([kb.sb(ph, "A1_f1%d" % i, [128, 512], F32) for i in range(2)])
        ff = Ring([kb.sb(ph, "A1_ff%d" % i, [128, 512], F32) for i in range(2)])
        lf = Ring([kb.sb(ph, "A1_lf%d" % i, [128, 512], F32) for i in range(2)])
        kk = Ring([kb.sb(ph, "A1_kk%d" % i, [128, 512], F32) for i in range(3)])
        ed = Ring([kb.sb(ph, "A1_ed%d" % i, [128, 512], F32) for i in range(3)])
        kst = Ring([kb.sb(ph, "A1_ks%d" % i, [128, 512], BF16) for i in range(2)])
        dec = Ring([kb.sb(ph, "A1_dc%d" % i, [128, 8], F32) for i in range(3)])
        Sr = Ring([kb.sb(ph, "A1_S%d" % i, [128, 512], F32) for i in range(2)])
        Sa = [kb.sb(ph, "A1_Sa%d" % c, [128, 512], F32) for c in range(2)]
        Sb = Ring([kb.sb(ph, "A1_Sb%d" % i, [128, 512], BF16) for i in range(4)])
        for v in vb.bufs:
            kb.op("pool", lambda e, v=v: e.memset(v[:], 1.0), [], [v])
        S = Sr.next()
        kb.op("pool", lambda e: e.memset(S[:], 0.0), [], [S])
        loaded = {}

        def load(tb):
            x = xf.next()
            kb.dma([(x[:], x_all[tb * 128:(tb + 1) * 128, :])], writes=[x], sembuf=x)
            r = rp.next()
            kb.dma([(r[:], rope_all[tb * 128:(tb + 1) * 128, :])], writes=[r], sembuf=r)
            loaded[tb] = (x, r)

        def mm_group(pbuf, t, col0):
            def f(e):
                for k in range(8):
                    i = e.matmul(pbuf[:], lhsT=t[:, k * 128:(k + 1) * 128], rhs=W[k][:, col0:col0 + 512], start=(k == 0), stop=(k == 7))
                return i
            kb.op("pe", f, [t] + W, [pbuf])

        stS = {'S': S}

        def body(tb):
            x, r = loaded.pop(tb)
            b = xb.next()
            CP(kb, "dve", b[:], x[:], [x], [b])
            p = pT.next()
            transposes(kb, p, b, ident, 8, [b])
            t = xT.next()
            CP(kb, "act", t[:], p[:], [p], [t])
            yield
            pk = pp.next()
            mm_group(pk, t, 0)
            k_f = kf.next()
            CP(kb, "act", k_f[:], pk[:], [pk], [k_f])
            pv = pp.next()
            mm_group(pv, t, 512)
            v = vb.next()
            CP(kb, "act", v[:].rearrange("p (h d) -> p h d", h=4)[:, :, 0:128], pv[:].rearrange("p (h d) -> p h d", h=4), [pv], [v])
            kb.dma([(V[tb, :, :], v[:])], reads=[v], sembuf=v)
            phf = pp.next()
            mm_group(phf, t, 1024)
            s_g = sg.next()
            sigmoid_exp(kb, s_g[:], phf[:], phf, sgt1.next(), sgt2.next(), s_g)
            phi = pp.next()
            mm_group(phi, t, 1536)
            h_i = hib.next()
            CP(kb, "dve", h_i[:], phi[:], [phi], [h_i])
            yield
            k_b = kbf.next()
            rope(kb, k_f, k_b, r, 8, rt.next())
            pkt = pkT.next()
            transposes(kb, pkt, k_b, ident, 4, [k_b])
            kt = kTs.next()
            CP(kb, "act", kt[:], pkt[:, 0:512], [pkt], [kt])
            kb.dma([(KT[tb, :, :], kt[:])], reads=[kt], sembuf=kt)
            f_1 = f1.next()
            TT(kb, "dve", f_1[:], s_g[:], oml[:], ALU.mult, [s_g, oml], [f_1])
            f_ = ff.next()
            TT(kb, "dve", f_[:], f_1[:], lb[:], ALU.add, [f_1, lb], [f_])
            l_f = lf.next()
            ACT(kb, l_f[:], f_[:], AF.Ln, [f_], [l_f])
            k_k = kk.next()
            ACT(kb, k_k[:], f_[:], AF.Copy, [f_], [k_k], scale=-1.0, bias=1.0)
            pb = pbd.next()
            kb.op("pe", lambda e: e.matmul(pb[:], lhsT=cU[:, 2, :], rhs=l_f[:], start=True, stop=True), [cU, l_f], [pb])
            pd = pdec.next()

            def fdec(e, pd=pd, l_f=l_f):
                for h in range(4):
                    i = e.matmul(pd[:, h * 2:(h + 1) * 2], lhsT=l_f[:, h * 128:(h + 1) * 128], rhs=cU[:, 3, 0:2], start=True, stop=True)
                return i
            kb.op("pe", fdec, [cU, l_f], [pd])
            e_d = ed.next()
            ACT(kb, e_d[:], pb[:], AF.Exp, [pb], [e_d])
            d_c = dec.next()
            ACT(kb, d_c[:], pd[:, 0:8], AF.Exp, [pd], [d_c])
            yield
            k_s = kst.next()
            TT(kb, "dve", k_s[:], k_k[:], e_d[:], ALU.mult, [k_k, e_d], [k_s])
            j = tb // 2
            for c in range(2):
                if not blended:
                    sb_ = Sb.next()
                    CP(kb, "act", sb_[:], stS['S'][:], [stS['S']], [sb_])
                    kb.dma([(SS[tb, c, :, :], sb_[:])], reads=[sb_], sembuf=sb_)
                elif tb % 2 == 0:
                    TS(kb, "dve", Sa[c][:], stS['S'][:], sel[:, 0:1], None, ALU.mult, None, [stS['S'], sel], [Sa[c]])
                else:
                    sb_ = Sb.next()
                    STT(kb, sb_[:], stS['S'][:], sel[:, 1:2], Sa[c][:], ALU.mult, ALU.add, [stS['S'], sel, Sa[c]], [sb_])
                    kb.dma([(SS[j, c, :, :], sb_[:])], reads=[sb_], sembuf=sb_)
                psb = pS.next()

                def fS(e, psb=psb, k_s=k_s, h_i=h_i, c=c):
                    for h in range(4):
                        i = e.matmul(psb[:, h * 128:(h + 1) * 128], lhsT=k_s[c * 64:(c + 1) * 64, h * 128:(h + 1) * 128],
                                     rhs=h_i[c * 64:(c + 1) * 64, h * 128:(h + 1) * 128], start=True, stop=True)
                    return i
                kb.op("pe", fS, [k_s, h_i], [psb])
                Sn = Sr.next()
                for h in range(4):
                    STT(kb, Sn[:, h * 128:(h + 1) * 128], stS['S'][:, h * 128:(h + 1) * 128], d_c[:, h * 2 + c:h * 2 + c + 1],
                        psb[:, h * 128:(h + 1) * 128], ALU.mult, ALU.add, [stS['S'], d_c, psb], [Sn])
                stS['S'] = Sn

        run_pairs(nb, body, load, width=3)
        kb.end_phase()


def phase_A2(kb, nbo, x_own, rope_own, w_in, lb_logits, hg_norm_g, consts, SS, QT, MIX, ss_pick=None):
    with contextlib.ExitStack() as ph:
        Wt = ph.enter_context(kb.nc.sbuf_tensor(kb.uname("A2_W"), [128, 8, 2560], BF16))
        W = [kb.view(Wt[:, k, :]) for k in range(8)]
        load_weight(kb, W, w_in, [(0, 512), (1536, 3584)], 8)
        ident = kb.sb(ph, "A2_id", [128, 128], BF16)
        kb.dma([(ident[:], consts["ident"][:, :])], writes=[ident], sembuf=ident)
        cU = kb.sb(ph, "A2_cU", [128, 4, 128], F32)
        kb.dma([(cU[:], consts["cU"][:, :, :])], writes=[cU], sembuf=cU)
        mhg = kb.sb(ph, "A2_mhg", [128, 128], F32)
        kb.dma([(mhg[:], consts["cU"][:, 0, :])], writes=[mhg], sembuf=mhg)
        gtab = kb.sb(ph, "A2_g", [128, 128], F32)
        kb.dma([(gtab[:], hg_norm_g.partition_broadcast(128))], writes=[gtab], sembuf=gtab)
        lb, oml = hgrn_gate_tables(kb, ph, lb_logits)
        ps = PSUM(kb, ph)
        pT = Ring([ps.bank(0, BF16)])
        pp = Ring([ps.bank(1), ps.bank(2), ps.bank(3)])
        pcs = Ring([[ps.bank(4), ps.bank(5)]])
        pt3 = Ring([ps.bank(6, BF16)])
        pA = Ring([ps.bank(7)])
        xf = Ring([kb.sb(ph, "A2_xf%d" % i, [128, 1024], F32) for i in range(4)])
        rp = Ring([kb.sb(ph, "A2_rp%d" % i, [128, 128], F32) for i in range(4)])
        s0r = Ring([kb.sb(ph, "A2_s0%d" % i, [128, 512], BF16) for i in range(3)])
        s1r = Ring([kb.sb(ph, "A2_s1%d" % i, [128, 512], BF16) for i in range(3)])
        xb = Ring([kb.sb(ph, "A2_xb%d" % i, [128, 1024], BF16) for i in range(2)])
        xT = Ring([kb.sb(ph, "A2_xT%d" % i, [128, 1024], BF16) for i in range(3)])
        qf = Ring([kb.sb(ph, "A2_qf%d" % i, [128, 512], F32) for i in range(3)])
        qbf = Ring([kb.sb(ph, "A2_qb%d" % i, [128, 512], BF16) for i in range(2)])
        rt = Ring([[kb.sb(ph, "A2_rt%d_%d" % (i, k), [128, 64], F32) for k in range(4)] for i in range(2)])
        qTs = Ring([kb.sb(ph, "A2_qT%d" % i, [128, 512], BF16) for i in range(2)])
        qs = Ring([kb.sb(ph, "A2_qs%d" % i, [128, 512], F32) for i in range(3)])
        sg = Ring([kb.sb(ph, "A2_sg%d" % i, [128, 512], F32) for i in range(3)])
        sgt1 = Ring([kb.sb(ph, "A2_sgt1%d" % i, [128, 512], F32) for i in range(1)])
        sgt2 = Ring([kb.sb(ph, "A2_sgt2%d" % i, [128, 512], F32) for i in range(1)])
        sgt3 = Ring([kb.sb(ph, "A2_sgt3%d" % i, [128, 512], F32) for i in range(1)])
        hib = Ring([kb.sb(ph, "A2_hi%d" % i, [128, 512], BF16) for i in range(3)])
        gs = Ring([kb.sb(ph, "A2_gs%d" % i, [128, 512], F32) for i in range(3)])
        f1 = Ring([kb.sb(ph, "A2_f1%d" % i, [128, 512], F32) for i in range(1)])
        ff = Ring([kb.sb(ph, "A2_ff%d" % i, [128, 512], F32) for i in range(1)])
        lf = Ring([kb.sb(ph, "A2_lf%d" % i, [128, 512], F32) for i in range(1)])
        kk = Ring([kb.sb(ph, "A2_kk%d" % i, [128, 512], F32) for i in range(3)])
        bsb = Ring([kb.sb(ph, "A2_bs%d" % i, [128, 512], F32) for i in range(3)])
        d1 = Ring([kb.sb(ph, "A2_d1%d" % i, [128, 512], F32) for i in range(3)])
        e1 = Ring([kb.sb(ph, "A2_e1%d" % i, [128, 512], F32) for i in range(1)])
        e2 = Ring([kb.sb(ph, "A2_e2%d" % i, [128, 512], F32) for i in range(1)])
        e3 = Ring([kb.sb(ph, "A2_e3%d" % i, [128, 512], F32) for i in range(1)])
        Qh = Ring([kb.sb(ph, "A2_Qh%d" % i, [128, 512], BF16) for i in range(2)])
        Kh = Ring([kb.sb(ph, "A2_Kh%d" % i, [128, 512], BF16) for i in range(2)])
        qo = Ring([kb.sb(ph, "A2_qo%d" % i, [128, 512], BF16) for i in range(2)])
        QhT = Ring([kb.sb(ph, "A2_QhT%d" % i, [128, 512], BF16) for i in range(3)])
        KhT = Ring([kb.sb(ph, "A2_KhT%d" % i, [128, 512], BF16) for i in range(3)])
        qz0 = Ring([kb.sb(ph, "A2_qz0%d" % i, [128, 4, 128], BF16) for i in range(3)])
        qz1 = Ring([kb.sb(ph, "A2_qz1%d" % i, [128, 4, 128], BF16) for i in range(3)])
        Am = Ring([kb.sb(ph, "A2_Am%d" % i, [128, 512], BF16) for i in range(3)])
        jk = Ring([kb.sb(ph, "A2_jk%d" % i, [128, 128], F32) for i in range(2)])
        sm = Ring([[kb.sb(ph, "A2_sm%d_%d" % (i, k), [128, 4], F32) for k in range(4)] for i in range(2)])
        on = Ring([kb.sb(ph, "A2_on%d" % i, [128, 512], F32) for i in range(1)])
        on2 = Ring([kb.sb(ph, "A2_o2%d" % i, [128, 512], F32) for i in range(1)])
        ob = Ring([kb.sb(ph, "A2_ob%d" % i, [128, 512], BF16) for i in range(2)])
        for z in qz0.bufs + qz1.bufs:
            kb.op("pool", lambda e, z=z: e.memset(z[:], 0.0), [], [z])
        loaded = {}
        if ss_pick is not None:
            cand = Ring([kb.sb(ph, "A2_cd%d" % i, [128, 512], BF16) for i in range(16)])
            candf = Ring([kb.sb(ph, "A2_cf%d" % i, [128, 512], F32) for i in range(1)])
            sel = kb.sb(ph, "A2_sel", [128, 2], F32)
            kb.dma([(sel[:], consts["sel"][:, :])], writes=[sel], sembuf=sel)

        def load(j):
            x = xf.next()
            kb.dma([(x[:], x_own[j * 128:(j + 1) * 128, :])], writes=[x], sembuf=x)
            r = rp.next()
            kb.dma([(r[:], rope_own[j * 128:(j + 1) * 128, :])], writes=[r], sembuf=r)
            if ss_pick is None:
                a0 = s0r.next()
                a1 = s1r.next()
                kb.dma([(a0[:], SS[j, 0, :, :])], writes=[a0], sembuf=a0)
                kb.dma([(a1[:], SS[j, 1, :, :])], writes=[a1], sembuf=a1)
            else:
                g0, g1 = ss_pick(j)
                a0 = []
                for c in (0, 1):
                    c0 = cand.next()
                    kb.dma([(c0[:], SS[g0, c, :, :])], writes=[c0], sembuf=c0)
                    c1 = cand.next()
                    kb.dma([(c1[:], SS[g1, c, :, :])], writes=[c1], sembuf=c1)
                    a0.append((c0, c1))
                a1 = None
            loaded[j] = (x, r, a0, a1)

        def mm_group(pbuf, t, col0):
            def f(e):
                for k in range(8):
                    i = e.matmul(pbuf[:], lhsT=t[:, k * 128:(k + 1) * 128], rhs=W[k][:, col0:col0 + 512], start=(k == 0), stop=(k == 7))
                return i
            kb.op("pe", f, [t] + W, [pbuf])

        def body(j):
            x, r, S0, S1 = loaded.pop(j)
            b = xb.next()
            CP(kb, "dve", b[:], x[:], [x], [b])
            p = pT.next()
            transposes(kb, p, b, ident, 8, [b])
            t = xT.next()
            CP(kb, "act", t[:], p[:], [p], [t])
            yield
            pq = pp.next()
            mm_group(pq, t, 0)
            q_f = qf.next()
            CP(kb, "act", q_f[:], pq[:], [pq], [q_f])
            phq = pp.next()
            mm_group(phq, t, 512)
            q_s = qs.next()
            sq_ = sgt3.next()
            sigmoid_exp(kb, sq_[:], phq[:], phq, sgt1.next(), sgt2.next(), sq_)
            TT(kb, "dve", q_s[:], sq_[:], phq[:], ALU.mult, [sq_, phq], [q_s])
            phf = pp.next()
            mm_group(phf, t, 1024)
            s_g = sg.next()
            sigmoid_exp(kb, s_g[:], phf[:], phf, sgt1.next(), sgt2.next(), s_g)
            phi = pp.next()
            mm_group(phi, t, 1536)
            h_i = hib.next()
            CP(kb, "dve", h_i[:], phi[:], [phi], [h_i])
            phg = pp.next()
            mm_group(phg, t, 2048)
            g_s = gs.next()
            sq_ = sgt3.next()
            sigmoid_exp(kb, sq_[:], phg[:], phg, sgt1.next(), sgt2.next(), sq_)
            TT(kb, "dve", g_s[:], sq_[:], phg[:], ALU.mult, [sq_, phg], [g_s])
            yield
            q_b = qbf.next()
            rope(kb, q_f, q_b, r, 8, rt.next())
            p3 = pt3.next()
            transposes(kb, kb.view(p3[:, 0:512]) if False else p3, q_b, ident, 4, [q_b])
            qt = qTs.next()
            CP(kb, "act", qt[:], p3[:, 0:512], [p3], [qt])
            kb.dma([(QT[j, :, :], qt[:])], reads=[qt], sembuf=qt)
            f_1 = f1.next()
            TT(kb, "dve", f_1[:], s_g[:], oml[:], ALU.mult, [s_g, oml], [f_1])
            f_ = ff.next()
            TT(kb, "dve", f_[:], f_1[:], lb[:], ALU.add, [f_1, lb], [f_])
            l_f = lf.next()
            ACT(kb, l_f[:], f_[:], AF.Ln, [f_], [l_f])
            k_k = kk.next()
            ACT(kb, k_k[:], f_[:], AF.Copy, [f_], [k_k], scale=-1.0, bias=1.0)
            pb, pbm = pcs.next()
            kb.op("pe", lambda e: e.matmul(pb[:], lhsT=cU[:, 0, :], rhs=l_f[:], start=True, stop=True), [cU, l_f], [pb])
            kb.op("pe", lambda e: e.matmul(pbm[:], lhsT=cU[:, 1, :], rhs=l_f[:], start=True, stop=True), [cU, l_f], [pbm])
            b_s = bsb.next()
            CP(kb, "act", b_s[:], pb[:], [pb], [b_s])
            d_1 = d1.next()
            TT(kb, "dve", d_1[:], b_s[:], pbm[:], ALU.subtract, [b_s, pbm], [d_1])
            yield
            e_1 = e1.next()
            ACT(kb, e_1[:], d_1[:], AF.Exp, [d_1], [e_1])
            e_2 = e2.next()
            ACT(kb, e_2[:], d_1[:], AF.Exp, [d_1], [e_2], scale=-1.0)
            e_3 = e3.next()
            ACT(kb, e_3[:], b_s[:], AF.Exp, [b_s], [e_3])
            Q_h = Qh.next()
            TT(kb, "dve", Q_h[:], q_s[:], e_1[:], ALU.mult, [q_s, e_1], [Q_h])
            K_h = Kh.next()
            TT(kb, "dve", K_h[:], k_k[:], e_2[:], ALU.mult, [k_k, e_2], [K_h])
            q_o = qo.next()
            TT(kb, "dve", q_o[:], q_s[:], e_3[:], ALU.mult, [q_s, e_3], [q_o])
            p3 = pt3.next()
            transposes(kb, p3, Q_h, ident, 4, [Q_h])
            Q_T = QhT.next()
            CP(kb, "act", Q_T[:], p3[:, 0:512], [p3], [Q_T])
            p3 = pt3.next()
            transposes(kb, p3, K_h, ident, 4, [K_h])
            K_T = KhT.next()
            CP(kb, "dve", K_T[:], p3[:, 0:512], [p3], [K_T])
            p3 = pt3.next()
            transposes(kb, p3, q_o, ident, 4, [q_o])
            z0 = qz0.next()
            z1 = qz1.next()
            p3v = p3[:, 0:512].rearrange("p (h t) -> p h t", h=4)
            CP(kb, "act", z0[:, :, 0:64], p3v[:, :, 0:64], [p3], [z0])
            CP(kb, "dve", z1[:, :, 64:128], p3v[:, :, 64:128], [p3], [z1])
            yield
            pa = pA.next()

            def fA(e, pa=pa, K_T=K_T, Q_T=Q_T):
                for h in range(4):
                    i = e.matmul(pa[:, h * 128:(h + 1) * 128], lhsT=K_T[:, h * 128:(h + 1) * 128], rhs=Q_T[:, h * 128:(h + 1) * 128],
                                 start=True, stop=True)
                return i
            kb.op("pe", fA, [K_T, Q_T], [pa])
            A_m = Am.next()
            TT(kb, "dve", A_m[:].rearrange("p (h t) -> p h t", h=4), pa[:].rearrange("p (h t) -> p h t", h=4),
               mhg[:].unsqueeze(1).broadcast_to([128, 4, 128]), ALU.mult, [pa, mhg], [A_m])
            yield
            if ss_pick is not None:
                cands = S0
                blended = []
                for c, ring in ((0, s0r), (1, s1r)):
                    c0, c1 = cands[c]
                    dst = ring.next()
                    tmpb = candf.next()
                    kb.op("act", lambda e, tmpb=tmpb, c0=c0: e.activation(out=tmpb[:], in_=c0[:], func=AF.Copy, scale=sel[:, 0:1]), [c0, sel], [tmpb])
                    STT(kb, dst[:], c1[:], sel[:, 1:2], tmpb[:], ALU.mult, ALU.add, [c1, sel, tmpb], [dst])
                    blended.append(dst)
                S0, S1 = blended
            po = pp.next()

            def fo(e, po=po, A_m=A_m, h_i=h_i, z0=z0, z1=z1, S0=S0, S1=S1):
                for h in range(4):
                    hs = slice(h * 128, (h + 1) * 128)
                    e.matmul(po[:, hs], lhsT=A_m[:, hs], rhs=h_i[:, hs], start=True, stop=False)
                    e.matmul(po[:, hs], lhsT=z0[:, h, :], rhs=S0[:, hs], start=False, stop=False)
                    i = e.matmul(po[:, hs], lhsT=z1[:, h, :], rhs=S1[:, hs], start=False, stop=True)
                return i
            kb.op("pe", fo, [A_m, h_i, z0, z1, S0, S1], [po])
            s = sm.next()
            for h in range(4):
                j_ = jk.next()
                ACT(kb, j_[:], po[:, h * 128:(h + 1) * 128], AF.Square, [po], [j_, s[0]], accum=s[0][:, h:h + 1])
            rstd_act(kb, s[2][:, 0:4], s[0][:, 0:4], s[1][:, 0:4], 1.0 / 128.0, s[0], s[1], s[2])
            o_n = on.next()
            TT(kb, "dve", o_n[:].rearrange("p (h t) -> p h t", h=4), po[:].rearrange("p (h t) -> p h t", h=4),
               s[2][:, 0:4].unsqueeze(2).broadcast_to([128, 4, 128]), ALU.mult, [po, s[2]], [o_n])
            o_2 = on2.next()
            TT(kb, "dve", o_2[:].rearrange("p (h t) -> p h t", h=4), o_n[:].rearrange("p (h t) -> p h t", h=4),
               gtab[:].unsqueeze(1).broadcast_to([128, 4, 128]), ALU.mult, [o_n, gtab], [o_2])
            o_b = ob.next()
            TT(kb, "dve", o_b[:], o_2[:], g_s[:], ALU.mult, [o_2, g_s], [o_b])
            kb.dma([(MIX[j * 128:(j + 1) * 128, 512:1024], o_b[:])], reads=[o_b], sembuf=o_b)

        run_pairs(nbo, body, load, width=3, prefetch=1)
        kb.end_phase()


def keyspec_parity(j):
    return [(k, (k - 2 * j) if k >= 2 * j else None, False) for k in range(2 * j + 2)]


def make_keyspec_ctx(nbh):
    def spec(i):
        g0, g1 = i - NCTX, i + nbh - NCTX
        out = []
        for k in range(g1 + 1):
            if k < g0:
                out.append((k, None, False))
            elif k == g0:
                out.append((k, 0, False))
            elif k < g1:
                out.append((k, None, True))
            else:
                out.append((k, 1, True))
        return out
    return spec


def phase_B(kb, S, nbo, KT, V, QT, da_lambda, sub_g, consts, MIX, keyspec=keyspec_parity):
    nb = S // 128
    with contextlib.ExitStack() as ph:
        KTt = ph.enter_context(kb.nc.sbuf_tensor(kb.uname("B_KT"), [128, nb, 512], BF16))
        Vt = ph.enter_context(kb.nc.sbuf_tensor(kb.uname("B_V"), [128, nb, 516], BF16))
        CH = 8
        nch = (nb + CH - 1) // CH
        KTc = [kb.view(KTt[:, c * CH:min(nb, (c + 1) * CH), :]) for c in range(nch)]
        Vc = [kb.view(Vt[:, c * CH:min(nb, (c + 1) * CH), :]) for c in range(nch)]
        ident = kb.sb(ph, "B_id", [128, 128], BF16)
        kb.dma([(ident[:], consts["ident"][:, :])], writes=[ident], sembuf=ident)
        msk = kb.sb(ph, "B_msk", [128, 2, 128], BF16)
        kb.dma([(msk[:], consts["mAB"][:, :, :])], writes=[msk], sembuf=msk)
        gtab = kb.sb(ph, "B_g", [128, 128], F32)
        kb.dma([(gtab[:], sub_g.partition_broadcast(128))], writes=[gtab], sembuf=gtab)
        lt = kb.sb(ph, "B_lt", [128, 4, 64], F32)
        kb.dma([(lt[:], da_lambda.partition_broadcast(128))], writes=[lt], sembuf=lt)
        l1 = kb.sb(ph, "B_l1", [128, 2, 64], F32)
        l2 = kb.sb(ph, "B_l2", [128, 2], F32)
        l3 = kb.sb(ph, "B_l3", [128, 2], F32)
        nlam = kb.sb(ph, "B_nlam", [128, 1], F32)
        TT(kb, "dve", l1[:, 0, :], lt[:, 0, :], lt[:, 1, :], ALU.mult, [lt], [l1])
        TT(kb, "dve", l1[:, 1, :], lt[:, 2, :], lt[:, 3, :], ALU.mult, [lt], [l1])
        kb.op("dve", lambda e: e.reduce_sum(out=l2[:, 0:2], in_=l1[:], axis=AX.X), [l1], [l2])
        ACT(kb, l3[:], l2[:], AF.Exp, [l2], [l3])
        l4 = kb.sb(ph, "B_l4", [128, 1], F32)
        TT(kb, "dve", l4[:], l3[:, 1:2], l3[:, 0:1], ALU.subtract, [l3], [l4])
        TS(kb, "dve", nlam[:], l4[:], -LAM_INIT0, None, ALU.add, None, [l4], [nlam])
        for c in range(nch):
            lo, hi = c * CH, min(nb, (c + 1) * CH)
            kb.dma([(KTc[c][:], KT[lo:hi, :, :].rearrange("n p f -> p n f"))], writes=[KTc[c]], sembuf=KTc[c])
            kb.dma([(Vc[c][:], V[lo:hi, :, :].rearrange("n p f -> p n f"))], writes=[Vc[c]], sembuf=Vc[c])
        ps = PSUM(kb, ph)
        pst = Ring([ps.bank(0), ps.bank(1), ps.bank(2)])
        pO = Ring([ps.bank(3), ps.bank(4)])
        qr = Ring([kb.sb(ph, "B_q%d" % i, [128, 512], BF16) for i in range(3)])
        ptr = Ring([kb.sb(ph, "B_pt%d" % i, [128, 512], BF16) for i in range(4)])
        rd = Ring([kb.sb(ph, "B_rd%d" % i, [128, 2], F32) for i in range(2)])
        dn = Ring([kb.sb(ph, "B_dn%d" % i, [128, 2], F32) for i in range(2)])
        cf = Ring([kb.sb(ph, "B_cf%d" % i, [128, 1], F32) for i in range(2)])
        t0r = Ring([kb.sb(ph, "B_t0%d" % i, [128, 128], F32) for i in range(2)])
        orr = Ring([kb.sb(ph, "B_o%d" % i, [128, 128], F32) for i in range(2)])
        jk = Ring([kb.sb(ph, "B_jk%d" % i, [128, 128], F32) for i in range(2)])
        sm = Ring([[kb.sb(ph, "B_sm%d_%d" % (i, k), [128, 1], F32) for k in range(3)] for i in range(2)])
        on = Ring([kb.sb(ph, "B_on%d" % i, [128, 128], F32) for i in range(2)])
        mixr = Ring([kb.sb(ph, "B_mx%d" % i, [128, 512], BF16) for i in range(2)])
        loaded = {}

        def load(j):
            q = qr.next()
            kb.dma([(q[:], QT[j, :, :])], writes=[q], sembuf=q)
            loaded[j] = q

        def chunk_bufs(kbs):
            cs = sorted(set(k // CH for k in kbs))
            return [KTc[c] for c in cs], [Vc[c] for c in cs]

        hb = kb.sb(ph, "B_hb", [128, 1], F32)
        kb.dma([(hb[:], consts["hb"][:, :])], writes=[hb], sembuf=hb)
        units = []
        for j in range(nbo):
            ents = keyspec(j)
            groups = []
            for ent in ents:
                if groups and len(groups[-1]) < 4 and groups[-1][-1][2] == ent[2]:
                    groups[-1].append(ent)
                else:
                    groups.append([ent])
            for h in range(4):
                for c in range(2):
                    for gi_, grp in enumerate(groups):
                        units.append((j, h, c, grp, gi_ == 0, gi_ == len(groups) - 1))
        state = {"O": None, "mix": None}

        def emit_ST(u):
            j, h, c, grp, first, last = u
            q = loaded[j]
            st = pst.next()
            kbs = [g_[0] for g_ in grp]
            kts, _ = chunk_bufs(kbs)
            high = grp[0][2]

            def f(e):
                for i, (kbk, mi, _) in enumerate(grp):
                    e_ = e.matmul(st[:, i * 128:(i + 1) * 128], lhsT=KTt[c * 64:(c + 1) * 64, kbk, h * 128:(h + 1) * 128],
                                  rhs=q[c * 64:(c + 1) * 64, h * 128:(h + 1) * 128], start=True, stop=(mi is None))
                    if mi is not None:
                        e_ = e.matmul(st[:, i * 128:(i + 1) * 128], lhsT=ident[:], rhs=msk[:, mi, :], start=False, stop=True)
                return e_
            kb.op("pe", f, kts + [q, ident, msk], [st])
            pt = ptr.next()
            n = len(kbs) * 128
            if high:
                kb.op("act", lambda e: e.activation(out=pt[:, 0:n], in_=st[:, 0:n], func=AF.Exp, scale=0.125, bias=hb[:, 0:1]), [st, hb], [pt])
            else:
                ACT(kb, pt[:, 0:n], st[:, 0:n], AF.Exp, [st], [pt], scale=0.125)
            return pt

        def emit_PV(u, pt):
            j, h, c, grp, first, last = u
            if c == 0 and first:
                state["O"] = pO.next()
            O = state["O"]
            kbs = [g_[0] for g_ in grp]
            _, vs = chunk_bufs(kbs)

            def f(e):
                for i, kbk in enumerate(kbs):
                    e_ = e.matmul(O[:, c * 129:(c + 1) * 129], lhsT=pt[:, i * 128:(i + 1) * 128], rhs=Vt[:, kbk, h * 129:(h + 1) * 129],
                                  start=(first and i == 0), stop=(last and i == len(kbs) - 1))
                return e_
            kb.op("pe", f, vs + [pt], [O])
            if c == 1 and last:
                finalize(j, h, O)

        def finalize(j, h, O):
            if h == 0:
                state["mix"] = mixr.next()
            mx = state["mix"]
            r_d = rd.next()
            d_n = dn.next()
            TS(kb, "dve", d_n[:, 0:2], O[:, 0:258].rearrange("p (c d) -> p c d", c=2)[:, :, 128], 1e-30, None, ALU.max, None, [O], [d_n])
            kb.op("dve", lambda e: e.reciprocal(out=r_d[:, 0:2], in_=d_n[:, 0:2]), [d_n], [r_d])
            c_f = cf.next()
            TT(kb, "dve", c_f[:], r_d[:, 1:2], nlam[:], ALU.mult, [r_d, nlam], [c_f])
            t_0 = t0r.next()
            TS(kb, "dve", t_0[:], O[:, 0:128], r_d[:, 0:1], None, ALU.mult, None, [O, r_d], [t_0])
            o_ = orr.next()
            STT(kb, o_[:], O[:, 129:257], c_f[:, 0:1], t_0[:], ALU.mult, ALU.add, [O, c_f, t_0], [o_])
            s = sm.next()
            j_ = jk.next()
            ACT(kb, j_[:], o_[:], AF.Square, [o_], [j_, s[0]], accum=s[0][:, 0:1])
            rstd_act(kb, s[2][:], s[0][:], s[1][:], 1.0 / 128.0, s[0], s[1], s[2])
            o_n = on.next()
            TS(kb, "dve", o_n[:], o_[:], s[2][:, 0:1], 1.0 - LAM_INIT0, ALU.mult, ALU.mult, [o_, s[2]], [o_n])
            TT(kb, "dve", mx[:, h * 128:(h + 1) * 128], o_n[:], gtab[:], ALU.mult, [o_n, gtab], [mx])
            if h == 3:
                kb.dma([(MIX[j * 128:(j + 1) * 128, 0:512], mx[:])], reads=[mx], sembuf=mx)

        for j in range(min(2, nbo)):
            load(j)
        nxt = min(2, nbo)
        prev = None
        for ui, u in enumerate(units):
            if ui > 0 and u[0] != units[ui - 1][0] and nxt < nbo:
                load(nxt)
                nxt += 1
            pt = emit_ST(u)
            if prev is not None:
                emit_PV(*prev)
            prev = (u, pt)
        emit_PV(*prev)
        kb.end_phase()


def rope_table(pos, ng):
    inv = ROPE_THETA ** (-np.arange(0, 16, 2, dtype=np.float32) / np.float32(16))
    ang = pos.astype(np.float32)[:, None] * inv[None, :].astype(np.float32)
    cos = np.cos(ang).astype(np.float32)
    sin = np.sin(ang).astype(np.float32)
    return np.concatenate([np.tile(cos, (1, ng)), np.tile(sin, (1, ng))], axis=1).astype(np.float32)


def const_cU():
    s = np.arange(128)[:, None]
    t = np.arange(128)[None, :]
    same = (s // 64) == (t // 64)
    cU = np.zeros((128, 4, 128), np.float32)
    cU[:, 0, :] = same & (s <= t)
    cU[:, 1, :] = same & ((s % 64) <= 31)
    cU[:, 2, :] = same & (s > t)
    cU[:, 3, 0] = (np.arange(128) // 64) == 0
    cU[:, 3, 1] = (np.arange(128) // 64) == 1
    return cU


def const_l0(h):
    k = np.arange(128)[:, None]
    q = np.arange(128)[None, :]
    tri = np.where(k <= q, 0.0, NEG).astype(np.float32)
    zero = np.zeros((128, 128), np.float32)
    full = np.full((128, 128), NEG, np.float32)
    mAB = np.stack([tri if h == 0 else zero, full if h == 0 else tri], axis=1).astype(NPBF)
    sel = np.zeros((128, 2), np.float32)
    sel[:, h] = 1.0
    return {"ident": np.eye(128).astype(NPBF), "cU": const_cU(), "sel": sel, "mAB": mAB, "hb": np.zeros((128, 1), np.float32)}


def dense_tail(kb, nbo, l, mix, xres, io, XA, XB, out, consts, w_out, mix_loader=None, p_in=None):
    phase_C(kb, nbo, mix, xres, w_out, io["ln1_g"][l:l + 1, :], io["ln1_b"][l:l + 1, :], XA, consts, mix_loader=mix_loader)
    phase_D1(kb, nbo, XA, io["ffn_w1"][l], io["ffn_w2"][l], io["ln2_g"][l:l + 1, :], io["ln2_b"][l:l + 1, :], XB, consts)
    phase_D2(kb, nbo, XB, io["p_own"] if p_in is None else p_in, io["ple_w_gate"][l], io["ple_w_proj"][l], io["ple_norm_g"][l:l + 1, :], out, consts)


WEIGHT_SHAPES = {
    "ev_w_in": [1, 1024, 3584], "ev_w_out": [1, 1024, 1024], "da_lambda": [1, 4, 64], "da_subln_g": [1, 128],
    "hg_lb_logits": [2, 512], "hg_norm_g": [1, 128], "od_w_in": [1, 1024, 3072], "od_w_out": [1, 1024, 1024],
    "ln1_g": [2, 1024], "ln1_b": [2, 1024], "ffn_w1": [2, 1024, 4096], "ffn_w2": [2, 4096, 1024],
    "ln2_g": [2, 1024], "ln2_b": [2, 1024], "ple_w_proj": [2, 256, 1024], "ple_w_gate": [2, 1024, 1024],
    "ple_norm_g": [2, 1024],
}
L0_W = ["ev_w_in", "ev_w_out", "da_lambda", "da_subln_g", "hg_lb_logits", "hg_norm_g", "ln1_g", "ln1_b", "ffn_w1",
        "ffn_w2", "ln2_g", "ln2_b", "ple_w_proj", "ple_w_gate", "ple_norm_g"]
L1_W = ["od_w_in", "od_w_out", "ln1_g", "ln1_b", "ffn_w1", "ffn_w2", "ln2_g", "ln2_b", "ple_w_proj", "ple_w_gate",
        "ple_norm_g"]


def build_l0(S, debug=False, phases="12BT"):
    nc = bass.Bass("TRN2", target_bir_lowering=False)
    nb, nbo = S // 128, S // 256
    So = S // 2

    def din(name, shape, dt=F32):
        return nc.dram_tensor(name, list(shape), dt, kind="ExternalInput").ap()

    def scr(name, shape, dt):
        return nc.dram_tensor(name, list(shape), dt, kind="ExternalOutput" if debug else "Internal").ap()

    io = {n: din(n, WEIGHT_SHAPES[n]) for n in L0_W}
    io["x_all"] = din("x_all", [S, D])
    io["x_own"] = din("x_own", [So, D])
    io["p_own"] = din("p_own", [So, PLE])
    io["rope_all"] = din("rope_all", [S, 128])
    io["rope_own"] = din("rope_own", [So, 128])
    consts = {"ident": din("ident", [128, 128], BF16), "cU": din("cU", [128, 4, 128]),
              "sel": din("sel", [128, 2]), "mAB": din("mAB", [128, 2, 128], BF16), "hb": din("hb", [128, 1])}
    out = nc.dram_tensor("x1_own", [So, D], F32, kind="ExternalOutput").ap()
    KT = scr("KT", [nb, 128, 512], BF16)
    V = scr("V", [nb, 128, 516], BF16)
    SS = scr("SS", [nbo, 2, 128, 512], BF16)
    QT = scr("QT", [nbo, 128, 512], BF16)
    MIX = scr("MIX", [So, D], BF16)
    XA = scr("XA", [So, D], F32)
    XB = scr("XB", [So, D], F32)
    kb = KB(nc)
    if "1" in phases:
        phase_A1(kb, S, io["x_all"], io["rope_all"], io["ev_w_in"][0], io["hg_lb_logits"], consts, KT, V, SS)
    if "2" in phases:
        phase_A2(kb, nbo, io["x_own"], io["rope_own"], io["ev_w_in"][0], io["hg_lb_logits"], io["hg_norm_g"], consts, SS, QT, MIX)
    if "B" in phases:
        phase_B(kb, S, nbo, KT, V, QT, io["da_lambda"][0], io["da_subln_g"], consts, MIX)
    if "T" in phases:
        dense_tail(kb, nbo, 0, MIX, io["x_own"], io, XA, XB, out, consts, io["ev_w_out"][0])
    return nc, kb


def l0_inputs(x_b, p0_b, weights, h):
    S = x_b.shape[0]
    nb = S // 128
    own = np.arange(nb).reshape(nb // 2, 2)[:, h]
    pos_own = (own[:, None] * 128 + np.arange(128)[None, :]).reshape(-1)
    m = {n: weights[n] for n in L0_W}
    m["x_all"] = x_b
    m["x_own"] = np.ascontiguousarray(x_b[pos_own])
    m["p_own"] = np.ascontiguousarray(p0_b[pos_own])
    m["rope_all"] = rope_table(np.arange(S), 8)
    m["rope_own"] = rope_table(pos_own, 8)
    m.update(const_l0(h))
    return m, pos_own


NCTX = 16


def phase_L1A(kb, nbo, x_co, rope_co, w_in, consts, KT1, QT1, V1):
    nblk = NCTX + nbo
    with contextlib.ExitStack() as ph:
        Wt = ph.enter_context(kb.nc.sbuf_tensor(kb.uname("F_W"), [128, 8, 3072], BF16))
        W = [kb.view(Wt[:, k, :]) for k in range(8)]
        load_weight(kb, W, w_in, [(0, 3072)], 8)
        ident = kb.sb(ph, "F_id", [128, 128], BF16)
        kb.dma([(ident[:], consts["ident"][:, :])], writes=[ident], sembuf=ident)
        ps = PSUM(kb, ph)
        pT = Ring([ps.bank(0, BF16)])
        pp = Ring([ps.bank(1), ps.bank(2), ps.bank(3), ps.bank(4)])
        pkT = Ring([ps.bank(5, BF16), ps.bank(6, BF16)])
        xf = Ring([kb.sb(ph, "F_xf%d" % i, [128, 1024], F32) for i in range(4)])
        rp = Ring([kb.sb(ph, "F_rp%d" % i, [128, 128], F32) for i in range(4)])
        xb = Ring([kb.sb(ph, "F_xb%d" % i, [128, 1024], BF16) for i in range(2)])
        xT = Ring([kb.sb(ph, "F_xT%d" % i, [128, 1024], BF16) for i in range(2)])
        kf = Ring([kb.sb(ph, "F_kf%d" % i, [128, 512], F32) for i in range(3)])
        kbf = Ring([kb.sb(ph, "F_kb%d" % i, [128, 1024], BF16) for i in range(4)])
        rt = Ring([[kb.sb(ph, "F_rt%d_%d" % (i, k), [128, 64], F32) for k in range(4)] for i in range(3)])
        kst = Ring([kb.sb(ph, "F_ks%d" % i, [128, 8, 512], BF16) for i in range(2)])
        qst = Ring([kb.sb(ph, "F_qs%d" % i, [128, 8, 512], BF16) for i in range(2)])
        vb = Ring([kb.sb(ph, "F_vb%d" % i, [128, 16, 65], BF16) for i in range(3)])
        for v in vb.bufs:
            kb.op("pool", lambda e, v=v: e.memset(v[:], 1.0), [], [v])
        loaded = {}

        def load(i):
            x = xf.next()
            kb.dma([(x[:], x_co[i * 128:(i + 1) * 128, :])], writes=[x], sembuf=x)
            r = rp.next()
            kb.dma([(r[:], rope_co[i * 128:(i + 1) * 128, :])], writes=[r], sembuf=r)
            loaded[i] = (x, r)

        def mm_group(pbuf, t, col0):
            def f(e):
                for k in range(8):
                    i_ = e.matmul(pbuf[:], lhsT=t[:, k * 128:(k + 1) * 128], rhs=W[k][:, col0:col0 + 512], start=(k == 0), stop=(k == 7))
                return i_
            kb.op("pe", f, [t] + W, [pbuf])

        def roped(t, r, col0):
            k_b = kbf.next()
            for half in range(2):
                pk = pp.next()
                mm_group(pk, t, col0 + half * 512)
                k_f = kf.next()
                CP(kb, "act", k_f[:], pk[:], [pk], [k_f])
                rope(kb, k_f, _Sub(k_b, half), r, 8, rt.next())
            return k_b

        def to_stage(k_b, stage, slot):
            pkt = pkT.next()
            transposes(kb, pkt, k_b, ident, 8, [k_b])
            CP(kb, "act", stage[:, :, slot * 128:(slot + 1) * 128], pkt[:].rearrange("p (h t) -> p h t", h=8), [pkt], [stage])

        stg = {}

        def body(i):
            x, r = loaded.pop(i)
            b = xb.next()
            CP(kb, "dve", b[:], x[:], [x], [b])
            p = pT.next()
            transposes(kb, p, b, ident, 8, [b])
            t = xT.next()
            CP(kb, "act", t[:], p[:], [p], [t])
            yield
            if i % 4 == 0:
                stg["k"] = kst.next()
            k_stage = stg["k"]
            k_b = roped(t, r, 1024)
            v = vb.next()
            for half in range(2):
                pv = pp.next()
                mm_group(pv, t, 2048 + half * 512)
                CP(kb, "dve" if half == 0 else "act", v[:, half * 8:(half + 1) * 8, 0:64], pv[:].rearrange("p (h d) -> p h d", h=8), [pv], [v])
            kb.dma([(V1[i * 128:(i + 1) * 128, :], v[:].rearrange("p h d -> p (h d)"))], reads=[v], sembuf=v)
            q_b = None
            if i >= NCTX:
                io_ = i - NCTX
                if io_ % 4 == 0:
                    stg["q"] = qst.next()
                q_stage = stg["q"]
                q_b = roped(t, r, 0)
            yield
            to_stage(k_b, k_stage, i % 4)
            if i % 4 == 3:
                g0 = (i // 4) * 512
                kb.dma([(KT1[:, :, g0:g0 + 512].rearrange("h p t -> p h t"), k_stage[:])], reads=[k_stage], sembuf=k_stage)
            if q_b is not None:
                to_stage(q_b, q_stage, io_ % 4)
                if io_ % 4 == 3:
                    g0 = (io_ // 4) * 512
                    kb.dma([(QT1[:, :, g0:g0 + 512].rearrange("h p t -> p h t"), q_stage[:])], reads=[q_stage], sembuf=q_stage)

        run_pairs(nblk, body, load)
        kb.end_phase()


class _Sub:
    def __init__(self, parent, half):
        self.parent = parent
        self.half = half

    def __getitem__(self, idx):
        return self.parent.t[:, self.half * 512:(self.half + 1) * 512][idx]

    def __getattr__(self, name):
        return getattr(self.parent, name)

    def __setattr__(self, name, value):
        if name in ("parent", "half"):
            object.__setattr__(self, name, value)
        else:
            setattr(self.parent, name, value)


def phase_L1B(kb, nreg, KT1, QT1, V1, consts, ON):
    DILS = (1, 4, 16)
    with contextlib.ExitStack() as ph:
        KTsb = kb.sb(ph, "G_KT", [128, 8, 4096], BF16)
        QTsb = kb.sb(ph, "G_QT", [128, 8, 2048], BF16)
        ident = kb.sb(ph, "G_id", [128, 128], BF16)
        kb.dma([(ident[:], consts["ident"][:, :])], writes=[ident], sembuf=ident)
        msk = kb.sb(ph, "G_msk", [128, 3, 128], BF16)
        kb.dma([(msk[:], consts["m1"][:, :, :])], writes=[msk], sembuf=msk)
        ps = PSUM(kb, ph)
        pst = Ring([ps.bank(0), ps.bank(1)])
        pO = Ring([[ps.bank(2), ps.bank(3), ps.bank(4)], [ps.bank(5), ps.bank(6), ps.bank(7)]])
        vr = Ring([kb.sb(ph, "G_v%d" % i, [128, 1040], BF16) for i in range(8)])
        ptr = Ring([kb.sb(ph, "G_pt%d" % i, [128, 512], BF16) for i in range(4)])
        osb = Ring([kb.sb(ph, "G_o%d" % i, [128, 1040], F32) for i in range(3)])
        for R in range(nreg):
            kb.dma([(KTsb[:, h, :], KT1[h, :, 2048 * R:2048 * R + 4096]) for h in range(8)], writes=[KTsb], sembuf=KTsb)
            kb.dma([(QTsb[:, h, :], QT1[h, :, 2048 * R:2048 * (R + 1)]) for h in range(8)], writes=[QTsb], sembuf=QTsb)
            qblocks = [(gi, d, r, n) for gi, d in enumerate(DILS) for n in range(16 // d) for r in range(d)]
            vt = {}

            def loadv(qi):
                gi, d, r, n = qblocks[qi]
                tiles = []
                for which in (0, 1):
                    U0 = 2048 * R + 2048 + 128 * d * (n - 1 + which)
                    v = vr.next()
                    src = V1.rearrange("(a s) f -> a s f", s=d)[U0 // d:U0 // d + 128, r, :]
                    kb.dma([(v[:], src)], writes=[v], sembuf=v)
                    tiles.append(v)
                vt[qi] = tiles

            units = [(qi, hp) for qi in range(len(qblocks)) for hp in range(8)]
            state = {}

            def emit_ST(u):
                qi, hp = u
                gi, d, r, n = qblocks[qi]
                st = pst.next()
                q0 = 128 * d * n + r
                k0 = 2048 + 128 * d * (n - 1) + r
                mprev = 2 if (R == 0 and n == 0) else 0

                def f(e):
                    for c in range(2):
                        qap = QTsb[c * 64:(c + 1) * 64, hp, q0:q0 + 127 * d + 1:d]
                        for which in range(2):
                            kk0 = k0 + which * 128 * d
                            col = (c * 2 + which) * 128
                            e.matmul(st[:, col:col + 128], lhsT=KTsb[c * 64:(c + 1) * 64, hp, kk0:kk0 + 127 * d + 1:d], rhs=qap,
                                     start=True, stop=False)
                            i_ = e.matmul(st[:, col:col + 128], lhsT=ident[:], rhs=msk[:, (mprev if which == 0 else 1), :],
                                          start=False, stop=True)
                    return i_
                kb.op("pe", f, [KTsb, QTsb, ident, msk], [st])
                pt = ptr.next()
                ACT(kb, pt[:], st[:], AF.Exp, [st], [pt], scale=0.125)
                return pt

            def emit_PV(u, pt):
                qi, hp = u
                gi, d, r, n = qblocks[qi]
                if hp == 0:
                    state["O"] = pO.next()
                O = state["O"]
                vp, vc = vt[qi]
                banks = set()

                def f(e):
                    for c in range(2):
                        h = hp * 2 + c
                        bk, off = h // 7, (h % 7) * 65
                        banks.add(bk)
                        e.matmul(O[bk][:, off:off + 65], lhsT=pt[:, (c * 2) * 128:(c * 2 + 1) * 128], rhs=vp[:, h * 65:(h + 1) * 65],
                                 start=True, stop=False)
                        i_ = e.matmul(O[bk][:, off:off + 65], lhsT=pt[:, (c * 2 + 1) * 128:(c * 2 + 2) * 128], rhs=vc[:, h * 65:(h + 1) * 65],
                                      start=False, stop=True)
                    return i_
                hs = (hp * 2, hp * 2 + 1)
                wb = [O[bk] for bk in sorted(set(h // 7 for h in hs))]
                kb.op("pe", f, [pt, vp, vc], wb)
                if hp == 7:
                    o = osb.next()
                    CP(kb, "dve", o[:, 0:455], O[0][:, 0:455], [O[0]], [o])
                    CP(kb, "act", o[:, 455:910], O[1][:, 0:455], [O[1]], [o])
                    CP(kb, "dve", o[:, 910:1040], O[2][:, 0:130], [O[2]], [o])
                    A0 = (2048 * R + 128 * d * n) // d
                    dst = ON[gi].rearrange("(a s) f -> a s f", s=d)[A0:A0 + 128, r, :]
                    kb.dma([(dst, o[:])], reads=[o], sembuf=o)
                    del vt[qi]

            loadv(0)
            loadv(1)
            prev = None
            for ui, u in enumerate(units):
                if u[1] == 0 and u[0] + 2 < len(qblocks):
                    loadv(u[0] + 2)
                pt = emit_ST(u)
                if prev is not None:
                    emit_PV(*prev)
                prev = (u, pt)
            emit_PV(*prev)
        kb.end_phase()


class MergeLoader:
    def __init__(self, ON):
        self.ON = ON

    def alloc(self, kb, ph):
        return {
            "a": Ring([[kb.sb(ph, "M_a%d_%d" % (i, g), [128, 1040], F32) for g in range(3)] for i in range(6)]),
            "s1": Ring([kb.sb(ph, "M_s1%d" % i, [128, 1040], F32) for i in range(2)]),
            "s2": Ring([kb.sb(ph, "M_s2%d" % i, [128, 1040], F32) for i in range(2)]),
            "rd": Ring([kb.sb(ph, "M_rd%d" % i, [128, 16], F32) for i in range(2)]),
        }

    def load(self, kb, ex, j, m):
        a = ex["a"].next()
        for g in range(3):
            kb.dma([(a[g][:], self.ON[g][j * 128:(j + 1) * 128, :])], writes=[a[g]], sembuf=a[g])
        s1 = ex["s1"].next()
        TT(kb, "dve", s1[:], a[0][:], a[1][:], ALU.add, [a[0], a[1]], [s1])
        s2 = ex["s2"].next()
        TT(kb, "dve", s2[:], s1[:], a[2][:], ALU.add, [s1, a[2]], [s2])
        rd = ex["rd"].next()
        s3 = s2[:].rearrange("p (h d) -> p h d", h=16)
        kb.op("dve", lambda e: e.reciprocal(out=rd[:], in_=s3[:, :, 64]), [s2], [rd])
        TT(kb, "dve", m[:].rearrange("p (h d) -> p h d", h=16), s3[:, :, 0:64], rd[:].unsqueeze(2).broadcast_to([128, 16, 64]),
           ALU.mult, [s2, rd], [m])


def const_l1(h):
    c = np.arange(128)[:, None]
    a = np.arange(128)[None, :]
    mprev = np.where(c >= a, 0.0, NEG).astype(np.float32)
    mcur = np.where(c <= a, 0.0, NEG).astype(np.float32)
    full = np.full((128, 128), NEG, np.float32)
    m1 = np.stack([mprev, mcur, full if h == 0 else mprev], axis=1).astype(NPBF)
    return {"ident": np.eye(128).astype(NPBF), "m1": m1}


def build_l1(S, debug=False, phases="ABT"):
    nc = bass.Bass("TRN2", target_bir_lowering=False)
    So = S // 2
    nbo = So // 128
    nreg = So // 2048
    T = 2048 + So

    def din(name, shape, dt=F32):
        return nc.dram_tensor(name, list(shape), dt, kind="ExternalInput").ap()

    def scr(name, shape, dt):
        return nc.dram_tensor(name, list(shape), dt, kind="ExternalOutput" if debug else "Internal").ap()

    io = {n: din(n, WEIGHT_SHAPES[n]) for n in L1_W}
    io["x_co"] = din("x_co", [T, D])
    io["p_own"] = din("p_own", [So, PLE])
    io["rope_co"] = din("rope_co", [T, 128])
    consts = {"ident": din("ident", [128, 128], BF16), "m1": din("m1", [128, 3, 128], BF16)}
    out = nc.dram_tensor("out_own", [So, D], F32, kind="ExternalOutput").ap()
    KT1 = scr("KT1", [8, 128, T], BF16)
    QT1 = scr("QT1", [8, 128, So], BF16)
    V1 = scr("V1", [T, 1040], BF16)
    ON = [scr("ON%d" % g, [So, 1040], F32) for g in range(3)]
    XA = scr("XA1", [So, D], F32)
    XB = scr("XB1", [So, D], F32)
    kb = KB(nc)
    if "A" in phases:
        phase_L1A(kb, nbo, io["x_co"], io["rope_co"], io["od_w_in"][0], consts, KT1, QT1, V1)
    if "B" in phases:
        phase_L1B(kb, nreg, KT1, QT1, V1, consts, ON)
    if "T" in phases:
        dense_tail(kb, nbo, 1, None, io["x_co"][2048:T, :], io, XA, XB, out, consts, io["od_w_out"][0], mix_loader=MergeLoader(ON))
    return nc, kb


def l1_inputs(x1_b, p1_b, weights, h):
    S = x1_b.shape[0]
    So = S // 2
    lo = So * h
    m = {n: weights[n] for n in L1_W}
    ctx = x1_b[lo - 2048:lo] if h == 1 else np.zeros((2048, D), np.float32)
    m["x_co"] = np.ascontiguousarray(np.concatenate([ctx, x1_b[lo:lo + So]], axis=0))
    m["p_own"] = np.ascontiguousarray(p1_b[lo:lo + So])
    m["rope_co"] = rope_table(np.arange(lo - 2048, lo + So), 8)
    m.update(const_l1(h))
    return m


ALL_W = list(WEIGHT_SHAPES.keys())


def const_fused(h):
    k = np.arange(128)[:, None]
    q = np.arange(128)[None, :]
    tri = np.where(k <= q, 0.0, NEG).astype(np.float32)
    zero = np.zeros((128, 128), np.float32)
    mAB = np.stack([tri if h == 0 else zero, zero if h == 0 else tri], axis=1).astype(NPBF)
    sel = np.zeros((128, 2), np.float32)
    sel[:, h] = 1.0
    c = {"ident": np.eye(128).astype(NPBF), "cU": const_cU(), "sel": sel, "mAB": mAB,
         "hb": np.full((128, 1), NEG if h == 0 else 0.0, np.float32)}
    c["m1"] = const_l1(h)["m1"]
    return c


def build_fused(S, debug=False):
    nc = bass.Bass("TRN2", target_bir_lowering=False)
    nb = S // 128
    So = S // 2
    nbh = So // 128
    nA = NCTX + nbh
    TA = nA * 128

    def din(name, shape, dt=F32):
        return nc.dram_tensor(name, list(shape), dt, kind="ExternalInput").ap()

    def scr(name, shape, dt):
        return nc.dram_tensor(name, list(shape), dt, kind="ExternalOutput" if debug else "Internal").ap()

    io = {n: din(n, WEIGHT_SHAPES[n]) for n in ALL_W}
    io["x_all"] = din("x_all", [S, D])
    io["x_A"] = din("x_A", [TA, D])
    io["p0_A"] = din("p0_A", [TA, PLE])
    io["p1_own"] = din("p1_own", [So, PLE])
    io["rope_all"] = din("rope_all", [S, 128])
    io["rope_A"] = din("rope_A", [TA, 128])
    consts = {"ident": din("ident", [128, 128], BF16), "cU": din("cU", [128, 4, 128]), "sel": din("sel", [128, 2]),
              "mAB": din("mAB", [128, 2, 128], BF16), "hb": din("hb", [128, 1]), "m1": din("m1", [128, 3, 128], BF16)}
    out = nc.dram_tensor("out_own", [So, D], F32, kind="ExternalOutput").ap()
    KT = scr("KT", [nb, 128, 512], BF16)
    V = scr("V", [nb, 128, 516], BF16)
    SS = scr("SS", [nb, 2, 128, 512], BF16)
    QT = scr("QT", [nA, 128, 512], BF16)
    MIX = scr("MIX", [TA, D], BF16)
    XA = scr("XA", [TA, D], F32)
    XB = scr("XB", [TA, D], F32)
    X1 = scr("X1", [TA, D], F32)
    KT1 = scr("KT1", [8, 128, TA], BF16)
    QT1 = scr("QT1", [8, 128, So], BF16)
    V1 = scr("V1", [TA, 1040], BF16)
    ON = [scr("ON%d" % g, [So, 1040], F32) for g in range(3)]
    XA1 = scr("XA1", [So, D], F32)
    XB1 = scr("XB1", [So, D], F32)
    kb = KB(nc)
    phase_A1(kb, S, io["x_all"], io["rope_all"], io["ev_w_in"][0], io["hg_lb_logits"], consts, KT, V, SS, blended=False)
    phase_A2(kb, nA, io["x_A"], io["rope_A"], io["ev_w_in"][0], io["hg_lb_logits"], io["hg_norm_g"], consts, SS, QT, MIX,
             ss_pick=lambda i: (max(i - NCTX, 0), i + nbh - NCTX))
    phase_B(kb, S, nA, KT, V, QT, io["da_lambda"][0], io["da_subln_g"], consts, MIX, keyspec=make_keyspec_ctx(nbh))
    dense_tail(kb, nA, 0, MIX, io["x_A"], io, XA, XB, X1, consts, io["ev_w_out"][0], p_in=io["p0_A"])
    phase_L1A(kb, nbh, X1, io["rope_A"], io["od_w_in"][0], consts, KT1, QT1, V1)
    phase_L1B(kb, So // 2048, KT1, QT1, V1, consts, ON)
    dense_tail(kb, nbh, 1, None, X1[2048:TA, :], io, XA1, XB1, out, consts, io["od_w_out"][0], mix_loader=MergeLoader(ON),
               p_in=io["p1_own"])
    return nc, kb


def fused_inputs(x_b, p_b, weights, h):
    S = x_b.shape[0]
    So = S // 2
    lo = So * h
    m = {n: weights[n] for n in ALL_W}
    m["x_all"] = x_b
    if h == 0:
        m["x_A"] = np.ascontiguousarray(np.concatenate([np.zeros((2048, D), np.float32), x_b[0:So]], axis=0))
        m["p0_A"] = np.ascontiguousarray(np.concatenate([np.zeros((2048, PLE), np.float32), p_b[0, 0:So]], axis=0))
    else:
        m["x_A"] = np.ascontiguousarray(x_b[lo - 2048:lo + So])
        m["p0_A"] = np.ascontiguousarray(p_b[0, lo - 2048:lo + So])
    m["p1_own"] = np.ascontiguousarray(p_b[1, lo:lo + So])
    m["rope_all"] = rope_table(np.arange(S), 8)
    m["rope_A"] = rope_table(np.arange(lo - 2048, lo + So), 8)
    m.update(const_fused(h))
    return m


_PROGS = {}


def _prog(kind, S):
    key = (kind, S)
    if key not in _PROGS:
        _PROGS[key] = (build_l0(S) if kind == 0 else build_l1(S) if kind == 1 else build_fused(S))[0]
    return _PROGS[key]


def kernel(**inputs):
    inputs = {k: np.ascontiguousarray(np.asarray(v, dtype=np.float32)) for k, v in inputs.items()}
    x, p = inputs["x"], inputs["p"]
    B, S, _ = x.shape
    So = S // 2
    n = 2 * B
    weights = {k: v for k, v in inputs.items() if k not in ("x", "p")}
    maps = [fused_inputs(x[c // 2], p[:, c // 2], weights, c % 2) for c in range(n)]
    res = run_bass_kernel_spmd(_prog(2, S), maps, core_ids=list(range(n)))
    out = np.empty((B, S, D), np.float32)
    for c in range(n):
        b, h = c // 2, c % 2
        out[b, h * So:(h + 1) * So] = res.results[c]["out_own"]
    return out
```

```python
import contextlib
import math
import numpy as np
import ml_dtypes
import concourse.bass as bass
import concourse.mybir as mybir
from concourse.bass_utils import run_bass_kernel_spmd

F32 = mybir.dt.float32
BF16 = mybir.dt.bfloat16
AF = mybir.ActivationFunctionType
ALU = mybir.AluOpType
AX = mybir.AxisListType
NPBF = ml_dtypes.bfloat16

D = 1024
FFN = 4096
PLE = 256
ALPHA = 4.0 ** 0.25
LN_EPS = 1e-5
LAM_INIT0 = 0.8 - 0.6 * math.exp(-0.3 * 0)
NEG = -30000.0
A2_STOP = 99
ROPE_THETA = 500000.0


class Buf:
    __slots__ = ("t", "w", "r", "dsem", "psum")

    def __init__(self, t):
        self.t = t
        self.w = None
        self.r = []
        self.dsem = None
        self.psum = False

    def __getitem__(self, idx):
        return self.t[idx]


class Ring:
    def __init__(self, bufs):
        self.bufs = bufs
        self.i = 0

    def next(self):
        b = self.bufs[self.i % len(self.bufs)]
        self.i += 1
        return b


class Eng:
    def __init__(self, name, eng, semidx):
        self.name = name
        self.eng = eng
        self.semidx = semidx
        self.waited = {}


class KB:
    def __init__(self, nc):
        self.nc = nc
        self.es = contextlib.ExitStack()
        self.sems = []
        self.semcnt = []
        self.semdma = []
        self.free_dsems = []
        self.E = {}
        for name, eng in (("pe", nc.tensor), ("act", nc.scalar), ("dve", nc.vector),
                          ("pool", nc.gpsimd), ("sp", nc.sync)):
            si = self._newsem("e_" + name, False)
            self.E[name] = Eng(name, eng, si)
        self.n_inst = 0
        self.phase_bufs = []

    def _newsem(self, name, isdma):
        s = self.es.enter_context(self.nc.semaphore(name))
        self.sems.append(s)
        self.semcnt.append(0)
        self.semdma.append(isdma)
        return len(self.sems) - 1

    def get_dsem(self):
        if self.free_dsems:
            return self.free_dsems.pop()
        return self._newsem("d%d" % len(self.sems), True)

    def sb(self, ph, name, shape, dtype):
        self.uid = getattr(self, "uid", 0) + 1
        name = "%s_u%d" % (name, self.uid)
        b = Buf(ph.enter_context(self.nc.sbuf_tensor(name, list(shape), dtype)))
        self.phase_bufs.append(b)
        return b

    def uname(self, name):
        self.uid = getattr(self, "uid", 0) + 1
        return "%s_u%d" % (name, self.uid)

    def view(self, ap):
        b = Buf(ap)
        self.phase_bufs.append(b)
        return b

    def _wait(self, E, toks):
        for (si, val) in toks:
            if self.semdma[si]:
                val = self.semcnt[si]
            if si == E.semidx and E.name == "pe":
                continue
            if E.waited.get(si, 0) >= val:
                continue
            E.eng.wait_ge(self.sems[si], val)
            E.waited[si] = val
            self.n_inst += 1

    def _deps(self, reads, writes):
        need = []
        for b in reads:
            if b.w is not None:
                need.append(b.w)
            if b.psum:
                need.extend(b.r)
        for b in writes:
            if b.w is not None:
                need.append(b.w)
            need.extend(b.r)
        return need

    def _commit(self, tok, reads, writes):
        for b in reads:
            if b.psum:
                b.w = tok
                b.r = []
            else:
                b.r.append(tok)
        for b in writes:
            b.w = tok
            b.r = []

    def op(self, en, fn, reads=(), writes=()):
        E = self.E[en]
        self._wait(E, self._deps(reads, writes))
        inst = fn(E.eng)
        si = E.semidx
        self.semcnt[si] += 1
        inst.then_inc(self.sems[si], 1)
        tok = (si, self.semcnt[si])
        self._commit(tok, reads, writes)
        self.n_inst += 1
        return tok

    def dma(self, pairs, reads=(), writes=(), sembuf=None, q="sp"):
        E = self.E[q]
        self._wait(E, self._deps(reads, writes))
        if sembuf.dsem is None:
            sembuf.dsem = self.get_dsem()
        si = sembuf.dsem
        for (o, i) in pairs:
            E.eng.dma_start(out=o, in_=i).then_inc(self.sems[si], 16)
            self.semcnt[si] += 16
            self.n_inst += 1
        tok = (si, self.semcnt[si])
        self._commit(tok, reads, writes)
        return tok

    def barrier(self):
        allt = [(si, self.semcnt[si]) for si in range(len(self.sems)) if self.semcnt[si] > 0]
        for E in self.E.values():
            self._wait(E, allt)

    def end_phase(self):
        self.barrier()
        for b in self.phase_bufs:
            if b.dsem is not None:
                self.free_dsems.append(b.dsem)
                b.dsem = None
        self.phase_bufs = []


def TT(kb, en, out, a, b, op, reads, writes):
    return kb.op(en, lambda e: e.tensor_tensor(out=out, in0=a, in1=b, op=op), reads, writes)


def TS(kb, en, out, a, s1, s2, op0, op1, reads, writes):
    if op1 is None:
        return kb.op(en, lambda e: e.tensor_scalar(out=out, in0=a, scalar1=s1, scalar2=None, op0=op0), reads, writes)
    return kb.op(en, lambda e: e.tensor_scalar(out=out, in0=a, scalar1=s1, scalar2=s2, op0=op0, op1=op1), reads, writes)


def STT(kb, out, a, s, b, op0, op1, reads, writes):
    return kb.op("dve", lambda e: e.scalar_tensor_tensor(out=out, in0=a, scalar=s, in1=b, op0=op0, op1=op1), reads, writes)


def ACT(kb, out, in_, func, reads, writes, scale=1.0, bias=0.0, accum=None):
    if accum is not None:
        return kb.op("act", lambda e: e.activation(out=out, in_=in_, func=func, scale=scale, bias=bias, accum_out=accum), reads, writes)
    return kb.op("act", lambda e: e.activation(out=out, in_=in_, func=func, scale=scale, bias=bias), reads, writes)


def CP(kb, en, out, in_, reads, writes):
    if en == "act":
        return kb.op("act", lambda e: e.copy(out=out, in_=in_), reads, writes)
    return kb.op(en, lambda e: e.tensor_copy(out=out, in_=in_), reads, writes)


def rstd_act(kb, out, in_, tmp, scale, reads_buf, tmp_buf, out_buf):
    ACT(kb, tmp, in_, AF.Ln, [reads_buf], [tmp_buf], scale=scale, bias=LN_EPS)
    ACT(kb, out, tmp, AF.Exp, [tmp_buf], [out_buf], scale=-0.5)


def sigmoid_exp(kb, out, in_, in_buf, t1, t2, out_buf, engs=None):
    ACT(kb, t1[:], in_, AF.Exp, [in_buf], [t1], scale=-1.0)
    ACT(kb, t2[:], t1[:], AF.Ln, [t1], [t2], bias=1.0)
    ACT(kb, out, t2[:], AF.Exp, [t2], [out_buf], scale=-1.0)


def run_pairs(n, body, load, width=2, prefetch=2):
    state = {"n": 0}

    def ensure(upto):
        while state["n"] <= min(upto, n - 1):
            load(state["n"])
            state["n"] += 1
    j = 0
    while j < n:
        grp = list(range(j, min(n, j + width)))
        ensure(grp[-1] + prefetch)
        alive = [body(b) for b in grp]
        while alive:
            for g in list(alive):
                try:
                    next(g)
                except StopIteration:
                    alive.remove(g)
        j += width


class PSUM:
    def __init__(self, kb, ph):
        kb.uid = getattr(kb, "uid", 0) + 1
        self.t = ph.enter_context(kb.nc.psum_tensor("PSALL_u%d" % kb.uid, [128, 4096], F32))
        self.kb = kb

    def bank(self, k, dtype=F32):
        ap = self.t[:, k * 512:(k + 1) * 512]
        if dtype == BF16:
            ap = ap.bitcast(BF16)
        b = self.kb.view(ap)
        b.psum = True
        return b


def load_weight(kb, W, src, col_slices, krows):
    ncols = sum(hi - lo for lo, hi in col_slices)
    with contextlib.ExitStack() as st:
        PW = 2048
        stage = Ring([kb.sb(st, "wst%d" % i, [128, PW], F32) for i in range(3)])
        engs = ["dve", "dve", "act"]
        n = 0
        for kc in range(krows):
            dst0 = 0
            for (lo, hi) in col_slices:
                c = lo
                while c < hi:
                    w = min(PW, hi - c)
                    sbuf = stage.next()
                    kb.dma([(sbuf[:, 0:w], src[kc * 128:(kc + 1) * 128, c:c + w])], writes=[sbuf], sembuf=sbuf)
                    CP(kb, engs[n % 2], W[kc][:, dst0:dst0 + w], sbuf[:, 0:w], [sbuf], [W[kc]])
                    n += 1
                    dst0 += w
                    c += w
        kb.barrier()


def transposes(kb, psT, src, ident, nblk, reads):
    def f(e):
        for c in range(nblk):
            i = e.transpose(psT[:, c * 128:(c + 1) * 128], src[:, c * 128:(c + 1) * 128], ident[:])
        return i
    return kb.op("pe", f, reads + [ident], [psT])


def rope(kb, src, dst, rp, ng, tmps):
    s3 = src[:].rearrange("p (g d) -> p g d", g=ng)
    d3 = dst[:].rearrange("p (g d) -> p g d", g=ng)
    cos = rp[:, 0:ng * 8].rearrange("p (g d) -> p g d", g=ng)
    sin = rp[:, ng * 8:2 * ng * 8].rearrange("p (g d) -> p g d", g=ng)
    t1, t2, t3, t4 = tmps
    v = lambda t: t[:, 0:ng * 8].rearrange("p (g d) -> p g d", g=ng)
    TT(kb, "dve", v(t1), s3[:, :, 0:8], cos, ALU.mult, [src, rp], [t1])
    TT(kb, "dve", v(t2), s3[:, :, 8:16], sin, ALU.mult, [src, rp], [t2])
    TT(kb, "dve", v(t3), s3[:, :, 0:8], sin, ALU.mult, [src, rp], [t3])
    TT(kb, "dve", v(t4), s3[:, :, 8:16], cos, ALU.mult, [src, rp], [t4])
    TT(kb, "dve", d3[:, :, 0:8], v(t1), v(t2), ALU.subtract, [t1, t2], [dst])
    TT(kb, "dve", d3[:, :, 8:16], v(t3), v(t4), ALU.add, [t3, t4], [dst])
    CP(kb, "dve", d3[:, :, 16:64], s3[:, :, 16:64], [src], [dst])


def layer_norm(kb, y, g, b, st, mv, sd, rs, tmp):
    ACT(kb, tmp[:], y[:], AF.Copy, [y], [tmp, st], accum=st[:, 0:1])
    ACT(kb, tmp[:], y[:], AF.Square, [y], [tmp, st], accum=st[:, 1:2])
    TS(kb, "dve", mv[:, 0:1], st[:, 0:1], 1.0 / D, None, ALU.mult, None, [st], [mv])
    TT(kb, "dve", mv[:, 1:2], mv[:, 0:1], mv[:, 0:1], ALU.mult, [mv], [mv])
    STT(kb, mv[:, 2:3], st[:, 1:2], 1.0 / D, mv[:, 1:2], ALU.mult, ALU.subtract, [st, mv], [mv])
    rstd_act(kb, rs[:, 0:1], mv[:, 2:3], sd[:, 0:1], 1.0, mv, sd, rs)
    STT(kb, rs[:, 1:2], mv[:, 0:1], -1.0, rs[:, 0:1], ALU.mult, ALU.mult, [mv, rs], [rs])
    kb.op("act", lambda e: e.activation(out=tmp[:], in_=y[:], func=AF.Identity, scale=rs[:, 0:1], bias=rs[:, 1:2]),
          [y, rs], [tmp])
    TT(kb, "dve", y[:], tmp[:], g[:], ALU.mult, [tmp, g], [y])
    TT(kb, "dve", tmp[:], y[:], b[:], ALU.add, [y, b], [tmp])
    return tmp


def phase_C(kb, nbo, mix_src, xres, w_out, ln_g, ln_b, xa_out, consts, mix_loader=None):
    with contextlib.ExitStack() as ph:
        Wt = ph.enter_context(kb.nc.sbuf_tensor(kb.uname("C_W"), [128, 8, 1024], BF16))
        W = [kb.view(Wt[:, k, :]) for k in range(8)]
        load_weight(kb, W, w_out, [(0, 1024)], 8)
        ident = kb.sb(ph, "C_id", [128, 128], BF16)
        kb.dma([(ident[:], consts["ident"][:, :])], writes=[ident], sembuf=ident)
        gt = kb.sb(ph, "C_g", [128, 1024], F32)
        bt = kb.sb(ph, "C_b", [128, 1024], F32)
        kb.dma([(gt[:], ln_g.partition_broadcast(128))], writes=[gt], sembuf=gt)
        kb.dma([(bt[:], ln_b.partition_broadcast(128))], writes=[bt], sembuf=bt)
        ps = PSUM(kb, ph)
        pT = Ring([ps.bank(0, BF16), ps.bank(1, BF16)])
        po = Ring([[ps.bank(2), ps.bank(3)], [ps.bank(4), ps.bank(5)]])
        mixr = Ring([kb.sb(ph, "C_mix%d" % i, [128, 1024], BF16) for i in range(6)])
        xr = Ring([kb.sb(ph, "C_x%d" % i, [128, 1024], F32) for i in range(6)])
        mT = Ring([kb.sb(ph, "C_mT%d" % i, [128, 1024], BF16) for i in range(4)])
        yr = Ring([kb.sb(ph, "C_y%d" % i, [128, 1024], F32) for i in range(4)])
        tmpr = Ring([kb.sb(ph, "C_t%d" % i, [128, 1024], F32) for i in range(4)])
        sm = Ring([[kb.sb(ph, "C_s%d_%d" % (i, k), [128, 12], F32) for k in range(4)] for i in range(4)])
        extra = mix_loader.alloc(kb, ph) if mix_loader is not None else None
        loaded = {}

        def load(j):
            x = xr.next()
            kb.dma([(x[:], xres[j * 128:(j + 1) * 128, :])], writes=[x], sembuf=x)
            m = mixr.next()
            if mix_loader is None:
                kb.dma([(m[:], mix_src[j * 128:(j + 1) * 128, :])], writes=[m], sembuf=m)
            else:
                mix_loader.load(kb, extra, j, m)
            loaded[j] = (x, m)

        def body(j):
            x, m = loaded.pop(j)
            if mix_loader is not None:
                mix_loader.finish(kb, extra, m)
            p = pT.next()
            transposes(kb, p, m, ident, 8, [m])
            t = mT.next()
            CP(kb, "act", t[:], p[:], [p], [t])
            yield
            pa, pb = po.next()
            for half, pp in ((0, pa), (1, pb)):
                def f(e, half=half, pp=pp):
                    for k in range(8):
                        i = e.matmul(pp[:], lhsT=t[:, k * 128:(k + 1) * 128], rhs=W[k][:, half * 512:(half + 1) * 512],
                                     start=(k == 0), stop=(k == 7))
                    return i
                kb.op("pe", f, [t] + W, [pp])
            y = yr.next()
            STT(kb, y[:, 0:512], x[:, 0:512], ALPHA, pa[:], ALU.mult, ALU.add, [x, pa], [y])
            STT(kb, y[:, 512:1024], x[:, 512:1024], ALPHA, pb[:], ALU.mult, ALU.add, [x, pb], [y])
            yield
            s = sm.next()
            o = layer_norm(kb, y, gt, bt, s[0], s[1], s[2], s[3], tmpr.next())
            kb.dma([(xa_out[j * 128:(j + 1) * 128, :], o[:])], reads=[o], sembuf=o)

        run_pairs(nbo, body, load, width=4)
        kb.end_phase()


def phase_D1(kb, nbo, xa, w1, w2, ln_g, ln_b, xb_out, consts):
    assert nbo % 4 == 0
    with contextlib.ExitStack() as ph:
        W1t = ph.enter_context(kb.nc.sbuf_tensor(kb.uname("D_W1"), [128, 8, FFN], BF16))
        W2t = ph.enter_context(kb.nc.sbuf_tensor(kb.uname("D_W2"), [128, 32, D], BF16))
        W1 = [kb.view(W1t[:, k, :]) for k in range(8)]
        W2 = [kb.view(W2t[:, k, :]) for k in range(32)]
        load_weight(kb, W1, w1, [(0, FFN)], 8)
        load_weight(kb, W2, w2, [(0, D)], 32)
        ident = kb.sb(ph, "D_id", [128, 128], BF16)
        kb.dma([(ident[:], consts["ident"][:, :])], writes=[ident], sembuf=ident)
        gt = kb.sb(ph, "D_g", [128, 1024], F32)
        bt = kb.sb(ph, "D_b", [128, 1024], F32)
        kb.dma([(gt[:], ln_g.partition_broadcast(128))], writes=[gt], sembuf=gt)
        kb.dma([(bt[:], ln_b.partition_broadcast(128))], writes=[bt], sembuf=bt)
        ps = PSUM(kb, ph)
        pT = Ring([ps.bank(0, BF16)])
        ph_ = Ring([ps.bank(1), ps.bank(2), ps.bank(3)])
        po = Ring([[ps.bank(4), ps.bank(5)], [ps.bank(6), ps.bank(7)]])
        xr = Ring([kb.sb(ph, "D_x%d" % i, [128, 1024], F32) for i in range(2)])
        xbr = Ring([kb.sb(ph, "D_xb%d" % i, [128, 1024], BF16) for i in range(2)])
        xT = kb.sb(ph, "D_xT", [128, 8, 512], BF16)
        hT = [kb.sb(ph, "D_hT%d" % c, [128, 512], BF16) for c in range(32)]
        rl = Ring([kb.sb(ph, "D_rl%d" % i, [128, 512], F32) for i in range(2)])
        yr = Ring([kb.sb(ph, "D_y%d" % i, [128, 1024], F32) for i in range(1)])
        tmpr = Ring([kb.sb(ph, "D_t%d" % i, [128, 1024], F32) for i in range(2)])
        sm = Ring([[kb.sb(ph, "D_s%d_%d" % (i, k), [128, 12], F32) for k in range(4)] for i in range(2)])
        ng = nbo // 4
        for g in range(ng):
            for tb in range(4):
                x = xr.next()
                j = g * 4 + tb
                kb.dma([(x[:], xa[j * 128:(j + 1) * 128, :])], writes=[x], sembuf=x)
                xb = xbr.next()
                CP(kb, "dve" if tb % 2 == 0 else "dve", xb[:], x[:], [x], [xb])
                p = pT.next()
                transposes(kb, p, xb, ident, 8, [xb])
                CP(kb, "act", xT[:, :, tb * 128:(tb + 1) * 128], p[:].rearrange("p (k t) -> p k t", k=8), [p], [xT])
            for c in range(32):
                pp = ph_.next()

                def f(e, c=c, pp=pp):
                    for k in range(8):
                        i = e.matmul(pp[:], lhsT=W1[k][:, c * 128:(c + 1) * 128], rhs=xT[:, k, :], start=(k == 0), stop=(k == 7))
                    return i
                kb.op("pe", f, [xT] + W1, [pp])
                r = rl.next()
                ACT(kb, r[:], pp[:], AF.Relu, [pp], [r])
                TT(kb, "dve" if c % 2 == 0 else "dve", hT[c][:], r[:], r[:], ALU.mult, [r], [hT[c]])
            for tb in range(4):
                pa, pb = po.next()
                for half, pp in ((0, pa), (1, pb)):
                    def f(e, half=half, pp=pp, tb=tb):
                        for c in range(32):
                            i = e.matmul(pp[:], lhsT=hT[c][:, tb * 128:(tb + 1) * 128], rhs=W2[c][:, half * 512:(half + 1) * 512],
                                         start=(c == 0), stop=(c == 31))
                        return i
                    kb.op("pe", f, hT + W2, [pp])
                x = xr.next()
                j = g * 4 + tb
                kb.dma([(x[:], xa[j * 128:(j + 1) * 128, :])], writes=[x], sembuf=x)
                y = yr.next()
                STT(kb, y[:, 0:512], x[:, 0:512], ALPHA, pa[:], ALU.mult, ALU.add, [x, pa], [y])
                STT(kb, y[:, 512:1024], x[:, 512:1024], ALPHA, pb[:], ALU.mult, ALU.add, [x, pb], [y])
                s = sm.next()
                o = layer_norm(kb, y, gt, bt, s[0], s[1], s[2], s[3], tmpr.next())
                kb.dma([(xb_out[j * 128:(j + 1) * 128, :], o[:])], reads=[o], sembuf=o)
        kb.end_phase()


def phase_D2(kb, nbo, xb_in, p_in, w_gate, w_proj, norm_g, out, consts):
    with contextlib.ExitStack() as ph:
        Wgt = ph.enter_context(kb.nc.sbuf_tensor(kb.uname("E_Wg"), [128, 8, D], BF16))
        Wpt = ph.enter_context(kb.nc.sbuf_tensor(kb.uname("E_Wp"), [128, 2, D], BF16))
        Wg = [kb.view(Wgt[:, k, :]) for k in range(8)]
        Wp = [kb.view(Wpt[:, k, :]) for k in range(2)]
        load_weight(kb, Wg, w_gate, [(0, D)], 8)
        load_weight(kb, Wp, w_proj, [(0, D)], 2)
        ident = kb.sb(ph, "E_id", [128, 128], BF16)
        kb.dma([(ident[:], consts["ident"][:, :])], writes=[ident], sembuf=ident)
        gt = kb.sb(ph, "E_g", [128, 1024], F32)
        kb.dma([(gt[:], norm_g.partition_broadcast(128))], writes=[gt], sembuf=gt)
        ps = PSUM(kb, ph)
        pT = Ring([ps.bank(0, BF16), ps.bank(1, BF16)])
        pg = Ring([[ps.bank(2), ps.bank(3)]])
        pe_ = Ring([[ps.bank(4), ps.bank(5)], [ps.bank(6), ps.bank(7)]])
        er = Ring([kb.sb(ph, "E_er%d" % i, [128, 1024], F32) for i in range(4)])
        xr = Ring([kb.sb(ph, "E_x%d" % i, [128, 1024], F32) for i in range(6)])
        pr = Ring([kb.sb(ph, "E_p%d" % i, [128, 256], F32) for i in range(6)])
        xbr = Ring([kb.sb(ph, "E_xb%d" % i, [128, 1280], BF16) for i in range(2)])
        xT = Ring([kb.sb(ph, "E_xT%d" % i, [128, 1280], BF16) for i in range(4)])
        gsb = Ring([kb.sb(ph, "E_gs%d" % i, [128, 1024], F32) for i in range(4)])
        gt1 = Ring([kb.sb(ph, "E_g1%d" % i, [128, 1024], F32) for i in range(4)])
        gt2 = Ring([kb.sb(ph, "E_g2%d" % i, [128, 1024], F32) for i in range(4)])
        esb = Ring([kb.sb(ph, "E_es%d" % i, [128, 1024], F32) for i in range(2)])
        e2 = Ring([kb.sb(ph, "E_e2%d" % i, [128, 1024], F32) for i in range(2)])
        junk = Ring([kb.sb(ph, "E_jk%d" % i, [128, 512], F32) for i in range(2)])
        outr = Ring([kb.sb(ph, "E_o%d" % i, [128, 1024], F32) for i in range(2)])
        sm = Ring([[kb.sb(ph, "E_s%d_%d" % (i, k), [128, 4], F32) for k in range(4)] for i in range(4)])
        loaded = {}

        def load(j):
            x = xr.next()
            kb.dma([(x[:], xb_in[j * 128:(j + 1) * 128, :])], writes=[x], sembuf=x)
            p = pr.next()
            kb.dma([(p[:], p_in[j * 128:(j + 1) * 128, :])], writes=[p], sembuf=p)
            loaded[j] = (x, p)

        def body(j):
            x, p = loaded.pop(j)
            xb = xbr.next()
            CP(kb, "dve", xb[:, 0:1024], x[:], [x], [xb])
            CP(kb, "dve", xb[:, 1024:1280], p[:], [p], [xb])
            pt1 = pT.next()
            transposes(kb, pt1, xb, ident, 8, [xb])
            t = xT.next()
            CP(kb, "act", t[:, 0:1024], pt1[:], [pt1], [t])
            pt2 = pT.next()

            def f2(e, xb=xb, pt2=pt2):
                for c in range(2):
                    i = e.transpose(pt2[:, c * 128:(c + 1) * 128], xb[:, 1024 + c * 128:1024 + (c + 1) * 128], ident[:])
                return i
            kb.op("pe", f2, [xb, ident], [pt2])
            CP(kb, "dve", t[:, 1024:1280], pt2[:, 0:256], [pt2], [t])
            yield
            ga, gb = pg.next()
            ea, eb = pe_.next()
            for half, pp in ((0, ga), (1, gb)):
                def f(e, half=half, pp=pp, t=t):
                    for k in range(8):
                        i = e.matmul(pp[:], lhsT=t[:, k * 128:(k + 1) * 128], rhs=Wg[k][:, half * 512:(half + 1) * 512],
                                     start=(k == 0), stop=(k == 7))
                    return i
                kb.op("pe", f, [t] + Wg, [pp])
            for half, pp in ((0, ea), (1, eb)):
                def f(e, half=half, pp=pp, t=t):
                    for k in range(2):
                        i = e.matmul(pp[:], lhsT=t[:, 1024 + k * 128:1024 + (k + 1) * 128], rhs=Wp[k][:, half * 512:(half + 1) * 512],
                                     start=(k == 0), stop=(k == 1))
                    return i
                kb.op("pe", f, [t] + Wp, [pp])
            s = sm.next()
            gs = gsb.next()
            g1_, g2_ = gt1.next(), gt2.next()
            ACT(kb, g1_[:, 0:512], ga[:], AF.Exp, [ga], [g1_], scale=-1.0)
            ACT(kb, g1_[:, 512:1024], gb[:], AF.Exp, [gb], [g1_], scale=-1.0)
            jk = junk.next()
            ACT(kb, jk[:], ea[:], AF.Square, [ea], [jk, s[0]], accum=s[0][:, 0:1])
            jk = junk.next()
            ACT(kb, jk[:], eb[:], AF.Square, [eb], [jk, s[0]], accum=s[0][:, 1:2])
            e_r = er.next()
            CP(kb, "dve", e_r[:, 0:512], ea[:], [ea], [e_r])
            CP(kb, "dve", e_r[:, 512:1024], eb[:], [eb], [e_r])
            yield
            ACT(kb, g2_[:], g1_[:], AF.Ln, [g1_], [g2_], bias=1.0)
            ACT(kb, gs[:], g2_[:], AF.Exp, [g2_], [gs], scale=-1.0)
            TT(kb, "dve", s[1][:, 0:1], s[0][:, 0:1], s[0][:, 1:2], ALU.add, [s[0]], [s[1]])
            rstd_act(kb, s[3][:, 0:1], s[1][:, 0:1], s[2][:, 0:1], 1.0 / D, s[1], s[2], s[3])
            es = esb.next()
            kb.op("act", lambda e, es=es, e_r=e_r, s=s: e.activation(out=es[:], in_=e_r[:], func=AF.Copy, scale=s[3][:, 0:1]), [e_r, s[3]], [es])
            ee = e2.next()
            TT(kb, "dve", ee[:], es[:], gt[:], ALU.mult, [es, gt], [ee])
            TT(kb, "dve", es[:], ee[:], gs[:], ALU.mult, [ee, gs], [es])
            o = outr.next()
            TT(kb, "dve", o[:], es[:], x[:], ALU.add, [es, x], [o])
            kb.dma([(out[j * 128:(j + 1) * 128, :], o[:])], reads=[o], sembuf=o)

        run_pairs(nbo, body, load, width=4)
        kb.end_phase()


def hgrn_gate_tables(kb, ph, lb_logits):
    lt = kb.sb(ph, "lbl", [128, 2, 512], F32)
    kb.dma([(lt[:], lb_logits.partition_broadcast(128))], writes=[lt], sembuf=lt)
    dl = kb.sb(ph, "lbd", [128, 512], F32)
    lb = kb.sb(ph, "lb", [128, 512], F32)
    oml = kb.sb(ph, "oml", [128, 512], F32)
    TT(kb, "dve", dl[:], lt[:, 0, :], lt[:, 1, :], ALU.subtract, [lt], [dl])
    lt1 = kb.sb(ph, "lbt1", [128, 512], F32)
    lt2 = kb.sb(ph, "lbt2", [128, 512], F32)
    sigmoid_exp(kb, lb[:], dl[:], dl, lt1, lt2, lb)
    TS(kb, "dve", oml[:], lb[:], -1.0, 1.0, ALU.mult, ALU.add, [lb], [oml])
    return lb, oml


def phase_A1(kb, S, x_all, rope_all, w_in, lb_logits, consts, KT, V, SS, blended=True):
    nb = S // 128
    with contextlib.ExitStack() as ph:
        Wt = ph.enter_context(kb.nc.sbuf_tensor(kb.uname("A1_W"), [128, 8, 2048], BF16))
        W = [kb.view(Wt[:, k, :]) for k in range(8)]
        load_weight(kb, W, w_in, [(512, 1024), (1024, 1536), (2048, 2560), (2560, 3072)], 8)
        ident = kb.sb(ph, "A1_id", [128, 128], BF16)
        kb.dma([(ident[:], consts["ident"][:, :])], writes=[ident], sembuf=ident)
        cU = kb.sb(ph, "A1_cU", [128, 4, 128], F32)
        kb.dma([(cU[:], consts["cU"][:, :, :])], writes=[cU], sembuf=cU)
        sel = kb.sb(ph, "A1_sel", [128, 2], F32)
        kb.dma([(sel[:], consts["sel"][:, :])], writes=[sel], sembuf=sel)
        lb, oml = hgrn_gate_tables(kb, ph, lb_logits)
        ps = PSUM(kb, ph)
        pT = Ring([ps.bank(0, BF16)])
        pp = Ring([ps.bank(1), ps.bank(2), ps.bank(3)])
        pkT = Ring([ps.bank(4, BF16)])
        pbd = Ring([ps.bank(5)])
        pdec = Ring([ps.bank(6)])
        pS = Ring([ps.bank(7)])
        xf = Ring([kb.sb(ph, "A1_xf%d" % i, [128, 1024], F32) for i in range(5)])
        rp = Ring([kb.sb(ph, "A1_rp%d" % i, [128, 128], F32) for i in range(5)])
        xb = Ring([kb.sb(ph, "A1_xb%d" % i, [128, 1024], BF16) for i in range(2)])
        xT = Ring([kb.sb(ph, "A1_xT%d" % i, [128, 1024], BF16) for i in range(3)])
        kf = Ring([kb.sb(ph, "A1_kf%d" % i, [128, 512], F32) for i in range(3)])
        kbf = Ring([kb.sb(ph, "A1_kb%d" % i, [128, 512], BF16) for i in range(2)])
        rt = Ring([[kb.sb(ph, "A1_rt%d_%d" % (i, k), [128, 64], F32) for k in range(4)] for i in range(2)])
        kTs = Ring([kb.sb(ph, "A1_kT%d" % i, [128, 512], BF16) for i in range(3)])
        vb = Ring([kb.sb(ph, "A1_vb%d" % i, [128, 516], BF16) for i in range(3)])
        sg = Ring([kb.sb(ph, "A1_sg%d" % i, [128, 512], F32) for i in range(3)])
        sgt1 = Ring([kb.sb(ph, "A1_sgt1%d" % i, [128, 512], F32) for i in range(2)])
        sgt2 = Ring([kb.sb(ph, "A1_sgt2%d" % i, [128, 512], F32) for i in range(2)])
        hib = Ring([kb.sb(ph, "A1_hi%d" % i, [128, 512], BF16) for i in range(3)])
        f1 = Ring([kb.sb(ph, "A1_f1%d" % i, [128, 512], F32) for i in range(2)])
        ff = Ring([kb.sb(ph, "A1_ff%d" % i, [128, 512], F32) for i in range(2)])
        lf = Ring([kb.sb(ph, "A1_lf%d" % i, [128, 512], F32) for i in range(2)])
        kk = Ring([kb.sb(ph, "A1_kk%d" % i, [128, 512], F32) for i in range(3)])
        ed = Ring([kb.sb(ph, "A1_ed%d" % i, [128, 512], F32) for i in range(3)])
        kst = Ring([kb.sb(ph, "A1_ks%d" % i, [128, 512], BF16) for i in range(2)])
        dec = Ring([kb.sb(ph, "A1_dc%d" % i, [128, 8], F32) for i in range(3)])
        Sr = Ring([kb.sb(ph, "A1_S%d" % i, [128, 512], F32) for i in range(2)])
        Sa = [kb.sb(ph, "A1_Sa%d" % c, [128, 512], F32) for c in range(2)]
        Sb = Ring([kb.sb(ph, "A1_Sb%d" % i, [128, 512], BF16) for i in range(4)])
        for v in vb.bufs:
            kb.op("pool", lambda e, v=v: e.memset(v[:], 1.0), [], [v])
        S = Sr.next()
        kb.op("pool", lambda e: e.memset(S[:], 0.0), [], [S])
        loaded = {}

        def load(tb):
            x = xf.next()
            kb.dma([(x[:], x_all[tb * 128:(tb + 1) * 128, :])], writes=[x], sembuf=x)
            r = rp.next()
            kb.dma([(r[:], rope_all[tb * 128:(tb + 1) * 128, :])], writes=[r], sembuf=r)
            loaded[tb] = (x, r)

        def mm_group(pbuf, t, col0):
            def f(e):
                for k in range(8):
                    i = e.matmul(pbuf[:], lhsT=t[:, k * 128:(k + 1) * 128], rhs=W[k][:, col0:col0 + 512], start=(k == 0), stop=(k == 7))
                return i
            kb.op("pe", f, [t] + W, [pbuf])

        stS = {'S': S}

        def body(tb):
            x, r = loaded.pop(tb)
            b = xb.next()
            CP(kb, "dve", b[:], x[:], [x], [b])
            p = pT.next()
            transposes(kb, p, b, ident, 8, [b])
            t = xT.next()
            CP(kb, "act", t[:], p[:], [p], [t])
            yield
            pk = pp.next()
            mm_group(pk, t, 0)
            k_f = kf.next()
            CP(kb, "act", k_f[:], pk[:], [pk], [k_f])
            pv = pp.next()
            mm_group(pv, t, 512)
            v = vb.next()
            CP(kb, "act", v[:].rearrange("p (h d) -> p h d", h=4)[:, :, 0:128], pv[:].rearrange("p (h d) -> p h d", h=4), [pv], [v])
            kb.dma([(V[tb, :, :], v[:])], reads=[v], sembuf=v)
            phf = pp.next()
            mm_group(phf, t, 1024)
            s_g = sg.next()
            sigmoid_exp(kb, s_g[:], phf[:], phf, sgt1.next(), sgt2.next(), s_g)
            phi = pp.next()
            mm_group(phi, t, 1536)
            h_i = hib.next()
            CP(kb, "dve", h_i[:], phi[:], [phi], [h_i])
            yield
            k_b = kbf.next()
            rope(kb, k_f, k_b, r, 8, rt.next())
            pkt = pkT.next()
            transposes(kb, pkt, k_b, ident, 4, [k_b])
            kt = kTs.next()
            CP(kb, "act", kt[:], pkt[:, 0:512], [pkt], [kt])
            kb.dma([(KT[tb, :, :], kt[:])], reads=[kt], sembuf=kt)
            f_1 = f1.next()
            TT(kb, "dve", f_1[:], s_g[:], oml[:], ALU.mult, [s_g, oml], [f_1])
            f_ = ff.next()
            TT(kb, "dve", f_[:], f_1[:], lb[:], ALU.add, [f_1, lb], [f_])
            l_f = lf.next()
            ACT(kb, l_f[:], f_[:], AF.Ln, [f_], [l_f])
            k_k = kk.next()
            ACT(kb, k_k[:], f_[:], AF.Copy, [f_], [k_k], scale=-1.0, bias=1.0)
            pb = pbd.next()
            kb.op("pe", lambda e: e.matmul(pb[:], lhsT=cU[:, 2, :], rhs=l_f[:], start=True, stop=True), [cU, l_f], [pb])
            pd = pdec.next()

            def fdec(e, pd=pd, l_f=l_f):
                for h in range(4):
                    i = e.matmul(pd[:, h * 2:(h + 1) * 2], lhsT=l_f[:, h * 128:(h + 1) * 128], rhs=cU[:, 3, 0:2], start=True, stop=True)
                return i
            kb.op("pe", fdec, [cU, l_f], [pd])
            e_d = ed.next()
            ACT(kb, e_d[:], pb[:], AF.Exp, [pb], [e_d])
            d_c = dec.next()
            ACT(kb, d_c[:], pd[:, 0:8], AF.Exp, [pd], [d_c])
            yield
            k_s = kst.next()
            TT(kb, "dve", k_s[:], k_k[:], e_d[:], ALU.mult, [k_k, e_d], [k_s])
            j = tb // 2
            for c in range(2):
                if not blended:
                    sb_ = Sb.next()
                    CP(kb, "act", sb_[:], stS['S'][:], [stS['S']], [sb_])
                    kb.dma([(SS[tb, c, :, :], sb_[:])], reads=[sb_], sembuf=sb_)
                elif tb % 2 == 0:
                    TS(kb, "dve", Sa[c][:], stS['S'][:], sel[:, 0:1], None, ALU.mult, None, [stS['S'], sel], [Sa[c]])
                else:
                    sb_ = Sb.next()
                    STT(kb, sb_[:], stS['S'][:], sel[:, 1:2], Sa[c][:], ALU.mult, ALU.add, [stS['S'], sel, Sa[c]], [sb_])
                    kb.dma([(SS[j, c, :, :], sb_[:])], reads=[sb_], sembuf=sb_)
                psb = pS.next()

                def fS(e, psb=psb, k_s=k_s, h_i=h_i, c=c):
                    for h in range(4):
                        i = e.matmul(psb[:, h * 128:(h + 1) * 128], lhsT=k_s[c * 64:(c + 1) * 64, h * 128:(h + 1) * 128],
                                     rhs=h_i[c * 64:(c + 1) * 64, h * 128:(h + 1) * 128], start=True, stop=True)
                    return i
                kb.op("pe", fS, [k_s, h_i], [psb])
                Sn = Sr.next()
                for h in range(4):
                    STT(kb, Sn[:, h * 128:(h + 1) * 128], stS['S'][:, h * 128:(h + 1) * 128], d_c[:, h * 2 + c:h * 2 + c + 1],
                        psb[:, h * 128:(h + 1) * 128], ALU.mult, ALU.add, [stS['S'], d_c, psb], [Sn])
                stS['S'] = Sn

        run_pairs(nb, body, load, width=3)
        kb.end_phase()


def phase_A2(kb, nbo, x_own, rope_own, w_in, lb_logits, hg_norm_g, consts, SS, QT, MIX, ss_pick=None):
    with contextlib.ExitStack() as ph:
        Wt = ph.enter_context(kb.nc.sbuf_tensor(kb.uname("A2_W"), [128, 8, 2560], BF16))
        W = [kb.view(Wt[:, k, :]) for k in range(8)]
        load_weight(kb, W, w_in, [(0, 512), (1536, 3584)], 8)
        ident = kb.sb(ph, "A2_id", [128, 128], BF16)
        kb.dma([(ident[:], consts["ident"][:, :])], writes=[ident], sembuf=ident)
        cU = kb.sb(ph, "A2_cU", [128, 4, 128], F32)
        kb.dma([(cU[:], consts["cU"][:, :, :])], writes=[cU], sembuf=cU)
        mhg = kb.sb(ph, "A2_mhg", [128, 128], F32)
        kb.dma([(mhg[:], consts["cU"][:, 0, :])], writes=[mhg], sembuf=mhg)
        gtab = kb.sb(ph, "A2_g", [128, 128], F32)
        kb.dma([(gtab[:], hg_norm_g.partition_broadcast(128))], writes=[gtab], sembuf=gtab)
        lb, oml = hgrn_gate_tables(kb, ph, lb_logits)
        ps = PSUM(kb, ph)
        pT = Ring([ps.bank(0, BF16)])
        pp = Ring([ps.bank(1), ps.bank(2), ps.bank(3)])
        pcs = Ring([[ps.bank(4), ps.bank(5)]])
        pt3 = Ring([ps.bank(6, BF16)])
        pA = Ring([ps.bank(7)])
        xf = Ring([kb.sb(ph, "A2_xf%d" % i, [128, 1024], F32) for i in range(4)])
        rp = Ring([kb.sb(ph, "A2_rp%d" % i, [128, 128], F32) for i in range(4)])
        s0r = Ring([kb.sb(ph, "A2_s0%d" % i, [128, 512], BF16) for i in range(3)])
        s1r = Ring([kb.sb(ph, "A2_s1%d" % i, [128, 512], BF16) for i in range(3)])
        xb = Ring([kb.sb(ph, "A2_xb%d" % i, [128, 1024], BF16) for i in range(2)])
        xT = Ring([kb.sb(ph, "A2_xT%d" % i, [128, 1024], BF16) for i in range(3)])
        qf = Ring([kb.sb(ph, "A2_qf%d" % i, [128, 512], F32) for i in range(3)])
        qbf = Ring([kb.sb(ph, "A2_qb%d" % i, [128, 512], BF16) for i in range(2)])
        rt = Ring([[kb.sb(ph, "A2_rt%d_%d" % (i, k), [128, 64], F32) for k in range(4)] for i in range(2)])
        qTs = Ring([kb.sb(ph, "A2_qT%d" % i, [128, 512], BF16) for i in range(2)])
        qs = Ring([kb.sb(ph, "A2_qs%d" % i, [128, 512], F32) for i in range(3)])
        sg = Ring([kb.sb(ph, "A2_sg%d" % i, [128, 512], F32) for i in range(3)])
        sgt1 = Ring([kb.sb(ph, "A2_sgt1%d" % i, [128, 512], F32) for i in range(1)])
        sgt2 = Ring([kb.sb(ph, "A2_sgt2%d" % i, [128, 512], F32) for i in range(1)])
        sgt3 = Ring([kb.sb(ph, "A2_sgt3%d" % i, [128, 512], F32) for i in range(1)])
        hib = Ring([kb.sb(ph, "A2_hi%d" % i, [128, 512], BF16) for i in range(3)])
        gs = Ring([kb.sb(ph, "A2_gs%d" % i, [128, 512], F32) for i in range(3)])
        f1 = Ring([kb.sb(ph, "A2_f1%d" % i, [128, 512], F32) for i in range(1)])
        ff = Ring([kb.sb(ph, "A2_ff%d" % i, [128, 512], F32) for i in range(1)])
        lf = Ring([kb.sb(ph, "A2_lf%d" % i, [128, 512], F32) for i in range(1)])
        kk = Ring([kb.sb(ph, "A2_kk%d" % i, [128, 512], F32) for i in range(3)])
        bsb = Ring([kb.sb(ph, "A2_bs%d" % i, [128, 512], F32) for i in range(3)])
        d1 = Ring([kb.sb(ph, "A2_d1%d" % i, [128, 512], F32) for i in range(3)])
        e1 = Ring([kb.sb(ph, "A2_e1%d" % i, [128, 512], F32) for i in range(1)])
        e2 = Ring([kb.sb(ph, "A2_e2%d" % i, [128, 512], F32) for i in range(1)])
        e3 = Ring([kb.sb(ph, "A2_e3%d" % i, [128, 512], F32) for i in range(1)])
        Qh = Ring([kb.sb(ph, "A2_Qh%d" % i, [128, 512], BF16) for i in range(2)])
        Kh = Ring([kb.sb(ph, "A2_Kh%d" % i, [128, 512], BF16) for i in range(2)])
        qo = Ring([kb.sb(ph, "A2_qo%d" % i, [128, 512], BF16) for i in range(2)])
        QhT = Ring([kb.sb(ph, "A2_QhT%d" % i, [128, 512], BF16) for i in range(3)])
        KhT = Ring([kb.sb(ph, "A2_KhT%d" % i, [128, 512], BF16) for i in range(3)])
        qz0 = Ring([kb.sb(ph, "A2_qz0%d" % i, [128, 4, 128], BF16) for i in range(3)])
        qz1 = Ring([kb.sb(ph, "A2_qz1%d" % i, [128, 4, 128], BF16) for i in range(3)])
        Am = Ring([kb.sb(ph, "A2_Am%d" % i, [128, 512], BF16) for i in range(3)])
        jk = Ring([kb.sb(ph, "A2_jk%d" % i, [128, 128], F32) for i in range(2)])
        sm = Ring([[kb.sb(ph, "A2_sm%d_%d" % (i, k), [128, 4], F32) for k in range(4)] for i in range(2)])
        on = Ring([kb.sb(ph, "A2_on%d" % i, [128, 512], F32) for i in range(1)])
        on2 = Ring([kb.sb(ph, "A2_o2%d" % i, [128, 512], F32) for i in range(1)])
        ob = Ring([kb.sb(ph, "A2_ob%d" % i, [128, 512], BF16) for i in range(2)])
        for z in qz0.bufs + qz1.bufs:
            kb.op("pool", lambda e, z=z: e.memset(z[:], 0.0), [], [z])
        loaded = {}
        if ss_pick is not None:
            cand = Ring([kb.sb(ph, "A2_cd%d" % i, [128, 512], BF16) for i in range(16)])
            candf = Ring([kb.sb(ph, "A2_cf%d" % i, [128, 512], F32) for i in range(1)])
            sel = kb.sb(ph, "A2_sel", [128, 2], F32)
            kb.dma([(sel[:], consts["sel"][:, :])], writes=[sel], sembuf=sel)

        def load(j):
            x = xf.next()
            kb.dma([(x[:], x_own[j * 128:(j + 1) * 128, :])], writes=[x], sembuf=x)
            r = rp.next()
            kb.dma([(r[:], rope_own[j * 128:(j + 1) * 128, :])], writes=[r], sembuf=r)
            if ss_pick is None:
                a0 = s0r.next()
                a1 = s1r.next()
                kb.dma([(a0[:], SS[j, 0, :, :])], writes=[a0], sembuf=a0)
                kb.dma([(a1[:], SS[j, 1, :, :])], writes=[a1], sembuf=a1)
            else:
                g0, g1 = ss_pick(j)
                a0 = []
                for c in (0, 1):
                    c0 = cand.next()
                    kb.dma([(c0[:], SS[g0, c, :, :])], writes=[c0], sembuf=c0)
                    c1 = cand.next()
                    kb.dma([(c1[:], SS[g1, c, :, :])], writes=[c1], sembuf=c1)
                    a0.append((c0, c1))
                a1 = None
            loaded[j] = (x, r, a0, a1)

        def mm_group(pbuf, t, col0):
            def f(e):
                for k in range(8):
                    i = e.matmul(pbuf[:], lhsT=t[:, k * 128:(k + 1) * 128], rhs=W[k][:, col0:col0 + 512], start=(k == 0), stop=(k == 7))
                return i
            kb.op("pe", f, [t] + W, [pbuf])

        def body(j):
            x, r, S0, S1 = loaded.pop(j)
            b = xb.next()
            CP(kb, "dve", b[:], x[:], [x], [b])
            p = pT.next()
            transposes(kb, p, b, ident, 8, [b])
            t = xT.next()
            CP(kb, "act", t[:], p[:], [p], [t])
            yield
            pq = pp.next()
            mm_group(pq, t, 0)
            q_f = qf.next()
            CP(kb, "act", q_f[:], pq[:], [pq], [q_f])
            phq = pp.next()
            mm_group(phq, t, 512)
            q_s = qs.next()
            sq_ = sgt3.next()
            sigmoid_exp(kb, sq_[:], phq[:], phq, sgt1.next(), sgt2.next(), sq_)
            TT(kb, "dve", q_s[:], sq_[:], phq[:], ALU.mult, [sq_, phq], [q_s])
            phf = pp.next()
            mm_group(phf, t, 1024)
            s_g = sg.next()
            sigmoid_exp(kb, s_g[:], phf[:], phf, sgt1.next(), sgt2.next(), s_g)
            phi = pp.next()
            mm_group(phi, t, 1536)
            h_i = hib.next()
            CP(kb, "dve", h_i[:], phi[:], [phi], [h_i])
            phg = pp.next()
            mm_group(phg, t, 2048)
            g_s = gs.next()
            sq_ = sgt3.next()
            sigmoid_exp(kb, sq_[:], phg[:], phg, sgt1.next(), sgt2.next(), sq_)
            TT(kb, "dve", g_s[:], sq_[:], phg[:], ALU.mult, [sq_, phg], [g_s])
            yield
            q_b = qbf.next()
            rope(kb, q_f, q_b, r, 8, rt.next())
            p3 = pt3.next()
            transposes(kb, kb.view(p3[:, 0:512]) if False else p3, q_b, ident, 4, [q_b])
            qt = qTs.next()
            CP(kb, "act", qt[:], p3[:, 0:512], [p3], [qt])
            kb.dma([(QT[j, :, :], qt[:])], reads=[qt], sembuf=qt)
            f_1 = f1.next()
            TT(kb, "dve", f_1[:], s_g[:], oml[:], ALU.mult, [s_g, oml], [f_1])
            f_ = ff.next()
            TT(kb, "dve", f_[:], f_1[:], lb[:], ALU.add, [f_1, lb], [f_])
            l_f = lf.next()
            ACT(kb, l_f[:], f_[:], AF.Ln, [f_], [l_f])
            k_k = kk.next()
            ACT(kb, k_k[:], f_[:], AF.Copy, [f_], [k_k], scale=-1.0, bias=1.0)
            pb, pbm = pcs.next()
            kb.op("pe", lambda e: e.matmul(pb[:], lhsT=cU[:, 0, :], rhs=l_f[:], start=True, stop=True), [cU, l_f], [pb])
            kb.op("pe", lambda e: e.matmul(pbm[:], lhsT=cU[:, 1, :], rhs=l_f[:], start=True, stop=True), [cU, l_f], [pbm])
            b_s = bsb.next()
            CP(kb, "act", b_s[:], pb[:], [pb], [b_s])
            d_1 = d1.next()
            TT(kb, "dve", d_1[:], b_s[:], pbm[:], ALU.subtract, [b_s, pbm], [d_1])
            yield
            e_1 = e1.next()
            ACT(kb, e_1[:], d_1[:], AF.Exp, [d_1], [e_1])
            e_2 = e2.next()
            ACT(kb, e_2[:], d_1[:], AF.Exp, [d_1], [e_2], scale=-1.0)
            e_3 = e3.next()
            ACT(kb, e_3[:], b_s[:], AF.Exp, [b_s], [e_3])
            Q_h = Qh.next()
            TT(kb, "dve", Q_h[:], q_s[:], e_1[:], ALU.mult, [q_s, e_1], [Q_h])
            K_h = Kh.next()
            TT(kb, "dve", K_h[:], k_k[:], e_2[:], ALU.mult, [k_k, e_2], [K_h])
            q_o = qo.next()
            TT(kb, "dve", q_o[:], q_s[:], e_3[:], ALU.mult, [q_s, e_3], [q_o])
            p3 = pt3.next()
            transposes(kb, p3, Q_h, ident, 4, [Q_h])
            Q_T = QhT.next()
            CP(kb, "act", Q_T[:], p3[:, 0:512], [p3], [Q_T])
            p3 = pt3.next()
            transposes(kb, p3, K_h, ident, 4, [K_h])
            K_T = KhT.next()
            CP(kb, "dve", K_T[:], p3[:, 0:512], [p3], [K_T])
            p3 = pt3.next()
            transposes(kb, p3, q_o, ident, 4, [q_o])
            z0 = qz0.next()
            z1 = qz1.next()
            p3v = p3[:, 0:512].rearrange("p (h t) -> p h t", h=4)
            CP(kb, "act", z0[:, :, 0:64], p3v[:, :, 0:64], [p3], [z0])
            CP(kb, "dve", z1[:, :, 64:128], p3v[:, :, 64:128], [p3], [z1])
            yield
            pa = pA.next()

            def fA(e, pa=pa, K_T=K_T, Q_T=Q_T):
                for h in range(4):
                    i = e.matmul(pa[:, h * 128:(h + 1) * 128], lhsT=K_T[:, h * 128:(h + 1) * 128], rhs=Q_T[:, h * 128:(h + 1) * 128],
                                 start=True, stop=True)
                return i
            kb.op("pe", fA, [K_T, Q_T], [pa])
            A_m = Am.next()
            TT(kb, "dve", A_m[:].rearrange("p (h t) -> p h t", h=4), pa[:].rearrange("p (h t) -> p h t", h=4),
               mhg[:].unsqueeze(1).broadcast_to([128, 4, 128]), ALU.mult, [pa, mhg], [A_m])
            yield
            if ss_pick is not None:
                cands = S0
                blended = []
                for c, ring in ((0, s0r), (1, s1r)):
                    c0, c1 = cands[c]
                    dst = ring.next()
                    tmpb = candf.next()
                    kb.op("act", lambda e, tmpb=tmpb, c0=c0: e.activation(out=tmpb[:], in_=c0[:], func=AF.Copy, scale=sel[:, 0:1]), [c0, sel], [tmpb])
                    STT(kb, dst[:], c1[:], sel[:, 1:2], tmpb[:], ALU.mult, ALU.add, [c1, sel, tmpb], [dst])
                    blended.append(dst)
                S0, S1 = blended
            po = pp.next()

            def fo(e, po=po, A_m=A_m, h_i=h_i, z0=z0, z1=z1, S0=S0, S1=S1):
                for h in range(4):
                    hs = slice(h * 128, (h + 1) * 128)
                    e.matmul(po[:, hs], lhsT=A_m[:, hs], rhs=h_i[:, hs], start=True, stop=False)
                    e.matmul(po[:, hs], lhsT=z0[:, h, :], rhs=S0[:, hs], start=False, stop=False)
                    i = e.matmul(po[:, hs], lhsT=z1[:, h, :], rhs=S1[:, hs], start=False, stop=True)
                return i
            kb.op("pe", fo, [A_m, h_i, z0, z1, S0, S1], [po])
            s = sm.next()
            for h in range(4):
                j_ = jk.next()
                ACT(kb, j_[:], po[:, h * 128:(h + 1) * 128], AF.Square, [po], [j_, s[0]], accum=s[0][:, h:h + 1])
            rstd_act(kb, s[2][:, 0:4], s[0][:, 0:4], s[1][:, 0:4], 1.0 / 128.0, s[0], s[1], s[2])
            o_n = on.next()
            TT(kb, "dve", o_n[:].rearrange("p (h t) -> p h t", h=4), po[:].rearrange("p (h t) -> p h t", h=4),
               s[2][:, 0:4].unsqueeze(2).broadcast_to([128, 4, 128]), ALU.mult, [po, s[2]], [o_n])
            o_2 = on2.next()
            TT(kb, "dve", o_2[:].rearrange("p (h t) -> p h t", h=4), o_n[:].rearrange("p (h t) -> p h t", h=4),
               gtab[:].unsqueeze(1).broadcast_to([128, 4, 128]), ALU.mult, [o_n, gtab], [o_2])
            o_b = ob.next()
            TT(kb, "dve", o_b[:], o_2[:], g_s[:], ALU.mult, [o_2, g_s], [o_b])
            kb.dma([(MIX[j * 128:(j + 1) * 128, 512:1024], o_b[:])], reads=[o_b], sembuf=o_b)

        run_pairs(nbo, body, load, width=3, prefetch=1)
        kb.end_phase()


def keyspec_parity(j):
    return [(k, (k - 2 * j) if k >= 2 * j else None, False) for k in range(2 * j + 2)]


def make_keyspec_ctx(nbh):
    def spec(i):
        g0, g1 = i - NCTX, i + nbh - NCTX
        out = []
        for k in range(g1 + 1):
            if k < g0:
                out.append((k, None, False))
            elif k == g0:
                out.append((k, 0, False))
            elif k < g1:
                out.append((k, None, True))
            else:
                out.append((k, 1, True))
        return out
    return spec


def phase_B(kb, S, nbo, KT, V, QT, da_lambda, sub_g, consts, MIX, keyspec=keyspec_parity):
    nb = S // 128
    with contextlib.ExitStack() as ph:
        KTt = ph.enter_context(kb.nc.sbuf_tensor(kb.uname("B_KT"), [128, nb, 512], BF16))
        Vt = ph.enter_context(kb.nc.sbuf_tensor(kb.uname("B_V"), [128, nb, 516], BF16))
        CH = 8
        nch = (nb + CH - 1) // CH
        KTc = [kb.view(KTt[:, c * CH:min(nb, (c + 1) * CH), :]) for c in range(nch)]
        Vc = [kb.view(Vt[:, c * CH:min(nb, (c + 1) * CH), :]) for c in range(nch)]
        ident = kb.sb(ph, "B_id", [128, 128], BF16)
        kb.dma([(ident[:], consts["ident"][:, :])], writes=[ident], sembuf=ident)
        msk = kb.sb(ph, "B_msk", [128, 2, 128], BF16)
        kb.dma([(msk[:], consts["mAB"][:, :, :])], writes=[msk], sembuf=msk)
        gtab = kb.sb(ph, "B_g", [128, 128], F32)
        kb.dma([(gtab[:], sub_g.partition_broadcast(128))], writes=[gtab], sembuf=gtab)
        lt = kb.sb(ph, "B_lt", [128, 4, 64], F32)
        kb.dma([(lt[:], da_lambda.partition_broadcast(128))], writes=[lt], sembuf=lt)
        l1 = kb.sb(ph, "B_l1", [128, 2, 64], F32)
        l2 = kb.sb(ph, "B_l2", [128, 2], F32)
        l3 = kb.sb(ph, "B_l3", [128, 2], F32)
        nlam = kb.sb(ph, "B_nlam", [128, 1], F32)
        TT(kb, "dve", l1[:, 0, :], lt[:, 0, :], lt[:, 1, :], ALU.mult, [lt], [l1])
        TT(kb, "dve", l1[:, 1, :], lt[:, 2, :], lt[:, 3, :], ALU.mult, [lt], [l1])
        kb.op("dve", lambda e: e.reduce_sum(out=l2[:, 0:2], in_=l1[:], axis=AX.X), [l1], [l2])
        ACT(kb, l3[:], l2[:], AF.Exp, [l2], [l3])
        l4 = kb.sb(ph, "B_l4", [128, 1], F32)
        TT(kb, "dve", l4[:], l3[:, 1:2], l3[:, 0:1], ALU.subtract, [l3], [l4])
        TS(kb, "dve", nlam[:], l4[:], -LAM_INIT0, None, ALU.add, None, [l4], [nlam])
        for c in range(nch):
            lo, hi = c * CH, min(nb, (c + 1) * CH)
            kb.dma([(KTc[c][:], KT[lo:hi, :, :].rearrange("n p f -> p n f"))], writes=[KTc[c]], sembuf=KTc[c])
            kb.dma([(Vc[c][:], V[lo:hi, :, :].rearrange("n p f -> p n f"))], writes=[Vc[c]], sembuf=Vc[c])
        ps = PSUM(kb, ph)
        pst = Ring([ps.bank(0), ps.bank(1), ps.bank(2)])
        pO = Ring([ps.bank(3), ps.bank(4)])
        qr = Ring([kb.sb(ph, "B_q%d" % i, [128, 512], BF16) for i in range(3)])
        ptr = Ring([kb.sb(ph, "B_pt%d" % i, [128, 512], BF16) for i in range(4)])
        rd = Ring([kb.sb(ph, "B_rd%d" % i, [128, 2], F32) for i in range(2)])
        dn = Ring([kb.sb(ph, "B_dn%d" % i, [128, 2], F32) for i in range(2)])
        cf = Ring([kb.sb(ph, "B_cf%d" % i, [128, 1], F32) for i in range(2)])
        t0r = Ring([kb.sb(ph, "B_t0%d" % i, [128, 128], F32) for i in range(2)])
        orr = Ring([kb.sb(ph, "B_o%d" % i, [128, 128], F32) for i in range(2)])
        jk = Ring([kb.sb(ph, "B_jk%d" % i, [128, 128], F32) for i in range(2)])
        sm = Ring([[kb.sb(ph, "B_sm%d_%d" % (i, k), [128, 1], F32) for k in range(3)] for i in range(2)])
        on = Ring([kb.sb(ph, "B_on%d" % i, [128, 128], F32) for i in range(2)])
        mixr = Ring([kb.sb(ph, "B_mx%d" % i, [128, 512], BF16) for i in range(2)])
        loaded = {}

        def load(j):
            q = qr.next()
            kb.dma([(q[:], QT[j, :, :])], writes=[q], sembuf=q)
            loaded[j] = q

        def chunk_bufs(kbs):
            cs = sorted(set(k // CH for k in kbs))
            return [KTc[c] for c in cs], [Vc[c] for c in cs]

        hb = kb.sb(ph, "B_hb", [128, 1], F32)
        kb.dma([(hb[:], consts["hb"][:, :])], writes=[hb], sembuf=hb)
        units = []
        for j in range(nbo):
            ents = keyspec(j)
            groups = []
            for ent in ents:
                if groups and len(groups[-1]) < 4 and groups[-1][-1][2] == ent[2]:
                    groups[-1].append(ent)
                else:
                    groups.append([ent])
            for h in range(4):
                for c in range(2):
                    for gi_, grp in enumerate(groups):
                        units.append((j, h, c, grp, gi_ == 0, gi_ == len(groups) - 1))
        state = {"O": None, "mix": None}

        def emit_ST(u):
            j, h, c, grp, first, last = u
            q = loaded[j]
            st = pst.next()
            kbs = [g_[0] for g_ in grp]
            kts, _ = chunk_bufs(kbs)
            high = grp[0][2]

            def f(e):
                for i, (kbk, mi, _) in enumerate(grp):
                    e_ = e.matmul(st[:, i * 128:(i + 1) * 128], lhsT=KTt[c * 64:(c + 1) * 64, kbk, h * 128:(h + 1) * 128],
                                  rhs=q[c * 64:(c + 1) * 64, h * 128:(h + 1) * 128], start=True, stop=(mi is None))
                    if mi is not None:
                        e_ = e.matmul(st[:, i * 128:(i + 1) * 128], lhsT=ident[:], rhs=msk[:, mi, :], start=False, stop=True)
                return e_
            kb.op("pe", f, kts + [q, ident, msk], [st])
            pt = ptr.next()
            n = len(kbs) * 128
            if high:
                kb.op("act", lambda e: e.activation(out=pt[:, 0:n], in_=st[:, 0:n], func=AF.Exp, scale=0.125, bias=hb[:, 0:1]), [st, hb], [pt])
            else:
                ACT(kb, pt[:, 0:n], st[:, 0:n], AF.Exp, [st], [pt], scale=0.125)
            return pt

        def emit_PV(u, pt):
            j, h, c, grp, first, last = u
            if c == 0 and first:
                state["O"] = pO.next()
            O = state["O"]
            kbs = [g_[0] for g_ in grp]
            _, vs = chunk_bufs(kbs)

            def f(e):
                for i, kbk in enumerate(kbs):
                    e_ = e.matmul(O[:, c * 129:(c + 1) * 129], lhsT=pt[:, i * 128:(i + 1) * 128], rhs=Vt[:, kbk, h * 129:(h + 1) * 129],
                                  start=(first and i == 0), stop=(last and i == len(kbs) - 1))
                return e_
            kb.op("pe", f, vs + [pt], [O])
            if c == 1 and last:
                finalize(j, h, O)

        def finalize(j, h, O):
            if h == 0:
                state["mix"] = mixr.next()
            mx = state["mix"]
            r_d = rd.next()
            d_n = dn.next()
            TS(kb, "dve", d_n[:, 0:2], O[:, 0:258].rearrange("p (c d) -> p c d", c=2)[:, :, 128], 1e-30, None, ALU.max, None, [O], [d_n])
            kb.op("dve", lambda e: e.reciprocal(out=r_d[:, 0:2], in_=d_n[:, 0:2]), [d_n], [r_d])
            c_f = cf.next()
            TT(kb, "dve", c_f[:], r_d[:, 1:2], nlam[:], ALU.mult, [r_d, nlam], [c_f])
            t_0 = t0r.next()
            TS(kb, "dve", t_0[:], O[:, 0:128], r_d[:, 0:1], None, ALU.mult, None, [O, r_d], [t_0])
            o_ = orr.next()
            STT(kb, o_[:], O[:, 129:257], c_f[:, 0:1], t_0[:], ALU.mult, ALU.add, [O, c_f, t_0], [o_])
            s = sm.next()
            j_ = jk.next()
            ACT(kb, j_[:], o_[:], AF.Square, [o_], [j_, s[0]], accum=s[0][:, 0:1])
            rstd_act(kb, s[2][:], s[0][:], s[1][:], 1.0 / 128.0, s[0], s[1], s[2])
            o_n = on.next()
            TS(kb, "dve", o_n[:], o_[:], s[2][:, 0:1], 1.0 - LAM_INIT0, ALU.mult, ALU.mult, [o_, s[2]], [o_n])
            TT(kb, "dve", mx[:, h * 128:(h + 1) * 128], o_n[:], gtab[:], ALU.mult, [o_n, gtab], [mx])
            if h == 3:
                kb.dma([(MIX[j * 128:(j + 1) * 128, 0:512], mx[:])], reads=[mx], sembuf=mx)

        for j in range(min(2, nbo)):
            load(j)
        nxt = min(2, nbo)
        prev = None
        for ui, u in enumerate(units):
            if ui > 0 and u[0] != units[ui - 1][0] and nxt < nbo:
                load(nxt)
                nxt += 1
            pt = emit_ST(u)
            if prev is not None:
                emit_PV(*prev)
            prev = (u, pt)
        emit_PV(*prev)
        kb.end_phase()


def rope_table(pos, ng):
    inv = ROPE_THETA ** (-np.arange(0, 16, 2, dtype=np.float32) / np.float32(16))
    ang = pos.astype(np.float32)[:, None] * inv[None, :].astype(np.float32)
    cos = np.cos(ang).astype(np.float32)
    sin = np.sin(ang).astype(np.float32)
    return np.concatenate([np.tile(cos, (1, ng)), np.tile(sin, (1, ng))], axis=1).astype(np.float32)


def const_cU():
    s = np.arange(128)[:, None]
    t = np.arange(128)[None, :]
    same = (s // 64) == (t // 64)
    cU = np.zeros((128, 4, 128), np.float32)
    cU[:, 0, :] = same & (s <= t)
    cU[:, 1, :] = same & ((s % 64) <= 31)
    cU[:, 2, :] = same & (s > t)
    cU[:, 3, 0] = (np.arange(128) // 64) == 0
    cU[:, 3, 1] = (np.arange(128) // 64) == 1
    return cU


def const_l0(h):
    k = np.arange(128)[:, None]
    q = np.arange(128)[None, :]
    tri = np.where(k <= q, 0.0, NEG).astype(np.float32)
    zero = np.zeros((128, 128), np.float32)
    full = np.full((128, 128), NEG, np.float32)
    mAB = np.stack([tri if h == 0 else zero, full if h == 0 else tri], axis=1).astype(NPBF)
    sel = np.zeros((128, 2), np.float32)
    sel[:, h] = 1.0
    return {"ident": np.eye(128).astype(NPBF), "cU": const_cU(), "sel": sel, "mAB": mAB, "hb": np.zeros((128, 1), np.float32)}


def dense_tail(kb, nbo, l, mix, xres, io, XA, XB, out, consts, w_out, mix_loader=None, p_in=None):
    phase_C(kb, nbo, mix, xres, w_out, io["ln1_g"][l:l + 1, :], io["ln1_b"][l:l + 1, :], XA, consts, mix_loader=mix_loader)
    phase_D1(kb, nbo, XA, io["ffn_w1"][l], io["ffn_w2"][l], io["ln2_g"][l:l + 1, :], io["ln2_b"][l:l + 1, :], XB, consts)
    phase_D2(kb, nbo, XB, io["p_own"] if p_in is None else p_in, io["ple_w_gate"][l], io["ple_w_proj"][l], io["ple_norm_g"][l:l + 1, :], out, consts)


WEIGHT_SHAPES = {
    "ev_w_in": [1, 1024, 3584], "ev_w_out": [1, 1024, 1024], "da_lambda": [1, 4, 64], "da_subln_g": [1, 128],
    "hg_lb_logits": [2, 512], "hg_norm_g": [1, 128], "od_w_in": [1, 1024, 3072], "od_w_out": [1, 1024, 1024],
    "ln1_g": [2, 1024], "ln1_b": [2, 1024], "ffn_w1": [2, 1024, 4096], "ffn_w2": [2, 4096, 1024],
    "ln2_g": [2, 1024], "ln2_b": [2, 1024], "ple_w_proj": [2, 256, 1024], "ple_w_gate": [2, 1024, 1024],
    "ple_norm_g": [2, 1024],
}
L0_W = ["ev_w_in", "ev_w_out", "da_lambda", "da_subln_g", "hg_lb_logits", "hg_norm_g", "ln1_g", "ln1_b", "ffn_w1",
        "ffn_w2", "ln2_g", "ln2_b", "ple_w_proj", "ple_w_gate", "ple_norm_g"]
L1_W = ["od_w_in", "od_w_out", "ln1_g", "ln1_b", "ffn_w1", "ffn_w2", "ln2_g", "ln2_b", "ple_w_proj", "ple_w_gate",
        "ple_norm_g"]


def build_l0(S, debug=False, phases="12BT"):
    nc = bass.Bass("TRN2", target_bir_lowering=False)
    nb, nbo = S // 128, S // 256
    So = S // 2

    def din(name, shape, dt=F32):
        return nc.dram_tensor(name, list(shape), dt, kind="ExternalInput").ap()

    def scr(name, shape, dt):
        return nc.dram_tensor(name, list(shape), dt, kind="ExternalOutput" if debug else "Internal").ap()

    io = {n: din(n, WEIGHT_SHAPES[n]) for n in L0_W}
    io["x_all"] = din("x_all", [S, D])
    io["x_own"] = din("x_own", [So, D])
    io["p_own"] = din("p_own", [So, PLE])
    io["rope_all"] = din("rope_all", [S, 128])
    io["rope_own"] = din("rope_own", [So, 128])
    consts = {"ident": din("ident", [128, 128], BF16), "cU": din("cU", [128, 4, 128]),
              "sel": din("sel", [128, 2]), "mAB": din("mAB", [128, 2, 128], BF16), "hb": din("hb", [128, 1])}
    out = nc.dram_tensor("x1_own", [So, D], F32, kind="ExternalOutput").ap()
    KT = scr("KT", [nb, 128, 512], BF16)
    V = scr("V", [nb, 128, 516], BF16)
    SS = scr("SS", [nbo, 2, 128, 512], BF16)
    QT = scr("QT", [nbo, 128, 512], BF16)
    MIX = scr("MIX", [So, D], BF16)
    XA = scr("XA", [So, D], F32)
    XB = scr("XB", [So, D], F32)
    kb = KB(nc)
    if "1" in phases:
        phase_A1(kb, S, io["x_all"], io["rope_all"], io["ev_w_in"][0], io["hg_lb_logits"], consts, KT, V, SS)
    if "2" in phases:
        phase_A2(kb, nbo, io["x_own"], io["rope_own"], io["ev_w_in"][0], io["hg_lb_logits"], io["hg_norm_g"], consts, SS, QT, MIX)
    if "B" in phases:
        phase_B(kb, S, nbo, KT, V, QT, io["da_lambda"][0], io["da_subln_g"], consts, MIX)
    if "T" in phases:
        dense_tail(kb, nbo, 0, MIX, io["x_own"], io, XA, XB, out, consts, io["ev_w_out"][0])
    return nc, kb


def l0_inputs(x_b, p0_b, weights, h):
    S = x_b.shape[0]
    nb = S // 128
    own = np.arange(nb).reshape(nb // 2, 2)[:, h]
    pos_own = (own[:, None] * 128 + np.arange(128)[None, :]).reshape(-1)
    m = {n: weights[n] for n in L0_W}
    m["x_all"] = x_b
    m["x_own"] = np.ascontiguousarray(x_b[pos_own])
    m["p_own"] = np.ascontiguousarray(p0_b[pos_own])
    m["rope_all"] = rope_table(np.arange(S), 8)
    m["rope_own"] = rope_table(pos_own, 8)
    m.update(const_l0(h))
    return m, pos_own


NCTX = 16


def phase_L1A(kb, nbo, x_co, rope_co, w_in, consts, KT1, QT1, V1):
    nblk = NCTX + nbo
    with contextlib.ExitStack() as ph:
        Wt = ph.enter_context(kb.nc.sbuf_tensor(kb.uname("F_W"), [128, 8, 3072], BF16))
        W = [kb.view(Wt[:, k, :]) for k in range(8)]
        load_weight(kb, W, w_in, [(0, 3072)], 8)
        ident = kb.sb(ph, "F_id", [128, 128], BF16)
        kb.dma([(ident[:], consts["ident"][:, :])], writes=[ident], sembuf=ident)
        ps = PSUM(kb, ph)
        pT = Ring([ps.bank(0, BF16)])
        pp = Ring([ps.bank(1), ps.bank(2), ps.bank(3), ps.bank(4)])
        pkT = Ring([ps.bank(5, BF16), ps.bank(6, BF16)])
        xf = Ring([kb.sb(ph, "F_xf%d" % i, [128, 1024], F32) for i in range(4)])
        rp = Ring([kb.sb(ph, "F_rp%d" % i, [128, 128], F32) for i in range(4)])
        xb = Ring([kb.sb(ph, "F_xb%d" % i, [128, 1024], BF16) for i in range(2)])
        xT = Ring([kb.sb(ph, "F_xT%d" % i, [128, 1024], BF16) for i in range(2)])
        kf = Ring([kb.sb(ph, "F_kf%d" % i, [128, 512], F32) for i in range(3)])
        kbf = Ring([kb.sb(ph, "F_kb%d" % i, [128, 1024], BF16) for i in range(4)])
        rt = Ring([[kb.sb(ph, "F_rt%d_%d" % (i, k), [128, 64], F32) for k in range(4)] for i in range(3)])
        kst = Ring([kb.sb(ph, "F_ks%d" % i, [128, 8, 512], BF16) for i in range(2)])
        qst = Ring([kb.sb(ph, "F_qs%d" % i, [128, 8, 512], BF16) for i in range(2)])
        vb = Ring([kb.sb(ph, "F_vb%d" % i, [128, 16, 65], BF16) for i in range(3)])
        for v in vb.bufs:
            kb.op("pool", lambda e, v=v: e.memset(v[:], 1.0), [], [v])
        loaded = {}

        def load(i):
            x = xf.next()
            kb.dma([(x[:], x_co[i * 128:(i + 1) * 128, :])], writes=[x], sembuf=x)
            r = rp.next()
            kb.dma([(r[:], rope_co[i * 128:(i + 1) * 128, :])], writes=[r], sembuf=r)
            loaded[i] = (x, r)

        def mm_group(pbuf, t, col0):
            def f(e):
                for k in range(8):
                    i_ = e.matmul(pbuf[:], lhsT=t[:, k * 128:(k + 1) * 128], rhs=W[k][:, col0:col0 + 512], start=(k == 0), stop=(k == 7))
                return i_
            kb.op("pe", f, [t] + W, [pbuf])

        def roped(t, r, col0):
            k_b = kbf.next()
            for half in range(2):
                pk = pp.next()
                mm_group(pk, t, col0 + half * 512)
                k_f = kf.next()
                CP(kb, "act", k_f[:], pk[:], [pk], [k_f])
                rope(kb, k_f, _Sub(k_b, half), r, 8, rt.next())
            return k_b

        def to_stage(k_b, stage, slot):
            pkt = pkT.next()
            transposes(kb, pkt, k_b, ident, 8, [k_b])
            CP(kb, "act", stage[:, :, slot * 128:(slot + 1) * 128], pkt[:].rearrange("p (h t) -> p h t", h=8), [pkt], [stage])

        stg = {}

        def body(i):
            x, r = loaded.pop(i)
            b = xb.next()
            CP(kb, "dve", b[:], x[:], [x], [b])
            p = pT.next()
            transposes(kb, p, b, ident, 8, [b])
            t = xT.next()
            CP(kb, "act", t[:], p[:], [p], [t])
            yield
            if i % 4 == 0:
                stg["k"] = kst.next()
            k_stage = stg["k"]
            k_b = roped(t, r, 1024)
            v = vb.next()
            for half in range(2):
                pv = pp.next()
                mm_group(pv, t, 2048 + half * 512)
                CP(kb, "dve" if half == 0 else "act", v[:, half * 8:(half + 1) * 8, 0:64], pv[:].rearrange("p (h d) -> p h d", h=8), [pv], [v])
            kb.dma([(V1[i * 128:(i + 1) * 128, :], v[:].rearrange("p h d -> p (h d)"))], reads=[v], sembuf=v)
            q_b = None
            if i >= NCTX:
                io_ = i - NCTX
                if io_ % 4 == 0:
                    stg["q"] = qst.next()
                q_stage = stg["q"]
                q_b = roped(t, r, 0)
            yield
            to_stage(k_b, k_stage, i % 4)
            if i % 4 == 3:
                g0 = (i // 4) * 512
                kb.dma([(KT1[:, :, g0:g0 + 512].rearrange("h p t -> p h t"), k_stage[:])], reads=[k_stage], sembuf=k_stage)
            if q_b is not None:
                to_stage(q_b, q_stage, io_ % 4)
                if io_ % 4 == 3:
                    g0 = (io_ // 4) * 512
                    kb.dma([(QT1[:, :, g0:g0 + 512].rearrange("h p t -> p h t"), q_stage[:])], reads=[q_stage], sembuf=q_stage)

        run_pairs(nblk, body, load)
        kb.end_phase()


class _Sub:
    def __init__(self, parent, half):
        self.parent = parent
        self.half = half

    def __getitem__(self, idx):
        return self.parent.t[:, self.half * 512:(self.half + 1) * 512][idx]

    def __getattr__(self, name):
        return getattr(self.parent, name)

    def __setattr__(self, name, value):
        if name in ("parent", "half"):
            object.__setattr__(self, name, value)
        else:
            setattr(self.parent, name, value)


def phase_L1B(kb, nreg, KT1, QT1, V1, consts, ON):
    DILS = (1, 4, 16)
    with contextlib.ExitStack() as ph:
        KTsb = kb.sb(ph, "G_KT", [128, 8, 4096], BF16)
        QTsb = kb.sb(ph, "G_QT", [128, 8, 2048], BF16)
        ident = kb.sb(ph, "G_id", [128, 128], BF16)
        kb.dma([(ident[:], consts["ident"][:, :])], writes=[ident], sembuf=ident)
        msk = kb.sb(ph, "G_msk", [128, 3, 128], BF16)
        kb.dma([(msk[:], consts["m1"][:, :, :])], writes=[msk], sembuf=msk)
        ps = PSUM(kb, ph)
        pst = Ring([ps.bank(0), ps.bank(1)])
        pO = Ring([[ps.bank(2), ps.bank(3), ps.bank(4)], [ps.bank(5), ps.bank(6), ps.bank(7)]])
        vr = Ring([kb.sb(ph, "G_v%d" % i, [128, 1040], BF16) for i in range(8)])
        ptr = Ring([kb.sb(ph, "G_pt%d" % i, [128, 512], BF16) for i in range(4)])
        osb = Ring([kb.sb(ph, "G_o%d" % i, [128, 1040], F32) for i in range(3)])
        for R in range(nreg):
            kb.dma([(KTsb[:, h, :], KT1[h, :, 2048 * R:2048 * R + 4096]) for h in range(8)], writes=[KTsb], sembuf=KTsb)
            kb.dma([(QTsb[:, h, :], QT1[h, :, 2048 * R:2048 * (R + 1)]) for h in range(8)], writes=[QTsb], sembuf=QTsb)
            qblocks = [(gi, d, r, n) for gi, d in enumerate(DILS) for n in range(16 // d) for r in range(d)]
            vt = {}

            def loadv(qi):
                gi, d, r, n = qblocks[qi]
                tiles = []
                for which in (0, 1):
                    U0 = 2048 * R + 2048 + 128 * d * (n - 1 + which)
                    v = vr.next()
                    src = V1.rearrange("(a s) f -> a s f", s=d)[U0 // d:U0 // d + 128, r, :]
                    kb.dma([(v[:], src)], writes=[v], sembuf=v)
                    tiles.append(v)
                vt[qi] = tiles

            units = [(qi, hp) for qi in range(len(qblocks)) for hp in range(8)]
            state = {}

            def emit_ST(u):
                qi, hp = u
                gi, d, r, n = qblocks[qi]
                st = pst.next()
                q0 = 128 * d * n + r
                k0 = 2048 + 128 * d * (n - 1) + r
                mprev = 2 if (R == 0 and n == 0) else 0

                def f(e):
                    for c in range(2):
                        qap = QTsb[c * 64:(c + 1) * 64, hp, q0:q0 + 127 * d + 1:d]
                        for which in range(2):
                            kk0 = k0 + which * 128 * d
                            col = (c * 2 + which) * 128
                            e.matmul(st[:, col:col + 128], lhsT=KTsb[c * 64:(c + 1) * 64, hp, kk0:kk0 + 127 * d + 1:d], rhs=qap,
                                     start=True, stop=False)
                            i_ = e.matmul(st[:, col:col + 128], lhsT=ident[:], rhs=msk[:, (mprev if which == 0 else 1), :],
                                          start=False, stop=True)
                    return i_
                kb.op("pe", f, [KTsb, QTsb, ident, msk], [st])
                pt = ptr.next()
                ACT(kb, pt[:], st[:], AF.Exp, [st], [pt], scale=0.125)
                return pt

            def emit_PV(u, pt):
                qi, hp = u
                gi, d, r, n = qblocks[qi]
                if hp == 0:
                    state["O"] = pO.next()
                O = state["O"]
                vp, vc = vt[qi]
                banks = set()

                def f(e):
                    for c in range(2):
                        h = hp * 2 + c
                        bk, off = h // 7, (h % 7) * 65
                        banks.add(bk)
                        e.matmul(O[bk][:, off:off + 65], lhsT=pt[:, (c * 2) * 128:(c * 2 + 1) * 128], rhs=vp[:, h * 65:(h + 1) * 65],
                                 start=True, stop=False)
                        i_ = e.matmul(O[bk][:, off:off + 65], lhsT=pt[:, (c * 2 + 1) * 128:(c * 2 + 2) * 128], rhs=vc[:, h * 65:(h + 1) * 65],
                                      start=False, stop=True)
                    return i_
                hs = (hp * 2, hp * 2 + 1)
                wb = [O[bk] for bk in sorted(set(h // 7 for h in hs))]
                kb.op("pe", f, [pt, vp, vc], wb)
                if hp == 7:
                    o = osb.next()
                    CP(kb, "dve", o[:, 0:455], O[0][:, 0:455], [O[0]], [o])
                    CP(kb, "act", o[:, 455:910], O[1][:, 0:455], [O[1]], [o])
                    CP(kb, "dve", o[:, 910:1040], O[2][:, 0:130], [O[2]], [o])
                    A0 = (2048 * R + 128 * d * n) // d
                    dst = ON[gi].rearrange("(a s) f -> a s f", s=d)[A0:A0 + 128, r, :]
                    kb.dma([(dst, o[:])], reads=[o], sembuf=o)
                    del vt[qi]

            loadv(0)
            loadv(1)
            prev = None
            for ui, u in enumerate(units):
                if u[1] == 0 and u[0] + 2 < len(qblocks):
                    loadv(u[0] + 2)
                pt = emit_ST(u)
                if prev is not None:
                    emit_PV(*prev)
                prev = (u, pt)
            emit_PV(*prev)
        kb.end_phase()


class MergeLoader:
    def __init__(self, ON):
        self.ON = ON

    def alloc(self, kb, ph):
        return {
            "a": Ring([[kb.sb(ph, "M_a%d_%d" % (i, g), [128, 1040], F32) for g in range(3)] for i in range(6)]),
            "s1": Ring([kb.sb(ph, "M_s1%d" % i, [128, 1040], F32) for i in range(2)]),
            "s2": Ring([kb.sb(ph, "M_s2%d" % i, [128, 1040], F32) for i in range(2)]),
            "rd": Ring([kb.sb(ph, "M_rd%d" % i, [128, 16], F32) for i in range(2)]),
        }

    def load(self, kb, ex, j, m):
        a = ex["a"].next()
        for g in range(3):
            kb.dma([(a[g][:], self.ON[g][j * 128:(j + 1) * 128, :])], writes=[a[g]], sembuf=a[g])
        ex.setdefault("pending", {})[id(m)] = a

    def finish(self, kb, ex, m):
        a = ex["pending"].pop(id(m))
        s1 = ex["s1"].next()
        TT(kb, "dve", s1[:], a[0][:], a[1][:], ALU.add, [a[0], a[1]], [s1])
        s2 = ex["s2"].next()
        TT(kb, "dve", s2[:], s1[:], a[2][:], ALU.add, [s1, a[2]], [s2])
        rd = ex["rd"].next()
        s3 = s2[:].rearrange("p (h d) -> p h d", h=16)
        kb.op("dve", lambda e: e.reciprocal(out=rd[:], in_=s3[:, :, 64]), [s2], [rd])
        TT(kb, "dve", m[:].rearrange("p (h d) -> p h d", h=16), s3[:, :, 0:64], rd[:].unsqueeze(2).broadcast_to([128, 16, 64]),
           ALU.mult, [s2, rd], [m])


def const_l1(h):
    c = np.arange(128)[:, None]
    a = np.arange(128)[None, :]
    mprev = np.where(c >= a, 0.0, NEG).astype(np.float32)
    mcur = np.where(c <= a, 0.0, NEG).astype(np.float32)
    full = np.full((128, 128), NEG, np.float32)
    m1 = np.stack([mprev, mcur, full if h == 0 else mprev], axis=1).astype(NPBF)
    return {"ident": np.eye(128).astype(NPBF), "m1": m1}


def build_l1(S, debug=False, phases="ABT"):
    nc = bass.Bass("TRN2", target_bir_lowering=False)
    So = S // 2
    nbo = So // 128
    nreg = So // 2048
    T = 2048 + So

    def din(name, shape, dt=F32):
        return nc.dram_tensor(name, list(shape), dt, kind="ExternalInput").ap()

    def scr(name, shape, dt):
        return nc.dram_tensor(name, list(shape), dt, kind="ExternalOutput" if debug else "Internal").ap()

    io = {n: din(n, WEIGHT_SHAPES[n]) for n in L1_W}
    io["x_co"] = din("x_co", [T, D])
    io["p_own"] = din("p_own", [So, PLE])
    io["rope_co"] = din("rope_co", [T, 128])
    consts = {"ident": din("ident", [128, 128], BF16), "m1": din("m1", [128, 3, 128], BF16)}
    out = nc.dram_tensor("out_own", [So, D], F32, kind="ExternalOutput").ap()
    KT1 = scr("KT1", [8, 128, T], BF16)
    QT1 = scr("QT1", [8, 128, So], BF16)
    V1 = scr("V1", [T, 1040], BF16)
    ON = [scr("ON%d" % g, [So, 1040], F32) for g in range(3)]
    XA = scr("XA1", [So, D], F32)
    XB = scr("XB1", [So, D], F32)
    kb = KB(nc)
    if "A" in phases:
        phase_L1A(kb, nbo, io["x_co"], io["rope_co"], io["od_w_in"][0], consts, KT1, QT1, V1)
    if "B" in phases:
        phase_L1B(kb, nreg, KT1, QT1, V1, consts, ON)
    if "T" in phases:
        dense_tail(kb, nbo, 1, None, io["x_co"][2048:T, :], io, XA, XB, out, consts, io["od_w_out"][0], mix_loader=MergeLoader(ON))
    return nc, kb


def l1_inputs(x1_b, p1_b, weights, h):
    S = x1_b.shape[0]
    So = S // 2
    lo = So * h
    m = {n: weights[n] for n in L1_W}
    ctx = x1_b[lo - 2048:lo] if h == 1 else np.zeros((2048, D), np.float32)
    m["x_co"] = np.ascontiguousarray(np.concatenate([ctx, x1_b[lo:lo + So]], axis=0))
    m["p_own"] = np.ascontiguousarray(p1_b[lo:lo + So])
    m["rope_co"] = rope_table(np.arange(lo - 2048, lo + So), 8)
    m.update(const_l1(h))
    return m


ALL_W = list(WEIGHT_SHAPES.keys())


def const_fused(h):
    k = np.arange(128)[:, None]
    q = np.arange(128)[None, :]
    tri = np.where(k <= q, 0.0, NEG).astype(np.float32)
    zero = np.zeros((128, 128), np.float32)
    mAB = np.stack([tri if h == 0 else zero, zero if h == 0 else tri], axis=1).astype(NPBF)
    sel = np.zeros((128, 2), np.float32)
    sel[:, h] = 1.0
    c = {"ident": np.eye(128).astype(NPBF), "cU": const_cU(), "sel": sel, "mAB": mAB,
         "hb": np.full((128, 1), NEG if h == 0 else 0.0, np.float32)}
    c["m1"] = const_l1(h)["m1"]
    return c


def build_fused(S, debug=False):
    nc = bass.Bass("TRN2", target_bir_lowering=False)
    nb = S // 128
    So = S // 2
    nbh = So // 128
    nA = NCTX + nbh
    TA = nA * 128

    def din(name, shape, dt=F32):
        return nc.dram_tensor(name, list(shape), dt, kind="ExternalInput").ap()

    def scr(name, shape, dt):
        return nc.dram_tensor(name, list(shape), dt, kind="ExternalOutput" if debug else "Internal").ap()

    io = {n: din(n, WEIGHT_SHAPES[n]) for n in ALL_W}
    io["x_all"] = din("x_all", [S, D])
    io["x_A"] = din("x_A", [TA, D])
    io["p0_A"] = din("p0_A", [TA, PLE])
    io["p1_own"] = din("p1_own", [So, PLE])
    io["rope_all"] = din("rope_all", [S, 128])
    io["rope_A"] = din("rope_A", [TA, 128])
    consts = {"ident": din("ident", [128, 128], BF16), "cU": din("cU", [128, 4, 128]), "sel": din("sel", [128, 2]),
              "mAB": din("mAB", [128, 2, 128], BF16), "hb": din("hb", [128, 1]), "m1": din("m1", [128, 3, 128], BF16)}
    out = nc.dram_tensor("out_own", [So, D], F32, kind="ExternalOutput").ap()
    KT = scr("KT", [nb, 128, 512], BF16)
    V = scr("V", [nb, 128, 516], BF16)
    SS = scr("SS", [nb, 2, 128, 512], BF16)
    QT = scr("QT", [nA, 128, 512], BF16)
    MIX = scr("MIX", [TA, D], BF16)
    XA = scr("XA", [TA, D], F32)
    XB = scr("XB", [TA, D], F32)
    X1 = scr("X1", [TA, D], F32)
    KT1 = scr("KT1", [8, 128, TA], BF16)
    QT1 = scr("QT1", [8, 128, So], BF16)
    V1 = scr("V1", [TA, 1040], BF16)
    ON = [scr("ON%d" % g, [So, 1040], F32) for g in range(3)]
    XA1 = scr("XA1", [So, D], F32)
    XB1 = scr("XB1", [So, D], F32)
    kb = KB(nc)
    phase_A1(kb, S, io["x_all"], io["rope_all"], io["ev_w_in"][0], io["hg_lb_logits"], consts, KT, V, SS, blended=False)
    phase_A2(kb, nA, io["x_A"], io["rope_A"], io["ev_w_in"][0], io["hg_lb_logits"], io["hg_norm_g"], consts, SS, QT, MIX,
             ss_pick=lambda i: (max(i - NCTX, 0), i + nbh - NCTX))
    phase_B(kb, S, nA, KT, V, QT, io["da_lambda"][0], io["da_subln_g"], consts, MIX, keyspec=make_keyspec_ctx(nbh))
    dense_tail(kb, nA, 0, MIX, io["x_A"], io, XA, XB, X1, consts, io["ev_w_out"][0], p_in=io["p0_A"])
    phase_L1A(kb, nbh, X1, io["rope_A"], io["od_w_in"][0], consts, KT1, QT1, V1)
    phase_L1B(kb, So // 2048, KT1, QT1, V1, consts, ON)
    dense_tail(kb, nbh, 1, None, X1[2048:TA, :], io, XA1, XB1, out, consts, io["od_w_out"][0], mix_loader=MergeLoader(ON),
               p_in=io["p1_own"])
    return nc, kb


def fused_inputs(x_b, p_b, weights, h):
    S = x_b.shape[0]
    So = S // 2
    lo = So * h
    m = {n: weights[n] for n in ALL_W}
    m["x_all"] = x_b
    if h == 0:
        m["x_A"] = np.ascontiguousarray(np.concatenate([np.zeros((2048, D), np.float32), x_b[0:So]], axis=0))
        m["p0_A"] = np.ascontiguousarray(np.concatenate([np.zeros((2048, PLE), np.float32), p_b[0, 0:So]], axis=0))
    else:
        m["x_A"] = np.ascontiguousarray(x_b[lo - 2048:lo + So])
        m["p0_A"] = np.ascontiguousarray(p_b[0, lo - 2048:lo + So])
    m["p1_own"] = np.ascontiguousarray(p_b[1, lo:lo + So])
    m["rope_all"] = rope_table(np.arange(S), 8)
    m["rope_A"] = rope_table(np.arange(lo - 2048, lo + So), 8)
    m.update(const_fused(h))
    return m


_PROGS = {}


def _prog(kind, S):
    key = (kind, S)
    if key not in _PROGS:
        _PROGS[key] = (build_l0(S) if kind == 0 else build_l1(S) if kind == 1 else build_fused(S))[0]
    return _PROGS[key]


def kernel(**inputs):
    inputs = {k: np.ascontiguousarray(np.asarray(v, dtype=np.float32)) for k, v in inputs.items()}
    x, p = inputs["x"], inputs["p"]
    B, S, _ = x.shape
    So = S // 2
    n = 2 * B
    weights = {k: v for k, v in inputs.items() if k not in ("x", "p")}
    maps = [fused_inputs(x[c // 2], p[:, c // 2], weights, c % 2) for c in range(n)]
    res = run_bass_kernel_spmd(_prog(2, S), maps, core_ids=list(range(n)))
    out = np.empty((B, S, D), np.float32)
    for c in range(n):
        b, h = c // 2, c % 2
        out[b, h * So:(h + 1) * So] = res.results[c]["out_own"]
    return out
```

```python
import contextlib
import math
import numpy as np
import ml_dtypes
import concourse.bass as bass
import concourse.mybir as mybir
from concourse.bass_utils import run_bass_kernel_spmd

F32 = mybir.dt.float32
BF16 = mybir.dt.bfloat16
AF = mybir.ActivationFunctionType
ALU = mybir.AluOpType
AX = mybir.AxisListType
NPBF = ml_dtypes.bfloat16

D = 1024
FFN = 4096
PLE = 256
ALPHA = 4.0 ** 0.25
LN_EPS = 1e-5
LAM_INIT0 = 0.8 - 0.6 * math.exp(-0.3 * 0)
NEG = -30000.0
A2_STOP = 99
ROPE_THETA = 500000.0


class Buf:
    __slots__ = ("t", "w", "r", "dsem", "psum")

    def __init__(self, t):
        self.t = t
        self.w = None
        self.r = []
        self.dsem = None
        self.psum = False

    def __getitem__(self, idx):
        return self.t[idx]


class Ring:
    def __init__(self, bufs):
        self.bufs = bufs
        self.i = 0

    def next(self):
        b = self.bufs[self.i % len(self.bufs)]
        self.i += 1
        return b


class Eng:
    def __init__(self, name, eng, semidx):
        self.name = name
        self.eng = eng
        self.semidx = semidx
        self.waited = {}


class KB:
    def __init__(self, nc):
        self.nc = nc
        self.es = contextlib.ExitStack()
        self.sems = []
        self.semcnt = []
        self.semdma = []
        self.free_dsems = []
        self.E = {}
        for name, eng in (("pe", nc.tensor), ("act", nc.scalar), ("dve", nc.vector),
                          ("pool", nc.gpsimd), ("sp", nc.sync)):
            si = self._newsem("e_" + name, False)
            self.E[name] = Eng(name, eng, si)
        self.n_inst = 0
        self.phase_bufs = []

    def _newsem(self, name, isdma):
        s = self.es.enter_context(self.nc.semaphore(name))
        self.sems.append(s)
        self.semcnt.append(0)
        self.semdma.append(isdma)
        return len(self.sems) - 1

    def get_dsem(self):
        if self.free_dsems:
            return self.free_dsems.pop()
        return self._newsem("d%d" % len(self.sems), True)

    def sb(self, ph, name, shape, dtype):
        self.uid = getattr(self, "uid", 0) + 1
        name = "%s_u%d" % (name, self.uid)
        b = Buf(ph.enter_context(self.nc.sbuf_tensor(name, list(shape), dtype)))
        self.phase_bufs.append(b)
        return b

    def uname(self, name):
        self.uid = getattr(self, "uid", 0) + 1
        return "%s_u%d" % (name, self.uid)

    def view(self, ap):
        b = Buf(ap)
        self.phase_bufs.append(b)
        return b

    def _wait(self, E, toks):
        for (si, val) in toks:
            if self.semdma[si]:
                val = self.semcnt[si]
            if si == E.semidx and E.name == "pe":
                continue
            if E.waited.get(si, 0) >= val:
                continue
            E.eng.wait_ge(self.sems[si], val)
            E.waited[si] = val
            self.n_inst += 1

    def _deps(self, reads, writes):
        need = []
        for b in reads:
            if b.w is not None:
                need.append(b.w)
            if b.psum:
                need.extend(b.r)
        for b in writes:
            if b.w is not None:
                need.append(b.w)
            need.extend(b.r)
        return need

    def _commit(self, tok, reads, writes):
        for b in reads:
            if b.psum:
                b.w = tok
                b.r = []
            else:
                b.r.append(tok)
        for b in writes:
            b.w = tok
            b.r = []

    def op(self, en, fn, reads=(), writes=()):
        E = self.E[en]
        self._wait(E, self._deps(reads, writes))
        inst = fn(E.eng)
        si = E.semidx
        self.semcnt[si] += 1
        inst.then_inc(self.sems[si], 1)
        tok = (si, self.semcnt[si])
        self._commit(tok, reads, writes)
        self.n_inst += 1
        return tok

    def dma(self, pairs, reads=(), writes=(), sembuf=None, q="sp"):
        E = self.E[q]
        self._wait(E, self._deps(reads, writes))
        if sembuf.dsem is None:
            sembuf.dsem = self.get_dsem()
        si = sembuf.dsem
        for (o, i) in pairs:
            E.eng.dma_start(out=o, in_=i).then_inc(self.sems[si], 16)
            self.semcnt[si] += 16
            self.n_inst += 1
        tok = (si, self.semcnt[si])
        self._commit(tok, reads, writes)
        return tok

    def barrier(self):
        allt = [(si, self.semcnt[si]) for si in range(len(self.sems)) if self.semcnt[si] > 0]
        for E in self.E.values():
            self._wait(E, allt)

    def end_phase(self):
        self.barrier()
        for b in self.phase_bufs:
            if b.dsem is not None:
                self.free_dsems.append(b.dsem)
                b.dsem = None
        self.phase_bufs = []


def TT(kb, en, out, a, b, op, reads, writes):
    return kb.op(en, lambda e: e.tensor_tensor(out=out, in0=a, in1=b, op=op), reads, writes)


def TS(kb, en, out, a, s1, s2, op0, op1, reads, writes):
    if op1 is None:
        return kb.op(en, lambda e: e.tensor_scalar(out=out, in0=a, scalar1=s1, scalar2=None, op0=op0), reads, writes)
    return kb.op(en, lambda e: e.tensor_scalar(out=out, in0=a, scalar1=s1, scalar2=s2, op0=op0, op1=op1), reads, writes)


def STT(kb, out, a, s, b, op0, op1, reads, writes):
    return kb.op("dve", lambda e: e.scalar_tensor_tensor(out=out, in0=a, scalar=s, in1=b, op0=op0, op1=op1), reads, writes)


def ACT(kb, out, in_, func, reads, writes, scale=1.0, bias=0.0, accum=None):
    if accum is not None:
        return kb.op("act", lambda e: e.activation(out=out, in_=in_, func=func, scale=scale, bias=bias, accum_out=accum), reads, writes)
    return kb.op("act", lambda e: e.activation(out=out, in_=in_, func=func, scale=scale, bias=bias), reads, writes)


def CP(kb, en, out, in_, reads, writes):
    if en == "act":
        return kb.op("act", lambda e: e.copy(out=out, in_=in_), reads, writes)
    return kb.op(en, lambda e: e.tensor_copy(out=out, in_=in_), reads, writes)


def rstd_act(kb, out, in_, tmp, scale, reads_buf, tmp_buf, out_buf):
    ACT(kb, tmp, in_, AF.Ln, [reads_buf], [tmp_buf], scale=scale, bias=LN_EPS)
    ACT(kb, out, tmp, AF.Exp, [tmp_buf], [out_buf], scale=-0.5)


def sigmoid_exp(kb, out, in_, in_buf, t1, t2, out_buf, engs=None):
    ACT(kb, t1[:], in_, AF.Exp, [in_buf], [t1], scale=-1.0)
    ACT(kb, t2[:], t1[:], AF.Ln, [t1], [t2], bias=1.0)
    ACT(kb, out, t2[:], AF.Exp, [t2], [out_buf], scale=-1.0)


def run_pairs(n, body, load, width=2, prefetch=2):
    state = {"n": 0}

    def ensure(upto):
        while state["n"] <= min(upto, n - 1):
            load(state["n"])
            state["n"] += 1
    j = 0
    while j < n:
        grp = list(range(j, min(n, j + width)))
        ensure(grp[-1] + prefetch)
        alive = [body(b) for b in grp]
        while alive:
            for g in list(alive):
                try:
                    next(g)
                except StopIteration:
                    alive.remove(g)
        j += width


class PSUM:
    def __init__(self, kb, ph):
        kb.uid = getattr(kb, "uid", 0) + 1
        self.t = ph.enter_context(kb.nc.psum_tensor("PSALL_u%d" % kb.uid, [128, 4096], F32))
        self.kb = kb

    def bank(self, k, dtype=F32):
        ap = self.t[:, k * 512:(k + 1) * 512]
        if dtype == BF16:
            ap = ap.bitcast(BF16)
        b = self.kb.view(ap)
        b.psum = True
        return b


def load_weight(kb, W, src, col_slices, krows):
    ncols = sum(hi - lo for lo, hi in col_slices)
    with contextlib.ExitStack() as st:
        PW = 2048
        stage = Ring([kb.sb(st, "wst%d" % i, [128, PW], F32) for i in range(3)])
        engs = ["dve", "dve", "act"]
        n = 0
        for kc in range(krows):
            dst0 = 0
            for (lo, hi) in col_slices:
                c = lo
                while c < hi:
                    w = min(PW, hi - c)
                    sbuf = stage.next()
                    kb.dma([(sbuf[:, 0:w], src[kc * 128:(kc + 1) * 128, c:c + w])], writes=[sbuf], sembuf=sbuf)
                    CP(kb, engs[n % 2], W[kc][:, dst0:dst0 + w], sbuf[:, 0:w], [sbuf], [W[kc]])
                    n += 1
                    dst0 += w
                    c += w
        kb.barrier()


def transposes(kb, psT, src, ident, nblk, reads):
    def f(e):
        for c in range(nblk):
            i = e.transpose(psT[:, c * 128:(c + 1) * 128], src[:, c * 128:(c + 1) * 128], ident[:])
        return i
    return kb.op("pe", f, reads + [ident], [psT])


def rope(kb, src, dst, rp, ng, tmps):
    s3 = src[:].rearrange("p (g d) -> p g d", g=ng)
    d3 = dst[:].rearrange("p (g d) -> p g d", g=ng)
    cos = rp[:, 0:ng * 8].rearrange("p (g d) -> p g d", g=ng)
    sin = rp[:, ng * 8:2 * ng * 8].rearrange("p (g d) -> p g d", g=ng)
    t1, t2, t3, t4 = tmps
    v = lambda t: t[:, 0:ng * 8].rearrange("p (g d) -> p g d", g=ng)
    TT(kb, "dve", v(t1), s3[:, :, 0:8], cos, ALU.mult, [src, rp], [t1])
    TT(kb, "dve", v(t2), s3[:, :, 8:16], sin, ALU.mult, [src, rp], [t2])
    TT(kb, "dve", v(t3), s3[:, :, 0:8], sin, ALU.mult, [src, rp], [t3])
    TT(kb, "dve", v(t4), s3[:, :, 8:16], cos, ALU.mult, [src, rp], [t4])
    TT(kb, "dve", d3[:, :, 0:8], v(t1), v(t2), ALU.subtract, [t1, t2], [dst])
    TT(kb, "dve", d3[:, :, 8:16], v(t3), v(t4), ALU.add, [t3, t4], [dst])
    CP(kb, "dve", d3[:, :, 16:64], s3[:, :, 16:64], [src], [dst])


def layer_norm(kb, y, g, b, st, mv, sd, rs, tmp):
    ACT(kb, tmp[:], y[:], AF.Copy, [y], [tmp, st], accum=st[:, 0:1])
    ACT(kb, tmp[:], y[:], AF.Square, [y], [tmp, st], accum=st[:, 1:2])
    TS(kb, "dve", mv[:, 0:1], st[:, 0:1], 1.0 / D, None, ALU.mult, None, [st], [mv])
    TT(kb, "dve", mv[:, 1:2], mv[:, 0:1], mv[:, 0:1], ALU.mult, [mv], [mv])
    STT(kb, mv[:, 2:3], st[:, 1:2], 1.0 / D, mv[:, 1:2], ALU.mult, ALU.subtract, [st, mv], [mv])
    rstd_act(kb, rs[:, 0:1], mv[:, 2:3], sd[:, 0:1], 1.0, mv, sd, rs)
    STT(kb, rs[:, 1:2], mv[:, 0:1], -1.0, rs[:, 0:1], ALU.mult, ALU.mult, [mv, rs], [rs])
    kb.op("act", lambda e: e.activation(out=tmp[:], in_=y[:], func=AF.Identity, scale=rs[:, 0:1], bias=rs[:, 1:2]),
          [y, rs], [tmp])
    TT(kb, "dve", y[:], tmp[:], g[:], ALU.mult, [tmp, g], [y])
    TT(kb, "dve", tmp[:], y[:], b[:], ALU.add, [y, b], [tmp])
    return tmp


def phase_C(kb, nbo, mix_src, xres, w_out, ln_g, ln_b, xa_out, consts, mix_loader=None):
    with contextlib.ExitStack() as ph:
        Wt = ph.enter_context(kb.nc.sbuf_tensor(kb.uname("C_W"), [128, 8, 1024], BF16))
        W = [kb.view(Wt[:, k, :]) for k in range(8)]
        load_weight(kb, W, w_out, [(0, 1024)], 8)
        ident = kb.sb(ph, "C_id", [128, 128], BF16)
        kb.dma([(ident[:], consts["ident"][:, :])], writes=[ident], sembuf=ident)
        gt = kb.sb(ph, "C_g", [128, 1024], F32)
        bt = kb.sb(ph, "C_b", [128, 1024], F32)
        kb.dma([(gt[:], ln_g.partition_broadcast(128))], writes=[gt], sembuf=gt)
        kb.dma([(bt[:], ln_b.partition_broadcast(128))], writes=[bt], sembuf=bt)
        ps = PSUM(kb, ph)
        pT = Ring([ps.bank(0, BF16), ps.bank(1, BF16)])
        po = Ring([[ps.bank(2), ps.bank(3)], [ps.bank(4), ps.bank(5)]])
        mixr = Ring([kb.sb(ph, "C_mix%d" % i, [128, 1024], BF16) for i in range(6)])
        xr = Ring([kb.sb(ph, "C_x%d" % i, [128, 1024], F32) for i in range(6)])
        mT = Ring([kb.sb(ph, "C_mT%d" % i, [128, 1024], BF16) for i in range(4)])
        yr = Ring([kb.sb(ph, "C_y%d" % i, [128, 1024], F32) for i in range(4)])
        tmpr = Ring([kb.sb(ph, "C_t%d" % i, [128, 1024], F32) for i in range(4)])
        sm = Ring([[kb.sb(ph, "C_s%d_%d" % (i, k), [128, 12], F32) for k in range(4)] for i in range(4)])
        extra = mix_loader.alloc(kb, ph) if mix_loader is not None else None
        loaded = {}

        def load(j):
            x = xr.next()
            kb.dma([(x[:], xres[j * 128:(j + 1) * 128, :])], writes=[x], sembuf=x)
            m = mixr.next()
            if mix_loader is None:
                kb.dma([(m[:], mix_src[j * 128:(j + 1) * 128, :])], writes=[m], sembuf=m)
            else:
                mix_loader.load(kb, extra, j, m)
            loaded[j] = (x, m)

        def body(j):
            x, m = loaded.pop(j)
            if mix_loader is not None:
                mix_loader.finish(kb, extra, m)
            p = pT.next()
            transposes(kb, p, m, ident, 8, [m])
            t = mT.next()
            CP(kb, "act", t[:], p[:], [p], [t])
            yield
            pa, pb = po.next()
            for half, pp in ((0, pa), (1, pb)):
                def f(e, half=half, pp=pp):
                    for k in range(8):
                        i = e.matmul(pp[:], lhsT=t[:, k * 128:(k + 1) * 128], rhs=W[k][:, half * 512:(half + 1) * 512],
                                     start=(k == 0), stop=(k == 7))
                    return i
                kb.op("pe", f, [t] + W, [pp])
            y = yr.next()
            STT(kb, y[:, 0:512], x[:, 0:512], ALPHA, pa[:], ALU.mult, ALU.add, [x, pa], [y])
            STT(kb, y[:, 512:1024], x[:, 512:1024], ALPHA, pb[:], ALU.mult, ALU.add, [x, pb], [y])
            yield
            s = sm.next()
            o = layer_norm(kb, y, gt, bt, s[0], s[1], s[2], s[3], tmpr.next())
            kb.dma([(xa_out[j * 128:(j + 1) * 128, :], o[:])], reads=[o], sembuf=o)

        run_pairs(nbo, body, load, width=4)
        kb.end_phase()


def phase_D1(kb, nbo, xa, w1, w2, ln_g, ln_b, xb_out, consts):
    assert nbo % 4 == 0
    with contextlib.ExitStack() as ph:
        W1t = ph.enter_context(kb.nc.sbuf_tensor(kb.uname("D_W1"), [128, 8, FFN], BF16))
        W2t = ph.enter_context(kb.nc.sbuf_tensor(kb.uname("D_W2"), [128, 32, D], BF16))
        W1 = [kb.view(W1t[:, k, :]) for k in range(8)]
        W2 = [kb.view(W2t[:, k, :]) for k in range(32)]
        load_weight(kb, W1, w1, [(0, FFN)], 8)
        load_weight(kb, W2, w2, [(0, D)], 32)
        ident = kb.sb(ph, "D_id", [128, 128], BF16)
        kb.dma([(ident[:], consts["ident"][:, :])], writes=[ident], sembuf=ident)
        gt = kb.sb(ph, "D_g", [128, 1024], F32)
        bt = kb.sb(ph, "D_b", [128, 1024], F32)
        kb.dma([(gt[:], ln_g.partition_broadcast(128))], writes=[gt], sembuf=gt)
        kb.dma([(bt[:], ln_b.partition_broadcast(128))], writes=[bt], sembuf=bt)
        ps = PSUM(kb, ph)
        pT = Ring([ps.bank(0, BF16)])
        ph_ = Ring([ps.bank(1), ps.bank(2), ps.bank(3)])
        po = Ring([[ps.bank(4), ps.bank(5)], [ps.bank(6), ps.bank(7)]])
        xr = Ring([kb.sb(ph, "D_x%d" % i, [128, 1024], F32) for i in range(2)])
        xbr = Ring([kb.sb(ph, "D_xb%d" % i, [128, 1024], BF16) for i in range(2)])
        xT = kb.sb(ph, "D_xT", [128, 8, 512], BF16)
        hT = [kb.sb(ph, "D_hT%d" % c, [128, 512], BF16) for c in range(32)]
        rl = Ring([kb.sb(ph, "D_rl%d" % i, [128, 512], F32) for i in range(2)])
        yr = Ring([kb.sb(ph, "D_y%d" % i, [128, 1024], F32) for i in range(1)])
        tmpr = Ring([kb.sb(ph, "D_t%d" % i, [128, 1024], F32) for i in range(2)])
        sm = Ring([[kb.sb(ph, "D_s%d_%d" % (i, k), [128, 12], F32) for k in range(4)] for i in range(2)])
        ng = nbo // 4
        def stage_load(g):
                for tb in range(4):
                    x = xr.next()
                    j = g * 4 + tb
                    kb.dma([(x[:], xa[j * 128:(j + 1) * 128, :])], writes=[x], sembuf=x)
                    xb = xbr.next()
                    CP(kb, "dve" if tb % 2 == 0 else "dve", xb[:], x[:], [x], [xb])
                    p = pT.next()
                    transposes(kb, p, xb, ident, 8, [xb])
                    CP(kb, "act", xT[:, :, tb * 128:(tb + 1) * 128], p[:].rearrange("p (k t) -> p k t", k=8), [p], [xT])

        def stage_w1(g):
                for c in range(32):
                    pp = ph_.next()

                    def f(e, c=c, pp=pp):
                        for k in range(8):
                            i = e.matmul(pp[:], lhsT=W1[k][:, c * 128:(c + 1) * 128], rhs=xT[:, k, :], start=(k == 0), stop=(k == 7))
                        return i
                    kb.op("pe", f, [xT] + W1, [pp])
                    r = rl.next()
                    ACT(kb, r[:], pp[:], AF.Relu, [pp], [r])
                    TT(kb, "dve" if c % 2 == 0 else "dve", hT[c][:], r[:], r[:], ALU.mult, [r], [hT[c]])

        def stage_w2(g):
                for tb in range(4):
                    pa, pb = po.next()
                    for half, pp in ((0, pa), (1, pb)):
                        def f(e, half=half, pp=pp, tb=tb):
                            for c in range(32):
                                i = e.matmul(pp[:], lhsT=hT[c][:, tb * 128:(tb + 1) * 128], rhs=W2[c][:, half * 512:(half + 1) * 512],
                                             start=(c == 0), stop=(c == 31))
                            return i
                        kb.op("pe", f, hT + W2, [pp])
                    x = xr.next()
                    j = g * 4 + tb
                    kb.dma([(x[:], xa[j * 128:(j + 1) * 128, :])], writes=[x], sembuf=x)
                    y = yr.next()
                    STT(kb, y[:, 0:512], x[:, 0:512], ALPHA, pa[:], ALU.mult, ALU.add, [x, pa], [y])
                    STT(kb, y[:, 512:1024], x[:, 512:1024], ALPHA, pb[:], ALU.mult, ALU.add, [x, pb], [y])
                    s = sm.next()
                    o = layer_norm(kb, y, gt, bt, s[0], s[1], s[2], s[3], tmpr.next())
                    kb.dma([(xb_out[j * 128:(j + 1) * 128, :], o[:])], reads=[o], sembuf=o)

        stage_load(0)
        for g in range(ng):
            stage_w1(g)
            if g + 1 < ng:
                stage_load(g + 1)
            stage_w2(g)
        kb.end_phase()


def phase_D2(kb, nbo, xb_in, p_in, w_gate, w_proj, norm_g, out, consts):
    with contextlib.ExitStack() as ph:
        Wgt = ph.enter_context(kb.nc.sbuf_tensor(kb.uname("E_Wg"), [128, 8, D], BF16))
        Wpt = ph.enter_context(kb.nc.sbuf_tensor(kb.uname("E_Wp"), [128, 2, D], BF16))
        Wg = [kb.view(Wgt[:, k, :]) for k in range(8)]
        Wp = [kb.view(Wpt[:, k, :]) for k in range(2)]
        load_weight(kb, Wg, w_gate, [(0, D)], 8)
        load_weight(kb, Wp, w_proj, [(0, D)], 2)
        ident = kb.sb(ph, "E_id", [128, 128], BF16)
        kb.dma([(ident[:], consts["ident"][:, :])], writes=[ident], sembuf=ident)
        gt = kb.sb(ph, "E_g", [128, 1024], F32)
        kb.dma([(gt[:], norm_g.partition_broadcast(128))], writes=[gt], sembuf=gt)
        ps = PSUM(kb, ph)
        pT = Ring([ps.bank(0, BF16), ps.bank(1, BF16)])
        pg = Ring([[ps.bank(2), ps.bank(3)]])
        pe_ = Ring([[ps.bank(4), ps.bank(5)], [ps.bank(6), ps.bank(7)]])
        er = Ring([kb.sb(ph, "E_er%d" % i, [128, 1024], F32) for i in range(4)])
        xr = Ring([kb.sb(ph, "E_x%d" % i, [128, 1024], F32) for i in range(6)])
        pr = Ring([kb.sb(ph, "E_p%d" % i, [128, 256], F32) for i in range(6)])
        xbr = Ring([kb.sb(ph, "E_xb%d" % i, [128, 1280], BF16) for i in range(2)])
        xT = Ring([kb.sb(ph, "E_xT%d" % i, [128, 1280], BF16) for i in range(4)])
        gsb = Ring([kb.sb(ph, "E_gs%d" % i, [128, 1024], F32) for i in range(4)])
        gt1 = Ring([kb.sb(ph, "E_g1%d" % i, [128, 1024], F32) for i in range(4)])
        gt2 = Ring([kb.sb(ph, "E_g2%d" % i, [128, 1024], F32) for i in range(4)])
        esb = Ring([kb.sb(ph, "E_es%d" % i, [128, 1024], F32) for i in range(2)])
        e2 = Ring([kb.sb(ph, "E_e2%d" % i, [128, 1024], F32) for i in range(2)])
        junk = Ring([kb.sb(ph, "E_jk%d" % i, [128, 512], F32) for i in range(2)])
        outr = Ring([kb.sb(ph, "E_o%d" % i, [128, 1024], F32) for i in range(2)])
        sm = Ring([[kb.sb(ph, "E_s%d_%d" % (i, k), [128, 4], F32) for k in range(4)] for i in range(4)])
        loaded = {}

        def load(j):
            x = xr.next()
            kb.dma([(x[:], xb_in[j * 128:(j + 1) * 128, :])], writes=[x], sembuf=x)
            p = pr.next()
            kb.dma([(p[:], p_in[j * 128:(j + 1) * 128, :])], writes=[p], sembuf=p)
            loaded[j] = (x, p)

        def body(j):
            x, p = loaded.pop(j)
            xb = xbr.next()
            CP(kb, "dve", xb[:, 0:1024], x[:], [x], [xb])
            CP(kb, "dve", xb[:, 1024:1280], p[:], [p], [xb])
            pt1 = pT.next()
            transposes(kb, pt1, xb, ident, 8, [xb])
            t = xT.next()
            CP(kb, "act", t[:, 0:1024], pt1[:], [pt1], [t])
            pt2 = pT.next()

            def f2(e, xb=xb, pt2=pt2):
                for c in range(2):
                    i = e.transpose(pt2[:, c * 128:(c + 1) * 128], xb[:, 1024 + c * 128:1024 + (c + 1) * 128], ident[:])
                return i
            kb.op("pe", f2, [xb, ident], [pt2])
            CP(kb, "dve", t[:, 1024:1280], pt2[:, 0:256], [pt2], [t])
            yield
            ga, gb = pg.next()
            ea, eb = pe_.next()
            for half, pp in ((0, ga), (1, gb)):
                def f(e, half=half, pp=pp, t=t):
                    for k in range(8):
                        i = e.matmul(pp[:], lhsT=t[:, k * 128:(k + 1) * 128], rhs=Wg[k][:, half * 512:(half + 1) * 512],
                                     start=(k == 0), stop=(k == 7))
                    return i
                kb.op("pe", f, [t] + Wg, [pp])
            for half, pp in ((0, ea), (1, eb)):
                def f(e, half=half, pp=pp, t=t):
                    for k in range(2):
                        i = e.matmul(pp[:], lhsT=t[:, 1024 + k * 128:1024 + (k + 1) * 128], rhs=Wp[k][:, half * 512:(half + 1) * 512],
                                     start=(k == 0), stop=(k == 1))
                    return i
                kb.op("pe", f, [t] + Wp, [pp])
            s = sm.next()
            gs = gsb.next()
            g1_, g2_ = gt1.next(), gt2.next()
            ACT(kb, g1_[:, 0:512], ga[:], AF.Exp, [ga], [g1_], scale=-1.0)
            ACT(kb, g1_[:, 512:1024], gb[:], AF.Exp, [gb], [g1_], scale=-1.0)
            jk = junk.next()
            ACT(kb, jk[:], ea[:], AF.Square, [ea], [jk, s[0]], accum=s[0][:, 0:1])
            jk = junk.next()
            ACT(kb, jk[:], eb[:], AF.Square, [eb], [jk, s[0]], accum=s[0][:, 1:2])
            e_r = er.next()
            CP(kb, "dve", e_r[:, 0:512], ea[:], [ea], [e_r])
            CP(kb, "dve", e_r[:, 512:1024], eb[:], [eb], [e_r])
            yield
            ACT(kb, g2_[:], g1_[:], AF.Ln, [g1_], [g2_], bias=1.0)
            ACT(kb, gs[:], g2_[:], AF.Exp, [g2_], [gs], scale=-1.0)
            TT(kb, "dve", s[1][:, 0:1], s[0][:, 0:1], s[0][:, 1:2], ALU.add, [s[0]], [s[1]])
            rstd_act(kb, s[3][:, 0:1], s[1][:, 0:1], s[2][:, 0:1], 1.0 / D, s[1], s[2], s[3])
            es = esb.next()
            kb.op("act", lambda e, es=es, e_r=e_r, s=s: e.activation(out=es[:], in_=e_r[:], func=AF.Copy, scale=s[3][:, 0:1]), [e_r, s[3]], [es])
            ee = e2.next()
            TT(kb, "dve", ee[:], es[:], gt[:], ALU.mult, [es, gt], [ee])
            TT(kb, "dve", es[:], ee[:], gs[:], ALU.mult, [ee, gs], [es])
            o = outr.next()
            TT(kb, "dve", o[:], es[:], x[:], ALU.add, [es, x], [o])
            kb.dma([(out[j * 128:(j + 1) * 128, :], o[:])], reads=[o], sembuf=o)

        run_pairs(nbo, body, load, width=4)
        kb.end_phase()


def hgrn_gate_tables(kb, ph, lb_logits):
    lt = kb.sb(ph, "lbl", [128, 2, 512], F32)
    kb.dma([(lt[:], lb_logits.partition_broadcast(128))], writes=[lt], sembuf=lt)
    dl = kb.sb(ph, "lbd", [128, 512], F32)
    lb = kb.sb(ph, "lb", [128, 512], F32)
    oml = kb.sb(ph, "oml", [128, 512], F32)
    TT(kb, "dve", dl[:], lt[:, 0, :], lt[:, 1, :], ALU.subtract, [lt], [dl])
    lt1 = kb.sb(ph, "lbt1", [128, 512], F32)
    lt2 = kb.sb(ph, "lbt2", [128, 512], F32)
    sigmoid_exp(kb, lb[:], dl[:], dl, lt1, lt2, lb)
    TS(kb, "dve", oml[:], lb[:], -1.0, 1.0, ALU.mult, ALU.add, [lb], [oml])
    return lb, oml


def phase_A1(kb, S, x_all, rope_all, w_in, lb_logits, consts, KT, V, SS, blended=True):
    nb = S // 128
    with contextlib.ExitStack() as ph:
        Wt = ph.enter_context(kb.nc.sbuf_tensor(kb.uname("A1_W"), [128, 8, 2048], BF16))
        W = [kb.view(Wt[:, k, :]) for k in range(8)]
        load_weight(kb, W, w_in, [(512, 1024), (1024, 1536), (2048, 2560), (2560, 3072)], 8)
        ident = kb.sb(ph, "A1_id", [128, 128], BF16)
        kb.dma([(ident[:], consts["ident"][:, :])], writes=[ident], sembuf=ident)
        cU = kb.sb(ph, "A1_cU", [128, 4, 128], F32)
        kb.dma([(cU[:], consts["cU"][:, :, :])], writes=[cU], sembuf=cU)
        sel = kb.sb(ph, "A1_sel", [128, 2], F32)
        kb.dma([(sel[:], consts["sel"][:, :])], writes=[sel], sembuf=sel)
        lb, oml = hgrn_gate_tables(kb, ph, lb_logits)
        ps = PSUM(kb, ph)
        pT = Ring([ps.bank(0, BF16)])
        pp = Ring([ps.bank(1), ps.bank(2), ps.bank(3)])
        pkT = Ring([ps.bank(4, BF16)])
        pbd = Ring([ps.bank(5)])
        pdec = Ring([ps.bank(6)])
        pS = Ring([ps.bank(7)])
        xf = Ring([kb.sb(ph, "A1_xf%d" % i, [128, 1024], F32) for i in range(5)])
        rp = Ring([kb.sb(ph, "A1_rp%d" % i, [128, 128], F32) for i in range(5)])
        xb = Ring([kb.sb(ph, "A1_xb%d" % i, [128, 1024], BF16) for i in range(2)])
        xT = Ring([kb.sb(ph, "A1_xT%d" % i, [128, 1024], BF16) for i in range(3)])
        kf = Ring([kb.sb(ph, "A1_kf%d" % i, [128, 512], F32) for i in range(3)])
        kbf = Ring([kb.sb(ph, "A1_kb%d" % i, [128, 512], BF16) for i in range(2)])
        rt = Ring([[kb.sb(ph, "A1_rt%d_%d" % (i, k), [128, 64], F32) for k in range(4)] for i in range(2)])
        kTs = Ring([kb.sb(ph, "A1_kT%d" % i, [128, 512], BF16) for i in range(3)])
        vb = Ring([kb.sb(ph, "A1_vb%d" % i, [128, 516], BF16) for i in range(3)])
        sg = Ring([kb.sb(ph, "A1_sg%d" % i, [128, 512], F32) for i in range(3)])
        sgt1 = Ring([kb.sb(ph, "A1_sgt1%d" % i, [128, 512], F32) for i in range(2)])
        sgt2 = Ring([kb.sb(ph, "A1_sgt2%d" % i, [128, 512], F32) for i in range(2)])
        hib = Ring([kb.sb(ph, "A1_hi%d" % i, [128, 512], BF16) for i in range(3)])
        f1 = Ring([kb.sb(ph, "A1_f1%d" % i, [128, 512], F32) for i in range(2)])
        ff = Ring([kb.sb(ph, "A1_ff%d" % i, [128, 512], F32) for i in range(2)])
        lf = Ring([kb.sb(ph, "A1_lf%d" % i, [128, 512], F32) for i in range(2)])
        kk = Ring([kb.sb(ph, "A1_kk%d" % i, [128, 512], F32) for i in range(3)])
        ed = Ring([kb.sb(ph, "A1_ed%d" % i, [128, 512], F32) for i in range(3)])
        kst = Ring([kb.sb(ph, "A1_ks%d" % i, [128, 512], BF16) for i in range(2)])
        dec = Ring([kb.sb(ph, "A1_dc%d" % i, [128, 8], F32) for i in range(3)])
        Sr = Ring([kb.sb(ph, "A1_S%d" % i, [128, 512], F32) for i in range(2)])
        Sa = [kb.sb(ph, "A1_Sa%d" % c, [128, 512], F32) for c in range(2)]
        Sb = Ring([kb.sb(ph, "A1_Sb%d" % i, [128, 512], BF16) for i in range(4)])
        for v in vb.bufs:
            kb.op("pool", lambda e, v=v: e.memset(v[:], 1.0), [], [v])
        S = Sr.next()
        kb.op("pool", lambda e: e.memset(S[:], 0.0), [], [S])
        loaded = {}

        def load(tb):
            x = xf.next()
            kb.dma([(x[:], x_all[tb * 128:(tb + 1) * 128, :])], writes=[x], sembuf=x)
            r = rp.next()
            kb.dma([(r[:], rope_all[tb * 128:(tb + 1) * 128, :])], writes=[r], sembuf=r)
            loaded[tb] = (x, r)

        def mm_group(pbuf, t, col0):
            def f(e):
                for k in range(8):
                    i = e.matmul(pbuf[:], lhsT=t[:, k * 128:(k + 1) * 128], rhs=W[k][:, col0:col0 + 512], start=(k == 0), stop=(k == 7))
                return i
            kb.op("pe", f, [t] + W, [pbuf])

        stS = {'S': S}

        def body(tb):
            x, r = loaded.pop(tb)
            b = xb.next()
            CP(kb, "dve", b[:], x[:], [x], [b])
            p = pT.next()
            transposes(kb, p, b, ident, 8, [b])
            t = xT.next()
            CP(kb, "act", t[:], p[:], [p], [t])
            yield
            pk = pp.next()
            mm_group(pk, t, 0)
            k_f = kf.next()
            CP(kb, "act", k_f[:], pk[:], [pk], [k_f])
            pv = pp.next()
            mm_group(pv, t, 512)
            v = vb.next()
            CP(kb, "act", v[:].rearrange("p (h d) -> p h d", h=4)[:, :, 0:128], pv[:].rearrange("p (h d) -> p h d", h=4), [pv], [v])
            kb.dma([(V[tb, :, :], v[:])], reads=[v], sembuf=v)
            phf = pp.next()
            mm_group(phf, t, 1024)
            s_g = sg.next()
            sigmoid_exp(kb, s_g[:], phf[:], phf, sgt1.next(), sgt2.next(), s_g)
            phi = pp.next()
            mm_group(phi, t, 1536)
            h_i = hib.next()
            CP(kb, "dve", h_i[:], phi[:], [phi], [h_i])
            yield
            k_b = kbf.next()
            rope(kb, k_f, k_b, r, 8, rt.next())
            pkt = pkT.next()
            transposes(kb, pkt, k_b, ident, 4, [k_b])
            kt = kTs.next()
            CP(kb, "act", kt[:], pkt[:, 0:512], [pkt], [kt])
            kb.dma([(KT[tb, :, :], kt[:])], reads=[kt], sembuf=kt)
            f_1 = f1.next()
            TT(kb, "dve", f_1[:], s_g[:], oml[:], ALU.mult, [s_g, oml], [f_1])
            f_ = ff.next()
            TT(kb, "dve", f_[:], f_1[:], lb[:], ALU.add, [f_1, lb], [f_])
            l_f = lf.next()
            ACT(kb, l_f[:], f_[:], AF.Ln, [f_], [l_f])
            k_k = kk.next()
            ACT(kb, k_k[:], f_[:], AF.Copy, [f_], [k_k], scale=-1.0, bias=1.0)
            pb = pbd.next()
            kb.op("pe", lambda e: e.matmul(pb[:], lhsT=cU[:, 2, :], rhs=l_f[:], start=True, stop=True), [cU, l_f], [pb])
            pd = pdec.next()

            def fdec(e, pd=pd, l_f=l_f):
                for h in range(4):
                    i = e.matmul(pd[:, h * 2:(h + 1) * 2], lhsT=l_f[:, h * 128:(h + 1) * 128], rhs=cU[:, 3, 0:2], start=True, stop=True)
                return i
            kb.op("pe", fdec, [cU, l_f], [pd])
            e_d = ed.next()
            ACT(kb, e_d[:], pb[:], AF.Exp, [pb], [e_d])
            d_c = dec.next()
            ACT(kb, d_c[:], pd[:, 0:8], AF.Exp, [pd], [d_c])
            yield
            k_s = kst.next()
            TT(kb, "dve", k_s[:], k_k[:], e_d[:], ALU.mult, [k_k, e_d], [k_s])
            j = tb // 2
            for c in range(2):
                if not blended:
                    sb_ = Sb.next()
                    CP(kb, "act", sb_[:], stS['S'][:], [stS['S']], [sb_])
                    kb.dma([(SS[tb, c, :, :], sb_[:])], reads=[sb_], sembuf=sb_)
                elif tb % 2 == 0:
                    TS(kb, "dve", Sa[c][:], stS['S'][:], sel[:, 0:1], None, ALU.mult, None, [stS['S'], sel], [Sa[c]])
                else:
                    sb_ = Sb.next()
                    STT(kb, sb_[:], stS['S'][:], sel[:, 1:2], Sa[c][:], ALU.mult, ALU.add, [stS['S'], sel, Sa[c]], [sb_])
                    kb.dma([(SS[j, c, :, :], sb_[:])], reads=[sb_], sembuf=sb_)
                psb = pS.next()

                def fS(e, psb=psb, k_s=k_s, h_i=h_i, c=c):
                    for h in range(4):
                        i = e.matmul(psb[:, h * 128:(h + 1) * 128], lhsT=k_s[c * 64:(c + 1) * 64, h * 128:(h + 1) * 128],
                                     rhs=h_i[c * 64:(c + 1) * 64, h * 128:(h + 1) * 128], start=True, stop=True)
                    return i
                kb.op("pe", fS, [k_s, h_i], [psb])
                Sn = Sr.next()
                for h in range(4):
                    STT(kb, Sn[:, h * 128:(h + 1) * 128], stS['S'][:, h * 128:(h + 1) * 128], d_c[:, h * 2 + c:h * 2 + c + 1],
                        psb[:, h * 128:(h + 1) * 128], ALU.mult, ALU.add, [stS['S'], d_c, psb], [Sn])
                stS['S'] = Sn

        run_pairs(nb, body, load, width=3)
        kb.end_phase()


def phase_A2(kb, nbo, x_own, rope_own, w_in, lb_logits, hg_norm_g, consts, SS, QT, MIX, ss_pick=None):
    with contextlib.ExitStack() as ph:
        Wt = ph.enter_context(kb.nc.sbuf_tensor(kb.uname("A2_W"), [128, 8, 2560], BF16))
        W = [kb.view(Wt[:, k, :]) for k in range(8)]
        load_weight(kb, W, w_in, [(0, 512), (1536, 3584)], 8)
        ident = kb.sb(ph, "A2_id", [128, 128], BF16)
        kb.dma([(ident[:], consts["ident"][:, :])], writes=[ident], sembuf=ident)
        cU = kb.sb(ph, "A2_cU", [128, 4, 128], F32)
        kb.dma([(cU[:], consts["cU"][:, :, :])], writes=[cU], sembuf=cU)
        mhg = kb.sb(ph, "A2_mhg", [128, 128], F32)
        kb.dma([(mhg[:], consts["cU"][:, 0, :])], writes=[mhg], sembuf=mhg)
        gtab = kb.sb(ph, "A2_g", [128, 128], F32)
        kb.dma([(gtab[:], hg_norm_g.partition_broadcast(128))], writes=[gtab], sembuf=gtab)
        lb, oml = hgrn_gate_tables(kb, ph, lb_logits)
        ps = PSUM(kb, ph)
        pT = Ring([ps.bank(0, BF16)])
        pp = Ring([ps.bank(1), ps.bank(2), ps.bank(3)])
        pcs = Ring([[ps.bank(4), ps.bank(5)]])
        pt3 = Ring([ps.bank(6, BF16)])
        pA = Ring([ps.bank(7)])
        xf = Ring([kb.sb(ph, "A2_xf%d" % i, [128, 1024], F32) for i in range(4)])
        rp = Ring([kb.sb(ph, "A2_rp%d" % i, [128, 128], F32) for i in range(4)])
        s0r = Ring([kb.sb(ph, "A2_s0%d" % i, [128, 512], BF16) for i in range(3)])
        s1r = Ring([kb.sb(ph, "A2_s1%d" % i, [128, 512], BF16) for i in range(3)])
        xb = Ring([kb.sb(ph, "A2_xb%d" % i, [128, 1024], BF16) for i in range(2)])
        xT = Ring([kb.sb(ph, "A2_xT%d" % i, [128, 1024], BF16) for i in range(3)])
        qf = Ring([kb.sb(ph, "A2_qf%d" % i, [128, 512], F32) for i in range(3)])
        qbf = Ring([kb.sb(ph, "A2_qb%d" % i, [128, 512], BF16) for i in range(2)])
        rt = Ring([[kb.sb(ph, "A2_rt%d_%d" % (i, k), [128, 64], F32) for k in range(4)] for i in range(2)])
        qTs = Ring([kb.sb(ph, "A2_qT%d" % i, [128, 512], BF16) for i in range(2)])
        qs = Ring([kb.sb(ph, "A2_qs%d" % i, [128, 512], F32) for i in range(3)])
        sg = Ring([kb.sb(ph, "A2_sg%d" % i, [128, 512], F32) for i in range(3)])
        sgt1 = Ring([kb.sb(ph, "A2_sgt1%d" % i, [128, 512], F32) for i in range(1)])
        sgt2 = Ring([kb.sb(ph, "A2_sgt2%d" % i, [128, 512], F32) for i in range(1)])
        sgt3 = Ring([kb.sb(ph, "A2_sgt3%d" % i, [128, 512], F32) for i in range(1)])
        hib = Ring([kb.sb(ph, "A2_hi%d" % i, [128, 512], BF16) for i in range(3)])
        gs = Ring([kb.sb(ph, "A2_gs%d" % i, [128, 512], F32) for i in range(3)])
        f1 = Ring([kb.sb(ph, "A2_f1%d" % i, [128, 512], F32) for i in range(1)])
        ff = Ring([kb.sb(ph, "A2_ff%d" % i, [128, 512], F32) for i in range(1)])
        lf = Ring([kb.sb(ph, "A2_lf%d" % i, [128, 512], F32) for i in range(1)])
        kk = Ring([kb.sb(ph, "A2_kk%d" % i, [128, 512], F32) for i in range(3)])
        bsb = Ring([kb.sb(ph, "A2_bs%d" % i, [128, 512], F32) for i in range(3)])
        d1 = Ring([kb.sb(ph, "A2_d1%d" % i, [128, 512], F32) for i in range(3)])
        e1 = Ring([kb.sb(ph, "A2_e1%d" % i, [128, 512], F32) for i in range(1)])
        e2 = Ring([kb.sb(ph, "A2_e2%d" % i, [128, 512], F32) for i in range(1)])
        e3 = Ring([kb.sb(ph, "A2_e3%d" % i, [128, 512], F32) for i in range(1)])
        Qh = Ring([kb.sb(ph, "A2_Qh%d" % i, [128, 512], BF16) for i in range(2)])
        Kh = Ring([kb.sb(ph, "A2_Kh%d" % i, [128, 512], BF16) for i in range(2)])
        qo = Ring([kb.sb(ph, "A2_qo%d" % i, [128, 512], BF16) for i in range(2)])
        QhT = Ring([kb.sb(ph, "A2_QhT%d" % i, [128, 512], BF16) for i in range(3)])
        KhT = Ring([kb.sb(ph, "A2_KhT%d" % i, [128, 512], BF16) for i in range(3)])
        qz0 = Ring([kb.sb(ph, "A2_qz0%d" % i, [128, 4, 128], BF16) for i in range(3)])
        qz1 = Ring([kb.sb(ph, "A2_qz1%d" % i, [128, 4, 128], BF16) for i in range(3)])
        Am = Ring([kb.sb(ph, "A2_Am%d" % i, [128, 512], BF16) for i in range(3)])
        jk = Ring([kb.sb(ph, "A2_jk%d" % i, [128, 128], F32) for i in range(2)])
        sm = Ring([[kb.sb(ph, "A2_sm%d_%d" % (i, k), [128, 4], F32) for k in range(4)] for i in range(2)])
        on = Ring([kb.sb(ph, "A2_on%d" % i, [128, 512], F32) for i in range(1)])
        on2 = Ring([kb.sb(ph, "A2_o2%d" % i, [128, 512], F32) for i in range(1)])
        ob = Ring([kb.sb(ph, "A2_ob%d" % i, [128, 512], BF16) for i in range(2)])
        for z in qz0.bufs + qz1.bufs:
            kb.op("pool", lambda e, z=z: e.memset(z[:], 0.0), [], [z])
        loaded = {}
        if ss_pick is not None:
            cand = Ring([kb.sb(ph, "A2_cd%d" % i, [128, 512], BF16) for i in range(16)])
            candf = Ring([kb.sb(ph, "A2_cf%d" % i, [128, 512], F32) for i in range(1)])
            sel = kb.sb(ph, "A2_sel", [128, 2], F32)
            kb.dma([(sel[:], consts["sel"][:, :])], writes=[sel], sembuf=sel)

        def load(j):
            x = xf.next()
            kb.dma([(x[:], x_own[j * 128:(j + 1) * 128, :])], writes=[x], sembuf=x)
            r = rp.next()
            kb.dma([(r[:], rope_own[j * 128:(j + 1) * 128, :])], writes=[r], sembuf=r)
            if ss_pick is None:
                a0 = s0r.next()
                a1 = s1r.next()
                kb.dma([(a0[:], SS[j, 0, :, :])], writes=[a0], sembuf=a0)
                kb.dma([(a1[:], SS[j, 1, :, :])], writes=[a1], sembuf=a1)
            else:
                g0, g1 = ss_pick(j)
                a0 = []
                for c in (0, 1):
                    c0 = cand.next()
                    kb.dma([(c0[:], SS[g0, c, :, :])], writes=[c0], sembuf=c0)
                    c1 = cand.next()
                    kb.dma([(c1[:], SS[g1, c, :, :])], writes=[c1], sembuf=c1)
                    a0.append((c0, c1))
                a1 = None
            loaded[j] = (x, r, a0, a1)

        def mm_group(pbuf, t, col0):
            def f(e):
                for k in range(8):
                    i = e.matmul(pbuf[:], lhsT=t[:, k * 128:(k + 1) * 128], rhs=W[k][:, col0:col0 + 512], start=(k == 0), stop=(k == 7))
                return i
            kb.op("pe", f, [t] + W, [pbuf])

        def body(j):
            x, r, S0, S1 = loaded.pop(j)
            b = xb.next()
            CP(kb, "dve", b[:], x[:], [x], [b])
            p = pT.next()
            transposes(kb, p, b, ident, 8, [b])
            t = xT.next()
            CP(kb, "act", t[:], p[:], [p], [t])
            yield
            pq = pp.next()
            mm_group(pq, t, 0)
            q_f = qf.next()
            CP(kb, "act", q_f[:], pq[:], [pq], [q_f])
            phq = pp.next()
            mm_group(phq, t, 512)
            q_s = qs.next()
            sq_ = sgt3.next()
            sigmoid_exp(kb, sq_[:], phq[:], phq, sgt1.next(), sgt2.next(), sq_)
            TT(kb, "dve", q_s[:], sq_[:], phq[:], ALU.mult, [sq_, phq], [q_s])
            phf = pp.next()
            mm_group(phf, t, 1024)
            s_g = sg.next()
            sigmoid_exp(kb, s_g[:], phf[:], phf, sgt1.next(), sgt2.next(), s_g)
            phi = pp.next()
            mm_group(phi, t, 1536)
            h_i = hib.next()
            CP(kb, "dve", h_i[:], phi[:], [phi], [h_i])
            phg = pp.next()
            mm_group(phg, t, 2048)
            g_s = gs.next()
            sq_ = sgt3.next()
            sigmoid_exp(kb, sq_[:], phg[:], phg, sgt1.next(), sgt2.next(), sq_)
            TT(kb, "dve", g_s[:], sq_[:], phg[:], ALU.mult, [sq_, phg], [g_s])
            yield
            q_b = qbf.next()
            rope(kb, q_f, q_b, r, 8, rt.next())
            p3 = pt3.next()
            transposes(kb, kb.view(p3[:, 0:512]) if False else p3, q_b, ident, 4, [q_b])
            qt = qTs.next()
            CP(kb, "act", qt[:], p3[:, 0:512], [p3], [qt])
            kb.dma([(QT[j, :, :], qt[:])], reads=[qt], sembuf=qt)
            f_1 = f1.next()
            TT(kb, "dve", f_1[:], s_g[:], oml[:], ALU.mult, [s_g, oml], [f_1])
            f_ = ff.next()
            TT(kb, "dve", f_[:], f_1[:], lb[:], ALU.add, [f_1, lb], [f_])
            l_f = lf.next()
            ACT(kb, l_f[:], f_[:], AF.Ln, [f_], [l_f])
            k_k = kk.next()
            ACT(kb, k_k[:], f_[:], AF.Copy, [f_], [k_k], scale=-1.0, bias=1.0)
            pb, pbm = pcs.next()
            kb.op("pe", lambda e: e.matmul(pb[:], lhsT=cU[:, 0, :], rhs=l_f[:], start=True, stop=True), [cU, l_f], [pb])
            kb.op("pe", lambda e: e.matmul(pbm[:], lhsT=cU[:, 1, :], rhs=l_f[:], start=True, stop=True), [cU, l_f], [pbm])
            b_s = bsb.next()
            CP(kb, "act", b_s[:], pb[:], [pb], [b_s])
            d_1 = d1.next()
            TT(kb, "dve", d_1[:], b_s[:], pbm[:], ALU.subtract, [b_s, pbm], [d_1])
            yield
            e_1 = e1.next()
            ACT(kb, e_1[:], d_1[:], AF.Exp, [d_1], [e_1])
            e_2 = e2.next()
            ACT(kb, e_2[:], d_1[:], AF.Exp, [d_1], [e_2], scale=-1.0)
            e_3 = e3.next()
            ACT(kb, e_3[:], b_s[:], AF.Exp, [b_s], [e_3])
            Q_h = Qh.next()
            TT(kb, "dve", Q_h[:], q_s[:], e_1[:], ALU.mult, [q_s, e_1], [Q_h])
            K_h = Kh.next()
            TT(kb, "dve", K_h[:], k_k[:], e_2[:], ALU.mult, [k_k, e_2], [K_h])
            q_o = qo.next()
            TT(kb, "dve", q_o[:], q_s[:], e_3[:], ALU.mult, [q_s, e_3], [q_o])
            p3 = pt3.next()
            transposes(kb, p3, Q_h, ident, 4, [Q_h])
            Q_T = QhT.next()
            CP(kb, "act", Q_T[:], p3[:, 0:512], [p3], [Q_T])
            p3 = pt3.next()
            transposes(kb, p3, K_h, ident, 4, [K_h])
            K_T = KhT.next()
            CP(kb, "dve", K_T[:], p3[:, 0:512], [p3], [K_T])
            p3 = pt3.next()
            transposes(kb, p3, q_o, ident, 4, [q_o])
            z0 = qz0.next()
            z1 = qz1.next()
            p3v = p3[:, 0:512].rearrange("p (h t) -> p h t", h=4)
            CP(kb, "act", z0[:, :, 0:64], p3v[:, :, 0:64], [p3], [z0])
            CP(kb, "dve", z1[:, :, 64:128], p3v[:, :, 64:128], [p3], [z1])
            yield
            pa = pA.next()

            def fA(e, pa=pa, K_T=K_T, Q_T=Q_T):
                for h in range(4):
                    i = e.matmul(pa[:, h * 128:(h + 1) * 128], lhsT=K_T[:, h * 128:(h + 1) * 128], rhs=Q_T[:, h * 128:(h + 1) * 128],
                                 start=True, stop=True)
                return i
            kb.op("pe", fA, [K_T, Q_T], [pa])
            A_m = Am.next()
            TT(kb, "dve", A_m[:].rearrange("p (h t) -> p h t", h=4), pa[:].rearrange("p (h t) -> p h t", h=4),
               mhg[:].unsqueeze(1).broadcast_to([128, 4, 128]), ALU.mult, [pa, mhg], [A_m])
            yield
            if ss_pick is not None:
                cands = S0
                blended = []
                for c, ring in ((0, s0r), (1, s1r)):
                    c0, c1 = cands[c]
                    dst = ring.next()
                    tmpb = candf.next()
                    kb.op("act", lambda e, tmpb=tmpb, c0=c0: e.activation(out=tmpb[:], in_=c0[:], func=AF.Copy, scale=sel[:, 0:1]), [c0, sel], [tmpb])
                    STT(kb, dst[:], c1[:], sel[:, 1:2], tmpb[:], ALU.mult, ALU.add, [c1, sel, tmpb], [dst])
                    blended.append(dst)
                S0, S1 = blended
            po = pp.next()

            def fo(e, po=po, A_m=A_m, h_i=h_i, z0=z0, z1=z1, S0=S0, S1=S1):
                for h in range(4):
                    hs = slice(h * 128, (h + 1) * 128)
                    e.matmul(po[:, hs], lhsT=A_m[:, hs], rhs=h_i[:, hs], start=True, stop=False)
                    e.matmul(po[:, hs], lhsT=z0[:, h, :], rhs=S0[:, hs], start=False, stop=False)
                    i = e.matmul(po[:, hs], lhsT=z1[:, h, :], rhs=S1[:, hs], start=False, stop=True)
                return i
            kb.op("pe", fo, [A_m, h_i, z0, z1, S0, S1], [po])
            s = sm.next()
            for h in range(4):
                j_ = jk.next()
                ACT(kb, j_[:], po[:, h * 128:(h + 1) * 128], AF.Square, [po], [j_, s[0]], accum=s[0][:, h:h + 1])
            rstd_act(kb, s[2][:, 0:4], s[0][:, 0:4], s[1][:, 0:4], 1.0 / 128.0, s[0], s[1], s[2])
            o_n = on.next()
            TT(kb, "dve", o_n[:].rearrange("p (h t) -> p h t", h=4), po[:].rearrange("p (h t) -> p h t", h=4),
               s[2][:, 0:4].unsqueeze(2).broadcast_to([128, 4, 128]), ALU.mult, [po, s[2]], [o_n])
            o_2 = on2.next()
            TT(kb, "dve", o_2[:].rearrange("p (h t) -> p h t", h=4), o_n[:].rearrange("p (h t) -> p h t", h=4),
               gtab[:].unsqueeze(1).broadcast_to([128, 4, 128]), ALU.mult, [o_n, gtab], [o_2])
            o_b = ob.next()
            TT(kb, "dve", o_b[:], o_2[:], g_s[:], ALU.mult, [o_2, g_s], [o_b])
            kb.dma([(MIX[j * 128:(j + 1) * 128, 512:1024], o_b[:])], reads=[o_b], sembuf=o_b)

        run_pairs(nbo, body, load, width=3, prefetch=1)
        kb.end_phase()


def keyspec_parity(j):
    return [(k, (k - 2 * j) if k >= 2 * j else None, False) for k in range(2 * j + 2)]


def make_keyspec_ctx(nbh):
    def spec(i):
        g0, g1 = i - NCTX, i + nbh - NCTX
        out = []
        for k in range(g1 + 1):
            if k < g0:
                out.append((k, None, False))
            elif k == g0:
                out.append((k, 0, False))
            elif k < g1:
                out.append((k, None, True))
            else:
                out.append((k, 1, True))
        return out
    return spec


def phase_B(kb, S, nbo, KT, V, QT, da_lambda, sub_g, consts, MIX, keyspec=keyspec_parity):
    nb = S // 128
    with contextlib.ExitStack() as ph:
        KTt = ph.enter_context(kb.nc.sbuf_tensor(kb.uname("B_KT"), [128, nb, 512], BF16))
        Vt = ph.enter_context(kb.nc.sbuf_tensor(kb.uname("B_V"), [128, nb, 516], BF16))
        CH = 8
        nch = (nb + CH - 1) // CH
        KTc = [kb.view(KTt[:, c * CH:min(nb, (c + 1) * CH), :]) for c in range(nch)]
        Vc = [kb.view(Vt[:, c * CH:min(nb, (c + 1) * CH), :]) for c in range(nch)]
        ident = kb.sb(ph, "B_id", [128, 128], BF16)
        kb.dma([(ident[:], consts["ident"][:, :])], writes=[ident], sembuf=ident)
        msk = kb.sb(ph, "B_msk", [128, 2, 128], BF16)
        kb.dma([(msk[:], consts["mAB"][:, :, :])], writes=[msk], sembuf=msk)
        gtab = kb.sb(ph, "B_g", [128, 128], F32)
        kb.dma([(gtab[:], sub_g.partition_broadcast(128))], writes=[gtab], sembuf=gtab)
        lt = kb.sb(ph, "B_lt", [128, 4, 64], F32)
        kb.dma([(lt[:], da_lambda.partition_broadcast(128))], writes=[lt], sembuf=lt)
        l1 = kb.sb(ph, "B_l1", [128, 2, 64], F32)
        l2 = kb.sb(ph, "B_l2", [128, 2], F32)
        l3 = kb.sb(ph, "B_l3", [128, 2], F32)
        nlam = kb.sb(ph, "B_nlam", [128, 1], F32)
        TT(kb, "dve", l1[:, 0, :], lt[:, 0, :], lt[:, 1, :], ALU.mult, [lt], [l1])
        TT(kb, "dve", l1[:, 1, :], lt[:, 2, :], lt[:, 3, :], ALU.mult, [lt], [l1])
        kb.op("dve", lambda e: e.reduce_sum(out=l2[:, 0:2], in_=l1[:], axis=AX.X), [l1], [l2])
        ACT(kb, l3[:], l2[:], AF.Exp, [l2], [l3])
        l4 = kb.sb(ph, "B_l4", [128, 1], F32)
        TT(kb, "dve", l4[:], l3[:, 1:2], l3[:, 0:1], ALU.subtract, [l3], [l4])
        TS(kb, "dve", nlam[:], l4[:], -LAM_INIT0, None, ALU.add, None, [l4], [nlam])
        for c in range(nch):
            lo, hi = c * CH, min(nb, (c + 1) * CH)
            kb.dma([(KTc[c][:], KT[lo:hi, :, :].rearrange("n p f -> p n f"))], writes=[KTc[c]], sembuf=KTc[c])
            kb.dma([(Vc[c][:], V[lo:hi, :, :].rearrange("n p f -> p n f"))], writes=[Vc[c]], sembuf=Vc[c])
        ps = PSUM(kb, ph)
        pst = Ring([ps.bank(0), ps.bank(1), ps.bank(2)])
        pO = Ring([ps.bank(3), ps.bank(4)])
        qr = Ring([kb.sb(ph, "B_q%d" % i, [128, 512], BF16) for i in range(3)])
        ptr = Ring([kb.sb(ph, "B_pt%d" % i, [128, 512], BF16) for i in range(4)])
        rd = Ring([kb.sb(ph, "B_rd%d" % i, [128, 2], F32) for i in range(2)])
        dn = Ring([kb.sb(ph, "B_dn%d" % i, [128, 2], F32) for i in range(2)])
        cf = Ring([kb.sb(ph, "B_cf%d" % i, [128, 1], F32) for i in range(2)])
        t0r = Ring([kb.sb(ph, "B_t0%d" % i, [128, 128], F32) for i in range(2)])
        orr = Ring([kb.sb(ph, "B_o%d" % i, [128, 128], F32) for i in range(2)])
        jk = Ring([kb.sb(ph, "B_jk%d" % i, [128, 128], F32) for i in range(2)])
        sm = Ring([[kb.sb(ph, "B_sm%d_%d" % (i, k), [128, 1], F32) for k in range(3)] for i in range(2)])
        on = Ring([kb.sb(ph, "B_on%d" % i, [128, 128], F32) for i in range(2)])
        mixr = Ring([kb.sb(ph, "B_mx%d" % i, [128, 512], BF16) for i in range(2)])
        loaded = {}

        def load(j):
            q = qr.next()
            kb.dma([(q[:], QT[j, :, :])], writes=[q], sembuf=q)
            loaded[j] = q

        def chunk_bufs(kbs):
            cs = sorted(set(k // CH for k in kbs))
            return [KTc[c] for c in cs], [Vc[c] for c in cs]

        hb = kb.sb(ph, "B_hb", [128, 1], F32)
        kb.dma([(hb[:], consts["hb"][:, :])], writes=[hb], sembuf=hb)
        units = []
        for j in range(nbo):
            ents = keyspec(j)
            groups = []
            for ent in ents:
                if groups and len(groups[-1]) < 4 and groups[-1][-1][2] == ent[2]:
                    groups[-1].append(ent)
                else:
                    groups.append([ent])
            for h in range(4):
                for c in range(2):
                    for gi_, grp in enumerate(groups):
                        units.append((j, h, c, grp, gi_ == 0, gi_ == len(groups) - 1))
        state = {"O": None, "mix": None}

        def emit_ST(u):
            j, h, c, grp, first, last = u
            q = loaded[j]
            st = pst.next()
            kbs = [g_[0] for g_ in grp]
            kts, _ = chunk_bufs(kbs)
            high = grp[0][2]

            def f(e):
                for i, (kbk, mi, _) in enumerate(grp):
                    e_ = e.matmul(st[:, i * 128:(i + 1) * 128], lhsT=KTt[c * 64:(c + 1) * 64, kbk, h * 128:(h + 1) * 128],
                                  rhs=q[c * 64:(c + 1) * 64, h * 128:(h + 1) * 128], start=True, stop=(mi is None))
                    if mi is not None:
                        e_ = e.matmul(st[:, i * 128:(i + 1) * 128], lhsT=ident[:], rhs=msk[:, mi, :], start=False, stop=True)
                return e_
            kb.op("pe", f, kts + [q, ident, msk], [st])
            pt = ptr.next()
            n = len(kbs) * 128
            if high:
                kb.op("act", lambda e: e.activation(out=pt[:, 0:n], in_=st[:, 0:n], func=AF.Exp, scale=0.125, bias=hb[:, 0:1]), [st, hb], [pt])
            else:
                ACT(kb, pt[:, 0:n], st[:, 0:n], AF.Exp, [st], [pt], scale=0.125)
            return pt

        def emit_PV(u, pt):
            j, h, c, grp, first, last = u
            if c == 0 and first:
                state["O"] = pO.next()
            O = state["O"]
            kbs = [g_[0] for g_ in grp]
            _, vs = chunk_bufs(kbs)

            def f(e):
                for i, kbk in enumerate(kbs):
                    e_ = e.matmul(O[:, c * 129:(c + 1) * 129], lhsT=pt[:, i * 128:(i + 1) * 128], rhs=Vt[:, kbk, h * 129:(h + 1) * 129],
                                  start=(first and i == 0), stop=(last and i == len(kbs) - 1))
                return e_
            kb.op("pe", f, vs + [pt], [O])
            if c == 1 and last:
                finalize(j, h, O)

        def finalize(j, h, O):
            if h == 0:
                state["mix"] = mixr.next()
            mx = state["mix"]
            r_d = rd.next()
            d_n = dn.next()
            TS(kb, "dve", d_n[:, 0:2], O[:, 0:258].rearrange("p (c d) -> p c d", c=2)[:, :, 128], 1e-30, None, ALU.max, None, [O], [d_n])
            kb.op("dve", lambda e: e.reciprocal(out=r_d[:, 0:2], in_=d_n[:, 0:2]), [d_n], [r_d])
            c_f = cf.next()
            TT(kb, "dve", c_f[:], r_d[:, 1:2], nlam[:], ALU.mult, [r_d, nlam], [c_f])
            t_0 = t0r.next()
            TS(kb, "dve", t_0[:], O[:, 0:128], r_d[:, 0:1], None, ALU.mult, None, [O, r_d], [t_0])
            o_ = orr.next()
            STT(kb, o_[:], O[:, 129:257], c_f[:, 0:1], t_0[:], ALU.mult, ALU.add, [O, c_f, t_0], [o_])
            s = sm.next()
            j_ = jk.next()
            ACT(kb, j_[:], o_[:], AF.Square, [o_], [j_, s[0]], accum=s[0][:, 0:1])
            rstd_act(kb, s[2][:], s[0][:], s[1][:], 1.0 / 128.0, s[0], s[1], s[2])
            o_n = on.next()
            TS(kb, "dve", o_n[:], o_[:], s[2][:, 0:1], 1.0 - LAM_INIT0, ALU.mult, ALU.mult, [o_, s[2]], [o_n])
            TT(kb, "dve", mx[:, h * 128:(h + 1) * 128], o_n[:], gtab[:], ALU.mult, [o_n, gtab], [mx])
            if h == 3:
                kb.dma([(MIX[j * 128:(j + 1) * 128, 0:512], mx[:])], reads=[mx], sembuf=mx)

        for j in range(min(2, nbo)):
            load(j)
        nxt = min(2, nbo)
        prev = None
        for ui, u in enumerate(units):
            if ui > 0 and u[0] != units[ui - 1][0] and nxt < nbo:
                load(nxt)
                nxt += 1
            pt = emit_ST(u)
            if prev is not None:
                emit_PV(*prev)
            prev = (u, pt)
        emit_PV(*prev)
        kb.end_phase()


def rope_table(pos, ng):
    inv = ROPE_THETA ** (-np.arange(0, 16, 2, dtype=np.float32) / np.float32(16))
    ang = pos.astype(np.float32)[:, None] * inv[None, :].astype(np.float32)
    cos = np.cos(ang).astype(np.float32)
    sin = np.sin(ang).astype(np.float32)
    return np.concatenate([np.tile(cos, (1, ng)), np.tile(sin, (1, ng))], axis=1).astype(np.float32)


def const_cU():
    s = np.arange(128)[:, None]
    t = np.arange(128)[None, :]
    same = (s // 64) == (t // 64)
    cU = np.zeros((128, 4, 128), np.float32)
    cU[:, 0, :] = same & (s <= t)
    cU[:, 1, :] = same & ((s % 64) <= 31)
    cU[:, 2, :] = same & (s > t)
    cU[:, 3, 0] = (np.arange(128) // 64) == 0
    cU[:, 3, 1] = (np.arange(128) // 64) == 1
    return cU


def const_l0(h):
    k = np.arange(128)[:, None]
    q = np.arange(128)[None, :]
    tri = np.where(k <= q, 0.0, NEG).astype(np.float32)
    zero = np.zeros((128, 128), np.float32)
    full = np.full((128, 128), NEG, np.float32)
    mAB = np.stack([tri if h == 0 else zero, full if h == 0 else tri], axis=1).astype(NPBF)
    sel = np.zeros((128, 2), np.float32)
    sel[:, h] = 1.0
    return {"ident": np.eye(128).astype(NPBF), "cU": const_cU(), "sel": sel, "mAB": mAB, "hb": np.zeros((128, 1), np.float32)}


def dense_tail(kb, nbo, l, mix, xres, io, XA, XB, out, consts, w_out, mix_loader=None, p_in=None):
    phase_C(kb, nbo, mix, xres, w_out, io["ln1_g"][l:l + 1, :], io["ln1_b"][l:l + 1, :], XA, consts, mix_loader=mix_loader)
    phase_D1(kb, nbo, XA, io["ffn_w1"][l], io["ffn_w2"][l], io["ln2_g"][l:l + 1, :], io["ln2_b"][l:l + 1, :], XB, consts)
    phase_D2(kb, nbo, XB, io["p_own"] if p_in is None else p_in, io["ple_w_gate"][l], io["ple_w_proj"][l], io["ple_norm_g"][l:l + 1, :], out, consts)


WEIGHT_SHAPES = {
    "ev_w_in": [1, 1024, 3584], "ev_w_out": [1, 1024, 1024], "da_lambda": [1, 4, 64], "da_subln_g": [1, 128],
    "hg_lb_logits": [2, 512], "hg_norm_g": [1, 128], "od_w_in": [1, 1024, 3072], "od_w_out": [1, 1024, 1024],
    "ln1_g": [2, 1024], "ln1_b": [2, 1024], "ffn_w1": [2, 1024, 4096], "ffn_w2": [2, 4096, 1024],
    "ln2_g": [2, 1024], "ln2_b": [2, 1024], "ple_w_proj": [2, 256, 1024], "ple_w_gate": [2, 1024, 1024],
    "ple_norm_g": [2, 1024],
}
L0_W = ["ev_w_in", "ev_w_out", "da_lambda", "da_subln_g", "hg_lb_logits", "hg_norm_g", "ln1_g", "ln1_b", "ffn_w1",
        "ffn_w2", "ln2_g", "ln2_b", "ple_w_proj", "ple_w_gate", "ple_norm_g"]
L1_W = ["od_w_in", "od_w_out", "ln1_g", "ln1_b", "ffn_w1", "ffn_w2", "ln2_g", "ln2_b", "ple_w_proj", "ple_w_gate",
        "ple_norm_g"]


def build_l0(S, debug=False, phases="12BT"):
    nc = bass.Bass("TRN2", target_bir_lowering=False)
    nb, nbo = S // 128, S // 256
    So = S // 2

    def din(name, shape, dt=F32):
        return nc.dram_tensor(name, list(shape), dt, kind="ExternalInput").ap()

    def scr(name, shape, dt):
        return nc.dram_tensor(name, list(shape), dt, kind="ExternalOutput" if debug else "Internal").ap()

    io = {n: din(n, WEIGHT_SHAPES[n]) for n in L0_W}
    io["x_all"] = din("x_all", [S, D])
    io["x_own"] = din("x_own", [So, D])
    io["p_own"] = din("p_own", [So, PLE])
    io["rope_all"] = din("rope_all", [S, 128])
    io["rope_own"] = din("rope_own", [So, 128])
    consts = {"ident": din("ident", [128, 128], BF16), "cU": din("cU", [128, 4, 128]),
              "sel": din("sel", [128, 2]), "mAB": din("mAB", [128, 2, 128], BF16), "hb": din("hb", [128, 1])}
    out = nc.dram_tensor("x1_own", [So, D], F32, kind="ExternalOutput").ap()
    KT = scr("KT", [nb, 128, 512], BF16)
    V = scr("V", [nb, 128, 516], BF16)
    SS = scr("SS", [nbo, 2, 128, 512], BF16)
    QT = scr("QT", [nbo, 128, 512], BF16)
    MIX = scr("MIX", [So, D], BF16)
    XA = scr("XA", [So, D], F32)
    XB = scr("XB", [So, D], F32)
    kb = KB(nc)
    if "1" in phases:
        phase_A1(kb, S, io["x_all"], io["rope_all"], io["ev_w_in"][0], io["hg_lb_logits"], consts, KT, V, SS)
    if "2" in phases:
        phase_A2(kb, nbo, io["x_own"], io["rope_own"], io["ev_w_in"][0], io["hg_lb_logits"], io["hg_norm_g"], consts, SS, QT, MIX)
    if "B" in phases:
        phase_B(kb, S, nbo, KT, V, QT, io["da_lambda"][0], io["da_subln_g"], consts, MIX)
    if "T" in phases:
        dense_tail(kb, nbo, 0, MIX, io["x_own"], io, XA, XB, out, consts, io["ev_w_out"][0])
    return nc, kb


def l0_inputs(x_b, p0_b, weights, h):
    S = x_b.shape[0]
    nb = S // 128
    own = np.arange(nb).reshape(nb // 2, 2)[:, h]
    pos_own = (own[:, None] * 128 + np.arange(128)[None, :]).reshape(-1)
    m = {n: weights[n] for n in L0_W}
    m["x_all"] = x_b
    m["x_own"] = np.ascontiguousarray(x_b[pos_own])
    m["p_own"] = np.ascontiguousarray(p0_b[pos_own])
    m["rope_all"] = rope_table(np.arange(S), 8)
    m["rope_own"] = rope_table(pos_own, 8)
    m.update(const_l0(h))
    return m, pos_own


NCTX = 16


def phase_L1A(kb, nbo, x_co, rope_co, w_in, consts, KT1, QT1, V1):
    nblk = NCTX + nbo
    with contextlib.ExitStack() as ph:
        Wt = ph.enter_context(kb.nc.sbuf_tensor(kb.uname("F_W"), [128, 8, 3072], BF16))
        W = [kb.view(Wt[:, k, :]) for k in range(8)]
        load_weight(kb, W, w_in, [(0, 3072)], 8)
        ident = kb.sb(ph, "F_id", [128, 128], BF16)
        kb.dma([(ident[:], consts["ident"][:, :])], writes=[ident], sembuf=ident)
        ps = PSUM(kb, ph)
        pT = Ring([ps.bank(0, BF16)])
        pp = Ring([ps.bank(1), ps.bank(2), ps.bank(3), ps.bank(4)])
        pkT = Ring([ps.bank(5, BF16), ps.bank(6, BF16)])
        xf = Ring([kb.sb(ph, "F_xf%d" % i, [128, 1024], F32) for i in range(4)])
        rp = Ring([kb.sb(ph, "F_rp%d" % i, [128, 128], F32) for i in range(4)])
        xb = Ring([kb.sb(ph, "F_xb%d" % i, [128, 1024], BF16) for i in range(2)])
        xT = Ring([kb.sb(ph, "F_xT%d" % i, [128, 1024], BF16) for i in range(2)])
        kf = Ring([kb.sb(ph, "F_kf%d" % i, [128, 512], F32) for i in range(3)])
        kbf = Ring([kb.sb(ph, "F_kb%d" % i, [128, 1024], BF16) for i in range(4)])
        rt = Ring([[kb.sb(ph, "F_rt%d_%d" % (i, k), [128, 64], F32) for k in range(4)] for i in range(3)])
        kst = Ring([kb.sb(ph, "F_ks%d" % i, [128, 8, 512], BF16) for i in range(2)])
        qst = Ring([kb.sb(ph, "F_qs%d" % i, [128, 8, 512], BF16) for i in range(2)])
        vb = Ring([kb.sb(ph, "F_vb%d" % i, [128, 16, 65], BF16) for i in range(3)])
        for v in vb.bufs:
            kb.op("pool", lambda e, v=v: e.memset(v[:], 1.0), [], [v])
        loaded = {}

        def load(i):
            x = xf.next()
            kb.dma([(x[:], x_co[i * 128:(i + 1) * 128, :])], writes=[x], sembuf=x)
            r = rp.next()
            kb.dma([(r[:], rope_co[i * 128:(i + 1) * 128, :])], writes=[r], sembuf=r)
            loaded[i] = (x, r)

        def mm_group(pbuf, t, col0):
            def f(e):
                for k in range(8):
                    i_ = e.matmul(pbuf[:], lhsT=t[:, k * 128:(k + 1) * 128], rhs=W[k][:, col0:col0 + 512], start=(k == 0), stop=(k == 7))
                return i_
            kb.op("pe", f, [t] + W, [pbuf])

        def roped(t, r, col0):
            k_b = kbf.next()
            for half in range(2):
                pk = pp.next()
                mm_group(pk, t, col0 + half * 512)
                k_f = kf.next()
                CP(kb, "act", k_f[:], pk[:], [pk], [k_f])
                rope(kb, k_f, _Sub(k_b, half), r, 8, rt.next())
            return k_b

        def to_stage(k_b, stage, slot):
            pkt = pkT.next()
            transposes(kb, pkt, k_b, ident, 8, [k_b])
            CP(kb, "act", stage[:, :, slot * 128:(slot + 1) * 128], pkt[:].rearrange("p (h t) -> p h t", h=8), [pkt], [stage])

        stg = {}

        def body(i):
            x, r = loaded.pop(i)
            b = xb.next()
            CP(kb, "dve", b[:], x[:], [x], [b])
            p = pT.next()
            transposes(kb, p, b, ident, 8, [b])
            t = xT.next()
            CP(kb, "act", t[:], p[:], [p], [t])
            yield
            if i % 4 == 0:
                stg["k"] = kst.next()
            k_stage = stg["k"]
            k_b = roped(t, r, 1024)
            v = vb.next()
            for half in range(2):
                pv = pp.next()
                mm_group(pv, t, 2048 + half * 512)
                CP(kb, "dve" if half == 0 else "act", v[:, half * 8:(half + 1) * 8, 0:64], pv[:].rearrange("p (h d) -> p h d", h=8), [pv], [v])
            kb.dma([(V1[i * 128:(i + 1) * 128, :], v[:].rearrange("p h d -> p (h d)"))], reads=[v], sembuf=v)
            q_b = None
            if i >= NCTX:
                io_ = i - NCTX
                if io_ % 4 == 0:
                    stg["q"] = qst.next()
                q_stage = stg["q"]
                q_b = roped(t, r, 0)
            yield
            to_stage(k_b, k_stage, i % 4)
            if i % 4 == 3:
                g0 = (i // 4) * 512
                kb.dma([(KT1[:, :, g0:g0 + 512].rearrange("h p t -> p h t"), k_stage[:])], reads=[k_stage], sembuf=k_stage)
            if q_b is not None:
                to_stage(q_b, q_stage, io_ % 4)
                if io_ % 4 == 3:
                    g0 = (io_ // 4) * 512
                    kb.dma([(QT1[:, :, g0:g0 + 512].rearrange("h p t -> p h t"), q_stage[:])], reads=[q_stage], sembuf=q_stage)

        run_pairs(nblk, body, load)
        kb.end_phase()


class _Sub:
    def __init__(self, parent, half):
        self.parent = parent
        self.half = half

    def __getitem__(self, idx):
        return self.parent.t[:, self.half * 512:(self.half + 1) * 512][idx]

    def __getattr__(self, name):
        return getattr(self.parent, name)

    def __setattr__(self, name, value):
        if name in ("parent", "half"):
            object.__setattr__(self, name, value)
        else:
            setattr(self.parent, name, value)


def phase_L1B(kb, nreg, KT1, QT1, V1, consts, ON):
    DILS = (1, 4, 16)
    with contextlib.ExitStack() as ph:
        KTsb = kb.sb(ph, "G_KT", [128, 8, 4096], BF16)
        QTsb = kb.sb(ph, "G_QT", [128, 8, 2048], BF16)
        ident = kb.sb(ph, "G_id", [128, 128], BF16)
        kb.dma([(ident[:], consts["ident"][:, :])], writes=[ident], sembuf=ident)
        msk = kb.sb(ph, "G_msk", [128, 3, 128], BF16)
        kb.dma([(msk[:], consts["m1"][:, :, :])], writes=[msk], sembuf=msk)
        ps = PSUM(kb, ph)
        pst = Ring([ps.bank(0), ps.bank(1)])
        pO = Ring([[ps.bank(2), ps.bank(3), ps.bank(4)], [ps.bank(5), ps.bank(6), ps.bank(7)]])
        vr = Ring([kb.sb(ph, "G_v%d" % i, [128, 1040], BF16) for i in range(8)])
        ptr = Ring([kb.sb(ph, "G_pt%d" % i, [128, 512], BF16) for i in range(4)])
        osb = Ring([kb.sb(ph, "G_o%d" % i, [128, 1040], F32) for i in range(3)])
        for R in range(nreg):
            kb.dma([(KTsb[:, h, :], KT1[h, :, 2048 * R:2048 * R + 4096]) for h in range(8)], writes=[KTsb], sembuf=KTsb)
            kb.dma([(QTsb[:, h, :], QT1[h, :, 2048 * R:2048 * (R + 1)]) for h in range(8)], writes=[QTsb], sembuf=QTsb)
            qblocks = [(gi, d, r, n) for gi, d in enumerate(DILS) for n in range(16 // d) for r in range(d)]
            vt = {}

            def loadv(qi):
                gi, d, r, n = qblocks[qi]
                tiles = []
                for which in (0, 1):
                    U0 = 2048 * R + 2048 + 128 * d * (n - 1 + which)
                    v = vr.next()
                    src = V1.rearrange("(a s) f -> a s f", s=d)[U0 // d:U0 // d + 128, r, :]
                    kb.dma([(v[:], src)], writes=[v], sembuf=v)
                    tiles.append(v)
                vt[qi] = tiles

            units = [(qi, hp) for qi in range(len(qblocks)) for hp in range(8)]
            state = {}

            def emit_ST(u):
                qi, hp = u
                gi, d, r, n = qblocks[qi]
                st = pst.next()
                q0 = 128 * d * n + r
                k0 = 2048 + 128 * d * (n - 1) + r
                mprev = 2 if (R == 0 and n == 0) else 0

                def f(e):
                    for c in range(2):
                        qap = QTsb[c * 64:(c + 1) * 64, hp, q0:q0 + 127 * d + 1:d]
                        for which in range(2):
                            kk0 = k0 + which * 128 * d
                            col = (c * 2 + which) * 128
                            e.matmul(st[:, col:col + 128], lhsT=KTsb[c * 64:(c + 1) * 64, hp, kk0:kk0 + 127 * d + 1:d], rhs=qap,
                                     start=True, stop=False)
                            i_ = e.matmul(st[:, col:col + 128], lhsT=ident[:], rhs=msk[:, (mprev if which == 0 else 1), :],
                                          start=False, stop=True)
                    return i_
                kb.op("pe", f, [KTsb, QTsb, ident, msk], [st])
                pt = ptr.next()
                ACT(kb, pt[:], st[:], AF.Exp, [st], [pt], scale=0.125)
                return pt

            def emit_PV(u, pt):
                qi, hp = u
                gi, d, r, n = qblocks[qi]
                if hp == 0:
                    state["O"] = pO.next()
                O = state["O"]
                vp, vc = vt[qi]
                banks = set()

                def f(e):
                    for c in range(2):
                        h = hp * 2 + c
                        bk, off = h // 7, (h % 7) * 65
                        banks.add(bk)
                        e.matmul(O[bk][:, off:off + 65], lhsT=pt[:, (c * 2) * 128:(c * 2 + 1) * 128], rhs=vp[:, h * 65:(h + 1) * 65],
                                 start=True, stop=False)
                        i_ = e.matmul(O[bk][:, off:off + 65], lhsT=pt[:, (c * 2 + 1) * 128:(c * 2 + 2) * 128], rhs=vc[:, h * 65:(h + 1) * 65],
                                      start=False, stop=True)
                    return i_
                hs = (hp * 2, hp * 2 + 1)
                wb = [O[bk] for bk in sorted(set(h // 7 for h in hs))]
                kb.op("pe", f, [pt, vp, vc], wb)
                if hp == 7:
                    o = osb.next()
                    CP(kb, "dve", o[:, 0:455], O[0][:, 0:455], [O[0]], [o])
                    CP(kb, "act", o[:, 455:910], O[1][:, 0:455], [O[1]], [o])
                    CP(kb, "dve", o[:, 910:1040], O[2][:, 0:130], [O[2]], [o])
                    A0 = (2048 * R + 128 * d * n) // d
                    dst = ON[gi].rearrange("(a s) f -> a s f", s=d)[A0:A0 + 128, r, :]
                    kb.dma([(dst, o[:])], reads=[o], sembuf=o)
                    del vt[qi]

            loadv(0)
            loadv(1)
            prev = None
            for ui, u in enumerate(units):
                if u[1] == 0 and u[0] + 2 < len(qblocks):
                    loadv(u[0] + 2)
                pt = emit_ST(u)
                if prev is not None:
                    emit_PV(*prev)
                prev = (u, pt)
            emit_PV(*prev)
        kb.end_phase()


class MergeLoader:
    def __init__(self, ON):
        self.ON = ON

    def alloc(self, kb, ph):
        return {
            "a": Ring([[kb.sb(ph, "M_a%d_%d" % (i, g), [128, 1040], F32) for g in range(3)] for i in range(6)]),
            "s1": Ring([kb.sb(ph, "M_s1%d" % i, [128, 1040], F32) for i in range(2)]),
            "s2": Ring([kb.sb(ph, "M_s2%d" % i, [128, 1040], F32) for i in range(2)]),
            "rd": Ring([kb.sb(ph, "M_rd%d" % i, [128, 16], F32) for i in range(2)]),
        }

    def load(self, kb, ex, j, m):
        a = ex["a"].next()
        for g in range(3):
            kb.dma([(a[g][:], self.ON[g][j * 128:(j + 1) * 128, :])], writes=[a[g]], sembuf=a[g])
        ex.setdefault("pending", {})[id(m)] = a

    def finish(self, kb, ex, m):
        a = ex["pending"].pop(id(m))
        s1 = ex["s1"].next()
        TT(kb, "dve", s1[:], a[0][:], a[1][:], ALU.add, [a[0], a[1]], [s1])
        s2 = ex["s2"].next()
        TT(kb, "dve", s2[:], s1[:], a[2][:], ALU.add, [s1, a[2]], [s2])
        rd = ex["rd"].next()
        s3 = s2[:].rearrange("p (h d) -> p h d", h=16)
        kb.op("dve", lambda e: e.reciprocal(out=rd[:], in_=s3[:, :, 64]), [s2], [rd])
        TT(kb, "dve", m[:].rearrange("p (h d) -> p h d", h=16), s3[:, :, 0:64], rd[:].unsqueeze(2).broadcast_to([128, 16, 64]),
           ALU.mult, [s2, rd], [m])


def const_l1(h):
    c = np.arange(128)[:, None]
    a = np.arange(128)[None, :]
    mprev = np.where(c >= a, 0.0, NEG).astype(np.float32)
    mcur = np.where(c <= a, 0.0, NEG).astype(np.float32)
    full = np.full((128, 128), NEG, np.float32)
    m1 = np.stack([mprev, mcur, full if h == 0 else mprev], axis=1).astype(NPBF)
    return {"ident": np.eye(128).astype(NPBF), "m1": m1}


def build_l1(S, debug=False, phases="ABT"):
    nc = bass.Bass("TRN2", target_bir_lowering=False)
    So = S // 2
    nbo = So // 128
    nreg = So // 2048
    T = 2048 + So

    def din(name, shape, dt=F32):
        return nc.dram_tensor(name, list(shape), dt, kind="ExternalInput").ap()

    def scr(name, shape, dt):
        return nc.dram_tensor(name, list(shape), dt, kind="ExternalOutput" if debug else "Internal").ap()

    io = {n: din(n, WEIGHT_SHAPES[n]) for n in L1_W}
    io["x_co"] = din("x_co", [T, D])
    io["p_own"] = din("p_own", [So, PLE])
    io["rope_co"] = din("rope_co", [T, 128])
    consts = {"ident": din("ident", [128, 128], BF16), "m1": din("m1", [128, 3, 128], BF16)}
    out = nc.dram_tensor("out_own", [So, D], F32, kind="ExternalOutput").ap()
    KT1 = scr("KT1", [8, 128, T], BF16)
    QT1 = scr("QT1", [8, 128, So], BF16)
    V1 = scr("V1", [T, 1040], BF16)
    ON = [scr("ON%d" % g, [So, 1040], F32) for g in range(3)]
    XA = scr("XA1", [So, D], F32)
    XB = scr("XB1", [So, D], F32)
    kb = KB(nc)
    if "A" in phases:
        phase_L1A(kb, nbo, io["x_co"], io["rope_co"], io["od_w_in"][0], consts, KT1, QT1, V1)
    if "B" in phases:
        phase_L1B(kb, nreg, KT1, QT1, V1, consts, ON)
    if "T" in phases:
        dense_tail(kb, nbo, 1, None, io["x_co"][2048:T, :], io, XA, XB, out, consts, io["od_w_out"][0], mix_loader=MergeLoader(ON))
    return nc, kb


def l1_inputs(x1_b, p1_b, weights, h):
    S = x1_b.shape[0]
    So = S // 2
    lo = So * h
    m = {n: weights[n] for n in L1_W}
    ctx = x1_b[lo - 2048:lo] if h == 1 else np.zeros((2048, D), np.float32)
    m["x_co"] = np.ascontiguousarray(np.concatenate([ctx, x1_b[lo:lo + So]], axis=0))
    m["p_own"] = np.ascontiguousarray(p1_b[lo:lo + So])
    m["rope_co"] = rope_table(np.arange(lo - 2048, lo + So), 8)
    m.update(const_l1(h))
    return m


ALL_W = list(WEIGHT_SHAPES.keys())


def const_fused(h):
    k = np.arange(128)[:, None]
    q = np.arange(128)[None, :]
    tri = np.where(k <= q, 0.0, NEG).astype(np.float32)
    zero = np.zeros((128, 128), np.float32)
    mAB = np.stack([tri if h == 0 else zero, zero if h == 0 else tri], axis=1).astype(NPBF)
    sel = np.zeros((128, 2), np.float32)
    sel[:, h] = 1.0
    c = {"ident": np.eye(128).astype(NPBF), "cU": const_cU(), "sel": sel, "mAB": mAB,
         "hb": np.full((128, 1), NEG if h == 0 else 0.0, np.float32)}
    c["m1"] = const_l1(h)["m1"]
    return c


def build_fused(S, debug=False):
    nc = bass.Bass("TRN2", target_bir_lowering=False)
    nb = S // 128
    So = S // 2
    nbh = So // 128
    nA = NCTX + nbh
    TA = nA * 128

    def din(name, shape, dt=F32):
        return nc.dram_tensor(name, list(shape), dt, kind="ExternalInput").ap()

    def scr(name, shape, dt):
        return nc.dram_tensor(name, list(shape), dt, kind="ExternalOutput" if debug else "Internal").ap()

    io = {n: din(n, WEIGHT_SHAPES[n]) for n in ALL_W}
    io["x_all"] = din("x_all", [S, D])
    io["x_A"] = din("x_A", [TA, D])
    io["p0_A"] = din("p0_A", [TA, PLE])
    io["p1_own"] = din("p1_own", [So, PLE])
    io["rope_all"] = din("rope_all", [S, 128])
    io["rope_A"] = din("rope_A", [TA, 128])
    consts = {"ident": din("ident", [128, 128], BF16), "cU": din("cU", [128, 4, 128]), "sel": din("sel", [128, 2]),
              "mAB": din("mAB", [128, 2, 128], BF16), "hb": din("hb", [128, 1]), "m1": din("m1", [128, 3, 128], BF16)}
    out = nc.dram_tensor("out_own", [So, D], F32, kind="ExternalOutput").ap()
    KT = scr("KT", [nb, 128, 512], BF16)
    V = scr("V", [nb, 128, 516], BF16)
    SS = scr("SS", [nb, 2, 128, 512], BF16)
    QT = scr("QT", [nA, 128, 512], BF16)
    MIX = scr("MIX", [TA, D], BF16)
    XA = scr("XA", [TA, D], F32)
    XB = scr("XB", [TA, D], F32)
    X1 = scr("X1", [TA, D], F32)
    KT1 = scr("KT1", [8, 128, TA], BF16)
    QT1 = scr("QT1", [8, 128, So], BF16)
    V1 = scr("V1", [TA, 1040], BF16)
    ON = [scr("ON%d" % g, [So, 1040], F32) for g in range(3)]
    XA1 = scr("XA1", [So, D], F32)
    XB1 = scr("XB1", [So, D], F32)
    kb = KB(nc)
    phase_A1(kb, S, io["x_all"], io["rope_all"], io["ev_w_in"][0], io["hg_lb_logits"], consts, KT, V, SS, blended=False)
    phase_A2(kb, nA, io["x_A"], io["rope_A"], io["ev_w_in"][0], io["hg_lb_logits"], io["hg_norm_g"], consts, SS, QT, MIX,
             ss_pick=lambda i: (max(i - NCTX, 0), i + nbh - NCTX))
    phase_B(kb, S, nA, KT, V, QT, io["da_lambda"][0], io["da_subln_g"], consts, MIX, keyspec=make_keyspec_ctx(nbh))
    dense_tail(kb, nA, 0, MIX, io["x_A"], io, XA, XB, X1, consts, io["ev_w_out"][0], p_in=io["p0_A"])
    phase_L1A(kb, nbh, X1, io["rope_A"], io["od_w_in"][0], consts, KT1, QT1, V1)
    phase_L1B(kb, So // 2048, KT1, QT1, V1, consts, ON)
    dense_tail(kb, nbh, 1, None, X1[2048:TA, :], io, XA1, XB1, out, consts, io["od_w_out"][0], mix_loader=MergeLoader(ON),
               p_in=io["p1_own"])
    return nc, kb


def fused_inputs(x_b, p_b, weights, h):
    S = x_b.shape[0]
    So = S // 2
    lo = So * h
    m = {n: weights[n] for n in ALL_W}
    m["x_all"] = x_b
    if h == 0:
        m["x_A"] = np.ascontiguousarray(np.concatenate([np.zeros((2048, D), np.float32), x_b[0:So]], axis=0))
        m["p0_A"] = np.ascontiguousarray(np.concatenate([np.zeros((2048, PLE), np.float32), p_b[0, 0:So]], axis=0))
    else:
        m["x_A"] = np.ascontiguousarray(x_b[lo - 2048:lo + So])
        m["p0_A"] = np.ascontiguousarray(p_b[0, lo - 2048:lo + So])
    m["p1_own"] = np.ascontiguousarray(p_b[1, lo:lo + So])
    m["rope_all"] = rope_table(np.arange(S), 8)
    m["rope_A"] = rope_table(np.arange(lo - 2048, lo + So), 8)
    m.update(const_fused(h))
    return m


_PROGS = {}


def _prog(kind, S):
    key = (kind, S)
    if key not in _PROGS:
        _PROGS[key] = (build_l0(S) if kind == 0 else build_l1(S) if kind == 1 else build_fused(S))[0]
    return _PROGS[key]


def kernel(**inputs):
    inputs = {k: np.ascontiguousarray(np.asarray(v, dtype=np.float32)) for k, v in inputs.items()}
    x, p = inputs["x"], inputs["p"]
    B, S, _ = x.shape
    So = S // 2
    n = 2 * B
    weights = {k: v for k, v in inputs.items() if k not in ("x", "p")}
    maps = [fused_inputs(x[c // 2], p[:, c // 2], weights, c % 2) for c in range(n)]
    res = run_bass_kernel_spmd(_prog(2, S), maps, core_ids=list(range(n)))
    out = np.empty((B, S, D), np.float32)
    for c in range(n):
        b, h = c // 2, c % 2
        out[b, h * So:(h + 1) * So] = res.results[c]["out_own"]
    return out
```

```python
import contextlib
import math
import numpy as np
import ml_dtypes
import concourse.bass as bass
import concourse.mybir as mybir
from concourse.bass_utils import run_bass_kernel_spmd

F32 = mybir.dt.float32
BF16 = mybir.dt.bfloat16
AF = mybir.ActivationFunctionType
ALU = mybir.AluOpType
AX = mybir.AxisListType
NPBF = ml_dtypes.bfloat16

D = 1024
FFN = 4096
PLE = 256
ALPHA = 4.0 ** 0.25
LN_EPS = 1e-5
LAM_INIT0 = 0.8 - 0.6 * math.exp(-0.3 * 0)
NEG = -30000.0
A2_STOP = 99
ROPE_THETA = 500000.0


class Buf:
    __slots__ = ("t", "w", "r", "dsem", "psum")

    def __init__(self, t):
        self.t = t
        self.w = None
        self.r = []
        self.dsem = None
        self.psum = False

    def __getitem__(self, idx):
        return self.t[idx]


class Ring:
    def __init__(self, bufs):
        self.bufs = bufs
        self.i = 0

    def next(self):
        b = self.bufs[self.i % len(self.bufs)]
        self.i += 1
        return b


class Eng:
    def __init__(self, name, eng, semidx):
        self.name = name
        self.eng = eng
        self.semidx = semidx
        self.waited = {}


class KB:
    def __init__(self, nc):
        self.nc = nc
        self.es = contextlib.ExitStack()
        self.sems = []
        self.semcnt = []
        self.semdma = []
        self.free_dsems = []
        self.E = {}
        for name, eng in (("pe", nc.tensor), ("act", nc.scalar), ("dve", nc.vector),
                          ("pool", nc.gpsimd), ("sp", nc.sync)):
            si = self._newsem("e_" + name, False)
            self.E[name] = Eng(name, eng, si)
        self.n_inst = 0
        self.phase_bufs = []

    def _newsem(self, name, isdma):
        s = self.es.enter_context(self.nc.semaphore(name))
        self.sems.append(s)
        self.semcnt.append(0)
        self.semdma.append(isdma)
        return len(self.sems) - 1

    def get_dsem(self):
        if self.free_dsems:
            return self.free_dsems.pop()
        return self._newsem("d%d" % len(self.sems), True)

    def sb(self, ph, name, shape, dtype):
        self.uid = getattr(self, "uid", 0) + 1
        name = "%s_u%d" % (name, self.uid)
        b = Buf(ph.enter_context(self.nc.sbuf_tensor(name, list(shape), dtype)))
        self.phase_bufs.append(b)
        return b

    def uname(self, name):
        self.uid = getattr(self, "uid", 0) + 1
        return "%s_u%d" % (name, self.uid)

    def view(self, ap):
        b = Buf(ap)
        self.phase_bufs.append(b)
        return b

    def _wait(self, E, toks):
        for (si, val) in toks:
            if self.semdma[si]:
                val = self.semcnt[si]
            if si == E.semidx and E.name == "pe":
                continue
            if E.waited.get(si, 0) >= val:
                continue
            E.eng.wait_ge(self.sems[si], val)
            E.waited[si] = val
            self.n_inst += 1

    def _deps(self, reads, writes):
        need = []
        for b in reads:
            if b.w is not None:
                need.append(b.w)
            if b.psum:
                need.extend(b.r)
        for b in writes:
            if b.w is not None:
                need.append(b.w)
            need.extend(b.r)
        return need

    def _commit(self, tok, reads, writes):
        for b in reads:
            if b.psum:
                b.w = tok
                b.r = []
            else:
                b.r.append(tok)
        for b in writes:
            b.w = tok
            b.r = []

    def op(self, en, fn, reads=(), writes=()):
        E = self.E[en]
        self._wait(E, self._deps(reads, writes))
        inst = fn(E.eng)
        si = E.semidx
        self.semcnt[si] += 1
        inst.then_inc(self.sems[si], 1)
        tok = (si, self.semcnt[si])
        self._commit(tok, reads, writes)
        self.n_inst += 1
        return tok

    def dma(self, pairs, reads=(), writes=(), sembuf=None, q="sp"):
        E = self.E[q]
        self._wait(E, self._deps(reads, writes))
        if sembuf.dsem is None:
            sembuf.dsem = self.get_dsem()
        si = sembuf.dsem
        for (o, i) in pairs:
            E.eng.dma_start(out=o, in_=i).then_inc(self.sems[si], 16)
            self.semcnt[si] += 16
            self.n_inst += 1
        tok = (si, self.semcnt[si])
        self._commit(tok, reads, writes)
        return tok

    def barrier(self):
        allt = [(si, self.semcnt[si]) for si in range(len(self.sems)) if self.semcnt[si] > 0]
        for E in self.E.values():
            self._wait(E, allt)

    def end_phase(self):
        self.barrier()
        for b in self.phase_bufs:
            if b.dsem is not None:
                self.free_dsems.append(b.dsem)
                b.dsem = None
        self.phase_bufs = []


def TT(kb, en, out, a, b, op, reads, writes):
    return kb.op(en, lambda e: e.tensor_tensor(out=out, in0=a, in1=b, op=op), reads, writes)


def TS(kb, en, out, a, s1, s2, op0, op1, reads, writes):
    if op1 is None:
        return kb.op(en, lambda e: e.tensor_scalar(out=out, in0=a, scalar1=s1, scalar2=None, op0=op0), reads, writes)
    return kb.op(en, lambda e: e.tensor_scalar(out=out, in0=a, scalar1=s1, scalar2=s2, op0=op0, op1=op1), reads, writes)


def STT(kb, out, a, s, b, op0, op1, reads, writes):
    return kb.op("dve", lambda e: e.scalar_tensor_tensor(out=out, in0=a, scalar=s, in1=b, op0=op0, op1=op1), reads, writes)


def ACT(kb, out, in_, func, reads, writes, scale=1.0, bias=0.0, accum=None):
    if accum is not None:
        return kb.op("act", lambda e: e.activation(out=out, in_=in_, func=func, scale=scale, bias=bias, accum_out=accum), reads, writes)
    return kb.op("act", lambda e: e.activation(out=out, in_=in_, func=func, scale=scale, bias=bias), reads, writes)


def CP(kb, en, out, in_, reads, writes):
    if en == "act":
        return kb.op("act", lambda e: e.copy(out=out, in_=in_), reads, writes)
    return kb.op(en, lambda e: e.tensor_copy(out=out, in_=in_), reads, writes)


def rstd_act(kb, out, in_, tmp, scale, reads_buf, tmp_buf, out_buf):
    ACT(kb, tmp, in_, AF.Ln, [reads_buf], [tmp_buf], scale=scale, bias=LN_EPS)
    ACT(kb, out, tmp, AF.Exp, [tmp_buf], [out_buf], scale=-0.5)


def sigmoid_exp(kb, out, in_, in_buf, t1, t2, out_buf, engs=None):
    ACT(kb, t1[:], in_, AF.Exp, [in_buf], [t1], scale=-1.0)
    ACT(kb, t2[:], t1[:], AF.Ln, [t1], [t2], bias=1.0)
    ACT(kb, out, t2[:], AF.Exp, [t2], [out_buf], scale=-1.0)


def run_pairs(n, body, load, width=2, prefetch=2):
    state = {"n": 0}

    def ensure(upto):
        while state["n"] <= min(upto, n - 1):
            load(state["n"])
            state["n"] += 1
    j = 0
    while j < n:
        grp = list(range(j, min(n, j + width)))
        ensure(grp[-1] + prefetch)
        alive = [body(b) for b in grp]
        while alive:
            for g in list(alive):
                try:
                    next(g)
                except StopIteration:
                    alive.remove(g)
        j += width


class PSUM:
    def __init__(self, kb, ph):
        kb.uid = getattr(kb, "uid", 0) + 1
        self.t = ph.enter_context(kb.nc.psum_tensor("PSALL_u%d" % kb.uid, [128, 4096], F32))
        self.kb = kb

    def bank(self, k, dtype=F32):
        ap = self.t[:, k * 512:(k + 1) * 512]
        if dtype == BF16:
            ap = ap.bitcast(BF16)
        b = self.kb.view(ap)
        b.psum = True
        return b


def load_weight(kb, W, src, col_slices, krows):
    ncols = sum(hi - lo for lo, hi in col_slices)
    with contextlib.ExitStack() as st:
        PW = 2048
        stage = Ring([kb.sb(st, "wst%d" % i, [128, PW], F32) for i in range(4)])
        engs = ["dve", "dve", "act"]
        n = 0
        for kc in range(krows):
            dst0 = 0
            for (lo, hi) in col_slices:
                c = lo
                while c < hi:
                    w = min(PW, hi - c)
                    sbuf = stage.next()
                    kb.dma([(sbuf[:, 0:w], src[kc * 128:(kc + 1) * 128, c:c + w])], writes=[sbuf], sembuf=sbuf,
                           q=("sp" if n % 2 == 0 else "pool"))
                    CP(kb, engs[n % 2], W[kc][:, dst0:dst0 + w], sbuf[:, 0:w], [sbuf], [W[kc]])
                    n += 1
                    dst0 += w
                    c += w
        kb.barrier()


def transposes(kb, psT, src, ident, nblk, reads):
    def f(e):
        for c in range(nblk):
            i = e.transpose(psT[:, c * 128:(c + 1) * 128], src[:, c * 128:(c + 1) * 128], ident[:])
        return i
    return kb.op("pe", f, reads + [ident], [psT])


def rope(kb, src, dst, rp, ng, tmps):
    s3 = src[:].rearrange("p (g d) -> p g d", g=ng)
    d3 = dst[:].rearrange("p (g d) -> p g d", g=ng)
    cos = rp[:, 0:ng * 8].rearrange("p (g d) -> p g d", g=ng)
    sin = rp[:, ng * 8:2 * ng * 8].rearrange("p (g d) -> p g d", g=ng)
    t1, t2, t3, t4 = tmps
    v = lambda t: t[:, 0:ng * 8].rearrange("p (g d) -> p g d", g=ng)
    TT(kb, "dve", v(t1), s3[:, :, 0:8], cos, ALU.mult, [src, rp], [t1])
    TT(kb, "dve", v(t2), s3[:, :, 8:16], sin, ALU.mult, [src, rp], [t2])
    TT(kb, "dve", v(t3), s3[:, :, 0:8], sin, ALU.mult, [src, rp], [t3])
    TT(kb, "dve", v(t4), s3[:, :, 8:16], cos, ALU.mult, [src, rp], [t4])
    TT(kb, "dve", d3[:, :, 0:8], v(t1), v(t2), ALU.subtract, [t1, t2], [dst])
    TT(kb, "dve", d3[:, :, 8:16], v(t3), v(t4), ALU.add, [t3, t4], [dst])
    CP(kb, "dve", d3[:, :, 16:64], s3[:, :, 16:64], [src], [dst])


def layer_norm(kb, y, g, b, st, mv, sd, rs, tmp):
    ACT(kb, tmp[:], y[:], AF.Copy, [y], [tmp, st], accum=st[:, 0:1])
    ACT(kb, tmp[:], y[:], AF.Square, [y], [tmp, st], accum=st[:, 1:2])
    TS(kb, "dve", mv[:, 0:1], st[:, 0:1], 1.0 / D, None, ALU.mult, None, [st], [mv])
    TT(kb, "dve", mv[:, 1:2], mv[:, 0:1], mv[:, 0:1], ALU.mult, [mv], [mv])
    STT(kb, mv[:, 2:3], st[:, 1:2], 1.0 / D, mv[:, 1:2], ALU.mult, ALU.subtract, [st, mv], [mv])
    rstd_act(kb, rs[:, 0:1], mv[:, 2:3], sd[:, 0:1], 1.0, mv, sd, rs)
    STT(kb, rs[:, 1:2], mv[:, 0:1], -1.0, rs[:, 0:1], ALU.mult, ALU.mult, [mv, rs], [rs])
    kb.op("act", lambda e: e.activation(out=tmp[:], in_=y[:], func=AF.Identity, scale=rs[:, 0:1], bias=rs[:, 1:2]),
          [y, rs], [tmp])
    TT(kb, "dve", y[:], tmp[:], g[:], ALU.mult, [tmp, g], [y])
    TT(kb, "dve", tmp[:], y[:], b[:], ALU.add, [y, b], [tmp])
    return tmp


def phase_C(kb, nbo, mix_src, xres, w_out, ln_g, ln_b, xa_out, consts, mix_loader=None):
    with contextlib.ExitStack() as ph:
        Wt = ph.enter_context(kb.nc.sbuf_tensor(kb.uname("C_W"), [128, 8, 1024], BF16))
        W = [kb.view(Wt[:, k, :]) for k in range(8)]
        load_weight(kb, W, w_out, [(0, 1024)], 8)
        ident = kb.sb(ph, "C_id", [128, 128], BF16)
        kb.dma([(ident[:], consts["ident"][:, :])], writes=[ident], sembuf=ident)
        gt = kb.sb(ph, "C_g", [128, 1024], F32)
        bt = kb.sb(ph, "C_b", [128, 1024], F32)
        kb.dma([(gt[:], ln_g.partition_broadcast(128))], writes=[gt], sembuf=gt)
        kb.dma([(bt[:], ln_b.partition_broadcast(128))], writes=[bt], sembuf=bt)
        ps = PSUM(kb, ph)
        pT = Ring([ps.bank(0, BF16), ps.bank(1, BF16)])
        po = Ring([[ps.bank(2), ps.bank(3)], [ps.bank(4), ps.bank(5)]])
        mixr = Ring([kb.sb(ph, "C_mix%d" % i, [128, 1024], BF16) for i in range(6)])
        xr = Ring([kb.sb(ph, "C_x%d" % i, [128, 1024], F32) for i in range(6)])
        mT = Ring([kb.sb(ph, "C_mT%d" % i, [128, 1024], BF16) for i in range(4)])
        yr = Ring([kb.sb(ph, "C_y%d" % i, [128, 1024], F32) for i in range(4)])
        tmpr = Ring([kb.sb(ph, "C_t%d" % i, [128, 1024], F32) for i in range(4)])
        sm = Ring([[kb.sb(ph, "C_s%d_%d" % (i, k), [128, 12], F32) for k in range(4)] for i in range(4)])
        extra = mix_loader.alloc(kb, ph) if mix_loader is not None else None
        loaded = {}

        def load(j):
            x = xr.next()
            kb.dma([(x[:], xres[j * 128:(j + 1) * 128, :])], writes=[x], sembuf=x)
            m = mixr.next()
            if mix_loader is None:
                kb.dma([(m[:], mix_src[j * 128:(j + 1) * 128, :])], writes=[m], sembuf=m)
            else:
                mix_loader.load(kb, extra, j, m)
            loaded[j] = (x, m)

        def body(j):
            x, m = loaded.pop(j)
            if mix_loader is not None:
                mix_loader.finish(kb, extra, m)
            p = pT.next()
            transposes(kb, p, m, ident, 8, [m])
            t = mT.next()
            CP(kb, "act", t[:], p[:], [p], [t])
            yield
            pa, pb = po.next()
            for half, pp in ((0, pa), (1, pb)):
                def f(e, half=half, pp=pp):
                    for k in range(8):
                        i = e.matmul(pp[:], lhsT=t[:, k * 128:(k + 1) * 128], rhs=W[k][:, half * 512:(half + 1) * 512],
                                     start=(k == 0), stop=(k == 7))
                    return i
                kb.op("pe", f, [t] + W, [pp])
            y = yr.next()
            STT(kb, y[:, 0:512], x[:, 0:512], ALPHA, pa[:], ALU.mult, ALU.add, [x, pa], [y])
            STT(kb, y[:, 512:1024], x[:, 512:1024], ALPHA, pb[:], ALU.mult, ALU.add, [x, pb], [y])
            yield
            s = sm.next()
            o = layer_norm(kb, y, gt, bt, s[0], s[1], s[2], s[3], tmpr.next())
            kb.dma([(xa_out[j * 128:(j + 1) * 128, :], o[:])], reads=[o], sembuf=o)

        run_pairs(nbo, body, load, width=4)
        kb.end_phase()


def phase_D1(kb, nbo, xa, w1, w2, ln_g, ln_b, xb_out, consts):
    assert nbo % 4 == 0
    with contextlib.ExitStack() as ph:
        W1t = ph.enter_context(kb.nc.sbuf_tensor(kb.uname("D_W1"), [128, 8, FFN], BF16))
        W2t = ph.enter_context(kb.nc.sbuf_tensor(kb.uname("D_W2"), [128, 32, D], BF16))
        W1 = [kb.view(W1t[:, k, :]) for k in range(8)]
        W2 = [kb.view(W2t[:, k, :]) for k in range(32)]
        load_weight(kb, W1, w1, [(0, FFN)], 8)
        load_weight(kb, W2, w2, [(0, D)], 32)
        ident = kb.sb(ph, "D_id", [128, 128], BF16)
        kb.dma([(ident[:], consts["ident"][:, :])], writes=[ident], sembuf=ident)
        gt = kb.sb(ph, "D_g", [128, 1024], F32)
        bt = kb.sb(ph, "D_b", [128, 1024], F32)
        kb.dma([(gt[:], ln_g.partition_broadcast(128))], writes=[gt], sembuf=gt)
        kb.dma([(bt[:], ln_b.partition_broadcast(128))], writes=[bt], sembuf=bt)
        ps = PSUM(kb, ph)
        pT = Ring([ps.bank(0, BF16)])
        ph_ = Ring([ps.bank(1), ps.bank(2), ps.bank(3)])
        po = Ring([[ps.bank(4), ps.bank(5)], [ps.bank(6), ps.bank(7)]])
        xr = Ring([kb.sb(ph, "D_x%d" % i, [128, 1024], F32) for i in range(2)])
        xbr = Ring([kb.sb(ph, "D_xb%d" % i, [128, 1024], BF16) for i in range(2)])
        xT = kb.sb(ph, "D_xT", [128, 8, 512], BF16)
        hT = [kb.sb(ph, "D_hT%d" % c, [128, 512], BF16) for c in range(32)]
        rl = Ring([kb.sb(ph, "D_rl%d" % i, [128, 512], F32) for i in range(2)])
        yr = Ring([kb.sb(ph, "D_y%d" % i, [128, 1024], F32) for i in range(1)])
        tmpr = Ring([kb.sb(ph, "D_t%d" % i, [128, 1024], F32) for i in range(2)])
        sm = Ring([[kb.sb(ph, "D_s%d_%d" % (i, k), [128, 12], F32) for k in range(4)] for i in range(2)])
        ng = nbo // 4
        def stage_load(g):
                for tb in range(4):
                    x = xr.next()
                    j = g * 4 + tb
                    kb.dma([(x[:], xa[j * 128:(j + 1) * 128, :])], writes=[x], sembuf=x)
                    xb = xbr.next()
                    CP(kb, "dve" if tb % 2 == 0 else "dve", xb[:], x[:], [x], [xb])
                    p = pT.next()
                    transposes(kb, p, xb, ident, 8, [xb])
                    CP(kb, "act", xT[:, :, tb * 128:(tb + 1) * 128], p[:].rearrange("p (k t) -> p k t", k=8), [p], [xT])

        def stage_w1(g):
                for c in range(32):
                    pp = ph_.next()

                    def f(e, c=c, pp=pp):
                        for k in range(8):
                            i = e.matmul(pp[:], lhsT=W1[k][:, c * 128:(c + 1) * 128], rhs=xT[:, k, :], start=(k == 0), stop=(k == 7))
                        return i
                    kb.op("pe", f, [xT] + W1, [pp])
                    r = rl.next()
                    ACT(kb, r[:], pp[:], AF.Relu, [pp], [r])
                    TT(kb, "dve" if c % 2 == 0 else "dve", hT[c][:], r[:], r[:], ALU.mult, [r], [hT[c]])

        def stage_w2(g):
                for tb in range(4):
                    pa, pb = po.next()
                    for half, pp in ((0, pa), (1, pb)):
                        def f(e, half=half, pp=pp, tb=tb):
                            for c in range(32):
                                i = e.matmul(pp[:], lhsT=hT[c][:, tb * 128:(tb + 1) * 128], rhs=W2[c][:, half * 512:(half + 1) * 512],
                                             start=(c == 0), stop=(c == 31))
                            return i
                        kb.op("pe", f, hT + W2, [pp])
                    x = xr.next()
                    j = g * 4 + tb
                    kb.dma([(x[:], xa[j * 128:(j + 1) * 128, :])], writes=[x], sembuf=x)
                    y = yr.next()
                    STT(kb, y[:, 0:512], x[:, 0:512], ALPHA, pa[:], ALU.mult, ALU.add, [x, pa], [y])
                    STT(kb, y[:, 512:1024], x[:, 512:1024], ALPHA, pb[:], ALU.mult, ALU.add, [x, pb], [y])
                    s = sm.next()
                    o = layer_norm(kb, y, gt, bt, s[0], s[1], s[2], s[3], tmpr.next())
                    kb.dma([(xb_out[j * 128:(j + 1) * 128, :], o[:])], reads=[o], sembuf=o)

        stage_load(0)
        for g in range(ng):
            stage_w1(g)
            if g + 1 < ng:
                stage_load(g + 1)
            stage_w2(g)
        kb.end_phase()


def phase_D2(kb, nbo, xb_in, p_in, w_gate, w_proj, norm_g, out, consts):
    with contextlib.ExitStack() as ph:
        Wgt = ph.enter_context(kb.nc.sbuf_tensor(kb.uname("E_Wg"), [128, 8, D], BF16))
        Wpt = ph.enter_context(kb.nc.sbuf_tensor(kb.uname("E_Wp"), [128, 2, D], BF16))
        Wg = [kb.view(Wgt[:, k, :]) for k in range(8)]
        Wp = [kb.view(Wpt[:, k, :]) for k in range(2)]
        load_weight(kb, Wg, w_gate, [(0, D)], 8)
        load_weight(kb, Wp, w_proj, [(0, D)], 2)
        ident = kb.sb(ph, "E_id", [128, 128], BF16)
        kb.dma([(ident[:], consts["ident"][:, :])], writes=[ident], sembuf=ident)
        gt = kb.sb(ph, "E_g", [128, 1024], F32)
        kb.dma([(gt[:], norm_g.partition_broadcast(128))], writes=[gt], sembuf=gt)
        ps = PSUM(kb, ph)
        pT = Ring([ps.bank(0, BF16), ps.bank(1, BF16)])
        pg = Ring([[ps.bank(2), ps.bank(3)]])
        pe_ = Ring([[ps.bank(4), ps.bank(5)], [ps.bank(6), ps.bank(7)]])
        er = Ring([kb.sb(ph, "E_er%d" % i, [128, 1024], F32) for i in range(4)])
        xr = Ring([kb.sb(ph, "E_x%d" % i, [128, 1024], F32) for i in range(6)])
        pr = Ring([kb.sb(ph, "E_p%d" % i, [128, 256], F32) for i in range(6)])
        xbr = Ring([kb.sb(ph, "E_xb%d" % i, [128, 1280], BF16) for i in range(2)])
        xT = Ring([kb.sb(ph, "E_xT%d" % i, [128, 1280], BF16) for i in range(4)])
        gsb = Ring([kb.sb(ph, "E_gs%d" % i, [128, 1024], F32) for i in range(4)])
        gt1 = Ring([kb.sb(ph, "E_g1%d" % i, [128, 1024], F32) for i in range(4)])
        gt2 = Ring([kb.sb(ph, "E_g2%d" % i, [128, 1024], F32) for i in range(4)])
        esb = Ring([kb.sb(ph, "E_es%d" % i, [128, 1024], F32) for i in range(2)])
        e2 = Ring([kb.sb(ph, "E_e2%d" % i, [128, 1024], F32) for i in range(2)])
        junk = Ring([kb.sb(ph, "E_jk%d" % i, [128, 512], F32) for i in range(2)])
        outr = Ring([kb.sb(ph, "E_o%d" % i, [128, 1024], F32) for i in range(2)])
        sm = Ring([[kb.sb(ph, "E_s%d_%d" % (i, k), [128, 4], F32) for k in range(4)] for i in range(4)])
        loaded = {}

        def load(j):
            x = xr.next()
            kb.dma([(x[:], xb_in[j * 128:(j + 1) * 128, :])], writes=[x], sembuf=x)
            p = pr.next()
            kb.dma([(p[:], p_in[j * 128:(j + 1) * 128, :])], writes=[p], sembuf=p)
            loaded[j] = (x, p)

        def body(j):
            x, p = loaded.pop(j)
            xb = xbr.next()
            CP(kb, "dve", xb[:, 0:1024], x[:], [x], [xb])
            CP(kb, "dve", xb[:, 1024:1280], p[:], [p], [xb])
            pt1 = pT.next()
            transposes(kb, pt1, xb, ident, 8, [xb])
            t = xT.next()
            CP(kb, "act", t[:, 0:1024], pt1[:], [pt1], [t])
            pt2 = pT.next()

            def f2(e, xb=xb, pt2=pt2):
                for c in range(2):
                    i = e.transpose(pt2[:, c * 128:(c + 1) * 128], xb[:, 1024 + c * 128:1024 + (c + 1) * 128], ident[:])
                return i
            kb.op("pe", f2, [xb, ident], [pt2])
            CP(kb, "dve", t[:, 1024:1280], pt2[:, 0:256], [pt2], [t])
            yield
            ga, gb = pg.next()
            ea, eb = pe_.next()
            for half, pp in ((0, ga), (1, gb)):
                def f(e, half=half, pp=pp, t=t):
                    for k in range(8):
                        i = e.matmul(pp[:], lhsT=t[:, k * 128:(k + 1) * 128], rhs=Wg[k][:, half * 512:(half + 1) * 512],
                                     start=(k == 0), stop=(k == 7))
                    return i
                kb.op("pe", f, [t] + Wg, [pp])
            for half, pp in ((0, ea), (1, eb)):
                def f(e, half=half, pp=pp, t=t):
                    for k in range(2):
                        i = e.matmul(pp[:], lhsT=t[:, 1024 + k * 128:1024 + (k + 1) * 128], rhs=Wp[k][:, half * 512:(half + 1) * 512],
                                     start=(k == 0), stop=(k == 1))
                    return i
                kb.op("pe", f, [t] + Wp, [pp])
            s = sm.next()
            gs = gsb.next()
            g1_, g2_ = gt1.next(), gt2.next()
            ACT(kb, g1_[:, 0:512], ga[:], AF.Exp, [ga], [g1_], scale=-1.0)
            ACT(kb, g1_[:, 512:1024], gb[:], AF.Exp, [gb], [g1_], scale=-1.0)
            jk = junk.next()
            ACT(kb, jk[:], ea[:], AF.Square, [ea], [jk, s[0]], accum=s[0][:, 0:1])
            jk = junk.next()
            ACT(kb, jk[:], eb[:], AF.Square, [eb], [jk, s[0]], accum=s[0][:, 1:2])
            e_r = er.next()
            CP(kb, "dve", e_r[:, 0:512], ea[:], [ea], [e_r])
            CP(kb, "dve", e_r[:, 512:1024], eb[:], [eb], [e_r])
            yield
            ACT(kb, g2_[:], g1_[:], AF.Ln, [g1_], [g2_], bias=1.0)
            ACT(kb, gs[:], g2_[:], AF.Exp, [g2_], [gs], scale=-1.0)
            TT(kb, "dve", s[1][:, 0:1], s[0][:, 0:1], s[0][:, 1:2], ALU.add, [s[0]], [s[1]])
            rstd_act(kb, s[3][:, 0:1], s[1][:, 0:1], s[2][:, 0:1], 1.0 / D, s[1], s[2], s[3])
            es = esb.next()
            kb.op("act", lambda e, es=es, e_r=e_r, s=s: e.activation(out=es[:], in_=e_r[:], func=AF.Copy, scale=s[3][:, 0:1]), [e_r, s[3]], [es])
            ee = e2.next()
            TT(kb, "dve", ee[:], es[:], gt[:], ALU.mult, [es, gt], [ee])
            TT(kb, "dve", es[:], ee[:], gs[:], ALU.mult, [ee, gs], [es])
            o = outr.next()
            TT(kb, "dve", o[:], es[:], x[:], ALU.add, [es, x], [o])
            kb.dma([(out[j * 128:(j + 1) * 128, :], o[:])], reads=[o], sembuf=o)

        run_pairs(nbo, body, load, width=4)
        kb.end_phase()


def hgrn_gate_tables(kb, ph, lb_logits):
    lt = kb.sb(ph, "lbl", [128, 2, 512], F32)
    kb.dma([(lt[:], lb_logits.partition_broadcast(128))], writes=[lt], sembuf=lt)
    dl = kb.sb(ph, "lbd", [128, 512], F32)
    lb = kb.sb(ph, "lb", [128, 512], F32)
    oml = kb.sb(ph, "oml", [128, 512], F32)
    TT(kb, "dve", dl[:], lt[:, 0, :], lt[:, 1, :], ALU.subtract, [lt], [dl])
    lt1 = kb.sb(ph, "lbt1", [128, 512], F32)
    lt2 = kb.sb(ph, "lbt2", [128, 512], F32)
    sigmoid_exp(kb, lb[:], dl[:], dl, lt1, lt2, lb)
    TS(kb, "dve", oml[:], lb[:], -1.0, 1.0, ALU.mult, ALU.add, [lb], [oml])
    return lb, oml


def phase_A1(kb, S, x_all, rope_all, w_in, lb_logits, consts, KT, V, SS, blended=True):
    nb = S // 128
    with contextlib.ExitStack() as ph:
        Wt = ph.enter_context(kb.nc.sbuf_tensor(kb.uname("A1_W"), [128, 8, 2048], BF16))
        W = [kb.view(Wt[:, k, :]) for k in range(8)]
        load_weight(kb, W, w_in, [(512, 1024), (1024, 1536), (2048, 2560), (2560, 3072)], 8)
        ident = kb.sb(ph, "A1_id", [128, 128], BF16)
        kb.dma([(ident[:], consts["ident"][:, :])], writes=[ident], sembuf=ident)
        cU = kb.sb(ph, "A1_cU", [128, 4, 128], F32)
        kb.dma([(cU[:], consts["cU"][:, :, :])], writes=[cU], sembuf=cU)
        sel = kb.sb(ph, "A1_sel", [128, 2], F32)
        kb.dma([(sel[:], consts["sel"][:, :])], writes=[sel], sembuf=sel)
        lb, oml = hgrn_gate_tables(kb, ph, lb_logits)
        ps = PSUM(kb, ph)
        pT = Ring([ps.bank(0, BF16)])
        pp = Ring([ps.bank(1), ps.bank(2), ps.bank(3)])
        pkT = Ring([ps.bank(4, BF16)])
        pbd = Ring([ps.bank(5)])
        pdec = Ring([ps.bank(6)])
        pS = Ring([ps.bank(7)])
        xf = Ring([kb.sb(ph, "A1_xf%d" % i, [128, 1024], F32) for i in range(5)])
        rp = Ring([kb.sb(ph, "A1_rp%d" % i, [128, 128], F32) for i in range(5)])
        xb = Ring([kb.sb(ph, "A1_xb%d" % i, [128, 1024], BF16) for i in range(2)])
        xT = Ring([kb.sb(ph, "A1_xT%d" % i, [128, 1024], BF16) for i in range(3)])
        kf = Ring([kb.sb(ph, "A1_kf%d" % i, [128, 512], F32) for i in range(3)])
        kbf = Ring([kb.sb(ph, "A1_kb%d" % i, [128, 512], BF16) for i in range(2)])
        rt = Ring([[kb.sb(ph, "A1_rt%d_%d" % (i, k), [128, 64], F32) for k in range(4)] for i in range(2)])
        kTs = Ring([kb.sb(ph, "A1_kT%d" % i, [128, 512], BF16) for i in range(3)])
        vb = Ring([kb.sb(ph, "A1_vb%d" % i, [128, 516], BF16) for i in range(3)])
        sg = Ring([kb.sb(ph, "A1_sg%d" % i, [128, 512], F32) for i in range(3)])
        sgt1 = Ring([kb.sb(ph, "A1_sgt1%d" % i, [128, 512], F32) for i in range(2)])
        sgt2 = Ring([kb.sb(ph, "A1_sgt2%d" % i, [128, 512], F32) for i in range(2)])
        hib = Ring([kb.sb(ph, "A1_hi%d" % i, [128, 512], BF16) for i in range(3)])
        f1 = Ring([kb.sb(ph, "A1_f1%d" % i, [128, 512], F32) for i in range(2)])
        ff = Ring([kb.sb(ph, "A1_ff%d" % i, [128, 512], F32) for i in range(2)])
        lf = Ring([kb.sb(ph, "A1_lf%d" % i, [128, 512], F32) for i in range(2)])
        kk = Ring([kb.sb(ph, "A1_kk%d" % i, [128, 512], F32) for i in range(3)])
        ed = Ring([kb.sb(ph, "A1_ed%d" % i, [128, 512], F32) for i in range(3)])
        kst = Ring([kb.sb(ph, "A1_ks%d" % i, [128, 512], BF16) for i in range(2)])
        dec = Ring([kb.sb(ph, "A1_dc%d" % i, [128, 8], F32) for i in range(3)])
        Sr = Ring([kb.sb(ph, "A1_S%d" % i, [128, 512], F32) for i in range(2)])
        Sa = [kb.sb(ph, "A1_Sa%d" % c, [128, 512], F32) for c in range(2)]
        Sb = Ring([kb.sb(ph, "A1_Sb%d" % i, [128, 512], BF16) for i in range(4)])
        for v in vb.bufs:
            kb.op("pool", lambda e, v=v: e.memset(v[:], 1.0), [], [v])
        S = Sr.next()
        kb.op("pool", lambda e: e.memset(S[:], 0.0), [], [S])
        loaded = {}

        def load(tb):
            x = xf.next()
            kb.dma([(x[:], x_all[tb * 128:(tb + 1) * 128, :])], writes=[x], sembuf=x)
            r = rp.next()
            kb.dma([(r[:], rope_all[tb * 128:(tb + 1) * 128, :])], writes=[r], sembuf=r)
            loaded[tb] = (x, r)

        def mm_group(pbuf, t, col0):
            def f(e):
                for k in range(8):
                    i = e.matmul(pbuf[:], lhsT=t[:, k * 128:(k + 1) * 128], rhs=W[k][:, col0:col0 + 512], start=(k == 0), stop=(k == 7))
                return i
            kb.op("pe", f, [t] + W, [pbuf])

        stS = {'S': S}

        def body(tb):
            x, r = loaded.pop(tb)
            b = xb.next()
            CP(kb, "dve", b[:], x[:], [x], [b])
            p = pT.next()
            transposes(kb, p, b, ident, 8, [b])
            t = xT.next()
            CP(kb, "act", t[:], p[:], [p], [t])
            yield
            pk = pp.next()
            mm_group(pk, t, 0)
            k_f = kf.next()
            CP(kb, "act", k_f[:], pk[:], [pk], [k_f])
            pv = pp.next()
            mm_group(pv, t, 512)
            v = vb.next()
            CP(kb, "act", v[:].rearrange("p (h d) -> p h d", h=4)[:, :, 0:128], pv[:].rearrange("p (h d) -> p h d", h=4), [pv], [v])
            kb.dma([(V[tb, :, :], v[:])], reads=[v], sembuf=v)
            phf = pp.next()
            mm_group(phf, t, 1024)
            s_g = sg.next()
            sigmoid_exp(kb, s_g[:], phf[:], phf, sgt1.next(), sgt2.next(), s_g)
            phi = pp.next()
            mm_group(phi, t, 1536)
            h_i = hib.next()
            CP(kb, "dve", h_i[:], phi[:], [phi], [h_i])
            yield
            k_b = kbf.next()
            rope(kb, k_f, k_b, r, 8, rt.next())
            pkt = pkT.next()
            transposes(kb, pkt, k_b, ident, 4, [k_b])
            kt = kTs.next()
            CP(kb, "act", kt[:], pkt[:, 0:512], [pkt], [kt])
            kb.dma([(KT[tb, :, :], kt[:])], reads=[kt], sembuf=kt)
            f_1 = f1.next()
            TT(kb, "dve", f_1[:], s_g[:], oml[:], ALU.mult, [s_g, oml], [f_1])
            f_ = ff.next()
            TT(kb, "dve", f_[:], f_1[:], lb[:], ALU.add, [f_1, lb], [f_])
            l_f = lf.next()
            ACT(kb, l_f[:], f_[:], AF.Ln, [f_], [l_f])
            k_k = kk.next()
            ACT(kb, k_k[:], f_[:], AF.Copy, [f_], [k_k], scale=-1.0, bias=1.0)
            pb = pbd.next()
            kb.op("pe", lambda e: e.matmul(pb[:], lhsT=cU[:, 2, :], rhs=l_f[:], start=True, stop=True), [cU, l_f], [pb])
            pd = pdec.next()

            def fdec(e, pd=pd, l_f=l_f):
                for h in range(4):
                    i = e.matmul(pd[:, h * 2:(h + 1) * 2], lhsT=l_f[:, h * 128:(h + 1) * 128], rhs=cU[:, 3, 0:2], start=True, stop=True)
                return i
            kb.op("pe", fdec, [cU, l_f], [pd])
            e_d = ed.next()
            ACT(kb, e_d[:], pb[:], AF.Exp, [pb], [e_d])
            d_c = dec.next()
            ACT(kb, d_c[:], pd[:, 0:8], AF.Exp, [pd], [d_c])
            yield
            k_s = kst.next()
            TT(kb, "dve", k_s[:], k_k[:], e_d[:], ALU.mult, [k_k, e_d], [k_s])
            j = tb // 2
            for c in range(2):
                if not blended:
                    sb_ = Sb.next()
                    CP(kb, "act", sb_[:], stS['S'][:], [stS['S']], [sb_])
                    kb.dma([(SS[tb, c, :, :], sb_[:])], reads=[sb_], sembuf=sb_)
                elif tb % 2 == 0:
                    TS(kb, "dve", Sa[c][:], stS['S'][:], sel[:, 0:1], None, ALU.mult, None, [stS['S'], sel], [Sa[c]])
                else:
                    sb_ = Sb.next()
                    STT(kb, sb_[:], stS['S'][:], sel[:, 1:2], Sa[c][:], ALU.mult, ALU.add, [stS['S'], sel, Sa[c]], [sb_])
                    kb.dma([(SS[j, c, :, :], sb_[:])], reads=[sb_], sembuf=sb_)
                psb = pS.next()

                def fS(e, psb=psb, k_s=k_s, h_i=h_i, c=c):
                    for h in range(4):
                        i = e.matmul(psb[:, h * 128:(h + 1) * 128], lhsT=k_s[c * 64:(c + 1) * 64, h * 128:(h + 1) * 128],
                                     rhs=h_i[c * 64:(c + 1) * 64, h * 128:(h + 1) * 128], start=True, stop=True)
                    return i
                kb.op("pe", fS, [k_s, h_i], [psb])
                Sn = Sr.next()
                for h in range(4):
                    STT(kb, Sn[:, h * 128:(h + 1) * 128], stS['S'][:, h * 128:(h + 1) * 128], d_c[:, h * 2 + c:h * 2 + c + 1],
                        psb[:, h * 128:(h + 1) * 128], ALU.mult, ALU.add, [stS['S'], d_c, psb], [Sn])
                stS['S'] = Sn

        run_pairs(nb, body, load, width=3)
        kb.end_phase()


def phase_A2(kb, nbo, x_own, rope_own, w_in, lb_logits, hg_norm_g, consts, SS, QT, MIX, ss_pick=None):
    with contextlib.ExitStack() as ph:
        Wt = ph.enter_context(kb.nc.sbuf_tensor(kb.uname("A2_W"), [128, 8, 2560], BF16))
        W = [kb.view(Wt[:, k, :]) for k in range(8)]
        load_weight(kb, W, w_in, [(0, 512), (1536, 3584)], 8)
        ident = kb.sb(ph, "A2_id", [128, 128], BF16)
        kb.dma([(ident[:], consts["ident"][:, :])], writes=[ident], sembuf=ident)
        cU = kb.sb(ph, "A2_cU", [128, 4, 128], F32)
        kb.dma([(cU[:], consts["cU"][:, :, :])], writes=[cU], sembuf=cU)
        mhg = kb.sb(ph, "A2_mhg", [128, 128], F32)
        kb.dma([(mhg[:], consts["cU"][:, 0, :])], writes=[mhg], sembuf=mhg)
        gtab = kb.sb(ph, "A2_g", [128, 128], F32)
        kb.dma([(gtab[:], hg_norm_g.partition_broadcast(128))], writes=[gtab], sembuf=gtab)
        lb, oml = hgrn_gate_tables(kb, ph, lb_logits)
        ps = PSUM(kb, ph)
        pT = Ring([ps.bank(0, BF16)])
        pp = Ring([ps.bank(1), ps.bank(2), ps.bank(3)])
        pcs = Ring([[ps.bank(4), ps.bank(5)]])
        pt3 = Ring([ps.bank(6, BF16)])
        pA = Ring([ps.bank(7)])
        xf = Ring([kb.sb(ph, "A2_xf%d" % i, [128, 1024], F32) for i in range(4)])
        rp = Ring([kb.sb(ph, "A2_rp%d" % i, [128, 128], F32) for i in range(4)])
        s0r = Ring([kb.sb(ph, "A2_s0%d" % i, [128, 512], BF16) for i in range(3)])
        s1r = Ring([kb.sb(ph, "A2_s1%d" % i, [128, 512], BF16) for i in range(3)])
        xb = Ring([kb.sb(ph, "A2_xb%d" % i, [128, 1024], BF16) for i in range(2)])
        xT = Ring([kb.sb(ph, "A2_xT%d" % i, [128, 1024], BF16) for i in range(3)])
        qf = Ring([kb.sb(ph, "A2_qf%d" % i, [128, 512], F32) for i in range(3)])
        qbf = Ring([kb.sb(ph, "A2_qb%d" % i, [128, 512], BF16) for i in range(2)])
        rt = Ring([[kb.sb(ph, "A2_rt%d_%d" % (i, k), [128, 64], F32) for k in range(4)] for i in range(2)])
        qTs = Ring([kb.sb(ph, "A2_qT%d" % i, [128, 512], BF16) for i in range(2)])
        qs = Ring([kb.sb(ph, "A2_qs%d" % i, [128, 512], F32) for i in range(3)])
        sg = Ring([kb.sb(ph, "A2_sg%d" % i, [128, 512], F32) for i in range(3)])
        sgt1 = Ring([kb.sb(ph, "A2_sgt1%d" % i, [128, 512], F32) for i in range(1)])
        sgt2 = Ring([kb.sb(ph, "A2_sgt2%d" % i, [128, 512], F32) for i in range(1)])
        sgt3 = Ring([kb.sb(ph, "A2_sgt3%d" % i, [128, 512], F32) for i in range(1)])
        hib = Ring([kb.sb(ph, "A2_hi%d" % i, [128, 512], BF16) for i in range(3)])
        gs = Ring([kb.sb(ph, "A2_gs%d" % i, [128, 512], F32) for i in range(3)])
        f1 = Ring([kb.sb(ph, "A2_f1%d" % i, [128, 512], F32) for i in range(1)])
        ff = Ring([kb.sb(ph, "A2_ff%d" % i, [128, 512], F32) for i in range(1)])
        lf = Ring([kb.sb(ph, "A2_lf%d" % i, [128, 512], F32) for i in range(1)])
        kk = Ring([kb.sb(ph, "A2_kk%d" % i, [128, 512], F32) for i in range(3)])
        bsb = Ring([kb.sb(ph, "A2_bs%d" % i, [128, 512], F32) for i in range(3)])
        d1 = Ring([kb.sb(ph, "A2_d1%d" % i, [128, 512], F32) for i in range(3)])
        e1 = Ring([kb.sb(ph, "A2_e1%d" % i, [128, 512], F32) for i in range(1)])
        e2 = Ring([kb.sb(ph, "A2_e2%d" % i, [128, 512], F32) for i in range(1)])
        e3 = Ring([kb.sb(ph, "A2_e3%d" % i, [128, 512], F32) for i in range(1)])
        Qh = Ring([kb.sb(ph, "A2_Qh%d" % i, [128, 512], BF16) for i in range(2)])
        Kh = Ring([kb.sb(ph, "A2_Kh%d" % i, [128, 512], BF16) for i in range(2)])
        qo = Ring([kb.sb(ph, "A2_qo%d" % i, [128, 512], BF16) for i in range(2)])
        QhT = Ring([kb.sb(ph, "A2_QhT%d" % i, [128, 512], BF16) for i in range(3)])
        KhT = Ring([kb.sb(ph, "A2_KhT%d" % i, [128, 512], BF16) for i in range(3)])
        qz0 = Ring([kb.sb(ph, "A2_qz0%d" % i, [128, 4, 128], BF16) for i in range(3)])
        qz1 = Ring([kb.sb(ph, "A2_qz1%d" % i, [128, 4, 128], BF16) for i in range(3)])
        Am = Ring([kb.sb(ph, "A2_Am%d" % i, [128, 512], BF16) for i in range(3)])
        jk = Ring([kb.sb(ph, "A2_jk%d" % i, [128, 128], F32) for i in range(2)])
        sm = Ring([[kb.sb(ph, "A2_sm%d_%d" % (i, k), [128, 4], F32) for k in range(4)] for i in range(2)])
        on = Ring([kb.sb(ph, "A2_on%d" % i, [128, 512], F32) for i in range(1)])
        on2 = Ring([kb.sb(ph, "A2_o2%d" % i, [128, 512], F32) for i in range(1)])
        ob = Ring([kb.sb(ph, "A2_ob%d" % i, [128, 512], BF16) for i in range(2)])
        for z in qz0.bufs + qz1.bufs:
            kb.op("pool", lambda e, z=z: e.memset(z[:], 0.0), [], [z])
        loaded = {}
        if ss_pick is not None:
            cand = Ring([kb.sb(ph, "A2_cd%d" % i, [128, 512], BF16) for i in range(16)])
            candf = Ring([kb.sb(ph, "A2_cf%d" % i, [128, 512], F32) for i in range(1)])
            sel = kb.sb(ph, "A2_sel", [128, 2], F32)
            kb.dma([(sel[:], consts["sel"][:, :])], writes=[sel], sembuf=sel)

        def load(j):
            x = xf.next()
            kb.dma([(x[:], x_own[j * 128:(j + 1) * 128, :])], writes=[x], sembuf=x)
            r = rp.next()
            kb.dma([(r[:], rope_own[j * 128:(j + 1) * 128, :])], writes=[r], sembuf=r)
            if ss_pick is None:
                a0 = s0r.next()
                a1 = s1r.next()
                kb.dma([(a0[:], SS[j, 0, :, :])], writes=[a0], sembuf=a0)
                kb.dma([(a1[:], SS[j, 1, :, :])], writes=[a1], sembuf=a1)
            else:
                g0, g1 = ss_pick(j)
                a0 = []
                for c in (0, 1):
                    c0 = cand.next()
                    kb.dma([(c0[:], SS[g0, c, :, :])], writes=[c0], sembuf=c0)
                    c1 = cand.next()
                    kb.dma([(c1[:], SS[g1, c, :, :])], writes=[c1], sembuf=c1)
                    a0.append((c0, c1))
                a1 = None
            loaded[j] = (x, r, a0, a1)

        def mm_group(pbuf, t, col0):
            def f(e):
                for k in range(8):
                    i = e.matmul(pbuf[:], lhsT=t[:, k * 128:(k + 1) * 128], rhs=W[k][:, col0:col0 + 512], start=(k == 0), stop=(k == 7))
                return i
            kb.op("pe", f, [t] + W, [pbuf])

        def body(j):
            x, r, S0, S1 = loaded.pop(j)
            b = xb.next()
            CP(kb, "dve", b[:], x[:], [x], [b])
            p = pT.next()
            transposes(kb, p, b, ident, 8, [b])
            t = xT.next()
            CP(kb, "act", t[:], p[:], [p], [t])
            yield
            pq = pp.next()
            mm_group(pq, t, 0)
            q_f = qf.next()
            CP(kb, "act", q_f[:], pq[:], [pq], [q_f])
            phq = pp.next()
            mm_group(phq, t, 512)
            q_s = qs.next()
            sq_ = sgt3.next()
            sigmoid_exp(kb, sq_[:], phq[:], phq, sgt1.next(), sgt2.next(), sq_)
            TT(kb, "dve", q_s[:], sq_[:], phq[:], ALU.mult, [sq_, phq], [q_s])
            phf = pp.next()
            mm_group(phf, t, 1024)
            s_g = sg.next()
            sigmoid_exp(kb, s_g[:], phf[:], phf, sgt1.next(), sgt2.next(), s_g)
            phi = pp.next()
            mm_group(phi, t, 1536)
            h_i = hib.next()
            CP(kb, "dve", h_i[:], phi[:], [phi], [h_i])
            phg = pp.next()
            mm_group(phg, t, 2048)
            g_s = gs.next()
            sq_ = sgt3.next()
            sigmoid_exp(kb, sq_[:], phg[:], phg, sgt1.next(), sgt2.next(), sq_)
            TT(kb, "dve", g_s[:], sq_[:], phg[:], ALU.mult, [sq_, phg], [g_s])
            yield
            q_b = qbf.next()
            rope(kb, q_f, q_b, r, 8, rt.next())
            p3 = pt3.next()
            transposes(kb, kb.view(p3[:, 0:512]) if False else p3, q_b, ident, 4, [q_b])
            qt = qTs.next()
            CP(kb, "act", qt[:], p3[:, 0:512], [p3], [qt])
            kb.dma([(QT[j, :, :], qt[:])], reads=[qt], sembuf=qt)
            f_1 = f1.next()
            TT(kb, "dve", f_1[:], s_g[:], oml[:], ALU.mult, [s_g, oml], [f_1])
            f_ = ff.next()
            TT(kb, "dve", f_[:], f_1[:], lb[:], ALU.add, [f_1, lb], [f_])
            l_f = lf.next()
            ACT(kb, l_f[:], f_[:], AF.Ln, [f_], [l_f])
            k_k = kk.next()
            ACT(kb, k_k[:], f_[:], AF.Copy, [f_], [k_k], scale=-1.0, bias=1.0)
            pb, pbm = pcs.next()
            kb.op("pe", lambda e: e.matmul(pb[:], lhsT=cU[:, 0, :], rhs=l_f[:], start=True, stop=True), [cU, l_f], [pb])
            kb.op("pe", lambda e: e.matmul(pbm[:], lhsT=cU[:, 1, :], rhs=l_f[:], start=True, stop=True), [cU, l_f], [pbm])
            b_s = bsb.next()
            CP(kb, "act", b_s[:], pb[:], [pb], [b_s])
            d_1 = d1.next()
            TT(kb, "dve", d_1[:], b_s[:], pbm[:], ALU.subtract, [b_s, pbm], [d_1])
            yield
            e_1 = e1.next()
            ACT(kb, e_1[:], d_1[:], AF.Exp, [d_1], [e_1])
            e_2 = e2.next()
            ACT(kb, e_2[:], d_1[:], AF.Exp, [d_1], [e_2], scale=-1.0)
            e_3 = e3.next()
            ACT(kb, e_3[:], b_s[:], AF.Exp, [b_s], [e_3])
            Q_h = Qh.next()
            TT(kb, "dve", Q_h[:], q_s[:], e_1[:], ALU.mult, [q_s, e_1], [Q_h])
            K_h = Kh.next()
            TT(kb, "dve", K_h[:], k_k[:], e_2[:], ALU.mult, [k_k, e_2], [K_h])
            q_o = qo.next()
            TT(kb, "dve", q_o[:], q_s[:], e_3[:], ALU.mult, [q_s, e_3], [q_o])
            p3 = pt3.next()
            transposes(kb, p3, Q_h, ident, 4, [Q_h])
            Q_T = QhT.next()
            CP(kb, "act", Q_T[:], p3[:, 0:512], [p3], [Q_T])
            p3 = pt3.next()
            transposes(kb, p3, K_h, ident, 4, [K_h])
            K_T = KhT.next()
            CP(kb, "dve", K_T[:], p3[:, 0:512], [p3], [K_T])
            p3 = pt3.next()
            transposes(kb, p3, q_o, ident, 4, [q_o])
            z0 = qz0.next()
            z1 = qz1.next()
            p3v = p3[:, 0:512].rearrange("p (h t) -> p h t", h=4)
            CP(kb, "act", z0[:, :, 0:64], p3v[:, :, 0:64], [p3], [z0])
            CP(kb, "dve", z1[:, :, 64:128], p3v[:, :, 64:128], [p3], [z1])
            yield
            pa = pA.next()

            def fA(e, pa=pa, K_T=K_T, Q_T=Q_T):
                for h in range(4):
                    i = e.matmul(pa[:, h * 128:(h + 1) * 128], lhsT=K_T[:, h * 128:(h + 1) * 128], rhs=Q_T[:, h * 128:(h + 1) * 128],
                                 start=True, stop=True)
                return i
            kb.op("pe", fA, [K_T, Q_T], [pa])
            A_m = Am.next()
            TT(kb, "dve", A_m[:].rearrange("p (h t) -> p h t", h=4), pa[:].rearrange("p (h t) -> p h t", h=4),
               mhg[:].unsqueeze(1).broadcast_to([128, 4, 128]), ALU.mult, [pa, mhg], [A_m])
            yield
            if ss_pick is not None:
                cands = S0
                blended = []
                for c, ring in ((0, s0r), (1, s1r)):
                    c0, c1 = cands[c]
                    dst = ring.next()
                    tmpb = candf.next()
                    kb.op("act", lambda e, tmpb=tmpb, c0=c0: e.activation(out=tmpb[:], in_=c0[:], func=AF.Copy, scale=sel[:, 0:1]), [c0, sel], [tmpb])
                    STT(kb, dst[:], c1[:], sel[:, 1:2], tmpb[:], ALU.mult, ALU.add, [c1, sel, tmpb], [dst])
                    blended.append(dst)
                S0, S1 = blended
            po = pp.next()

            def fo(e, po=po, A_m=A_m, h_i=h_i, z0=z0, z1=z1, S0=S0, S1=S1):
                for h in range(4):
                    hs = slice(h * 128, (h + 1) * 128)
                    e.matmul(po[:, hs], lhsT=A_m[:, hs], rhs=h_i[:, hs], start=True, stop=False)
                    e.matmul(po[:, hs], lhsT=z0[:, h, :], rhs=S0[:, hs], start=False, stop=False)
                    i = e.matmul(po[:, hs], lhsT=z1[:, h, :], rhs=S1[:, hs], start=False, stop=True)
                return i
            kb.op("pe", fo, [A_m, h_i, z0, z1, S0, S1], [po])
            s = sm.next()
            for h in range(4):
                j_ = jk.next()
                ACT(kb, j_[:], po[:, h * 128:(h + 1) * 128], AF.Square, [po], [j_, s[0]], accum=s[0][:, h:h + 1])
            rstd_act(kb, s[2][:, 0:4], s[0][:, 0:4], s[1][:, 0:4], 1.0 / 128.0, s[0], s[1], s[2])
            o_n = on.next()
            TT(kb, "dve", o_n[:].rearrange("p (h t) -> p h t", h=4), po[:].rearrange("p (h t) -> p h t", h=4),
               s[2][:, 0:4].unsqueeze(2).broadcast_to([128, 4, 128]), ALU.mult, [po, s[2]], [o_n])
            o_2 = on2.next()
            TT(kb, "dve", o_2[:].rearrange("p (h t) -> p h t", h=4), o_n[:].rearrange("p (h t) -> p h t", h=4),
               gtab[:].unsqueeze(1).broadcast_to([128, 4, 128]), ALU.mult, [o_n, gtab], [o_2])
            o_b = ob.next()
            TT(kb, "dve", o_b[:], o_2[:], g_s[:], ALU.mult, [o_2, g_s], [o_b])
            kb.dma([(MIX[j * 128:(j + 1) * 128, 512:1024], o_b[:])], reads=[o_b], sembuf=o_b)

        run_pairs(nbo, body, load, width=3, prefetch=1)
        kb.end_phase()


def keyspec_parity(j):
    return [(k, (k - 2 * j) if k >= 2 * j else None, False) for k in range(2 * j + 2)]


def make_keyspec_ctx(nbh):
    def spec(i):
        g0, g1 = i - NCTX, i + nbh - NCTX
        out = []
        for k in range(g1 + 1):
            if k < g0:
                out.append((k, None, False))
            elif k == g0:
                out.append((k, 0, False))
            elif k < g1:
                out.append((k, None, True))
            else:
                out.append((k, 1, True))
        return out
    return spec


def phase_B(kb, S, nbo, KT, V, QT, da_lambda, sub_g, consts, MIX, keyspec=keyspec_parity):
    nb = S // 128
    with contextlib.ExitStack() as ph:
        KTt = ph.enter_context(kb.nc.sbuf_tensor(kb.uname("B_KT"), [128, nb, 512], BF16))
        Vt = ph.enter_context(kb.nc.sbuf_tensor(kb.uname("B_V"), [128, nb, 516], BF16))
        CH = 8
        nch = (nb + CH - 1) // CH
        KTc = [kb.view(KTt[:, c * CH:min(nb, (c + 1) * CH), :]) for c in range(nch)]
        Vc = [kb.view(Vt[:, c * CH:min(nb, (c + 1) * CH), :]) for c in range(nch)]
        ident = kb.sb(ph, "B_id", [128, 128], BF16)
        kb.dma([(ident[:], consts["ident"][:, :])], writes=[ident], sembuf=ident)
        msk = kb.sb(ph, "B_msk", [128, 2, 128], BF16)
        kb.dma([(msk[:], consts["mAB"][:, :, :])], writes=[msk], sembuf=msk)
        gtab = kb.sb(ph, "B_g", [128, 128], F32)
        kb.dma([(gtab[:], sub_g.partition_broadcast(128))], writes=[gtab], sembuf=gtab)
        lt = kb.sb(ph, "B_lt", [128, 4, 64], F32)
        kb.dma([(lt[:], da_lambda.partition_broadcast(128))], writes=[lt], sembuf=lt)
        l1 = kb.sb(ph, "B_l1", [128, 2, 64], F32)
        l2 = kb.sb(ph, "B_l2", [128, 2], F32)
        l3 = kb.sb(ph, "B_l3", [128, 2], F32)
        nlam = kb.sb(ph, "B_nlam", [128, 1], F32)
        TT(kb, "dve", l1[:, 0, :], lt[:, 0, :], lt[:, 1, :], ALU.mult, [lt], [l1])
        TT(kb, "dve", l1[:, 1, :], lt[:, 2, :], lt[:, 3, :], ALU.mult, [lt], [l1])
        kb.op("dve", lambda e: e.reduce_sum(out=l2[:, 0:2], in_=l1[:], axis=AX.X), [l1], [l2])
        ACT(kb, l3[:], l2[:], AF.Exp, [l2], [l3])
        l4 = kb.sb(ph, "B_l4", [128, 1], F32)
        TT(kb, "dve", l4[:], l3[:, 1:2], l3[:, 0:1], ALU.subtract, [l3], [l4])
        TS(kb, "dve", nlam[:], l4[:], -LAM_INIT0, None, ALU.add, None, [l4], [nlam])
        for c in range(nch):
            lo, hi = c * CH, min(nb, (c + 1) * CH)
            kb.dma([(KTc[c][:], KT[lo:hi, :, :].rearrange("n p f -> p n f"))], writes=[KTc[c]], sembuf=KTc[c])
            kb.dma([(Vc[c][:], V[lo:hi, :, :].rearrange("n p f -> p n f"))], writes=[Vc[c]], sembuf=Vc[c])
        ps = PSUM(kb, ph)
        pst = Ring([ps.bank(0), ps.bank(1), ps.bank(2)])
        pO = Ring([ps.bank(3), ps.bank(4)])
        qr = Ring([kb.sb(ph, "B_q%d" % i, [128, 512], BF16) for i in range(3)])
        ptr = Ring([kb.sb(ph, "B_pt%d" % i, [128, 512], BF16) for i in range(4)])
        rd = Ring([kb.sb(ph, "B_rd%d" % i, [128, 2], F32) for i in range(2)])
        dn = Ring([kb.sb(ph, "B_dn%d" % i, [128, 2], F32) for i in range(2)])
        cf = Ring([kb.sb(ph, "B_cf%d" % i, [128, 1], F32) for i in range(2)])
        t0r = Ring([kb.sb(ph, "B_t0%d" % i, [128, 128], F32) for i in range(2)])
        orr = Ring([kb.sb(ph, "B_o%d" % i, [128, 128], F32) for i in range(2)])
        jk = Ring([kb.sb(ph, "B_jk%d" % i, [128, 128], F32) for i in range(2)])
        sm = Ring([[kb.sb(ph, "B_sm%d_%d" % (i, k), [128, 1], F32) for k in range(3)] for i in range(2)])
        on = Ring([kb.sb(ph, "B_on%d" % i, [128, 128], F32) for i in range(2)])
        mixr = Ring([kb.sb(ph, "B_mx%d" % i, [128, 512], BF16) for i in range(2)])
        loaded = {}

        def load(j):
            q = qr.next()
            kb.dma([(q[:], QT[j, :, :])], writes=[q], sembuf=q)
            loaded[j] = q

        def chunk_bufs(kbs):
            cs = sorted(set(k // CH for k in kbs))
            return [KTc[c] for c in cs], [Vc[c] for c in cs]

        hb = kb.sb(ph, "B_hb", [128, 1], F32)
        kb.dma([(hb[:], consts["hb"][:, :])], writes=[hb], sembuf=hb)
        units = []
        for j in range(nbo):
            ents = keyspec(j)
            groups = []
            for ent in ents:
                if groups and len(groups[-1]) < 4 and groups[-1][-1][2] == ent[2]:
                    groups[-1].append(ent)
                else:
                    groups.append([ent])
            for h in range(4):
                for c in range(2):
                    for gi_, grp in enumerate(groups):
                        units.append((j, h, c, grp, gi_ == 0, gi_ == len(groups) - 1))
        state = {"O": None, "mix": None}

        def emit_ST(u):
            j, h, c, grp, first, last = u
            q = loaded[j]
            st = pst.next()
            kbs = [g_[0] for g_ in grp]
            kts, _ = chunk_bufs(kbs)
            high = grp[0][2]

            def f(e):
                for i, (kbk, mi, _) in enumerate(grp):
                    e_ = e.matmul(st[:, i * 128:(i + 1) * 128], lhsT=KTt[c * 64:(c + 1) * 64, kbk, h * 128:(h + 1) * 128],
                                  rhs=q[c * 64:(c + 1) * 64, h * 128:(h + 1) * 128], start=True, stop=(mi is None))
                    if mi is not None:
                        e_ = e.matmul(st[:, i * 128:(i + 1) * 128], lhsT=ident[:], rhs=msk[:, mi, :], start=False, stop=True)
                return e_
            kb.op("pe", f, kts + [q, ident, msk], [st])
            pt = ptr.next()
            n = len(kbs) * 128
            if high:
                kb.op("act", lambda e: e.activation(out=pt[:, 0:n], in_=st[:, 0:n], func=AF.Exp, scale=0.125, bias=hb[:, 0:1]), [st, hb], [pt])
            else:
                ACT(kb, pt[:, 0:n], st[:, 0:n], AF.Exp, [st], [pt], scale=0.125)
            return pt

        def emit_PV(u, pt):
            j, h, c, grp, first, last = u
            if c == 0 and first:
                state["O"] = pO.next()
            O = state["O"]
            kbs = [g_[0] for g_ in grp]
            _, vs = chunk_bufs(kbs)

            def f(e):
                for i, kbk in enumerate(kbs):
                    e_ = e.matmul(O[:, c * 129:(c + 1) * 129], lhsT=pt[:, i * 128:(i + 1) * 128], rhs=Vt[:, kbk, h * 129:(h + 1) * 129],
                                  start=(first and i == 0), stop=(last and i == len(kbs) - 1))
                return e_
            kb.op("pe", f, vs + [pt], [O])
            if c == 1 and last:
                finalize(j, h, O)

        def finalize(j, h, O):
            if h == 0:
                state["mix"] = mixr.next()
            mx = state["mix"]
            r_d = rd.next()
            d_n = dn.next()
            TS(kb, "dve", d_n[:, 0:2], O[:, 0:258].rearrange("p (c d) -> p c d", c=2)[:, :, 128], 1e-30, None, ALU.max, None, [O], [d_n])
            kb.op("dve", lambda e: e.reciprocal(out=r_d[:, 0:2], in_=d_n[:, 0:2]), [d_n], [r_d])
            c_f = cf.next()
            TT(kb, "dve", c_f[:], r_d[:, 1:2], nlam[:], ALU.mult, [r_d, nlam], [c_f])
            t_0 = t0r.next()
            TS(kb, "dve", t_0[:], O[:, 0:128], r_d[:, 0:1], None, ALU.mult, None, [O, r_d], [t_0])
            o_ = orr.next()
            STT(kb, o_[:], O[:, 129:257], c_f[:, 0:1], t_0[:], ALU.mult, ALU.add, [O, c_f, t_0], [o_])
            s = sm.next()
            j_ = jk.next()
            ACT(kb, j_[:], o_[:], AF.Square, [o_], [j_, s[0]], accum=s[0][:, 0:1])
            rstd_act(kb, s[2][:], s[0][:], s[1][:], 1.0 / 128.0, s[0], s[1], s[2])
            o_n = on.next()
            TS(kb, "dve", o_n[:], o_[:], s[2][:, 0:1], 1.0 - LAM_INIT0, ALU.mult, ALU.mult, [o_, s[2]], [o_n])
            TT(kb, "dve", mx[:, h * 128:(h + 1) * 128], o_n[:], gtab[:], ALU.mult, [o_n, gtab], [mx])
            if h == 3:
                kb.dma([(MIX[j * 128:(j + 1) * 128, 0:512], mx[:])], reads=[mx], sembuf=mx)

        for j in range(min(2, nbo)):
            load(j)
        nxt = min(2, nbo)
        prev = None
        for ui, u in enumerate(units):
            if ui > 0 and u[0] != units[ui - 1][0] and nxt < nbo:
                load(nxt)
                nxt += 1
            pt = emit_ST(u)
            if prev is not None:
                emit_PV(*prev)
            prev = (u, pt)
        emit_PV(*prev)
        kb.end_phase()


def rope_table(pos, ng):
    inv = ROPE_THETA ** (-np.arange(0, 16, 2, dtype=np.float32) / np.float32(16))
    ang = pos.astype(np.float32)[:, None] * inv[None, :].astype(np.float32)
    cos = np.cos(ang).astype(np.float32)
    sin = np.sin(ang).astype(np.float32)
    return np.concatenate([np.tile(cos, (1, ng)), np.tile(sin, (1, ng))], axis=1).astype(np.float32)


def const_cU():
    s = np.arange(128)[:, None]
    t = np.arange(128)[None, :]
    same = (s // 64) == (t // 64)
    cU = np.zeros((128, 4, 128), np.float32)
    cU[:, 0, :] = same & (s <= t)
    cU[:, 1, :] = same & ((s % 64) <= 31)
    cU[:, 2, :] = same & (s > t)
    cU[:, 3, 0] = (np.arange(128) // 64) == 0
    cU[:, 3, 1] = (np.arange(128) // 64) == 1
    return cU


def const_l0(h):
    k = np.arange(128)[:, None]
    q = np.arange(128)[None, :]
    tri = np.where(k <= q, 0.0, NEG).astype(np.float32)
    zero = np.zeros((128, 128), np.float32)
    full = np.full((128, 128), NEG, np.float32)
    mAB = np.stack([tri if h == 0 else zero, full if h == 0 else tri], axis=1).astype(NPBF)
    sel = np.zeros((128, 2), np.float32)
    sel[:, h] = 1.0
    return {"ident": np.eye(128).astype(NPBF), "cU": const_cU(), "sel": sel, "mAB": mAB, "hb": np.zeros((128, 1), np.float32)}


def dense_tail(kb, nbo, l, mix, xres, io, XA, XB, out, consts, w_out, mix_loader=None, p_in=None):
    phase_C(kb, nbo, mix, xres, w_out, io["ln1_g"][l:l + 1, :], io["ln1_b"][l:l + 1, :], XA, consts, mix_loader=mix_loader)
    phase_D1(kb, nbo, XA, io["ffn_w1"][l], io["ffn_w2"][l], io["ln2_g"][l:l + 1, :], io["ln2_b"][l:l + 1, :], XB, consts)
    phase_D2(kb, nbo, XB, io["p_own"] if p_in is None else p_in, io["ple_w_gate"][l], io["ple_w_proj"][l], io["ple_norm_g"][l:l + 1, :], out, consts)


WEIGHT_SHAPES = {
    "ev_w_in": [1, 1024, 3584], "ev_w_out": [1, 1024, 1024], "da_lambda": [1, 4, 64], "da_subln_g": [1, 128],
    "hg_lb_logits": [2, 512], "hg_norm_g": [1, 128], "od_w_in": [1, 1024, 3072], "od_w_out": [1, 1024, 1024],
    "ln1_g": [2, 1024], "ln1_b": [2, 1024], "ffn_w1": [2, 1024, 4096], "ffn_w2": [2, 4096, 1024],
    "ln2_g": [2, 1024], "ln2_b": [2, 1024], "ple_w_proj": [2, 256, 1024], "ple_w_gate": [2, 1024, 1024],
    "ple_norm_g": [2, 1024],
}
L0_W = ["ev_w_in", "ev_w_out", "da_lambda", "da_subln_g", "hg_lb_logits", "hg_norm_g", "ln1_g", "ln1_b", "ffn_w1",
        "ffn_w2", "ln2_g", "ln2_b", "ple_w_proj", "ple_w_gate", "ple_norm_g"]
L1_W = ["od_w_in", "od_w_out", "ln1_g", "ln1_b", "ffn_w1", "ffn_w2", "ln2_g", "ln2_b", "ple_w_proj", "ple_w_gate",
        "ple_norm_g"]


def build_l0(S, debug=False, phases="12BT"):
    nc = bass.Bass("TRN2", target_bir_lowering=False)
    nb, nbo = S // 128, S // 256
    So = S // 2

    def din(name, shape, dt=F32):
        return nc.dram_tensor(name, list(shape), dt, kind="ExternalInput").ap()

    def scr(name, shape, dt):
        return nc.dram_tensor(name, list(shape), dt, kind="ExternalOutput" if debug else "Internal").ap()

    io = {n: din(n, WEIGHT_SHAPES[n]) for n in L0_W}
    io["x_all"] = din("x_all", [S, D])
    io["x_own"] = din("x_own", [So, D])
    io["p_own"] = din("p_own", [So, PLE])
    io["rope_all"] = din("rope_all", [S, 128])
    io["rope_own"] = din("rope_own", [So, 128])
    consts = {"ident": din("ident", [128, 128], BF16), "cU": din("cU", [128, 4, 128]),
              "sel": din("sel", [128, 2]), "mAB": din("mAB", [128, 2, 128], BF16), "hb": din("hb", [128, 1])}
    out = nc.dram_tensor("x1_own", [So, D], F32, kind="ExternalOutput").ap()
    KT = scr("KT", [nb, 128, 512], BF16)
    V = scr("V", [nb, 128, 516], BF16)
    SS = scr("SS", [nbo, 2, 128, 512], BF16)
    QT = scr("QT", [nbo, 128, 512], BF16)
    MIX = scr("MIX", [So, D], BF16)
    XA = scr("XA", [So, D], F32)
    XB = scr("XB", [So, D], F32)
    kb = KB(nc)
    if "1" in phases:
        phase_A1(kb, S, io["x_all"], io["rope_all"], io["ev_w_in"][0], io["hg_lb_logits"], consts, KT, V, SS)
    if "2" in phases:
        phase_A2(kb, nbo, io["x_own"], io["rope_own"], io["ev_w_in"][0], io["hg_lb_logits"], io["hg_norm_g"], consts, SS, QT, MIX)
    if "B" in phases:
        phase_B(kb, S, nbo, KT, V, QT, io["da_lambda"][0], io["da_subln_g"], consts, MIX)
    if "T" in phases:
        dense_tail(kb, nbo, 0, MIX, io["x_own"], io, XA, XB, out, consts, io["ev_w_out"][0])
    return nc, kb


def l0_inputs(x_b, p0_b, weights, h):
    S = x_b.shape[0]
    nb = S // 128
    own = np.arange(nb).reshape(nb // 2, 2)[:, h]
    pos_own = (own[:, None] * 128 + np.arange(128)[None, :]).reshape(-1)
    m = {n: weights[n] for n in L0_W}
    m["x_all"] = x_b
    m["x_own"] = np.ascontiguousarray(x_b[pos_own])
    m["p_own"] = np.ascontiguousarray(p0_b[pos_own])
    m["rope_all"] = rope_table(np.arange(S), 8)
    m["rope_own"] = rope_table(pos_own, 8)
    m.update(const_l0(h))
    return m, pos_own


NCTX = 16


def phase_L1A(kb, nbo, x_co, rope_co, w_in, consts, KT1, QT1, V1):
    nblk = NCTX + nbo
    with contextlib.ExitStack() as ph:
        Wt = ph.enter_context(kb.nc.sbuf_tensor(kb.uname("F_W"), [128, 8, 3072], BF16))
        W = [kb.view(Wt[:, k, :]) for k in range(8)]
        load_weight(kb, W, w_in, [(0, 3072)], 8)
        ident = kb.sb(ph, "F_id", [128, 128], BF16)
        kb.dma([(ident[:], consts["ident"][:, :])], writes=[ident], sembuf=ident)
        ps = PSUM(kb, ph)
        pT = Ring([ps.bank(0, BF16)])
        pp = Ring([ps.bank(1), ps.bank(2), ps.bank(3), ps.bank(4)])
        pkT = Ring([ps.bank(5, BF16), ps.bank(6, BF16)])
        xf = Ring([kb.sb(ph, "F_xf%d" % i, [128, 1024], F32) for i in range(4)])
        rp = Ring([kb.sb(ph, "F_rp%d" % i, [128, 128], F32) for i in range(4)])
        xb = Ring([kb.sb(ph, "F_xb%d" % i, [128, 1024], BF16) for i in range(2)])
        xT = Ring([kb.sb(ph, "F_xT%d" % i, [128, 1024], BF16) for i in range(2)])
        kf = Ring([kb.sb(ph, "F_kf%d" % i, [128, 512], F32) for i in range(3)])
        kbf = Ring([kb.sb(ph, "F_kb%d" % i, [128, 1024], BF16) for i in range(4)])
        rt = Ring([[kb.sb(ph, "F_rt%d_%d" % (i, k), [128, 64], F32) for k in range(4)] for i in range(3)])
        kst = Ring([kb.sb(ph, "F_ks%d" % i, [128, 8, 512], BF16) for i in range(2)])
        qst = Ring([kb.sb(ph, "F_qs%d" % i, [128, 8, 512], BF16) for i in range(2)])
        vb = Ring([kb.sb(ph, "F_vb%d" % i, [128, 16, 65], BF16) for i in range(3)])
        for v in vb.bufs:
            kb.op("pool", lambda e, v=v: e.memset(v[:], 1.0), [], [v])
        loaded = {}

        def load(i):
            x = xf.next()
            kb.dma([(x[:], x_co[i * 128:(i + 1) * 128, :])], writes=[x], sembuf=x)
            r = rp.next()
            kb.dma([(r[:], rope_co[i * 128:(i + 1) * 128, :])], writes=[r], sembuf=r)
            loaded[i] = (x, r)

        def mm_group(pbuf, t, col0):
            def f(e):
                for k in range(8):
                    i_ = e.matmul(pbuf[:], lhsT=t[:, k * 128:(k + 1) * 128], rhs=W[k][:, col0:col0 + 512], start=(k == 0), stop=(k == 7))
                return i_
            kb.op("pe", f, [t] + W, [pbuf])

        def roped(t, r, col0):
            k_b = kbf.next()
            for half in range(2):
                pk = pp.next()
                mm_group(pk, t, col0 + half * 512)
                k_f = kf.next()
                CP(kb, "act", k_f[:], pk[:], [pk], [k_f])
                rope(kb, k_f, _Sub(k_b, half), r, 8, rt.next())
            return k_b

        def to_stage(k_b, stage, slot):
            pkt = pkT.next()
            transposes(kb, pkt, k_b, ident, 8, [k_b])
            CP(kb, "act", stage[:, :, slot * 128:(slot + 1) * 128], pkt[:].rearrange("p (h t) -> p h t", h=8), [pkt], [stage])

        stg = {}

        def body(i):
            x, r = loaded.pop(i)
            b = xb.next()
            CP(kb, "dve", b[:], x[:], [x], [b])
            p = pT.next()
            transposes(kb, p, b, ident, 8, [b])
            t = xT.next()
            CP(kb, "act", t[:], p[:], [p], [t])
            yield
            if i % 4 == 0:
                stg["k"] = kst.next()
            k_stage = stg["k"]
            k_b = roped(t, r, 1024)
            v = vb.next()
            for half in range(2):
                pv = pp.next()
                mm_group(pv, t, 2048 + half * 512)
                CP(kb, "dve" if half == 0 else "act", v[:, half * 8:(half + 1) * 8, 0:64], pv[:].rearrange("p (h d) -> p h d", h=8), [pv], [v])
            kb.dma([(V1[i * 128:(i + 1) * 128, :], v[:].rearrange("p h d -> p (h d)"))], reads=[v], sembuf=v)
            q_b = None
            if i >= NCTX:
                io_ = i - NCTX
                if io_ % 4 == 0:
                    stg["q"] = qst.next()
                q_stage = stg["q"]
                q_b = roped(t, r, 0)
            yield
            to_stage(k_b, k_stage, i % 4)
            if i % 4 == 3:
                g0 = (i // 4) * 512
                kb.dma([(KT1[:, :, g0:g0 + 512].rearrange("h p t -> p h t"), k_stage[:])], reads=[k_stage], sembuf=k_stage)
            if q_b is not None:
                to_stage(q_b, q_stage, io_ % 4)
                if io_ % 4 == 3:
                    g0 = (io_ // 4) * 512
                    kb.dma([(QT1[:, :, g0:g0 + 512].rearrange("h p t -> p h t"), q_stage[:])], reads=[q_stage], sembuf=q_stage)

        run_pairs(nblk, body, load)
        kb.end_phase()


class _Sub:
    def __init__(self, parent, half):
        self.parent = parent
        self.half = half

    def __getitem__(self, idx):
        return self.parent.t[:, self.half * 512:(self.half + 1) * 512][idx]

    def __getattr__(self, name):
        return getattr(self.parent, name)

    def __setattr__(self, name, value):
        if name in ("parent", "half"):
            object.__setattr__(self, name, value)
        else:
            setattr(self.parent, name, value)


def phase_L1B(kb, nreg, KT1, QT1, V1, consts, ON):
    DILS = (1, 4, 16)
    with contextlib.ExitStack() as ph:
        KTsb = kb.sb(ph, "G_KT", [128, 8, 4096], BF16)
        QTsb = kb.sb(ph, "G_QT", [128, 8, 2048], BF16)
        ident = kb.sb(ph, "G_id", [128, 128], BF16)
        kb.dma([(ident[:], consts["ident"][:, :])], writes=[ident], sembuf=ident)
        msk = kb.sb(ph, "G_msk", [128, 3, 128], BF16)
        kb.dma([(msk[:], consts["m1"][:, :, :])], writes=[msk], sembuf=msk)
        ps = PSUM(kb, ph)
        pst = Ring([ps.bank(0), ps.bank(1)])
        pO = Ring([[ps.bank(2), ps.bank(3), ps.bank(4)], [ps.bank(5), ps.bank(6), ps.bank(7)]])
        vr = Ring([kb.sb(ph, "G_v%d" % i, [128, 1040], BF16) for i in range(8)])
        ptr = Ring([kb.sb(ph, "G_pt%d" % i, [128, 512], BF16) for i in range(4)])
        osb = Ring([kb.sb(ph, "G_o%d" % i, [128, 1040], F32) for i in range(3)])
        for R in range(nreg):
            kb.dma([(KTsb[:, h, :], KT1[h, :, 2048 * R:2048 * R + 4096]) for h in range(8)], writes=[KTsb], sembuf=KTsb)
            kb.dma([(QTsb[:, h, :], QT1[h, :, 2048 * R:2048 * (R + 1)]) for h in range(8)], writes=[QTsb], sembuf=QTsb)
            qblocks = [(gi, d, r, n) for gi, d in enumerate(DILS) for n in range(16 // d) for r in range(d)]
            vt = {}

            def loadv(qi):
                gi, d, r, n = qblocks[qi]
                tiles = []
                for which in (0, 1):
                    U0 = 2048 * R + 2048 + 128 * d * (n - 1 + which)
                    v = vr.next()
                    src = V1.rearrange("(a s) f -> a s f", s=d)[U0 // d:U0 // d + 128, r, :]
                    kb.dma([(v[:], src)], writes=[v], sembuf=v)
                    tiles.append(v)
                vt[qi] = tiles

            units = [(qi, hp) for qi in range(len(qblocks)) for hp in range(8)]
            state = {}

            def emit_ST(u):
                qi, hp = u
                gi, d, r, n = qblocks[qi]
                st = pst.next()
                q0 = 128 * d * n + r
                k0 = 2048 + 128 * d * (n - 1) + r
                mprev = 2 if (R == 0 and n == 0) else 0

                def f(e):
                    for c in range(2):
                        qap = QTsb[c * 64:(c + 1) * 64, hp, q0:q0 + 127 * d + 1:d]
                        for which in range(2):
                            kk0 = k0 + which * 128 * d
                            col = (c * 2 + which) * 128
                            e.matmul(st[:, col:col + 128], lhsT=KTsb[c * 64:(c + 1) * 64, hp, kk0:kk0 + 127 * d + 1:d], rhs=qap,
                                     start=True, stop=False)
                            i_ = e.matmul(st[:, col:col + 128], lhsT=ident[:], rhs=msk[:, (mprev if which == 0 else 1), :],
                                          start=False, stop=True)
                    return i_
                kb.op("pe", f, [KTsb, QTsb, ident, msk], [st])
                pt = ptr.next()
                ACT(kb, pt[:], st[:], AF.Exp, [st], [pt], scale=0.125)
                return pt

            def emit_PV(u, pt):
                qi, hp = u
                gi, d, r, n = qblocks[qi]
                if hp == 0:
                    state["O"] = pO.next()
                O = state["O"]
                vp, vc = vt[qi]
                banks = set()

                def f(e):
                    for c in range(2):
                        h = hp * 2 + c
                        bk, off = h // 7, (h % 7) * 65
                        banks.add(bk)
                        e.matmul(O[bk][:, off:off + 65], lhsT=pt[:, (c * 2) * 128:(c * 2 + 1) * 128], rhs=vp[:, h * 65:(h + 1) * 65],
                                 start=True, stop=False)
                        i_ = e.matmul(O[bk][:, off:off + 65], lhsT=pt[:, (c * 2 + 1) * 128:(c * 2 + 2) * 128], rhs=vc[:, h * 65:(h + 1) * 65],
                                      start=False, stop=True)
                    return i_
                hs = (hp * 2, hp * 2 + 1)
                wb = [O[bk] for bk in sorted(set(h // 7 for h in hs))]
                kb.op("pe", f, [pt, vp, vc], wb)
                if hp == 7:
                    o = osb.next()
                    CP(kb, "dve", o[:, 0:455], O[0][:, 0:455], [O[0]], [o])
                    CP(kb, "act", o[:, 455:910], O[1][:, 0:455], [O[1]], [o])
                    CP(kb, "dve", o[:, 910:1040], O[2][:, 0:130], [O[2]], [o])
                    A0 = (2048 * R + 128 * d * n) // d
                    dst = ON[gi].rearrange("(a s) f -> a s f", s=d)[A0:A0 + 128, r, :]
                    kb.dma([(dst, o[:])], reads=[o], sembuf=o)
                    del vt[qi]

            loadv(0)
            loadv(1)
            prev = None
            for ui, u in enumerate(units):
                if u[1] == 0 and u[0] + 2 < len(qblocks):
                    loadv(u[0] + 2)
                pt = emit_ST(u)
                if prev is not None:
                    emit_PV(*prev)
                prev = (u, pt)
            emit_PV(*prev)
        kb.end_phase()


class MergeLoader:
    def __init__(self, ON):
        self.ON = ON

    def alloc(self, kb, ph):
        return {
            "a": Ring([[kb.sb(ph, "M_a%d_%d" % (i, g), [128, 1040], F32) for g in range(3)] for i in range(6)]),
            "s1": Ring([kb.sb(ph, "M_s1%d" % i, [128, 1040], F32) for i in range(2)]),
            "s2": Ring([kb.sb(ph, "M_s2%d" % i, [128, 1040], F32) for i in range(2)]),
            "rd": Ring([kb.sb(ph, "M_rd%d" % i, [128, 16], F32) for i in range(2)]),
        }

    def load(self, kb, ex, j, m):
        a = ex["a"].next()
        for g in range(3):
            kb.dma([(a[g][:], self.ON[g][j * 128:(j + 1) * 128, :])], writes=[a[g]], sembuf=a[g])
        ex.setdefault("pending", {})[id(m)] = a

    def finish(self, kb, ex, m):
        a = ex["pending"].pop(id(m))
        s1 = ex["s1"].next()
        TT(kb, "dve", s1[:], a[0][:], a[1][:], ALU.add, [a[0], a[1]], [s1])
        s2 = ex["s2"].next()
        TT(kb, "dve", s2[:], s1[:], a[2][:], ALU.add, [s1, a[2]], [s2])
        rd = ex["rd"].next()
        s3 = s2[:].rearrange("p (h d) -> p h d", h=16)
        kb.op("dve", lambda e: e.reciprocal(out=rd[:], in_=s3[:, :, 64]), [s2], [rd])
        TT(kb, "dve", m[:].rearrange("p (h d) -> p h d", h=16), s3[:, :, 0:64], rd[:].unsqueeze(2).broadcast_to([128, 16, 64]),
           ALU.mult, [s2, rd], [m])


def const_l1(h):
    c = np.arange(128)[:, None]
    a = np.arange(128)[None, :]
    mprev = np.where(c >= a, 0.0, NEG).astype(np.float32)
    mcur = np.where(c <= a, 0.0, NEG).astype(np.float32)
    full = np.full((128, 128), NEG, np.float32)
    m1 = np.stack([mprev, mcur, full if h == 0 else mprev], axis=1).astype(NPBF)
    return {"ident": np.eye(128).astype(NPBF), "m1": m1}


def build_l1(S, debug=False, phases="ABT"):
    nc = bass.Bass("TRN2", target_bir_lowering=False)
    So = S // 2
    nbo = So // 128
    nreg = So // 2048
    T = 2048 + So

    def din(name, shape, dt=F32):
        return nc.dram_tensor(name, list(shape), dt, kind="ExternalInput").ap()

    def scr(name, shape, dt):
        return nc.dram_tensor(name, list(shape), dt, kind="ExternalOutput" if debug else "Internal").ap()

    io = {n: din(n, WEIGHT_SHAPES[n]) for n in L1_W}
    io["x_co"] = din("x_co", [T, D])
    io["p_own"] = din("p_own", [So, PLE])
    io["rope_co"] = din("rope_co", [T, 128])
    consts = {"ident": din("ident", [128, 128], BF16), "m1": din("m1", [128, 3, 128], BF16)}
    out = nc.dram_tensor("out_own", [So, D], F32, kind="ExternalOutput").ap()
    KT1 = scr("KT1", [8, 128, T], BF16)
    QT1 = scr("QT1", [8, 128, So], BF16)
    V1 = scr("V1", [T, 1040], BF16)
    ON = [scr("ON%d" % g, [So, 1040], F32) for g in range(3)]
    XA = scr("XA1", [So, D], F32)
    XB = scr("XB1", [So, D], F32)
    kb = KB(nc)
    if "A" in phases:
        phase_L1A(kb, nbo, io["x_co"], io["rope_co"], io["od_w_in"][0], consts, KT1, QT1, V1)
    if "B" in phases:
        phase_L1B(kb, nreg, KT1, QT1, V1, consts, ON)
    if "T" in phases:
        dense_tail(kb, nbo, 1, None, io["x_co"][2048:T, :], io, XA, XB, out, consts, io["od_w_out"][0], mix_loader=MergeLoader(ON))
    return nc, kb


def l1_inputs(x1_b, p1_b, weights, h):
    S = x1_b.shape[0]
    So = S // 2
    lo = So * h
    m = {n: weights[n] for n in L1_W}
    ctx = x1_b[lo - 2048:lo] if h == 1 else np.zeros((2048, D), np.float32)
    m["x_co"] = np.ascontiguousarray(np.concatenate([ctx, x1_b[lo:lo + So]], axis=0))
    m["p_own"] = np.ascontiguousarray(p1_b[lo:lo + So])
    m["rope_co"] = rope_table(np.arange(lo - 2048, lo + So), 8)
    m.update(const_l1(h))
    return m


ALL_W = list(WEIGHT_SHAPES.keys())


def const_fused(h):
    k = np.arange(128)[:, None]
    q = np.arange(128)[None, :]
    tri = np.where(k <= q, 0.0, NEG).astype(np.float32)
    zero = np.zeros((128, 128), np.float32)
    mAB = np.stack([tri if h == 0 else zero, zero if h == 0 else tri], axis=1).astype(NPBF)
    sel = np.zeros((128, 2), np.float32)
    sel[:, h] = 1.0
    c = {"ident": np.eye(128).astype(NPBF), "cU": const_cU(), "sel": sel, "mAB": mAB,
         "hb": np.full((128, 1), NEG if h == 0 else 0.0, np.float32)}
    c["m1"] = const_l1(h)["m1"]
    return c


def build_fused(S, debug=False):
    nc = bass.Bass("TRN2", target_bir_lowering=False)
    nb = S // 128
    So = S // 2
    nbh = So // 128
    nA = NCTX + nbh
    TA = nA * 128

    def din(name, shape, dt=F32):
        return nc.dram_tensor(name, list(shape), dt, kind="ExternalInput").ap()

    def scr(name, shape, dt):
        return nc.dram_tensor(name, list(shape), dt, kind="ExternalOutput" if debug else "Internal").ap()

    io = {n: din(n, WEIGHT_SHAPES[n]) for n in ALL_W}
    io["x_all"] = din("x_all", [S, D])
    io["x_A"] = din("x_A", [TA, D])
    io["p0_A"] = din("p0_A", [TA, PLE])
    io["p1_own"] = din("p1_own", [So, PLE])
    io["rope_all"] = din("rope_all", [S, 128])
    io["rope_A"] = din("rope_A", [TA, 128])
    consts = {"ident": din("ident", [128, 128], BF16), "cU": din("cU", [128, 4, 128]), "sel": din("sel", [128, 2]),
              "mAB": din("mAB", [128, 2, 128], BF16), "hb": din("hb", [128, 1]), "m1": din("m1", [128, 3, 128], BF16)}
    out = nc.dram_tensor("out_own", [So, D], F32, kind="ExternalOutput").ap()
    KT = scr("KT", [nb, 128, 512], BF16)
    V = scr("V", [nb, 128, 516], BF16)
    SS = scr("SS", [nb, 2, 128, 512], BF16)
    QT = scr("QT", [nA, 128, 512], BF16)
    MIX = scr("MIX", [TA, D], BF16)
    XA = scr("XA", [TA, D], F32)
    XB = scr("XB", [TA, D], F32)
    X1 = scr("X1", [TA, D], F32)
    KT1 = scr("KT1", [8, 128, TA], BF16)
    QT1 = scr("QT1", [8, 128, So], BF16)
    V1 = scr("V1", [TA, 1040], BF16)
    ON = [scr("ON%d" % g, [So, 1040], F32) for g in range(3)]
    XA1 = scr("XA1", [So, D], F32)
    XB1 = scr("XB1", [So, D], F32)
    kb = KB(nc)
    phase_A1(kb, S, io["x_all"], io["rope_all"], io["ev_w_in"][0], io["hg_lb_logits"], consts, KT, V, SS, blended=False)
    phase_A2(kb, nA, io["x_A"], io["rope_A"], io["ev_w_in"][0], io["hg_lb_logits"], io["hg_norm_g"], consts, SS, QT, MIX,
             ss_pick=lambda i: (max(i - NCTX, 0), i + nbh - NCTX))
    phase_B(kb, S, nA, KT, V, QT, io["da_lambda"][0], io["da_subln_g"], consts, MIX, keyspec=make_keyspec_ctx(nbh))
    dense_tail(kb, nA, 0, MIX, io["x_A"], io, XA, XB, X1, consts, io["ev_w_out"][0], p_in=io["p0_A"])
    phase_L1A(kb, nbh, X1, io["rope_A"], io["od_w_in"][0], consts, KT1, QT1, V1)
    phase_L1B(kb, So // 2048, KT1, QT1, V1, consts, ON)
    dense_tail(kb, nbh, 1, None, X1[2048:TA, :], io, XA1, XB1, out, consts, io["od_w_out"][0], mix_loader=MergeLoader(ON),
               p_in=io["p1_own"])
    return nc, kb


def fused_inputs(x_b, p_b, weights, h):
    S = x_b.shape[0]
    So = S // 2
    lo = So * h
    m = {n: weights[n] for n in ALL_W}
    m["x_all"] = x_b
    if h == 0:
        m["x_A"] = np.ascontiguousarray(np.concatenate([np.zeros((2048, D), np.float32), x_b[0:So]], axis=0))
        m["p0_A"] = np.ascontiguousarray(np.concatenate([np.zeros((2048, PLE), np.float32), p_b[0, 0:So]], axis=0))
    else:
        m["x_A"] = np.ascontiguousarray(x_b[lo - 2048:lo + So])
        m["p0_A"] = np.ascontiguousarray(p_b[0, lo - 2048:lo + So])
    m["p1_own"] = np.ascontiguousarray(p_b[1, lo:lo + So])
    m["rope_all"] = rope_table(np.arange(S), 8)
    m["rope_A"] = rope_table(np.arange(lo - 2048, lo + So), 8)
    m.update(const_fused(h))
    return m


_PROGS = {}


def _prog(kind, S):
    key = (kind, S)
    if key not in _PROGS:
        _PROGS[key] = (build_l0(S) if kind == 0 else build_l1(S) if kind == 1 else build_fused(S))[0]
    return _PROGS[key]


def kernel(**inputs):
    inputs = {k: np.ascontiguousarray(np.asarray(v, dtype=np.float32)) for k, v in inputs.items()}
    x, p = inputs["x"], inputs["p"]
    B, S, _ = x.shape
    So = S // 2
    n = 2 * B
    weights = {k: v for k, v in inputs.items() if k not in ("x", "p")}
    maps = [fused_inputs(x[c // 2], p[:, c // 2], weights, c % 2) for c in range(n)]
    res = run_bass_kernel_spmd(_prog(2, S), maps, core_ids=list(range(n)))
    out = np.empty((B, S, D), np.float32)
    for c in range(n):
        b, h = c // 2, c % 2
        out[b, h * So:(h + 1) * So] = res.results[c]["out_own"]
    return out
```
